# Optimizing a Trainium2 kernel written in Bass

```python
import math
import jax, jax.numpy as jnp
from jax import lax
import numpy as np

D_MODEL = 1024
BATCH = 8
SEQ = 2048
DEPTH = 4
DEC_BATCH = 128
DEC_SEQ = 1
PAST_LEN = 8192
PAGE_SIZE = 128

F32 = jnp.float32
RET_HEADS = 4
RET_DK = 128
RET_DV = 128
RET_CHUNK = 128
ROPE_BASE = 10000.0
SSD_HEADS = 16
SSD_HEAD_DIM = 64
SSD_GROUPS = 2
SSD_STATE = 128
SSD_CONV = 4
SSD_CHUNK = 128
SSD_INNER = SSD_HEADS * SSD_HEAD_DIM
SSD_CONV_DIM = SSD_INNER + 2 * SSD_GROUPS * SSD_STATE
SWA_Q_HEADS = 8
SWA_KV_HEADS = 2
SWA_HEAD_DIM = 64
WINDOW = 128
SWA_QBLK = 128
N_BUCKETS = 32
BUCKET_MAX_DIST = 128
D_FF = ((8 * D_MODEL // 3 + 255) // 256) * 256
ALPHA = (2 * DEPTH) ** 0.25
BETA = (8 * DEPTH) ** -0.25
EPS = 1e-5
IN_SIZES = (RET_HEADS * RET_DK, RET_HEADS * RET_DK, RET_HEADS * RET_DV, RET_HEADS * RET_DV,
            SSD_INNER, SSD_CONV_DIM, SSD_HEADS,
            SWA_Q_HEADS * SWA_HEAD_DIM, SWA_KV_HEADS * SWA_HEAD_DIM, SWA_KV_HEADS * SWA_HEAD_DIM,
            3 * D_MODEL)
N_IN = sum(IN_SIZES)

kernel_name = 'hybrid_retention_ssd_swa_deepnorm_step'


def _offsets(sizes):
    out, t = [], 0
    for s in sizes[:-1]:
        t += s
        out.append(t)
    return out


def _layer_norm(x, g, b):
    xf = x.astype(F32)
    mu = jnp.mean(xf, -1, keepdims=True)
    var = jnp.mean(jnp.square(xf - mu), -1, keepdims=True)
    return ((xf - mu) * lax.rsqrt(var + EPS) * g.astype(F32) + b.astype(F32)).astype(x.dtype)


def _head_norm(o):
    mu = jnp.mean(o, -1, keepdims=True)
    var = jnp.mean(jnp.square(o - mu), -1, keepdims=True)
    return (o - mu) * lax.rsqrt(var + EPS)


def _group_rms(y, w):
    yg = y.reshape(y.shape[:-1] + (SSD_GROUPS, SSD_INNER // SSD_GROUPS))
    yg = yg * lax.rsqrt(jnp.mean(jnp.square(yg), -1, keepdims=True) + EPS)
    return yg.reshape(y.shape) * w.astype(F32)


def _rotary(x, pos):
    half = x.shape[-1] // 2
    inv = ROPE_BASE ** (-jnp.arange(half, dtype=F32) / half)
    ang = pos[:, None] * inv[None, :]
    cos = jnp.cos(ang)[None, :, None, :]
    sin = jnp.sin(ang)[None, :, None, :]
    xf = x.astype(F32)
    x1, x2 = xf[..., :half], xf[..., half:]
    return jnp.concatenate([x1 * cos - x2 * sin, x1 * sin + x2 * cos], -1)


def _retention(q, k, v, s0):
    bsz, L, H, dk = q.shape
    dv = v.shape[-1]
    c = RET_CHUNK if L % RET_CHUNK == 0 else L
    n = L // c
    lg = jnp.log(1.0 - 2.0 ** (-5.0 - jnp.arange(H, dtype=F32)))
    qc = q.reshape(bsz, n, c, H, dk)
    kc = k.reshape(bsz, n, c, H, dk)
    vc = v.reshape(bsz, n, c, H, dv).astype(F32)
    i = jnp.arange(c, dtype=F32)
    rel = i[:, None] - i[None, :]
    dmat = jnp.where(rel >= 0, jnp.exp(lg[:, None, None] * jnp.maximum(rel, 0.0)), 0.0)
    sc = jnp.einsum('bnihd,bnjhd->bnhij', qc, kc) * dmat
    o_intra = jnp.einsum('bnhij,bnjhv->bnihv', sc, vc)
    kdec = jnp.exp(lg[None, :] * (c - 1 - i)[:, None])
    kv = jnp.einsum('bnjhd,jh,bnjhv->nbhdv', kc, kdec, vc)
    cdec = jnp.exp(lg * c)[None, :, None, None]

    def step(s, kv_c):
        return cdec * s + kv_c, s

    s_fin, s_before = lax.scan(step, s0.astype(F32), kv)
    qdec = jnp.exp(lg[None, :] * (i + 1.0)[:, None])
    o_cross = jnp.einsum('bnihd,nbhdv,ih->bnihv', qc, s_before, qdec)
    return (o_intra + o_cross).reshape(bsz, L, H, dv), s_fin


def _causal_conv(xbc, buf, w, b):
    L = xbc.shape[1]
    ext = jnp.concatenate([buf.astype(xbc.dtype), xbc], axis=1)
    out = sum(w[t] * ext[:, t:t + L] for t in range(SSD_CONV)) + b
    return jax.nn.silu(out), ext[:, -(SSD_CONV - 1):]


def _ssd(x, dt, a, bm, cm, s0):
    bsz, L, H, P = x.shape
    G, N = bm.shape[2], bm.shape[3]
    hg = H // G
    c = SSD_CHUNK if L % SSD_CHUNK == 0 else L
    n = L // c
    xc = x.reshape(bsz, n, c, G, hg, P).astype(F32)
    dtc = dt.reshape(bsz, n, c, G, hg)
    bc = bm.reshape(bsz, n, c, G, N).astype(F32)
    cc = cm.reshape(bsz, n, c, G, N).astype(F32)
    cs = jnp.cumsum(dtc * a.reshape(G, hg), axis=2)
    seg = cs[:, :, :, None] - cs[:, :, None, :]
    causal = jnp.tril(jnp.ones((c, c), dtype=bool))[:, :, None, None]
    lmat = jnp.exp(jnp.where(causal, seg, -jnp.inf))
    cb = jnp.einsum('bcigs,bcjgs->bcijg', cc, bc)
    wmat = cb[..., None] * lmat * dtc[:, :, None]
    y_intra = jnp.einsum('bcijgh,bcjghp->bcighp', wmat, xc)
    dec_end = jnp.exp(cs[:, :, -1:] - cs)
    st = jnp.einsum('bcjgs,bcjgh,bcjghp->cbghps', bc, dec_end * dtc, xc)
    chunk_dec = jnp.exp(cs[:, :, -1]).transpose(1, 0, 2, 3)

    def step(s, inp):
        st_c, dec_c = inp
        return dec_c[..., None, None] * s + st_c, s

    s_fin, s_before = lax.scan(step, s0.astype(F32).reshape(bsz, G, hg, P, N), (st, chunk_dec))
    y_cross = jnp.einsum('bcigs,cbghps,bcigh->bcighp', cc, s_before, jnp.exp(cs))
    return (y_intra + y_cross).reshape(bsz, L, H, P), s_fin.reshape(bsz, H, P, N)


def _t5_bucket(dist):
    max_exact = N_BUCKETS // 2
    df = jnp.maximum(dist, 1).astype(F32)
    large = max_exact + (jnp.log(df / max_exact) / math.log(BUCKET_MAX_DIST / max_exact)
                         * (N_BUCKETS - max_exact)).astype(jnp.int32)
    large = jnp.minimum(large, N_BUCKETS - 1)
    return jnp.where(dist < max_exact, dist, large)


def _swa(q, k, v, kbuf, vbuf, buf_valid, sinks, rel_bias):
    bsz, L = q.shape[:2]
    G = SWA_Q_HEADS // SWA_KV_HEADS
    qs = SWA_QBLK if L % SWA_QBLK == 0 else L
    nb = L // qs
    k_ext = jnp.concatenate([kbuf.astype(k.dtype), k], axis=1)
    v_ext = jnp.concatenate([vbuf.astype(v.dtype), v], axis=1)
    kidx = (jnp.arange(nb) * qs)[:, None] + jnp.arange(qs + WINDOW)[None, :]
    kb = k_ext[:, kidx]
    vb = v_ext[:, kidx]
    qb = q.reshape(bsz, nb, qs, SWA_KV_HEADS, G, SWA_HEAD_DIM)
    s = jnp.einsum('bnqhgd,bnkhd->bnhgqk', qb, kb).astype(F32) * SWA_HEAD_DIM ** -0.5
    dist = jnp.arange(qs)[:, None] + WINDOW - jnp.arange(qs + WINDOW)[None, :]
    band = (dist >= 0) & (dist <= WINDOW)
    key_ok = (kidx >= WINDOW) | buf_valid
    mask = band[None] & key_ok[:, None, :]
    bias = rel_bias[_t5_bucket(jnp.maximum(dist, 0))].astype(F32)
    bias = bias.transpose(2, 0, 1).reshape(SWA_KV_HEADS, G, qs, qs + WINDOW)
    logits = jnp.where(mask[None, :, None, None], s + bias, -1e30)
    sink = jnp.broadcast_to(sinks.astype(F32).reshape(1, 1, SWA_KV_HEADS, G, 1, 1), logits.shape[:-1] + (1,))
    p = jax.nn.softmax(jnp.concatenate([logits, sink], axis=-1), axis=-1)[..., :-1]
    o = jnp.einsum('bnhgqk,bnkhd->bnqhgd', p, vb.astype(F32))
    o = o.reshape(bsz, L, SWA_Q_HEADS * SWA_HEAD_DIM).astype(q.dtype)
    return o, k_ext[:, -WINDOW:], v_ext[:, -WINDOW:]


def _layer(x, pos, s_ret, s_ssm, s_conv, kbuf, vbuf, buf_valid, rel_bias, p):
    (w_in, conv_w, conv_b, dt_bias, a_log, d_skip, ssd_norm_w, sinks,
     w_br_ret, w_br_ssd, w_br_swa, w_out, ln1_g, ln1_b, ln2_g, ln2_b,
     w_ffn_gate, w_ffn_up, w_ffn_down) = p
    bsz, L, _ = x.shape
    u = x @ w_in
    (q_r, k_r, v_r, g_r, z, xbc, dt_raw, q_c, k_c, v_c, gates) = jnp.split(u, _offsets(IN_SIZES), axis=-1)

    q_r = _rotary(q_r.reshape(bsz, L, RET_HEADS, RET_DK), pos)
    k_r = _rotary(k_r.reshape(bsz, L, RET_HEADS, RET_DK), pos) * RET_DK ** -0.5
    o_r, s_ret_new = _retention(q_r, k_r, v_r.reshape(bsz, L, RET_HEADS, RET_DV), s_ret)
    o_r = _head_norm(o_r).reshape(bsz, L, RET_HEADS * RET_DV) * jax.nn.silu(g_r.astype(F32))
    br_a = o_r.astype(x.dtype) @ w_br_ret

    xbc, s_conv_new = _causal_conv(xbc, s_conv, conv_w, conv_b)
    xs, bm, cm = jnp.split(xbc, [SSD_INNER, SSD_INNER + SSD_GROUPS * SSD_STATE], axis=-1)
    xs = xs.reshape(bsz, L, SSD_HEADS, SSD_HEAD_DIM)
    dt = jax.nn.softplus(dt_raw.astype(F32) + dt_bias.astype(F32))
    a = -jnp.exp(a_log.astype(F32))
    y, s_ssm_new = _ssd(xs, dt, a, bm.reshape(bsz, L, SSD_GROUPS, SSD_STATE),
                        cm.reshape(bsz, L, SSD_GROUPS, SSD_STATE), s_ssm)
    y = y + d_skip.astype(F32)[:, None] * xs.astype(F32)
    y = y.reshape(bsz, L, SSD_INNER) * jax.nn.silu(z.astype(F32))
    br_b = _group_rms(y, ssd_norm_w).astype(x.dtype) @ w_br_ssd

    o_c, kbuf_new, vbuf_new = _swa(q_c.reshape(bsz, L, SWA_Q_HEADS, SWA_HEAD_DIM),
                                   k_c.reshape(bsz, L, SWA_KV_HEADS, SWA_HEAD_DIM),
                                   v_c.reshape(bsz, L, SWA_KV_HEADS, SWA_HEAD_DIM),
                                   kbuf, vbuf, buf_valid, sinks, rel_bias)
    br_c = o_c @ w_br_swa

    g_a, g_b, g_c = jnp.split(jax.nn.sigmoid(gates), 3, axis=-1)
    mix = (g_a * br_a + g_b * br_b + g_c * br_c) @ w_out
    x = _layer_norm(ALPHA * x + mix, ln1_g, ln1_b)
    ffn = (jax.nn.silu(x @ w_ffn_gate) * (x @ w_ffn_up)) @ w_ffn_down
    x = _layer_norm(ALPHA * x + ffn, ln2_g, ln2_b)
    return x, (s_ret_new.astype(s_ret.dtype), s_ssm_new.astype(s_ssm.dtype),
               s_conv_new.astype(s_conv.dtype), kbuf_new.astype(kbuf.dtype), vbuf_new.astype(vbuf.dtype))


def _trunk(x, pos, states, buf_valid, rel_bias, params):
    new = [[], [], [], [], []]
    for l in range(DEPTH):
        p = tuple(w[l] for w in params)
        x, st = _layer(x, pos, states[0][l], states[1][l], states[2][l], states[3][l], states[4][l],
                       buf_valid, rel_bias, p)
        for j in range(5):
            new[j].append(st[j])
    return x, [jnp.stack(s) for s in new]


def setup_inputs(seed: int = 0) -> dict:
    key = jax.random.key(seed)
    ks = jax.random.split(key, 32)

    def nrm(k, shape, scale):
        return jax.random.normal(k, shape, F32) * scale

    dt0 = jnp.exp(jax.random.uniform(ks[10], (DEPTH, SSD_HEADS), F32, math.log(1e-3), math.log(1e-1)))
    return {
        'x_prompt': nrm(ks[0], (BATCH, SEQ, D_MODEL), 1.0),
        'x_sample': nrm(ks[1], (DEC_BATCH, DEC_SEQ, D_MODEL), 1.0),
        'state_ret': nrm(ks[2], (DEPTH, DEC_BATCH, RET_HEADS, RET_DK, RET_DV), 1.0),
        'state_ssm': nrm(ks[3], (DEPTH, DEC_BATCH, SSD_HEADS, SSD_HEAD_DIM, SSD_STATE), 0.5),
        'state_conv': nrm(ks[4], (DEPTH, DEC_BATCH, SSD_CONV - 1, SSD_CONV_DIM), 1.0),
        'cache_swa_k': nrm(ks[5], (DEPTH, DEC_BATCH, WINDOW, SWA_KV_HEADS, SWA_HEAD_DIM), 1.0),
        'cache_swa_v': nrm(ks[6], (DEPTH, DEC_BATCH, WINDOW, SWA_KV_HEADS, SWA_HEAD_DIM), 1.0),
        'w_in': nrm(ks[7], (DEPTH, D_MODEL, N_IN), D_MODEL ** -0.5),
        'conv_w': nrm(ks[8], (DEPTH, SSD_CONV, SSD_CONV_DIM), SSD_CONV ** -0.5),
        'conv_b': nrm(ks[9], (DEPTH, SSD_CONV_DIM), 0.02),
        'dt_bias': dt0 + jnp.log(-jnp.expm1(-dt0)),
        'a_log': jnp.log(jax.random.uniform(ks[11], (DEPTH, SSD_HEADS), F32, 1.0, 16.0)),
        'd_skip': 1.0 + nrm(ks[12], (DEPTH, SSD_HEADS), 0.1),
        'ssd_norm_w': 1.0 + nrm(ks[13], (DEPTH, SSD_INNER), 0.05),
        'sinks': nrm(ks[14], (DEPTH, SWA_Q_HEADS), 0.5),
        'rel_bias': nrm(ks[15], (N_BUCKETS, SWA_Q_HEADS), 0.2),
        'w_br_ret': nrm(ks[16], (DEPTH, RET_HEADS * RET_DV, D_MODEL), (RET_HEADS * RET_DV) ** -0.5 * BETA),
        'w_br_ssd': nrm(ks[17], (DEPTH, SSD_INNER, D_MODEL), SSD_INNER ** -0.5 * BETA),
        'w_br_swa': nrm(ks[18], (DEPTH, SWA_Q_HEADS * SWA_HEAD_DIM, D_MODEL), (SWA_Q_HEADS * SWA_HEAD_DIM) ** -0.5 * BETA),
        'w_out': nrm(ks[19], (DEPTH, D_MODEL, D_MODEL), D_MODEL ** -0.5 * BETA),
        'ln1_g': 1.0 + nrm(ks[20], (DEPTH, D_MODEL), 0.05),
        'ln1_b': nrm(ks[21], (DEPTH, D_MODEL), 0.02),
        'ln2_g': 1.0 + nrm(ks[22], (DEPTH, D_MODEL), 0.05),
        'ln2_b': nrm(ks[23], (DEPTH, D_MODEL), 0.02),
        'w_ffn_gate': nrm(ks[24], (DEPTH, D_MODEL, D_FF), D_MODEL ** -0.5),
        'w_ffn_up': nrm(ks[25], (DEPTH, D_MODEL, D_FF), D_MODEL ** -0.5 * BETA),
        'w_ffn_down': nrm(ks[26], (DEPTH, D_FF, D_MODEL), D_FF ** -0.5 * BETA),
    }


def reference(x_prompt, x_sample, state_ret, state_ssm, state_conv, cache_swa_k, cache_swa_v,
              w_in, conv_w, conv_b, dt_bias, a_log, d_skip, ssd_norm_w, sinks, rel_bias,
              w_br_ret, w_br_ssd, w_br_swa, w_out, ln1_g, ln1_b, ln2_g, ln2_b,
              w_ffn_gate, w_ffn_up, w_ffn_down):
    params = (w_in, conv_w, conv_b, dt_bias, a_log, d_skip, ssd_norm_w, sinks,
              w_br_ret, w_br_ssd, w_br_swa, w_out, ln1_g, ln1_b, ln2_g, ln2_b,
              w_ffn_gate, w_ffn_up, w_ffn_down)
    bsz, dtp = x_prompt.shape[0], x_prompt.dtype
    empty = (jnp.zeros((DEPTH, bsz) + state_ret.shape[2:], dtp),
             jnp.zeros((DEPTH, bsz) + state_ssm.shape[2:], dtp),
             jnp.zeros((DEPTH, bsz) + state_conv.shape[2:], dtp),
             jnp.zeros((DEPTH, bsz) + cache_swa_k.shape[2:], dtp),
             jnp.zeros((DEPTH, bsz) + cache_swa_v.shape[2:], dtp))
    pos_p = jnp.arange(x_prompt.shape[1], dtype=F32)
    y_prompt, st_p = _trunk(x_prompt, pos_p, empty, False, rel_bias, params)
    ret_p, ssm_p, conv_p, k_p, v_p = st_p
    pos_s = PAST_LEN + jnp.arange(x_sample.shape[1], dtype=F32)
    y_sample, st_s = _trunk(x_sample, pos_s, (state_ret, state_ssm, state_conv, cache_swa_k, cache_swa_v),
                            True, rel_bias, params)
    ret_s, ssm_s, conv_s, k_s, v_s = st_s
    return (y_prompt, y_sample, ret_p, ssm_p, conv_p, k_p, v_p, ret_s, ssm_s, conv_s, k_s, v_s)
```

```python
import math
import numpy as np
import ml_dtypes
from concourse.bass_utils import run_bass_kernel_spmd
import concourse.bass as bass
import concourse.mybir as mybir

F32 = mybir.dt.float32
BF16 = mybir.dt.bfloat16
I32 = mybir.dt.int32
AF = mybir.ActivationFunctionType
ALU = mybir.AluOpType
AX = mybir.AxisListType


class Dep:
    __slots__ = ("w", "r")

    def __init__(self):
        self.w = None
        self.r = {}


class Tile:
    __slots__ = ("h", "deps", "name", "psum")

    def __init__(self, h, deps=None, name=None):
        self.psum = False
        self.h = h
        self.deps = deps if deps is not None else [Dep()]
        self.name = name

    def __getitem__(self, k):
        return self.h[k]

    def sub(self, i):
        return Tile(self.h, [self.deps[i]], self.name)


class KB:
    ENG = ("pe", "act", "dve", "pool", "sp")

    def __init__(self, n_dma_sems=30):
        nc = bass.Bass("TRN2", target_bir_lowering=False)
        self.nc = nc
        self.eng = {"pe": nc.tensor, "act": nc.scalar, "dve": nc.vector, "pool": nc.gpsimd, "sp": nc.sync}
        self.sems = {}
        self.tick = {}
        for e in self.ENG:
            self.sems[e] = nc.alloc_semaphore("s_" + e)
            self.tick[e] = 0
        self.known = {e: {} for e in self.ENG}
        self.dpool = {}
        for q in ("sp", "pool", "act"):
            lst = []
            for i in range(n_dma_sems):
                key = "d_%s_%d" % (q, i)
                self.sems[key] = nc.alloc_semaphore(key)
                self.tick[key] = 0
                lst.append(key)
            self.dpool[q] = [lst, 0]
        self.n_ins = 0
        self.n_wait = 0
        self.strict = True
        self.attach_waits = True
        self.phase = 'init'
        self.pe_log = []
        self.out_events = []

    def sb(self, name, shape, dt, ncell=1):
        h = self.nc.alloc_sbuf_tensor(name, list(shape), dt)
        return Tile(h, [Dep() for _ in range(ncell)], name)

    def ps(self, name, shape=(128, 512), dt=F32):
        h = self.nc.alloc_psum_tensor(name, list(shape), dt)
        t = Tile(h, None, name)
        t.psum = True
        return t

    def dram(self, name, shape, dt, kind):
        h = self.nc.dram_tensor(name, list(shape), dt, kind=kind)
        return h

    def _needs(self, reads, writes, e=None):
        needs = {}

        def add(ev):
            if ev is None:
                return
            k, v = ev
            if needs.get(k, 0) < v:
                needs[k] = v

        for t in reads:
            for d in t.deps:
                add(d.w)
                if t.psum:
                    for k, v in d.r.items():
                        if k != e:
                            add((k, v))
        for t in writes:
            for d in t.deps:
                if d.w is not None and (self.strict or d.w[0] != e):
                    add(d.w)
                for k, v in d.r.items():
                    if self.strict or k != e:
                        add((k, v))
        return needs

    def _emit_waits(self, e, needs, attach=False):
        kn = self.known[e]
        eo = self.eng[e]
        todo = []
        for k, v in needs.items():
            if e == "pe" and k == "pe":
                continue
            if kn.get(k, 0) < v:
                todo.append((k, v))
                kn[k] = v
        held = None
        if attach and todo and self.attach_waits:
            held = todo.pop()
        for k, v in todo:
            eo.wait_ge(self.sems[k], v)
            self.n_wait += 1
        return held

    def _attach(self, ins, held):
        if held is not None:
            ins._wait_ge(self.sems[held[0]], held[1])

    def _record(self, ev, reads, writes):
        k, v = ev
        for t in reads:
            for d in t.deps:
                d.r[k] = v
        for t in writes:
            for d in t.deps:
                d.w = ev
                d.r = {}

    def op(self, e, fn, reads=(), writes=()):
        needs = self._needs(reads, writes, e)
        held = self._emit_waits(e, needs, attach=True)
        ins = fn(self.eng[e])
        self._attach(ins, held)
        self.tick[e] += 1
        ins.then_inc(self.sems[e], 1)
        self._record((e, self.tick[e]), reads, writes)
        self.n_ins += 1
        return ins

    def mm(self, fns, reads=(), writes=()):
        needs = self._needs(reads, writes, "pe")
        held = self._emit_waits("pe", needs, attach=True)
        ins = None
        for fn in fns:
            ins = fn(self.eng["pe"])
            if held is not None:
                self._attach(ins, held)
                held = None
            self.n_ins += 1
            self.pe_log.append(self.phase)
        self.tick["pe"] += 1
        ins.then_inc(self.sems["pe"], 1)
        self._record(("pe", self.tick["pe"]), reads, writes)
        return ins

    def dma(self, q, out_ap, in_ap, reads=(), writes=(), is_output=False, **kw):
        lst, idx = self.dpool[q]
        key = lst[idx % len(lst)]
        self.dpool[q][1] = idx + 1
        needs = self._needs(reads, writes)
        if self.tick[key] > 0:
            if needs.get(key, 0) < self.tick[key]:
                needs[key] = self.tick[key]
        held = self._emit_waits(q, needs, attach=True)
        ins = self.eng[q].dma_start(out=out_ap, in_=in_ap, **kw)
        self._attach(ins, held)
        self.tick[key] += 16
        ins.then_inc(self.sems[key], 16)
        ev = (key, self.tick[key])
        self._record(ev, reads, writes)
        if is_output:
            self.out_events.append(ev)
        self.n_ins += 1
        return ins

    def finish(self):
        needs = {}
        for k, v in self.out_events:
            if needs.get(k, 0) < v:
                needs[k] = v
        for q in self.dpool:
            for key in self.dpool[q][0]:
                if self.tick[key] > 0:
                    needs[key] = max(needs.get(key, 0), self.tick[key])
        for e in ("pe", "act", "dve", "pool"):
            if self.tick[e] > 0:
                needs[e] = self.tick[e]
        self._emit_waits("sp", needs)


D = 1024
NIN = 8464
DFF = 2816
ALPHA = 8 ** 0.25
EPS = 1e-5
TB = 512
NEG = -30000.0
OQ, OK_, OV, OG, OZ, OXBC, ODT, OQC, OKC, OVC, OGATE = 0, 512, 1024, 1536, 2048, 3072, 4608, 4624, 5136, 5264, 5392


class Gen:
    def __init__(self, depth=4, nblk=4, do_sample=True, dbg=None):
        self.depth, self.nblk, self.do_sample = depth, nblk, do_sample
        self.kb = kb = KB()
        self.nc = nc = kb.nc
        self.L = nblk * TB
        self.dbg = dbg or {}
        self.outs = []
        self.ins = {}
        self._decl()
        self._alloc()
        self._setup_consts()
        if self.dbg.get("setup_only"):
            kb.finish()
            return
        for bi in self.dbg.get('blocks', range(nblk)):
            self.block(bi)
        if self.do_sample and hasattr(self, "sample_pass"):
            self.sample_pass()
        kb.finish()

    def din(self, name, shape, dt=F32):
        h = self.nc.dram_tensor(name, list(shape), dt, kind="ExternalInput")
        self.ins[name] = (tuple(shape), dt)
        return h

    def dout(self, name, shape, dt=F32):
        h = self.nc.dram_tensor(name, list(shape), dt, kind="ExternalOutput")
        self.outs.append(name)
        return h

    def _decl(self):
        dp, L = self.depth, self.L
        d = self.din
        self.xp = d("xp", [L, D])
        self.w_in = d("w_in", [dp * D, NIN])
        self.conv_w = d("conv_w", [dp, 128, 12, 4])
        self.conv_b = d("conv_b", [dp, 128, 12])
        self.dt_bias = d("dt_bias", [dp, 16])
        self.a_log = d("a_log", [dp, 16])
        self.d_skip = d("d_skip", [dp, 16])
        self.norm_w = d("ssd_norm_w", [dp, 1024])
        self.sinks = d("sinks", [dp, 8])
        self.rel_bias = d("rel_bias", [1, 256])
        self.w_br_ret = d("w_br_ret", [dp * 512, D])
        self.w_br_ssd = d("w_br_ssd", [dp * 1024, D])
        self.w_br_swa = d("w_br_swa", [dp * 512, D])
        self.w_out = d("w_out", [dp * D, D])
        self.lnp_d = d("lnp", [dp, 128, 4, 8])
        self.w_gate = d("w_ffn_gate", [dp * D, DFF])
        self.w_up = d("w_ffn_up", [dp * D, DFF])
        self.w_down = d("w_ffn_down", [dp * 8 * 128, DFF])
        self.c_rot = d("c_rot", [L, 256])
        self.c_f32 = d("c_f32", [128, 128 * 4 + 4 * 128 + 8 + 2 * 128])
        self.c_oh = d("c_oh", [128, 2 * 128 * 32], BF16)
        o = self.dout
        self.yp = o("yp", [L, D])
        self.ret_p = o("ret_p", [dp, 4, 128, 128])
        self.ssm_p = o("ssm_p", [dp, 1024, 128])
        self.conv_p = o("conv_p", [dp, 3, 1536])
        self.k_p = o("k_p", [dp, 128, 128])
        self.v_p = o("v_p", [dp, 128, 128])

    def _alloc(self):
        kb, dp = self.kb, self.depth
        sb = kb.sb
        self.xr_off = self.nc.bump_sbuf(16384 + 8192)[0]
        self.xr = Tile(self.nc.alloc_sbuf_tensor_at("xr", [128, 8, TB], F32, offset=self.xr_off), None, "xr")
        self.xb = Tile(self.nc.alloc_sbuf_tensor_at("xb", [128, 8, TB], BF16, offset=self.xr_off + 16384), None, "xb")
        self.NS = 4
        self.wr = [sb("wr%d" % i, [128, 4096], BF16) for i in range(self.NS)]
        self.wi = 0
        self.wcache = {}
        self.use_wcache = self.dbg.get('wcache', False)
        self.psf = [kb.ps("psf%d" % i, [128, 512], F32) for i in range(6)]
        self.psb = [kb.ps("psb%d" % i, [128, 1024], BF16) for i in range(2)]
        self.pfi = 0
        self.pbi = 0
        self.retS = [sb("retS%d" % l, [128, 4, 128], F32) for l in range(dp)]
        self.ssmS = [sb("ssmS%d" % l, [128, 1024], F32) for l in range(dp)]
        self.hist = [sb("hist%d" % l, [128, 12, 3], F32) for l in range(dp)]
        self.kprev = [sb("kprev%d" % l, [128, 128], BF16) for l in range(dp)]
        self.vprev = [sb("vprev%d" % l, [128, 2, 80], BF16) for l in range(dp)]
        self.cf = sb("cf", [128, 128 * 4 + 4 * 128 + 8 + 2 * 128], F32)
        self.id_b = sb("id_b", [128, 128], BF16)
        self.ones_b = sb("ones_b", [128, 128], BF16)
        self.negT = sb("negT", [128, 128], BF16)
        self.btab = sb("btab", [128, 2, 8, 128], F32)
        self.rot = sb("rot", [128, 4, 256], F32)
        self.cw = sb("cw", [128, 12, 4], F32); self.cb = sb("cb", [128, 12], F32)
        self.lnp = sb("lnp_s", [128, 4, 8], F32)
        self.sm16 = sb("sm16", [128, 3, 16], F32)
        self.esink = sb("esink", [128, 8], F32)
        self.orT = sb("orT", [128, 4, TB], BF16)
        self.yT = sb("yT", [128, 8, TB], BF16)
        self.ocT = sb("ocT", [128, 4, TB], BF16)
        self.mrg = sb("mrg", [128, 8, TB], BF16)
        self.xtok = [sb("xtok0", [128, 1024], F32)]
        self.xti = 0
        nc = self.nc
        ASZ = 64 * 1024
        self.abase = nc.bump_sbuf(ASZ)[0]
        self.asz = ASZ
        st = {"off": 0}

        def begin():
            st["off"] = 0

        def al(name, shape, dt):
            n = 1
            for x in shape[1:]:
                n *= x
            nb = n * (4 if dt == F32 else 2)
            nb = (nb + 31) // 32 * 32
            assert st["off"] + nb <= ASZ, (name, st["off"], nb)
            h = nc.alloc_sbuf_tensor_at(name, list(shape), dt, offset=self.abase + st["off"])
            st["off"] += nb
            return Tile(h, None, name)
        begin()
        self.s_oh = al("s_oh", [128, 8192], BF16); self.s_prod = al("s_prod", [128, 4096], F32); self.s_rb = al("s_rb", [128, 256], F32)
        begin()
        self.qrot = al("qrot", [128, 512], BF16); self.krot = al("krot", [128, 512], BF16)
        self.rtA = al("rtA", [128, 4, 2, 64], F32); self.rtB = al("rtB", [128, 4, 2, 64], F32)
        self.rtA2 = al("rtA2", [128, 4, 2, 64], F32); self.rtB2 = al("rtB2", [128, 4, 2, 64], F32); self.kraw = al("kraw", [128, 512], F32)
        self.v_r = al("v_r", [128, 4, 512], BF16)
        self.gsil = al("gsil", [128, 4, 512], F32)
        self.qT = al("qT", [128, 4, TB], BF16); self.kT = al("kT", [128, 4, TB], BF16)
        self.kdk = al("kdk", [128, 4, 4, 128], BF16)
        self.scm = al("scm", [128, 4, 128], BF16)
        self.otmp = al("otmp", [128, 4, 128], F32); self.o_r = al("o_r", [128, 4, 128], F32)
        self.st6 = al("st6", [128, 4, 6], F32); self.mv = al("mv", [128, 4, 2], F32)
        self.rs4 = al("rs4", [128, 4], F32); self.rs4b = al("rs4b", [128, 4], F32)
        self.og = al("og", [128, 4, 128], F32); self.ogb = al("ogb", [128, 512], BF16)
        self.retSb = al("retSb", [128, 4, 128], BF16)
        print("arena A", st["off"])
        begin()
        self.normw = al("normw", [128, 1024], F32)
        self.zs = al("zs", [128, 1024], F32)
        self._xbcT_off = self.abase + st["off"]
        self.xbcT = al("xbcT", [128, 4, 3 + TB], F32)
        self.cacc = al("cacc", [128, TB], F32)
        self.xsT = al("xsT", [128, 8, TB], BF16)
        self.BT = al("BT", [128, 2, TB], BF16); self.CT = al("CT", [128, 2, TB], BF16)
        self.dts = al("dts", [128, 4, 8, 16], F32)
        self.cshl = al("cshl", [128, 4, 2, 16], BF16)
        self.xs_tok = al("xs_tok", [128, 1024], BF16); self.xw = al("xw", [128, 1024], BF16); self.xD = al("xD", [128, 1024], BF16)
        self.B_tok = al("B_tok", [128, 256], BF16)
        self.cbT = al("cbT", [128, 2, 128], F32)
        self.Dhl = al("Dhl", [128, 2, 4, 128], BF16)
        self.Ep = al("Ep", [128, 4, 128], F32)
        self.wmT = al("wmT", [128, 16, 128], BF16)
        self.ytmp = al("ytmp", [128, 1024], F32)
        self.ssq = al("ssq", [128, 4], F32)
        self.ynb = al("ynb", [128, 1024], BF16)
        self.stmp = al("stmp", [128, 1024], F32)
        self.ssmSb = al("ssmSb", [128, 1024], BF16)
        print("arena B", st["off"])
        self.Dhl2 = [self.Dhl, Tile(nc.alloc_sbuf_tensor_at("Dhl_b", [128, 2, 4, 128], BF16, offset=self._xbcT_off), [Dep()], "Dhl_b")]
        self.Ep2 = [self.Ep, Tile(nc.alloc_sbuf_tensor_at("Ep_b", [128, 4, 128], F32, offset=self._xbcT_off + 2048), [Dep()], "Ep_b")]
        begin()
        self.qcT = al("qcT", [128, 4, TB], BF16)
        self.kTe = al("kTe", [128, 128 + TB], BF16)
        self.vaug = al("vaug", [128, 5, 2, 80], BF16)
        self.lg = al("lg", [128, 4, 128], F32)
        self.pT = al("pT", [128, 2, 2, 4, 128], BF16)
        self.den = al("den", [128, 8], F32); self.rden = al("rden", [128, 8], F32)
        self.oc_tok = al("oc_tok", [128, 8, 64], BF16)
        self.kvo = al("kvo", [128, 2, 128], F32)
        print("arena C", st["off"])
        begin()
        self.hT = al("hT", [128, 22, TB], BF16)
        self.sg = al("sg", [128, TB], F32); self.gt = al("gt", [128, TB], F32)
        self.ysq = al("ysq", [128, 8, TB], BF16)
        self.lnm = al("lnm", [128, TB], F32); self.lnr = al("lnr", [128, TB], F32); self.lnt = al("lnt", [128, TB], F32)
        self.lnu = al("lnu", [128, TB], F32); self.lnu2 = al("lnu2", [128, TB], F32)
        print("arena DEF", st["off"])
        self.macc = Tile(nc.alloc_sbuf_tensor_at("macc", [128, 8, TB], F32, offset=self.abase), None, "macc")

    def fence(self):
        kb = self.kb
        needs = {e: kb.tick[e] for e in ("pe", "act", "dve", "pool") if kb.tick[e] > 0}
        for key in kb.dpool["sp"][0]:
            if kb.tick[key] > 0:
                needs[key] = kb.tick[key]
        for e in ("act", "dve", "sp"):
            kb._emit_waits(e, dict(needs))

    def nps(self):
        t = self.psf[self.pfi % len(self.psf)]
        self.pfi += 1
        return t

    def npb(self):
        t = self.psb[self.pbi % len(self.psb)]
        self.pbi += 1
        return t

    def wload(self, src, kc, n):
        src3, key = src
        t = self.wr[self.wi % self.NS]
        self.wi += 1
        flat = t[:, 0:kc * n]
        v = flat.rearrange("p (k c) -> p k c", k=kc)
        ent = self.wcache.get(key) if self.use_wcache else None
        if ent is None:
            self.kb.dma("pool", v, src3, writes=[t])
            if self.use_wcache:
                h = self.nc.dram_tensor("wc%d" % len(self.wcache), [128, kc * n], BF16, kind="Internal")
                tl = Tile(h, None, "wc")
                self.wcache[key] = (h, tl)
                self.kb.dma("pool", h.ap()[:, :], flat, reads=[t], writes=[tl])
        else:
            h, tl = ent
            self.kb.dma(self.dbg.get("wq", "sp"), flat, h.ap()[:, :], reads=[tl], writes=[t])
        return t, v

    def wsrc_down(self, l, m):
        r0 = (l * 8 + m) * 128
        return (self.w_down.ap()[r0:r0 + 128, :].rearrange("p (k c) -> p k c", k=22), ("w_down", l, m))

    def wsrc(self, w, l, K, c0, n):
        return (w.ap()[l * K:(l + 1) * K, c0:c0 + n].rearrange("(k p) c -> p k c", p=128), (w.name, l, c0, n))

    def _setup_consts(self):
        kb = self.kb
        cf = self.cf
        kb.dma("sp", cf[:], self.c_f32.ap()[:, :], writes=[cf])
        self.ident_f = cf[:, 0:128]
        self.tri_f = cf[:, 128:256]
        self.ones_f = cf[:, 256:384]
        o = 512
        self.dmatT = cf[:, o:o + 512].rearrange("p (h i) -> p h i", h=4)
        self.kdec = cf[:, o + 512:o + 516]
        self.qdec = cf[:, o + 516:o + 520]
        self.mask01 = cf[:, o + 520:o + 520 + 256].rearrange("p (f q) -> p f q", f=2)
        kb.op("act", lambda e: e.copy(out=self.id_b[:], in_=self.ident_f), reads=[cf], writes=[self.id_b])
        kb.op("act", lambda e: e.copy(out=self.ones_b[:], in_=self.ones_f), reads=[cf], writes=[self.ones_b])
        kb.op("act", lambda e: e.copy(out=self.negT[:], in_=self.mask01[:, 1, :]), reads=[cf], writes=[self.negT])
        for l in range(self.depth):
            for t in (self.retS[l], self.ssmS[l], self.hist[l]):
                kb.op("pool", lambda e, t=t: e.memset(t[:], 0.0), writes=[t])
        oh = self.s_oh
        ohv = oh[:, :]
        kb.dma("sp", ohv, self.c_oh.ap()[:, :], writes=[oh])
        rb = self.s_rb
        kb.dma("sp", rb[:, 0:256], self.rel_bias.ap().partition_broadcast(128), writes=[rb])
        ohq = ohv.rearrange("p (f q b) -> p f q b", f=2, q=128)
        prod = self.s_prod
        pv = prod[:, :].rearrange("p (q b) -> p q b", b=32)
        rbv = rb[:, 0:256].rearrange("p (b h) -> p b h", h=8)
        for hf in range(2):
            for h in range(8):
                kb.op("dve", lambda e, hf=hf, h=h: e.tensor_tensor(
                    out=pv, in0=ohq[:, hf, :, :], in1=rbv[:, :, h].unsqueeze(1).broadcast_to([128, 128, 32]), op=ALU.mult),
                    reads=[oh, rb], writes=[prod])
                kb.op("dve", lambda e, hf=hf, h=h: e.tensor_reduce(out=self.btab[:, hf, h, :], in_=pv, axis=AX.X, op=ALU.add),
                      reads=[prod], writes=[self.btab])
            kb.op("dve", lambda e, hf=hf: e.tensor_tensor(
                out=self.btab[:, hf, :, :], in0=self.btab[:, hf, :, :],
                in1=self.mask01[:, hf, :].unsqueeze(1).broadcast_to([128, 8, 128]), op=ALU.add),
                reads=[cf, self.btab], writes=[self.btab])

    def block(self, bi):
        if not self.dbg.get('noload'): self.load_x(bi)
        for l in range(self.depth):
            self.layer(bi, l)
        if not self.dbg.get('nostore'): self.store_y(bi)

    def load_x(self, bi):
        self.kb.phase = 'load_x'
        kb = self.kb
        if not (self.dbg.get('norot2') and getattr(self, '_lx', 0) >= 1): kb.dma("sp", self.rot[:], self.c_rot.ap()[bi * TB:(bi + 1) * TB, :].rearrange("(c p) f -> p c f", p=128), writes=[self.rot])
        self._lx = getattr(self, '_lx', 0) + 1
        for c in range(4 if self._lx == 1 else self.dbg.get('nchunk', 4)):
            xt = self.xtok[0]; self.xti += 1
            r0 = bi * TB + c * 128
            kb.dma(self.dbg.get("xq", "sp"), xt[:], self.xp.ap()[r0:r0 + 128, :], writes=[xt])
            for half in range(2):
                ps = self.nps()
                kb.mm([lambda pe, j=j, half=half, xt=xt, ps=ps: pe.transpose(
                    out=ps[:, j * 128:(j + 1) * 128], in_=xt[:, (half * 4 + j) * 128:(half * 4 + j + 1) * 128], identity=self.ident_f)
                    for j in range(4)], reads=[xt, self.cf], writes=[ps])
                pv = ps[:, :].rearrange("p (j t) -> p j t", j=4)
                if self._lx > 1 and self.dbg.get('nocopy'):
                    continue
                kb.op("act", lambda e, half=half, c=c, pv=pv: e.copy(out=self.xr[:, half * 4:half * 4 + 4, c * 128:(c + 1) * 128], in_=pv),
                      reads=[ps], writes=[self.xr])
                kb.op("dve", lambda e, half=half, c=c: e.tensor_copy(out=self.xb[:, half * 4:half * 4 + 4, c * 128:(c + 1) * 128],
                                                                      in_=self.xr[:, half * 4:half * 4 + 4, c * 128:(c + 1) * 128]),
                      reads=[self.xr], writes=[self.xb])

    def store_y(self, bi):
        self.kb.phase = 'store_y'
        kb = self.kb
        for c in range(4):
            xt = self.xtok[0]; self.xti += 1
            for half in range(2):
                ps = self.nps()
                kb.mm([lambda pe, j=j, half=half, c=c, ps=ps: pe.transpose(
                    out=ps[:, j * 128:(j + 1) * 128], in_=self.xr[:, half * 4 + j, c * 128:(c + 1) * 128], identity=self.ident_f)
                    for j in range(4)], reads=[self.xr, self.cf], writes=[ps])
                kb.op("act" if half else "dve",
                      (lambda e, half=half, xt=xt, ps=ps: e.copy(out=xt[:, half * 512:(half + 1) * 512], in_=ps[:, :])) if half else
                      (lambda e, half=half, xt=xt, ps=ps: e.tensor_copy(out=xt[:, half * 512:(half + 1) * 512], in_=ps[:, :])),
                      reads=[ps], writes=[xt])
            r0 = bi * TB + c * 128
            kb.dma("sp", self.yp.ap()[r0:r0 + 128, :], xt[:], reads=[xt], is_output=True)

    def layer(self, bi, l):
        ph = self.dbg.get("phases", "PABCDEF")
        if "P" in ph: self.load_params(bi, l)
        if "A" in ph: self.phaseA(bi, l)
        if "B" in ph: self.phaseB(bi, l)
        if "C" in ph: self.phaseC(bi, l)
        if "D" in ph: self.phaseD(bi, l)
        if "E" in ph: self.phaseE(bi, l)
        if "F" in ph: self.phaseF(bi, l)

    def load_params(self, bi, l):
        self.kb.phase = 'load_params'
        kb = self.kb
        kb.dma("sp", self.cw[:], self.conv_w.ap()[l], writes=[self.cw])
        kb.dma("sp", self.cb[:], self.conv_b.ap()[l], writes=[self.cb])
        kb.dma("sp", self.lnp[:], self.lnp_d.ap()[l], writes=[self.lnp])
        for i, w in enumerate((self.dt_bias, self.a_log, self.d_skip)):
            kb.dma("sp", self.sm16[:, i, :], w.ap()[l:l + 1, :].partition_broadcast(128), writes=[self.sm16])
        kb.dma("sp", self.esink[:], self.sinks.ap()[l:l + 1, :].partition_broadcast(128), writes=[self.esink])
        kb.op("act", lambda e: e.activation(out=self.esink[:], in_=self.esink[:], func=AF.Exp), reads=[self.esink], writes=[self.esink])
        kb.op("act", lambda e: e.activation(out=self.sm16[:, 1, :], in_=self.sm16[:, 1, :], func=AF.Exp), reads=[self.sm16], writes=[self.sm16])
        kb.op("dve", lambda e: e.tensor_scalar(out=self.sm16[:, 1, :], in0=self.sm16[:, 1, :], scalar1=-1.0, scalar2=None, op0=ALU.mult),
              reads=[self.sm16], writes=[self.sm16])

    def proj_tok(self, Wt, Wv, c, ncols, col0=0):
        ps = self.nps()
        self.kb.mm([lambda pe, k=k, ps=ps: pe.matmul(ps[:, 0:ncols], self.xb[:, k, c * 128:(c + 1) * 128], Wv[:, k, col0:col0 + ncols],
                                                     start=(k == 0), stop=(k == 7)) for k in range(8)],
                   reads=[self.xb, Wt], writes=[ps])
        return ps

    def proj_feat(self, Wt, Wv, t0, src=None, srct=None, kc=8):
        ps = self.nps()
        src = self.xb if src is None else src
        self.kb.mm([lambda pe, k=k, ps=ps: pe.matmul(ps[:, :], Wv[:, k, t0:t0 + 128], src[:, k, :],
                                                     start=(k == 0), stop=(k == kc - 1)) for k in range(kc)],
                   reads=[src, Wt], writes=[ps])
        return ps

    def transp_b(self, src_tile, src_aps, dst_tile, dst_ap):
        n = len(src_aps)
        pb = self.npb()
        self.kb.mm([lambda pe, j=j, pb=pb: pe.transpose(out=pb[:, j * 128:(j + 1) * 128], in_=src_aps[j], identity=self.id_b[:])
                    for j in range(n)], reads=[src_tile, self.id_b], writes=[pb])
        self.kb.op("act", lambda e, pb=pb: e.copy(out=dst_ap, in_=pb[:, 0:n * 128].rearrange("p (j t) -> p j t", j=n)),
                   reads=[pb], writes=[dst_tile])
        return pb

    def phaseA(self, bi, l):
        self.kb.phase = 'phaseA'
        kb = self.kb
        self.fence()
        g = [2.0 ** (-5 - h) for h in range(4)]
        cdec = [float(np.exp(np.float64(128) * np.log1p(-gg))) for gg in g]
        for gi, off in enumerate((OQ, OK_, OV, OG)):
            Wt, Wv = self.wload(self.wsrc(self.w_in, l, D, off, 512), 8, 512)
            for c in range(4):
                ps = self.proj_tok(Wt, Wv, c, 512)
                if gi < 2:
                    dst = self.qrot if gi == 0 else self.krot
                    eng = "dve" if gi == 0 else self.dbg.get("rot_k_eng", "pool")
                    if eng == "pool":
                        kb.op("act", lambda e, ps=ps: e.copy(out=self.kraw[:], in_=ps[:, :]), reads=[ps], writes=[self.kraw])
                        srct, psv = self.kraw, self.kraw[:].rearrange("p (h t e) -> p h t e", h=4, t=2)
                    else:
                        srct, psv = ps, ps[:, :].rearrange("p (h t e) -> p h t e", h=4, t=2)
                    tA = self.rtA if gi == 0 else self.rtA2
                    tB = self.rtB if gi == 0 else self.rtB2
                    cosb = self.rot[:, c, gi * 128:gi * 128 + 64].unsqueeze(1).unsqueeze(1).broadcast_to([128, 4, 2, 64])
                    sinb = self.rot[:, c, gi * 128 + 64:gi * 128 + 128].unsqueeze(1).broadcast_to([128, 4, 64])
                    kb.op(eng, lambda e, psv=psv, cosb=cosb, tA=tA: e.tensor_tensor(out=tA[:], in0=psv, in1=cosb, op=ALU.mult),
                          reads=[srct, self.rot], writes=[tA])
                    kb.op(eng, lambda e, psv=psv, sinb=sinb, tB=tB: e.tensor_tensor(out=tB[:, :, 0, :], in0=psv[:, :, 1, :], in1=sinb, op=ALU.mult),
                          reads=[srct, self.rot], writes=[tB])
                    kb.op(eng, lambda e, psv=psv, sinb=sinb, tB=tB: e.tensor_tensor(out=tB[:, :, 1, :], in0=psv[:, :, 0, :], in1=sinb, op=ALU.mult),
                          reads=[srct, self.rot], writes=[tB])
                    dv = dst[:].rearrange("p (h t e) -> p h t e", h=4, t=2)
                    kb.op(eng, lambda e, dv=dv, tA=tA, tB=tB: e.tensor_tensor(out=dv[:, :, 0, :], in0=tA[:, :, 0, :], in1=tB[:, :, 0, :], op=ALU.subtract),
                          reads=[tA, tB], writes=[dst])
                    kb.op(eng, lambda e, dv=dv, tA=tA, tB=tB: e.tensor_tensor(out=dv[:, :, 1, :], in0=tA[:, :, 1, :], in1=tB[:, :, 1, :], op=ALU.add),
                          reads=[tA, tB], writes=[dst])
                    dT = self.qT if gi == 0 else self.kT
                    self.transp_b(dst, [dst[:, h * 128:(h + 1) * 128] for h in range(4)], dT, dT[:, :, c * 128:(c + 1) * 128])
                    if gi == 1:
                        kb.op("dve", lambda e, c=c: e.tensor_tensor(
                            out=self.kdk[:, c, :, :], in0=self.krot[:].rearrange("p (h e) -> p h e", h=4),
                            in1=self.kdec.unsqueeze(2).broadcast_to([128, 4, 128]), op=ALU.mult),
                            reads=[self.krot, self.cf], writes=[self.kdk])
                elif gi == 2:
                    kb.op("act", lambda e, c=c, ps=ps: e.copy(out=self.v_r[:, c, :], in_=ps[:, :]), reads=[ps], writes=[self.v_r])
                else:
                    kb.op("act", lambda e, c=c, ps=ps: e.activation(out=self.gsil[:, c, :], in_=ps[:, :], func=AF.Silu), reads=[ps], writes=[self.gsil])
        S, Sb = self.retS[l], self.retSb
        kb.op("act", lambda e: e.copy(out=Sb[:], in_=S[:]), reads=[S], writes=[Sb])
        for c in range(4):
            cs = slice(c * 128, (c + 1) * 128)
            ps1 = self.nps()
            kb.mm([lambda pe, h=h, ps1=ps1: pe.matmul(ps1[:, h * 128:(h + 1) * 128], self.kT[:, h, cs], self.qT[:, h, cs], start=True, stop=True)
                   for h in range(4)], reads=[self.kT, self.qT], writes=[ps1])
            kb.op("dve", lambda e, ps1=ps1: e.tensor_tensor(out=self.scm[:], in0=ps1[:, :].rearrange("p (h i) -> p h i", h=4), in1=self.dmatT, op=ALU.mult),
                  reads=[ps1, self.cf], writes=[self.scm])
            psA = self.nps(); psB = self.nps(); psC = self.nps()
            kb.mm([lambda pe, h=h, psA=psA: pe.matmul(psA[:, h * 128:(h + 1) * 128], self.scm[:, h, :], self.v_r[:, c, h * 128:(h + 1) * 128], start=True, stop=True)
                   for h in range(4)], reads=[self.scm, self.v_r], writes=[psA])
            kb.mm([lambda pe, h=h, psB=psB: pe.matmul(psB[:, h * 128:(h + 1) * 128], self.qT[:, h, cs], Sb[:, h, :], start=True, stop=True)
                   for h in range(4)], reads=[self.qT, Sb], writes=[psB])
            kb.mm([lambda pe, h=h, psC=psC: pe.matmul(psC[:, h * 128:(h + 1) * 128], self.kdk[:, c, h, :], self.v_r[:, c, h * 128:(h + 1) * 128], start=True, stop=True)
                   for h in range(4)], reads=[self.kdk, self.v_r], writes=[psC])
            for h in range(4):
                kb.op("dve", lambda e, h=h, psC=psC: e.scalar_tensor_tensor(out=S[:, h, :], in0=S[:, h, :], scalar=cdec[h], in1=psC[:, h * 128:(h + 1) * 128],
                                                                              op0=ALU.mult, op1=ALU.add), reads=[S, psC], writes=[S])
            kb.op("act", lambda e: e.copy(out=Sb[:], in_=S[:]), reads=[S], writes=[Sb])
            kb.op("dve", lambda e, psB=psB: e.tensor_tensor(out=self.otmp[:], in0=psB[:, :].rearrange("p (h v) -> p h v", h=4),
                                                             in1=self.qdec.unsqueeze(2).broadcast_to([128, 4, 128]), op=ALU.mult),
                  reads=[psB, self.cf], writes=[self.otmp])
            kb.op("dve", lambda e, psA=psA: e.tensor_tensor(out=self.o_r[:], in0=psA[:, :].rearrange("p (h v) -> p h v", h=4), in1=self.otmp[:], op=ALU.add),
                  reads=[psA, self.otmp], writes=[self.o_r])
            for h in range(4):
                kb.op("dve", lambda e, h=h: e.bn_stats(out=self.st6[:, h, :], in_=self.o_r[:, h, :]), reads=[self.o_r], writes=[self.st6])
            for h in range(4):
                kb.op("dve", lambda e, h=h: e.bn_aggr(out=self.mv[:, h, :], in_=self.st6[:, h, :]), reads=[self.st6], writes=[self.mv])
            kb.op("act", lambda e: e.activation(out=self.rs4[:], in_=self.mv[:, :, 1], func=AF.Ln, bias=EPS, scale=1.0), reads=[self.mv], writes=[self.rs4])
            kb.op("act", lambda e: e.activation(out=self.rs4b[:], in_=self.rs4[:], func=AF.Exp, scale=-0.5), reads=[self.rs4], writes=[self.rs4b])
            for h in range(4):
                kb.op("dve", lambda e, h=h: e.scalar_tensor_tensor(out=self.og[:, h, :], in0=self.o_r[:, h, :], scalar=self.mv[:, h, 0:1],
                                                                     in1=self.gsil[:, c, h * 128:(h + 1) * 128], op0=ALU.subtract, op1=ALU.mult),
                      reads=[self.o_r, self.mv, self.gsil], writes=[self.og])
            for h in range(4):
                kb.op("act", lambda e, h=h: e.activation(out=self.ogb[:, h * 128:(h + 1) * 128], in_=self.og[:, h, :], func=AF.Identity, scale=self.rs4b[:, h:h + 1]),
                      reads=[self.og, self.rs4b], writes=[self.ogb])
            self.transp_b(self.ogb, [self.ogb[:, h * 128:(h + 1) * 128] for h in range(4)], self.orT, self.orT[:, :, cs])
        if bi == self.nblk - 1:
            kb.dma("sp", self.ret_p.ap()[l].rearrange("h d v -> d h v"), S[:], reads=[S], is_output=True)

    def phaseB(self, bi, l):
        self.kb.phase = 'phaseB'
        kb = self.kb
        self.fence()
        kb.dma("sp", self.normw[:], self.norm_w.ap()[l:l + 1, :].partition_broadcast(128), writes=[self.normw])
        Wt, Wv = self.wload(self.wsrc(self.w_in, l, D, ODT, 16), 8, 16)
        dts = self.dts
        for c in range(4):
            ps = self.proj_tok(Wt, Wv, c, 16)
            DT, CS, ECS, DEDT, CDEC, NB, T1, T2 = [dts[:, c, i, :] for i in range(8)]
            kb.op("dve", lambda e, ps=ps, T1=T1: e.tensor_tensor(out=T1, in0=ps[:, 0:16], in1=self.sm16[:, 0, :], op=ALU.add), reads=[ps, self.sm16], writes=[dts])
            kb.op("act", lambda e, T1=T1, T2=T2: e.activation(out=T2, in_=T1, func=AF.Exp), reads=[dts], writes=[dts])
            kb.op("act", lambda e, DT=DT, T2=T2: e.activation(out=DT, in_=T2, func=AF.Ln, bias=1.0, scale=1.0), reads=[dts], writes=[dts])
            kb.op("dve", lambda e, DT=DT, T1=T1: e.tensor_tensor(out=T1, in0=DT, in1=self.sm16[:, 1, :], op=ALU.mult), reads=[dts, self.sm16], writes=[dts])
            ps2 = self.nps()
            kb.mm([lambda pe, ps2=ps2, T1=T1: pe.matmul(ps2[:, 0:16], self.tri_f, T1, start=True, stop=True),
                   lambda pe, ps2=ps2, T1=T1: pe.matmul(ps2[:, 16:32], self.ones_f, T1, start=True, stop=True)],
                  reads=[self.cf, dts], writes=[ps2])
            kb.op("act", lambda e, ps2=ps2, CS=CS: e.copy(out=CS, in_=ps2[:, 0:16]), reads=[ps2], writes=[dts])
            kb.op("act", lambda e, ps2=ps2, ECS=ECS: e.activation(out=ECS, in_=ps2[:, 0:16], func=AF.Exp), reads=[ps2], writes=[dts])
            kb.op("act", lambda e, ps2=ps2, CDEC=CDEC: e.activation(out=CDEC, in_=ps2[:, 16:32], func=AF.Exp), reads=[ps2], writes=[dts])
            kb.op("dve", lambda e, ps2=ps2, T2=T2, CS=CS: e.tensor_tensor(out=T2, in0=ps2[:, 16:32], in1=CS, op=ALU.subtract), reads=[ps2, dts], writes=[dts])
            kb.op("act", lambda e, T2=T2: e.activation(out=T2, in_=T2, func=AF.Exp), reads=[dts], writes=[dts])
            kb.op("dve", lambda e, T2=T2, DT=DT, DEDT=DEDT: e.tensor_tensor(out=DEDT, in0=T2, in1=DT, op=ALU.mult), reads=[dts], writes=[dts])
            kb.op("act", lambda e, DT=DT, T1=T1: e.activation(out=T1, in_=DT, func=AF.Ln), reads=[dts], writes=[dts])
            kb.op("dve", lambda e, T1=T1, CS=CS, NB=NB: e.tensor_tensor(out=NB, in0=T1, in1=CS, op=ALU.subtract), reads=[dts], writes=[dts])
            kb.op("act", lambda e, CS=CS, c=c: e.copy(out=self.cshl[:, c, 0, :], in_=CS), reads=[dts], writes=[self.cshl])
            kb.op("dve", lambda e, CS=CS, c=c: e.tensor_tensor(out=self.cshl[:, c, 1, :], in0=CS, in1=self.cshl[:, c, 0, :], op=ALU.subtract),
                  reads=[dts, self.cshl], writes=[self.cshl])
        if self.dbg.get('bstop', 99) <= 1: return
        for g3 in range(3):
            Wt, Wv = self.wload(self.wsrc(self.w_in, l, D, OXBC + g3 * 512, 512), 8, 512)
            kb.op("dve", lambda e, g3=g3: e.tensor_copy(out=self.xbcT[:, :, 0:3], in_=self.hist[l][:, g3 * 4:(g3 + 1) * 4, :]),
                  reads=[self.hist[l]], writes=[self.xbcT])
            for t in range(4):
                tt = g3 * 4 + t
                ps = self.proj_feat(Wt, Wv, t * 128)
                kb.op("act", lambda e, t=t, ps=ps: e.copy(out=self.xbcT[:, t, 3:3 + TB], in_=ps[:, :]), reads=[ps], writes=[self.xbcT])
                kb.op("dve", lambda e, t=t, tt=tt: e.tensor_scalar(out=self.cacc[:], in0=self.xbcT[:, t, 0:TB], scalar1=self.cw[:, tt, 0:1], scalar2=None, op0=ALU.mult),
                      reads=[self.xbcT, self.cw], writes=[self.cacc])
                for tau in range(1, 4):
                    kb.op("dve", lambda e, t=t, tt=tt, tau=tau: e.scalar_tensor_tensor(
                        out=self.cacc[:], in0=self.xbcT[:, t, tau:tau + TB], scalar=self.cw[:, tt, tau:tau + 1], in1=self.cacc[:], op0=ALU.mult, op1=ALU.add),
                        reads=[self.xbcT, self.cw, self.cacc], writes=[self.cacc])
                if tt < 8:
                    dtile, dap = self.xsT, self.xsT[:, tt, :]
                elif tt < 10:
                    dtile, dap = self.BT, self.BT[:, tt - 8, :]
                else:
                    dtile, dap = self.CT, self.CT[:, tt - 10, :]
                kb.op("act", lambda e, tt=tt, dap=dap: e.activation(out=dap, in_=self.cacc[:], func=AF.Silu, bias=self.cb[:, tt:tt + 1], scale=1.0),
                      reads=[self.cacc, self.cb], writes=[dtile])
            kb.op("dve", lambda e, g3=g3: e.tensor_copy(out=self.hist[l][:, g3 * 4:(g3 + 1) * 4, :], in_=self.xbcT[:, :, TB:TB + 3]),
                  reads=[self.xbcT], writes=[self.hist[l]])
        if self.dbg.get('bstop', 99) <= 2: return
        S, Sb = self.ssmS[l], self.ssmSb
        kb.op("act", lambda e: e.copy(out=Sb[:], in_=S[:]), reads=[S], writes=[Sb])
        Zw = [self.wload(self.wsrc(self.w_in, l, D, OZ + g2 * 512, 512), 8, 512) for g2 in range(2)]
        for c in range(4):
            cs = slice(c * 128, (c + 1) * 128)
            DT, CS, ECS, DEDT, CDEC, NB, T1, T2 = [dts[:, c, i, :] for i in range(8)]
            for g2 in range(2):
                ps = self.proj_tok(Zw[g2][0], Zw[g2][1], c, 512)
                kb.op("act", lambda e, ps=ps, g2=g2: e.activation(out=self.zs[:, g2 * 512:(g2 + 1) * 512], in_=ps[:, :], func=AF.Silu),
                      reads=[ps], writes=[self.zs])
            if self.dbg.get('bstop', 99) <= 2.3: continue
            pb = self.npb()
            kb.mm([lambda pe, j=j, pb=pb: pe.transpose(out=pb[:, j * 128:(j + 1) * 128], in_=self.xsT[:, j, cs], identity=self.id_b[:]) for j in range(8)],
                  reads=[self.xsT, self.id_b], writes=[pb])
            kb.op("act", lambda e, pb=pb: e.copy(out=self.xs_tok[:], in_=pb[:, :]), reads=[pb], writes=[self.xs_tok])
            if self.dbg.get('bstop', 99) <= 2.6: continue
            pbv = self.xs_tok[:].rearrange("p (h q) -> p h q", h=16)
            kb.op("dve", lambda e, pbv=pbv, DEDT=DEDT: e.tensor_tensor(out=self.xw[:].rearrange("p (h q) -> p h q", h=16), in0=pbv,
                                                                        in1=DEDT.unsqueeze(2).broadcast_to([128, 16, 64]), op=ALU.mult),
                  reads=[self.xs_tok, dts], writes=[self.xw])
            kb.op("dve", lambda e, pbv=pbv: e.tensor_tensor(out=self.xD[:].rearrange("p (h q) -> p h q", h=16), in0=pbv,
                                                             in1=self.sm16[:, 2, :].unsqueeze(2).broadcast_to([128, 16, 64]), op=ALU.mult),
                  reads=[self.xs_tok, self.sm16], writes=[self.xD])
            if self.dbg.get('bstop', 99) <= 2.8: continue
            pb2 = self.npb()
            kb.mm([lambda pe, j=j, pb2=pb2: pe.transpose(out=pb2[:, j * 128:(j + 1) * 128], in_=self.BT[:, j, cs], identity=self.id_b[:]) for j in range(2)],
                  reads=[self.BT, self.id_b], writes=[pb2])
            kb.op("act", lambda e, pb2=pb2: e.copy(out=self.B_tok[:], in_=pb2[:, 0:256]), reads=[pb2], writes=[self.B_tok])
            if self.dbg.get('bstop', 99) <= 3: continue
            psc = self.nps()
            kb.mm([lambda pe, g=g, psc=psc: pe.matmul(psc[:, g * 128:(g + 1) * 128], self.BT[:, g, cs], self.CT[:, g, cs], start=True, stop=True) for g in range(2)],
                  reads=[self.BT, self.CT], writes=[psc])
            kb.op("act", lambda e, psc=psc: e.copy(out=self.cbT[:], in_=psc[:, 0:256].rearrange("p (g i) -> p g i", g=2)), reads=[psc], writes=[self.cbT])
            def build_D(hq):
                Dhl = self.Dhl2[hq % 2]
                for hl in range(2):
                    kb.op("dve", lambda e, hl=hl, c=c, hq=hq, Dhl=Dhl: e.tensor_tensor(
                        out=Dhl[:, hl, :, :], in0=self.id_b[:].unsqueeze(1).broadcast_to([128, 4, 128]),
                        in1=self.cshl[:, c, hl, hq * 4:(hq + 1) * 4].unsqueeze(2).broadcast_to([128, 4, 128]), op=ALU.mult),
                        reads=[self.id_b, self.cshl], writes=[Dhl])
            build_D(0)
            for hq in range(4):
                Dhl = self.Dhl2[hq % 2]; Ep = self.Ep2[hq % 2]
                if hq < 3:
                    build_D(hq + 1)
                pse = self.nps()
                kb.mm([lambda pe, pse=pse, hq=hq, Dhl=Dhl: pe.matmul(pse[:, :], self.ones_b[:], Dhl[:, 0, :, :], start=True, stop=False),
                       lambda pe, pse=pse, hq=hq, Dhl=Dhl: pe.matmul(pse[:, :], self.ones_b[:], Dhl[:, 1, :, :], start=False, stop=False),
                       lambda pe, pse=pse: pe.matmul(pse[:, :], self.id_b[:], self.negT[:].unsqueeze(1).broadcast_to([128, 4, 128]), start=False, stop=True)],
                      reads=[self.ones_b, Dhl, self.id_b, self.negT], writes=[pse])
                for hh in range(4):
                    h = hq * 4 + hh
                    kb.op("act", lambda e, pse=pse, hh=hh, h=h, NB=NB, Ep=Ep: e.activation(out=Ep[:, hh, :], in_=pse[:, hh * 128:(hh + 1) * 128], func=AF.Exp,
                                                                                  bias=NB[:, h:h + 1], scale=1.0), reads=[pse, dts], writes=[Ep])
                g = hq // 2
                kb.op("dve", lambda e, hq=hq, g=g, Ep=Ep: e.tensor_tensor(out=self.wmT[:, hq * 4:(hq + 1) * 4, :], in0=Ep[:],
                                                                    in1=self.cbT[:, g, :].unsqueeze(1).broadcast_to([128, 4, 128]), op=ALU.mult),
                      reads=[Ep, self.cbT], writes=[self.wmT])
            if self.dbg.get('bstop', 99) <= 4: continue
            psY = [self.nps(), self.nps()]
            for g in range(2):
                fns = [lambda pe, g=g: pe.matmul(psY[g][:, :], self.id_b[:], self.xD[:, g * 512:(g + 1) * 512], start=True, stop=False)]
                for hh in range(8):
                    h = g * 8 + hh
                    fns.append(lambda pe, g=g, hh=hh, h=h: pe.matmul(psY[g][:, hh * 64:(hh + 1) * 64], self.wmT[:, h, :], self.xs_tok[:, h * 64:(h + 1) * 64],
                                                                     start=False, stop=(hh == 7), skip_group_check=True))
                kb.mm(fns, reads=[self.id_b, self.xD, self.wmT, self.xs_tok], writes=[psY[g]])
            psZ = [self.nps(), self.nps()]
            for g in range(2):
                kb.mm([lambda pe, g=g: pe.matmul(psZ[g][:, :], self.CT[:, g, cs], Sb[:, g * 512:(g + 1) * 512], start=True, stop=True)],
                      reads=[self.CT, Sb], writes=[psZ[g]])
            for g in range(2):
                gs = slice(g * 512, (g + 1) * 512)
                kb.op("dve", lambda e, g=g, gs=gs, ECS=ECS: e.tensor_tensor(out=self.ytmp[:, gs].rearrange("p (h q) -> p h q", h=8),
                                                                          in0=psZ[g][:, :].rearrange("p (h q) -> p h q", h=8),
                                                                          in1=ECS[:, g * 8:(g + 1) * 8].unsqueeze(2).broadcast_to([128, 8, 64]), op=ALU.mult),
                      reads=[psZ[g], dts], writes=[self.ytmp])
                kb.op("dve", lambda e, g=g, gs=gs: e.tensor_tensor(out=self.ytmp[:, gs], in0=self.ytmp[:, gs], in1=psY[g][:, :], op=ALU.add),
                      reads=[psY[g], self.ytmp], writes=[self.ytmp])
            psS = [self.nps(), self.nps()]
            for g in range(2):
                kb.mm([lambda pe, g=g: pe.matmul(psS[g][:, :], self.B_tok[:, g * 128:(g + 1) * 128], self.xw[:, g * 512:(g + 1) * 512], start=True, stop=True)],
                      reads=[self.B_tok, self.xw], writes=[psS[g]])
            kb.op("dve", lambda e, CDEC=CDEC: e.tensor_tensor(out=self.stmp[:].rearrange("p (h q) -> p h q", h=16), in0=S[:].rearrange("p (h q) -> p h q", h=16),
                                                              in1=CDEC.unsqueeze(2).broadcast_to([128, 16, 64]), op=ALU.mult), reads=[S, dts], writes=[self.stmp])
            for g in range(2):
                gs = slice(g * 512, (g + 1) * 512)
                kb.op("dve", lambda e, g=g, gs=gs: e.tensor_tensor(out=S[:, gs], in0=self.stmp[:, gs], in1=psS[g][:, :], op=ALU.add),
                      reads=[self.stmp, psS[g]], writes=[S])
            kb.op("act", lambda e: e.copy(out=Sb[:], in_=S[:]), reads=[S], writes=[Sb])
            if self.dbg.get('bstop', 99) <= 5: continue
            kb.op("dve", lambda e, c=c: e.tensor_tensor(out=self.ytmp[:], in0=self.ytmp[:], in1=self.zs[:], op=ALU.mult),
                  reads=[self.ytmp, self.zs], writes=[self.ytmp])
            for g in range(2):
                gs = slice(g * 512, (g + 1) * 512)
                kb.op("act", lambda e, g=g, gs=gs: e.activation(out=self.stmp[:, gs], in_=self.ytmp[:, gs], func=AF.Square, accum_out=self.ssq[:, g:g + 1]),
                      reads=[self.ytmp], writes=[self.stmp, self.ssq])
            kb.op("act", lambda e: e.activation(out=self.ssq[:, 2:4], in_=self.ssq[:, 0:2], func=AF.Ln, bias=EPS, scale=1.0 / 512), reads=[self.ssq], writes=[self.ssq])
            kb.op("act", lambda e: e.activation(out=self.ssq[:, 2:4], in_=self.ssq[:, 2:4], func=AF.Exp, scale=-0.5), reads=[self.ssq], writes=[self.ssq])
            for g in range(2):
                gs = slice(g * 512, (g + 1) * 512)
                kb.op("dve", lambda e, g=g, gs=gs: e.scalar_tensor_tensor(out=self.ynb[:, gs], in0=self.ytmp[:, gs], scalar=self.ssq[:, 2 + g:3 + g],
                                                                           in1=self.normw[:, gs], op0=ALU.mult, op1=ALU.mult),
                      reads=[self.ytmp, self.ssq, self.normw], writes=[self.ynb])
            self.transp_b(self.ynb, [self.ynb[:, j * 128:(j + 1) * 128] for j in range(8)], self.yT, self.yT[:, :, cs])
        if self.dbg.get('bstop', 99) <= 6: return
        if bi == self.nblk - 1:
            for half in range(2):
                ps = self.nps()
                kb.mm([lambda pe, j=j, half=half, ps=ps: pe.transpose(out=ps[:, j * 128:(j + 1) * 128], in_=S[:, (half * 4 + j) * 128:(half * 4 + j + 1) * 128],
                                                                       identity=self.ident_f) for j in range(4)], reads=[S, self.cf], writes=[ps])
                kb.op("act", lambda e, ps=ps: e.copy(out=self.ytmp[:, 0:512], in_=ps[:, :]), reads=[ps], writes=[self.ytmp])
                kb.dma("sp", self.ssm_p.ap()[l, half * 512:(half + 1) * 512, :].rearrange("(j p) n -> p j n", p=128),
                       self.ytmp[:, 0:512].rearrange("p (j n) -> p j n", j=4), reads=[self.ytmp], is_output=True)
            for half in range(3):
                ps = self.nps()
                kb.mm([lambda pe, j=j, half=half, ps=ps: pe.transpose(out=ps[0:3, j * 128:(j + 1) * 128], in_=self.hist[l][:, half * 4 + j, :],
                                                                       identity=self.ident_f) for j in range(4)], reads=[self.hist[l], self.cf], writes=[ps])
                kb.op("act", lambda e, ps=ps: e.copy(out=self.stmp[0:3, 0:512], in_=ps[0:3, :]), reads=[ps], writes=[self.stmp])
                kb.dma("sp", self.conv_p.ap()[l, :, half * 512:(half + 1) * 512], self.stmp[0:3, 0:512], reads=[self.stmp], is_output=True)

    def phaseC(self, bi, l):
        self.kb.phase = 'phaseC'
        kb = self.kb
        last = (bi == self.nblk - 1)
        self.fence()
        kb.op("act", lambda e: e.activation(out=self.vaug[:, :, :, 64:65], in_=self.cf[:, 0:10].rearrange("p (a b c) -> p a b c", a=5, b=2), func=AF.Identity, scale=0.0, bias=1.0),
              reads=[self.cf], writes=[self.vaug])
        Wt, Wv = self.wload(self.wsrc(self.w_in, l, D, OQC, 512), 8, 512)
        for t in range(4):
            ps = self.proj_feat(Wt, Wv, t * 128)
            kb.op("act", lambda e, t=t, ps=ps: e.copy(out=self.qcT[:, t, :], in_=ps[:, :]), reads=[ps], writes=[self.qcT])
        if self.dbg.get('cstop', 99) <= 1: return
        Wt, Wv = self.wload(self.wsrc(self.w_in, l, D, OKC, 256), 8, 256)
        if bi > 0:
            kb.op("act", lambda e: e.copy(out=self.kTe[:, 0:128], in_=self.kprev[l][:]), reads=[self.kprev[l]], writes=[self.kTe])
            kb.op("act", lambda e: e.copy(out=self.vaug[:, 0, :, 0:64], in_=self.vprev[l][:, :, 0:64]), reads=[self.vprev[l]], writes=[self.vaug])
        if self.dbg.get('cstop', 99) <= 1.5: return
        ps = self.proj_feat(Wt, Wv, 0)
        kb.op("act", lambda e, ps=ps: e.copy(out=self.kTe[:, 128:128 + TB], in_=ps[:, :]), reads=[ps], writes=[self.kTe])
        kb.op("act", lambda e: e.copy(out=self.kprev[l][:], in_=self.kTe[:, TB:TB + 128]), reads=[self.kTe], writes=[self.kprev[l]])
        if self.dbg.get('cstop', 99) <= 1.7: return
        for c in range(4):
            ps = self.proj_tok(Wt, Wv, c, 128, col0=128)
            kb.op("act", lambda e, c=c, ps=ps: e.copy(out=self.vaug[:, c + 1, :, 0:64], in_=ps[:, 0:128].rearrange("p (g e) -> p g e", g=2)),
                  reads=[ps], writes=[self.vaug])
            if last and c == 3 and not self.dbg.get('nokv'):
                kb.op("act", lambda e, ps=ps: e.copy(out=self.kvo[:, 1, :], in_=ps[:, 0:128]), reads=[ps], writes=[self.kvo])
                ps2 = self.proj_tok(Wt, Wv, c, 128, col0=0)
                kb.op("act", lambda e, ps2=ps2: e.copy(out=self.kvo[:, 0, :], in_=ps2[:, 0:128]), reads=[ps2], writes=[self.kvo])
                kb.dma("sp", self.k_p.ap()[l, :, :], self.kvo[:, 0, :], reads=[self.kvo], is_output=True)
                kb.dma("sp", self.v_p.ap()[l, :, :], self.kvo[:, 1, :], reads=[self.kvo], is_output=True)
        kb.op("act", lambda e: e.copy(out=self.vprev[l][:], in_=self.vaug[:, 4, :, :]), reads=[self.vaug], writes=[self.vprev[l]])
        if self.dbg.get('cstop', 99) <= 2: return
        for c in range(4):
            n = bi * 4 + c
            cs = slice(c * 128, (c + 1) * 128)
            hfs = [1] if n == 0 else [0, 1]
            for g in range(2):
                for hf in hfs:
                    ps = self.nps()
                    kc0 = (c + hf) * 128
                    kb.mm([lambda pe, ps=ps, g=g, kc0=kc0: pe.matmul(ps[:, :], self.kTe[g * 64:(g + 1) * 64, kc0:kc0 + 128], self.qcT[g * 64:(g + 1) * 64, :, cs],
                                                                      start=True, stop=True)], reads=[self.kTe, self.qcT], writes=[ps])
                    kb.op("dve", lambda e, ps=ps, g=g, hf=hf: e.scalar_tensor_tensor(out=self.lg[:], in0=ps[:, :].rearrange("p (h q) -> p h q", h=4), scalar=0.125,
                                                                                    in1=self.btab[:, hf, g * 4:(g + 1) * 4, :], op0=ALU.mult, op1=ALU.add),
                          reads=[ps, self.btab], writes=[self.lg])
                    kb.op("act", lambda e, g=g, hf=hf: e.activation(out=self.pT[:, hf, g, :, :], in_=self.lg[:], func=AF.Exp), reads=[self.lg], writes=[self.pT])
            if self.dbg.get('cstop', 99) <= 3: continue
            psO = [self.nps(), self.nps()]
            for g in range(2):
                fns = []
                for h4 in range(4):
                    for i, hf in enumerate(hfs):
                        fns.append(lambda pe, g=g, h4=h4, hf=hf, i=i: pe.matmul(psO[g][:, h4 * 65:(h4 + 1) * 65], self.pT[:, hf, g, h4, :], self.vaug[:, c + hf, g, 0:65],
                                                                               start=(i == 0), stop=(i == len(hfs) - 1)))
                kb.mm(fns, reads=[self.pT, self.vaug], writes=[psO[g]])
                if self.dbg.get('cstop', 99) <= 4: continue
                ov = psO[g][:, 0:260].rearrange("p (h e) -> p h e", h=4)
                kb.op("dve", lambda e, g=g, ov=ov: e.tensor_tensor(out=self.den[:, g * 4:(g + 1) * 4], in0=ov[:, :, 64], in1=self.esink[:, g * 4:(g + 1) * 4], op=ALU.add),
                      reads=[psO[g], self.esink], writes=[self.den])
                kb.op("dve", lambda e, g=g: e.reciprocal(out=self.rden[:, g * 4:(g + 1) * 4], in_=self.den[:, g * 4:(g + 1) * 4]), reads=[self.den], writes=[self.rden])
                kb.op("dve", lambda e, g=g, ov=ov: e.tensor_tensor(out=self.oc_tok[:, g * 4:(g + 1) * 4, :], in0=ov[:, :, 0:64],
                                                                    in1=self.rden[:, g * 4:(g + 1) * 4].unsqueeze(2).broadcast_to([128, 4, 64]), op=ALU.mult),
                      reads=[psO[g], self.rden], writes=[self.oc_tok])
            if self.dbg.get('cstop', 99) <= 5: continue
            ocv = self.oc_tok[:].rearrange("p h e -> p (h e)")
            self.transp_b(self.oc_tok, [ocv[:, j * 128:(j + 1) * 128] for j in range(4)], self.ocT, self.ocT[:, :, cs])

    def phaseD(self, bi, l):
        self.kb.phase = 'phaseD'
        kb = self.kb
        self.fence()
        brs = [(self.w_br_ret, 512, 4, self.orT), (self.w_br_ssd, 1024, 8, self.yT), (self.w_br_swa, 512, 4, self.ocT)]
        for b, (wbr, K, kc, src) in enumerate(brs):
            for j in range(2):
                Gt, Gv = self.wload(self.wsrc(self.w_in, l, D, OGATE + b * 1024 + j * 512, 512), 8, 512)
                Bt, Bv = self.wload(self.wsrc(wbr, l, K, j * 512, 512), kc, 512)
                for t in range(4):
                    m = j * 4 + t
                    psG = self.proj_feat(Gt, Gv, t * 128)
                    psB = self.proj_feat(Bt, Bv, t * 128, src=src, kc=kc)
                    kb.op("act", lambda e, psG=psG: e.activation(out=self.sg[:], in_=psG[:, :], func=AF.Sigmoid), reads=[psG], writes=[self.sg])
                    if b == 0:
                        kb.op("dve", lambda e, m=m, psB=psB: e.tensor_tensor(out=self.macc[:, m, :], in0=self.sg[:], in1=psB[:, :], op=ALU.mult),
                              reads=[self.sg, psB], writes=[self.macc])
                    else:
                        kb.op("dve", lambda e, psB=psB: e.tensor_tensor(out=self.gt[:], in0=self.sg[:], in1=psB[:, :], op=ALU.mult),
                              reads=[self.sg, psB], writes=[self.gt])
                        if b == 1:
                            kb.op("dve", lambda e, m=m: e.tensor_tensor(out=self.macc[:, m, :], in0=self.macc[:, m, :], in1=self.gt[:], op=ALU.add),
                                  reads=[self.macc, self.gt], writes=[self.macc])
                        else:
                            kb.op("dve", lambda e, m=m: e.tensor_tensor(out=self.mrg[:, m, :], in0=self.macc[:, m, :], in1=self.gt[:], op=ALU.add),
                                  reads=[self.macc, self.gt], writes=[self.mrg])

    def resid_ln(self, pss_fn, gi):
        kb = self.kb
        for m in range(8):
            ps = pss_fn(m)
            kb.op("dve", lambda e, m=m, ps=ps: e.scalar_tensor_tensor(out=self.xr[:, m, :], in0=self.xr[:, m, :], scalar=ALPHA, in1=ps[:, :], op0=ALU.mult, op1=ALU.add),
                  reads=[self.xr, ps], writes=[self.xr])
            kb.op("act", lambda e, m=m: e.copy(out=self.xb[:, m, :], in_=self.xr[:, m, :]), reads=[self.xr], writes=[self.xb])
            kb.op("act", lambda e, m=m: e.activation(out=self.ysq[:, m, :], in_=self.xr[:, m, :], func=AF.Square), reads=[self.xr], writes=[self.ysq])
        ps1 = self.nps(); ps2 = self.nps()
        kb.mm([lambda pe, k=k: pe.matmul(ps1[:, :], self.ones_b[:], self.xb[:, k, :], start=(k == 0), stop=(k == 7)) for k in range(8)],
              reads=[self.ones_b, self.xb], writes=[ps1])
        kb.mm([lambda pe, k=k: pe.matmul(ps2[:, :], self.ones_b[:], self.ysq[:, k, :], start=(k == 0), stop=(k == 7)) for k in range(8)],
              reads=[self.ones_b, self.ysq], writes=[ps2])
        kb.op("act", lambda e: e.activation(out=self.lnm[:], in_=ps1[:, :], func=AF.Identity, scale=1.0 / D), reads=[ps1], writes=[self.lnm])
        kb.op("dve", lambda e: e.tensor_tensor(out=self.lnt[:], in0=self.lnm[:], in1=self.lnm[:], op=ALU.mult), reads=[self.lnm], writes=[self.lnt])
        kb.op("dve", lambda e: e.scalar_tensor_tensor(out=self.lnt[:], in0=ps2[:, :], scalar=1.0 / D, in1=self.lnt[:], op0=ALU.mult, op1=ALU.subtract),
              reads=[ps2, self.lnt], writes=[self.lnt])
        kb.op("act", lambda e: e.activation(out=self.lnt[:], in_=self.lnt[:], func=AF.Ln, bias=EPS, scale=1.0), reads=[self.lnt], writes=[self.lnt])
        kb.op("act", lambda e: e.activation(out=self.lnr[:], in_=self.lnt[:], func=AF.Exp, scale=-0.5), reads=[self.lnt], writes=[self.lnr])
        for m in range(8):
            eng = "pool" if (m % 2 == 1 and self.dbg.get("ln_pool", True)) else "dve"
            lnu = self.lnu2 if eng == "pool" else self.lnu
            kb.op(eng, lambda e, m=m, lnu=lnu: e.tensor_tensor(out=lnu[:], in0=self.xr[:, m, :], in1=self.lnm[:], op=ALU.subtract), reads=[self.xr, self.lnm], writes=[lnu])
            kb.op(eng, lambda e, m=m, lnu=lnu: e.tensor_tensor(out=lnu[:], in0=lnu[:], in1=self.lnr[:], op=ALU.mult), reads=[lnu, self.lnr], writes=[lnu])
            kb.op("act", lambda e, m=m, lnu=lnu: e.activation(out=self.xb[:, m, :], in_=lnu[:], func=AF.Identity, bias=self.lnp[:, gi + 1, m:m + 1], scale=self.lnp[:, gi, m:m + 1]),
                  reads=[lnu, self.lnp], writes=[self.xb])
            kb.op("act", lambda e, m=m, lnu=lnu: e.activation(out=self.xr[:, m, :], in_=lnu[:], func=AF.Identity, bias=self.lnp[:, gi + 1, m:m + 1], scale=self.lnp[:, gi, m:m + 1]),
                  reads=[lnu, self.lnp], writes=[self.xr])

    def phaseE(self, bi, l):
        self.kb.phase = 'phaseE'
        W = {}

        def pss(m):
            j = m // 4
            if (m % 4) == 0:
                W["t"], W["v"] = self.wload(self.wsrc(self.w_out, l, D, j * 512, 512), 8, 512)
            return self.proj_feat(W["t"], W["v"], (m % 4) * 128, src=self.mrg)
        self.resid_ln(pss, 0)

    def phaseF(self, bi, l):
        self.kb.phase = 'phaseF'
        kb = self.kb
        for jg in range(6):
            n = 512 if jg < 5 else 256
            Gt, Gv = self.wload(self.wsrc(self.w_gate, l, D, jg * 512, n), 8, n)
            Ut, Uv = self.wload(self.wsrc(self.w_up, l, D, jg * 512, n), 8, n)
            for t in range(n // 128):
                j = jg * 4 + t
                psG = self.proj_feat(Gt, Gv, t * 128)
                psU = self.proj_feat(Ut, Uv, t * 128)
                kb.op("act", lambda e, psG=psG: e.activation(out=self.sg[:], in_=psG[:, :], func=AF.Silu), reads=[psG], writes=[self.sg])
                kb.op("dve", lambda e, j=j, psU=psU: e.tensor_tensor(out=self.hT[:, j, :], in0=self.sg[:], in1=psU[:, :], op=ALU.mult),
                      reads=[self.sg, psU], writes=[self.hT])

        def pss(m):
            Wt, Wv = self.wload(self.wsrc_down(l, m), 22, 128)
            return self.proj_feat(Wt, Wv, 0, src=self.hT, kc=22)
        self.resid_ln(pss, 2)


NB = 16


class GenS(Gen):
    def _decl(self):
        Gen._decl(self)
        if not self.do_sample:
            return
        dp = self.depth
        d, o = self.din, self.dout
        self.xs = d("xs", [NB, D])
        self.st_ret = d("st_ret", [dp, NB, 4, 128, 128])
        self.st_ssm = d("st_ssm", [dp, NB, 16, 64, 128])
        self.st_conv = d("st_conv", [dp, NB, 3, 1536])
        self.ck = d("ck", [dp, NB, 128, 128])
        self.cv = d("cv", [dp, NB, 128, 128])
        self.c_rots = d("c_rots", [NB, 256])
        self.c_sel = d("c_sel", [NB, 16 * 128 + 2 * 128])
        self.c_eye = d("c_eye", [128, 256])
        self.conv_w_n = d("conv_w_n", [dp, 1, 4 * 1536])
        self.conv_b_n = d("conv_b_n", [dp, 1, 1536])
        self.ln_n = d("ln_n", [dp, 4, 1, 1024])
        self.ys = o("ys", [NB, D])
        self.ret_s = o("ret_s", [dp, NB, 4, 128, 128])
        self.ssm_s = o("ssm_s", [dp, NB, 16, 64, 128])
        self.conv_s = o("conv_s", [dp, NB, 3, 1536])
        self.k_s = o("k_s", [dp, NB, 128, 128])
        self.v_s = o("v_s", [dp, NB, 128, 128])

    def _alloc(self):
        Gen._alloc(self)
        if not self.do_sample:
            return
        nc = self.nc
        st = {"off": 0, "base": None, "size": 0}

        def region(base, size):
            st["base"], st["size"], st["off"] = base, size, 0

        def al(name, shape, dt):
            n = 1
            for x in shape[1:]:
                n *= x
            nb = (n * (4 if dt == F32 else 2) + 31) // 32 * 32
            assert st["off"] + nb <= st["size"], (name, st["off"], nb, st["size"])
            h = nc.alloc_sbuf_tensor_at(name, list(shape), dt, offset=st["base"] + st["off"])
            st["off"] += nb
            return Tile(h, None, name)
        region(self.xr_off, 16384 + 8192)
        self.sx = al("sx", [NB, 1024], F32)
        self.sxT = al("sxT", [128, 8, NB], BF16)
        self.sxconv = al("sxconv", [NB, 1536], F32)
        self.s_orT = al("s_orT", [128, 4, NB], BF16); self.s_yT = al("s_yT", [128, 8, NB], BF16); self.s_ocT = al("s_ocT", [128, 4, NB], BF16)
        self.s_mT = al("s_mT", [128, 8, NB], BF16); self.s_hT = al("s_hT", [128, 22, NB], BF16)
        self.s_sel = al("s_sel", [NB, 16 * 128 + 256], F32)
        self.s_eye = al("s_eye", [128, 256], F32)
        self.s_rot = al("s_rot", [NB, 256], F32)
        self.s_b0 = al("s_b0", [NB, 8], F32)
        A = self.abase
        region(A, self.asz)
        self.r_q = al("r_q", [NB, 512], F32); self.r_k = al("r_k", [NB, 512], F32)
        self.r_tA = al("r_tA", [NB, 4, 2, 64], F32); self.r_tB = al("r_tB", [NB, 4, 2, 64], F32)
        self.r_qb = al("r_qb", [NB, 512], BF16); self.r_kb = al("r_kb", [NB, 512], BF16); self.r_vb = al("r_vb", [NB, 512], BF16)
        self.r_g = al("r_g", [NB, 512], F32)
        self.r_qT = al("r_qT", [128, 4, NB], BF16)
        self.r_qTm = al("r_qTm", [128, 4, NB, NB], BF16)
        self.r_vbd = al("r_vbd", [NB, 4, 4, 128], BF16)
        self.r_S = [al("r_S%d" % i, [128, 4, 4, 128], F32) for i in range(2)]
        self.r_Sb = al("r_Sb", [128, NB, 4, 128], BF16)
        self.r_o = al("r_o", [NB, 4, 128], F32)
        self.r_st6 = al("r_st6", [NB, 4, 6], F32); self.r_mv = al("r_mv", [NB, 4, 2], F32)
        self.r_rs = al("r_rs", [NB, 4], F32); self.r_rsb = al("r_rsb", [NB, 4], F32)
        self.r_og = al("r_og", [NB, 4, 128], F32); self.r_ogb = al("r_ogb", [NB, 512], BF16)
        print("arena S-RET", st["off"])
        region(A, self.asz)
        self.c_w = al("c_w", [NB, 4, 1536], F32); self.c_buf = al("c_buf", [NB, 3, 1536], F32)
        self.c_new = al("c_new", [NB, 1536], F32); self.c_acc = al("c_acc", [NB, 1536], F32)
        print("arena S-CONV", st["off"])
        region(A, self.asz)
        self.d_z = al("d_z", [NB, 1024], F32)
        self.d_h = al("d_h", [128, 4, 8, 128], F32); self.d_t = al("d_t", [128, 4, 8, 128], F32)
        self.d_sm = al("d_sm", [NB, 8, 16], F32)
        self.d_xdt = al("d_xdt", [NB, 1024], F32)
        self.d_xdtT = al("d_xdtT", [128, NB, 8], F32)
        self.d_R = al("d_R", [NB, 2, 4, 128], F32)
        self.d_RA = al("d_RA", [NB, 2, NB, 8], F32)
        self.d_dA = al("d_dA", [128, NB, 8], F32)
        self.d_BC = al("d_BC", [128, 2, 4, 128], F32)
        self.d_yT = al("d_yT", [128, NB, 8], F32)
        self.d_y = al("d_y", [NB, 1024], F32); self.d_y2 = self.d_xdt
        self.d_ssq = al("d_ssq", [NB, 4], F32)
        self.d_nw = al("d_nw", [NB, 1024], F32)
        self.d_ynb = al("d_ynb", [NB, 1024], BF16)
        print("arena S-SSD", st["off"])
        region(A, self.asz)
        self.a_K = al("a_K", [128, NB, 128], F32); self.a_V = al("a_V", [128, NB, 128], F32)
        self.a_Vb = al("a_Vb", [128, NB, 2, 80], BF16)
        self.a_q = al("a_q", [NB, 512], F32); self.a_kn = al("a_kn", [NB, 128], F32); self.a_vn = al("a_vn", [NB, 2, 80], F32)
        self.a_pr = al("a_pr", [128, 4, 2, 64], F32)
        self.a_s = al("a_s", [128, NB, 8], F32)
        self.a_p = al("a_p", [128, NB, 8], BF16)
        self.a_Pm = al("a_Pm", [128, NB, 8, NB], BF16)
        self.a_sn = al("a_sn", [NB, 4, 2, 64], F32); self.a_s8 = al("a_s8", [NB, 8], F32); self.a_pn = al("a_pn", [NB, 8], F32)
        self.a_ou = al("a_ou", [NB, 8, 65], F32); self.a_t = al("a_t", [NB, 8, 65], F32)
        self.a_den = al("a_den", [NB, 8], F32)
        self.a_ob = al("a_ob", [NB, 8, 64], BF16)
        self.a_es = al("a_es", [NB, 8], F32)
        print("arena S-SWA", st["off"])
        region(A, self.asz)
        self.m_sg = al("m_sg", [NB, 512], F32); self.m_t = al("m_t", [NB, 512], F32)
        self.m_acc = al("m_acc", [NB, 1024], F32)
        self.m_b = al("m_b", [NB, 1024], BF16)
        self.m_g = al("m_g", [NB, 1024], F32); self.m_bb = al("m_bb", [NB, 1024], F32)
        self.m_st = al("m_st", [NB, 2, 6], F32); self.m_mv = al("m_mv", [NB, 2], F32); self.m_rs = al("m_rs", [NB, 2], F32)
        self.m_u = al("m_u", [NB, 1024], F32)
        self.m_h = al("m_h", [NB, 2816], F32); self.m_hb = al("m_hb", [NB, 2816], BF16)
        print("arena S-MLP", st["off"])

    def sproj(self, Wt, Wv, n, src=None, kc=8, col0=0):
        src = self.sxT if src is None else src
        ps = self.nps()
        self.kb.mm([lambda pe, k=k, ps=ps: pe.matmul(ps[0:NB, 0:n], src[:, k, :], Wv[:, k, col0:col0 + n], start=(k == 0), stop=(k == kc - 1))
                    for k in range(kc)], reads=[src, Wt], writes=[ps])
        return ps

    def s_transp(self, src_tile, src_aps, dst_tile, dst_ap):
        n = len(src_aps)
        pb = self.npb()
        self.kb.mm([lambda pe, j=j, pb=pb: pe.transpose(out=pb[:, j * NB:(j + 1) * NB], in_=src_aps[j], identity=self.id_b[0:NB, 0:NB])
                    for j in range(n)], reads=[src_tile, self.id_b], writes=[pb])
        self.kb.op("act", lambda e, pb=pb: e.copy(out=dst_ap, in_=pb[:, 0:n * NB].rearrange("p (j t) -> p j t", j=n)), reads=[pb], writes=[dst_tile])

    def s_tokmajor_to_T(self, src, ncols, dst):
        nt = ncols // 128
        for j0 in range(0, nt, 8):
            j1 = min(nt, j0 + 8)
            self.s_transp(src, [src[0:NB, j * 128:(j + 1) * 128] for j in range(j0, j1)], dst, dst[:, j0:j1, :])

    def sample_pass(self):
        kb = self.kb
        self.fence()
        kb.dma("sp", self.sx[:], self.xs.ap()[:, :], writes=[self.sx])
        kb.dma("sp", self.s_sel[:], self.c_sel.ap()[:, :], writes=[self.s_sel])
        kb.dma("sp", self.s_eye[:], self.c_eye.ap()[:, :], writes=[self.s_eye])
        kb.dma("sp", self.s_rot[:], self.c_rots.ap()[:, :], writes=[self.s_rot])
        kb.dma("sp", self.s_b0[:], self.rel_bias.ap()[0:1, 0:8].partition_broadcast(NB), writes=[self.s_b0])
        self.s_refresh_xT()
        for l in range(self.depth):
            self.load_params(0, l)
            self.s_ret(l)
            self.s_conv(l)
            self.s_ssd(l)
            self.s_swa(l)
            self.s_mlp(l)
        kb.dma("sp", self.ys.ap()[:, :], self.sx[:], reads=[self.sx], is_output=True)

    def s_refresh_xT(self):
        kb = self.kb
        kb.op("act", lambda e: e.copy(out=self.m_b[:], in_=self.sx[:]), reads=[self.sx], writes=[self.m_b])
        self.s_tokmajor_to_T(self.m_b, 1024, self.sxT)

    def s_ret(self, l):
        self.kb.phase = 's_ret'
        kb = self.kb
        self.fence()
        gam = [1.0 - 2.0 ** (-5 - h) for h in range(4)]
        for gi, off in enumerate((OQ, OK_, OV, OG)):
            Wt, Wv = self.wload(self.wsrc(self.w_in, l, D, off, 512), 8, 512)
            ps = self.sproj(Wt, Wv, 512)
            pv = ps[0:NB, :]
            if gi < 2:
                dst = self.r_q if gi == 0 else self.r_k
                psv = pv.rearrange("p (h t e) -> p h t e", h=4, t=2)
                cosb = self.s_rot[:, gi * 128:gi * 128 + 64].unsqueeze(1).unsqueeze(1).broadcast_to([NB, 4, 2, 64])
                sinb = self.s_rot[:, gi * 128 + 64:gi * 128 + 128].unsqueeze(1).broadcast_to([NB, 4, 64])
                kb.op("dve", lambda e, psv=psv, cosb=cosb: e.tensor_tensor(out=self.r_tA[:], in0=psv, in1=cosb, op=ALU.mult), reads=[ps, self.s_rot], writes=[self.r_tA])
                kb.op("dve", lambda e, psv=psv, sinb=sinb: e.tensor_tensor(out=self.r_tB[:, :, 0, :], in0=psv[:, :, 1, :], in1=sinb, op=ALU.mult), reads=[ps, self.s_rot], writes=[self.r_tB])
                kb.op("dve", lambda e, psv=psv, sinb=sinb: e.tensor_tensor(out=self.r_tB[:, :, 1, :], in0=psv[:, :, 0, :], in1=sinb, op=ALU.mult), reads=[ps, self.s_rot], writes=[self.r_tB])
                dv = dst[:].rearrange("p (h t e) -> p h t e", h=4, t=2)
                kb.op("dve", lambda e, dv=dv: e.tensor_tensor(out=dv[:, :, 0, :], in0=self.r_tA[:, :, 0, :], in1=self.r_tB[:, :, 0, :], op=ALU.subtract), reads=[self.r_tA, self.r_tB], writes=[dst])
                kb.op("dve", lambda e, dv=dv: e.tensor_tensor(out=dv[:, :, 1, :], in0=self.r_tA[:, :, 1, :], in1=self.r_tB[:, :, 1, :], op=ALU.add), reads=[self.r_tA, self.r_tB], writes=[dst])
                db = self.r_qb if gi == 0 else self.r_kb
                kb.op("act", lambda e, dst=dst, db=db: e.copy(out=db[:], in_=dst[:]), reads=[dst], writes=[db])
            elif gi == 2:
                kb.op("act", lambda e, pv=pv: e.copy(out=self.r_vb[:], in_=pv), reads=[ps], writes=[self.r_vb])
            else:
                kb.op("act", lambda e, pv=pv: e.activation(out=self.r_g[:], in_=pv, func=AF.Silu), reads=[ps], writes=[self.r_g])
        self.s_transp(self.r_qb, [self.r_qb[0:NB, h * 128:(h + 1) * 128] for h in range(4)], self.r_qT, self.r_qT[:, :, :])
        kb.op("dve", lambda e: e.tensor_tensor(out=self.r_qTm[:], in0=self.r_qT[:].unsqueeze(2).broadcast_to([128, 4, NB, NB]),
                                                in1=self.s_eye[:].rearrange("p (a b) -> p a b", a=NB).unsqueeze(1).broadcast_to([128, 4, NB, NB]), op=ALU.mult),
              reads=[self.r_qT, self.s_eye], writes=[self.r_qTm])
        sel = self.s_sel[:, 0:NB * 128].rearrange("p (b k) -> p b k", b=NB)
        for sg in range(4):
            S = self.r_S[sg % 2]
            kb.dma("sp", S[:], self.st_ret.ap()[l, sg * 4:(sg + 1) * 4].rearrange("b h d v -> d b h v"), writes=[S])
            kb.op("dve", lambda e, sg=sg: e.tensor_tensor(
                out=self.r_vbd[:], in0=self.r_vb[:].rearrange("p (h v) -> p h v", h=4).unsqueeze(2).broadcast_to([NB, 4, 4, 128]),
                in1=sel[:, sg * 4:(sg + 1) * 4, 0:1].rearrange("p b o -> p o b").unsqueeze(3).broadcast_to([NB, 4, 4, 128]), op=ALU.mult),
                reads=[self.r_vb, self.s_sel], writes=[self.r_vbd])
            for h in range(4):
                ps = self.nps()
                kb.mm([lambda pe, h=h, ps=ps: pe.matmul(ps[:, :], self.r_kb[0:NB, h * 128:(h + 1) * 128], self.r_vbd[:, h, :, :], start=True, stop=True)],
                      reads=[self.r_kb, self.r_vbd], writes=[ps])
                kb.op("dve", lambda e, h=h, ps=ps, S=S: e.scalar_tensor_tensor(out=S[:, :, h, :], in0=S[:, :, h, :], scalar=gam[h],
                                                                               in1=ps[:, :].rearrange("p (b v) -> p b v", b=4), op0=ALU.mult, op1=ALU.add),
                      reads=[S, ps], writes=[S])
            kb.op("act", lambda e, sg=sg, S=S: e.copy(out=self.r_Sb[:, sg * 4:(sg + 1) * 4, :, :], in_=S[:]), reads=[S], writes=[self.r_Sb])
            kb.dma("sp", self.ret_s.ap()[l, sg * 4:(sg + 1) * 4].rearrange("b h d v -> d b h v"), S[:], reads=[S], is_output=True)
        pso = self.nps()
        fns = []
        for h in range(4):
            for b in range(NB):
                fns.append(lambda pe, h=h, b=b: pe.matmul(pso[0:NB, h * 128:(h + 1) * 128], self.r_qTm[:, h, b, :], self.r_Sb[:, b, h, :],
                                                          start=(b == 0), stop=(b == NB - 1)))
        kb.mm(fns, reads=[self.r_qTm, self.r_Sb], writes=[pso])
        kb.op("act", lambda e: e.copy(out=self.r_o[:], in_=pso[0:NB, :].rearrange("p (h v) -> p h v", h=4)), reads=[pso], writes=[self.r_o])
        for h in range(4):
            kb.op("dve", lambda e, h=h: e.bn_stats(out=self.r_st6[:, h, :], in_=self.r_o[:, h, :]), reads=[self.r_o], writes=[self.r_st6])
        for h in range(4):
            kb.op("dve", lambda e, h=h: e.bn_aggr(out=self.r_mv[:, h, :], in_=self.r_st6[:, h, :]), reads=[self.r_st6], writes=[self.r_mv])
        kb.op("act", lambda e: e.activation(out=self.r_rs[:], in_=self.r_mv[:, :, 1], func=AF.Ln, bias=EPS, scale=1.0), reads=[self.r_mv], writes=[self.r_rs])
        kb.op("act", lambda e: e.activation(out=self.r_rsb[:], in_=self.r_rs[:], func=AF.Exp, scale=-0.5), reads=[self.r_rs], writes=[self.r_rsb])
        for h in range(4):
            kb.op("dve", lambda e, h=h: e.scalar_tensor_tensor(out=self.r_og[:, h, :], in0=self.r_o[:, h, :], scalar=self.r_mv[:, h, 0:1],
                                                                 in1=self.r_g[:, h * 128:(h + 1) * 128], op0=ALU.subtract, op1=ALU.mult),
                  reads=[self.r_o, self.r_mv, self.r_g], writes=[self.r_og])
        for h in range(4):
            kb.op("act", lambda e, h=h: e.activation(out=self.r_ogb[:, h * 128:(h + 1) * 128], in_=self.r_og[:, h, :], func=AF.Identity, scale=self.r_rsb[:, h:h + 1]),
                  reads=[self.r_og, self.r_rsb], writes=[self.r_ogb])
        self.s_tokmajor_to_T(self.r_ogb, 512, self.s_orT)

    def s_conv(self, l):
        self.kb.phase = 's_conv'
        kb = self.kb
        self.fence()
        kb.dma("sp", self.c_w[:].rearrange("p t c -> p (t c)"), self.conv_w_n.ap()[l].partition_broadcast(NB), writes=[self.c_w])
        kb.dma("sp", self.c_buf[:], self.st_conv.ap()[l], writes=[self.c_buf])
        kb.dma("sp", self.c_acc[:], self.conv_b_n.ap()[l].partition_broadcast(NB), writes=[self.c_acc])
        for g3 in range(3):
            Wt, Wv = self.wload(self.wsrc(self.w_in, l, D, OXBC + g3 * 512, 512), 8, 512)
            ps = self.sproj(Wt, Wv, 512)
            kb.op("act", lambda e, ps=ps, g3=g3: e.copy(out=self.c_new[:, g3 * 512:(g3 + 1) * 512], in_=ps[0:NB, :]), reads=[ps], writes=[self.c_new])
        kb.dma("sp", self.conv_s.ap()[l, :, 0:2, :], self.st_conv.ap()[l, :, 1:3, :], is_output=True)
        kb.dma("sp", self.conv_s.ap()[l, :, 2, :], self.c_new[:], reads=[self.c_new], is_output=True)
        for tau in range(4):
            src = self.c_buf[:, tau, :] if tau < 3 else self.c_new[:]
            kb.op("dve", lambda e, tau=tau, src=src: e.tensor_tensor(out=self.c_w[:, tau, :], in0=self.c_w[:, tau, :], in1=src, op=ALU.mult),
                  reads=[self.c_w, self.c_buf, self.c_new], writes=[self.c_w])
            kb.op("dve", lambda e, tau=tau: e.tensor_tensor(out=self.c_acc[:], in0=self.c_acc[:], in1=self.c_w[:, tau, :], op=ALU.add),
                  reads=[self.c_w, self.c_acc], writes=[self.c_acc])
        kb.op("act", lambda e: e.activation(out=self.sxconv[:], in_=self.c_acc[:], func=AF.Silu), reads=[self.c_acc], writes=[self.sxconv])

    def s_ssd(self, l):
        self.kb.phase = 's_ssd'
        kb = self.kb
        self.fence()
        kb.dma("sp", self.d_nw[:], self.norm_w.ap()[l:l + 1, :].partition_broadcast(NB), writes=[self.d_nw])
        for g2 in range(2):
            Wt, Wv = self.wload(self.wsrc(self.w_in, l, D, OZ + g2 * 512, 512), 8, 512)
            ps = self.sproj(Wt, Wv, 512)
            kb.op("act", lambda e, ps=ps, g2=g2: e.activation(out=self.d_z[:, g2 * 512:(g2 + 1) * 512], in_=ps[0:NB, :], func=AF.Silu), reads=[ps], writes=[self.d_z])
        Wt, Wv = self.wload(self.wsrc(self.w_in, l, D, ODT, 16), 8, 16)
        ps = self.sproj(Wt, Wv, 16)
        sm = self.d_sm
        DT, DA, T1, T2 = [sm[:, i, :] for i in range(4)]
        p16 = self.sm16[0:NB, :, :]
        kb.op("dve", lambda e, ps=ps: e.tensor_tensor(out=T1, in0=ps[0:NB, 0:16], in1=p16[:, 0, :], op=ALU.add), reads=[ps, self.sm16], writes=[sm])
        kb.op("act", lambda e: e.activation(out=T2, in_=T1, func=AF.Exp), reads=[sm], writes=[sm])
        kb.op("act", lambda e: e.activation(out=DT, in_=T2, func=AF.Ln, bias=1.0, scale=1.0), reads=[sm], writes=[sm])
        kb.op("dve", lambda e: e.tensor_tensor(out=T1, in0=DT, in1=p16[:, 1, :], op=ALU.mult), reads=[sm, self.sm16], writes=[sm])
        kb.op("act", lambda e: e.activation(out=DA, in_=T1, func=AF.Exp), reads=[sm], writes=[sm])
        xs = self.sxconv[:, 0:1024]
        kb.op("dve", lambda e: e.tensor_tensor(out=self.d_xdt[:].rearrange("p (hh g q) -> p g hh q", hh=8, g=2), in0=xs.rearrange("p (g hh q) -> p g hh q", g=2, hh=8),
                                                in1=DT.rearrange("p (g hh) -> p g hh", g=2).unsqueeze(3).broadcast_to([NB, 2, 8, 64]), op=ALU.mult), reads=[self.sxconv, sm], writes=[self.d_xdt])
        pst = self.nps()
        kb.mm([lambda pe, hh=hh: pe.transpose(out=pst[:, hh * NB:(hh + 1) * NB], in_=self.d_xdt[:, hh * 128:(hh + 1) * 128], identity=self.ident_f[0:NB, 0:NB]) for hh in range(8)],
              reads=[self.d_xdt, self.cf], writes=[pst])
        kb.op("act", lambda e: e.copy(out=self.d_xdtT[:].rearrange("p b hh -> p hh b"), in_=pst[:, 0:8 * NB].rearrange("p (hh b) -> p hh b", hh=8)),
              reads=[pst], writes=[self.d_xdtT])
        eye = self.s_sel[:, 0:NB * 128].rearrange("p (b k) -> p b k", b=NB)[:, :, 0]
        gsel = self.s_sel[:, NB * 128:NB * 128 + 256].rearrange("p (g k) -> p g k", g=2)
        kb.op("dve", lambda e: e.tensor_tensor(out=self.d_RA[:], in0=DA.rearrange("p (g hh) -> p g hh", g=2).unsqueeze(2).broadcast_to([NB, 2, NB, 8]),
                                                in1=eye.unsqueeze(1).unsqueeze(3).broadcast_to([NB, 2, NB, 8]), op=ALU.mult), reads=[sm, self.s_sel], writes=[self.d_RA])
        psa = self.nps()
        kb.mm([lambda pe, g=g: pe.matmul(psa[:, 0:NB * 8], gsel[:, g, :], self.d_RA[:, g, :, :], start=(g == 0), stop=(g == 1)) for g in range(2)],
              reads=[self.s_sel, self.d_RA], writes=[psa])
        kb.op("act", lambda e: e.copy(out=self.d_dA[:], in_=psa[:, 0:NB * 8].rearrange("p (b hh) -> p b hh", b=NB)), reads=[psa], writes=[self.d_dA])
        Bc = self.sxconv[:, 1024:1536].rearrange("p (t g n) -> p t g n", t=2, g=2)
        for sg in range(4):
            bs = slice(sg * 4, (sg + 1) * 4)
            H = self.d_h
            for g in range(2):
                for bb in range(4):
                    kb.dma("sp", H[g * 64:(g + 1) * 64, bb, :, :], self.st_ssm.ap()[l, sg * 4 + bb, g * 8:(g + 1) * 8].rearrange("hh p n -> p hh n"), writes=[H])
            for t in range(2):
                kb.op("dve", lambda e, sg=sg, t=t: e.tensor_tensor(out=self.d_R[:], in0=Bc[:, t, :, :].unsqueeze(2).broadcast_to([NB, 2, 4, 128]),
                                                                    in1=eye[:, sg * 4:(sg + 1) * 4].unsqueeze(1).unsqueeze(3).broadcast_to([NB, 2, 4, 128]), op=ALU.mult),
                      reads=[self.sxconv, self.s_sel], writes=[self.d_R])
                psb_ = self.nps()
                kb.mm([lambda pe, g=g, psb_=psb_: pe.matmul(psb_[:, :], gsel[:, g, :], self.d_R[:, g, :, :], start=(g == 0), stop=(g == 1)) for g in range(2)],
                      reads=[self.s_sel, self.d_R], writes=[psb_])
                kb.op("act", lambda e, t=t, psb_=psb_: e.copy(out=self.d_BC[:, t, :, :], in_=psb_[:, :].rearrange("p (b n) -> p b n", b=4)), reads=[psb_], writes=[self.d_BC])
            kb.op("dve", lambda e, bs=bs: e.tensor_tensor(out=H[:], in0=H[:], in1=self.d_dA[:, bs, :].unsqueeze(3).broadcast_to([128, 4, 8, 128]), op=ALU.mult),
                  reads=[H, self.d_dA], writes=[H])
            kb.op("dve", lambda e, bs=bs: e.tensor_tensor(out=self.d_t[:], in0=self.d_xdtT[:, bs, :].unsqueeze(3).broadcast_to([128, 4, 8, 128]),
                                                           in1=self.d_BC[:, 0, :, :].unsqueeze(2).broadcast_to([128, 4, 8, 128]), op=ALU.mult),
                  reads=[self.d_xdtT, self.d_BC], writes=[self.d_t])
            kb.op("dve", lambda e: e.tensor_tensor(out=H[:], in0=H[:], in1=self.d_t[:], op=ALU.add), reads=[H, self.d_t], writes=[H])
            for g in range(2):
                for bb in range(4):
                    kb.dma("sp", self.ssm_s.ap()[l, sg * 4 + bb, g * 8:(g + 1) * 8].rearrange("hh p n -> p hh n"), H[g * 64:(g + 1) * 64, bb, :, :], reads=[H], is_output=True)
            kb.op("dve", lambda e: e.tensor_tensor(out=self.d_t[:], in0=H[:], in1=self.d_BC[:, 1, :, :].unsqueeze(2).broadcast_to([128, 4, 8, 128]), op=ALU.mult),
                  reads=[H, self.d_BC], writes=[self.d_t])
            kb.op("dve", lambda e, bs=bs: e.tensor_reduce(out=self.d_yT[:, bs, :], in_=self.d_t[:], axis=AX.X, op=ALU.add), reads=[self.d_t], writes=[self.d_yT])
        psy = [self.nps(), self.nps()]
        for half in range(2):
            kb.mm([lambda pe, hh=hh, half=half: pe.transpose(out=psy[half][0:NB, (hh % 4) * 128:(hh % 4 + 1) * 128], in_=self.d_yT[:, :, hh], identity=self.ident_f)
                   for hh in range(half * 4, half * 4 + 4)], reads=[self.d_yT, self.cf], writes=[psy[half]])
            yv = self.d_y[:].rearrange("p (g hh q) -> p hh g q", g=2, hh=8)
            kb.op("act", lambda e, half=half, yv=yv: e.copy(out=yv[:, half * 4:half * 4 + 4, :, :], in_=psy[half][0:NB, :].rearrange("p (hh g q) -> p hh g q", hh=4, g=2)),
                  reads=[psy[half]], writes=[self.d_y])
        kb.op("dve", lambda e: e.tensor_tensor(out=self.d_y2[:].rearrange("p (h q) -> p h q", h=16), in0=xs.rearrange("p (h q) -> p h q", h=16),
                                                in1=p16[:, 2, :].unsqueeze(2).broadcast_to([NB, 16, 64]), op=ALU.mult), reads=[self.sxconv, self.sm16], writes=[self.d_y2])
        kb.op("dve", lambda e: e.tensor_tensor(out=self.d_y[:], in0=self.d_y[:], in1=self.d_y2[:], op=ALU.add), reads=[self.d_y, self.d_y2], writes=[self.d_y])
        kb.op("dve", lambda e: e.tensor_tensor(out=self.d_y[:], in0=self.d_y[:], in1=self.d_z[:], op=ALU.mult), reads=[self.d_y, self.d_z], writes=[self.d_y])
        for g in range(2):
            gs = slice(g * 512, (g + 1) * 512)
            kb.op("act", lambda e, g=g, gs=gs: e.activation(out=self.d_y2[:, gs], in_=self.d_y[:, gs], func=AF.Square, accum_out=self.d_ssq[:, g:g + 1]),
                  reads=[self.d_y], writes=[self.d_y2, self.d_ssq])
        kb.op("act", lambda e: e.activation(out=self.d_ssq[:, 2:4], in_=self.d_ssq[:, 0:2], func=AF.Ln, bias=EPS, scale=1.0 / 512), reads=[self.d_ssq], writes=[self.d_ssq])
        kb.op("act", lambda e: e.activation(out=self.d_ssq[:, 2:4], in_=self.d_ssq[:, 2:4], func=AF.Exp, scale=-0.5), reads=[self.d_ssq], writes=[self.d_ssq])
        for g in range(2):
            gs = slice(g * 512, (g + 1) * 512)
            kb.op("dve", lambda e, g=g, gs=gs: e.scalar_tensor_tensor(out=self.d_ynb[:, gs], in0=self.d_y[:, gs], scalar=self.d_ssq[:, 2 + g:3 + g],
                                                                       in1=self.d_nw[:, gs], op0=ALU.mult, op1=ALU.mult),
                  reads=[self.d_y, self.d_ssq, self.d_nw], writes=[self.d_ynb])
        self.s_tokmajor_to_T(self.d_ynb, 1024, self.s_yT)

    def s_swa(self, l):
        self.kb.phase = 's_swa'
        kb = self.kb
        self.fence()
        kb.dma("sp", self.a_K[:], self.ck.ap()[l].rearrange("b k e -> k b e"), writes=[self.a_K])
        kb.dma("sp", self.a_V[:], self.cv.ap()[l].rearrange("b k e -> k b e"), writes=[self.a_V])
        kb.dma("sp", self.a_es[:], self.sinks.ap()[l:l + 1, :].partition_broadcast(NB), writes=[self.a_es])
        kb.op("act", lambda e: e.activation(out=self.a_es[:], in_=self.a_es[:], func=AF.Exp), reads=[self.a_es], writes=[self.a_es])
        kb.dma("sp", self.k_s.ap()[l, :, 0:127, :], self.ck.ap()[l, :, 1:128, :], is_output=True)
        kb.dma("sp", self.v_s.ap()[l, :, 0:127, :], self.cv.ap()[l, :, 1:128, :], is_output=True)
        Wt, Wv = self.wload(self.wsrc(self.w_in, l, D, OQC, 512), 8, 512)
        ps = self.sproj(Wt, Wv, 512)
        kb.op("act", lambda e, ps=ps: e.copy(out=self.a_q[:], in_=ps[0:NB, :]), reads=[ps], writes=[self.a_q])
        Wt, Wv = self.wload(self.wsrc(self.w_in, l, D, OKC, 256), 8, 256)
        ps = self.sproj(Wt, Wv, 256)
        kb.op("act", lambda e, ps=ps: e.copy(out=self.a_kn[:], in_=ps[0:NB, 0:128]), reads=[ps], writes=[self.a_kn])
        kb.op("act", lambda e: e.activation(out=self.a_vn[:, :, 64:65], in_=self.cf[0:NB, 0:2].unsqueeze(2), func=AF.Identity, scale=0.0, bias=1.0), reads=[self.cf], writes=[self.a_vn])
        kb.op("act", lambda e, ps=ps: e.copy(out=self.a_vn[:, :, 0:64], in_=ps[0:NB, 128:256].rearrange("p (g e) -> p g e", g=2)), reads=[ps], writes=[self.a_vn])
        kb.dma("sp", self.k_s.ap()[l, :, 127, :], self.a_kn[:], reads=[self.a_kn], is_output=True)
        kb.dma("sp", self.v_s.ap()[l, :, 127, :].rearrange("b (g e) -> b g e", g=2), self.a_vn[:, :, 0:64], reads=[self.a_vn], is_output=True)
        kb.op("act", lambda e: e.activation(out=self.a_Vb[:, :, :, 64:65], in_=self.cf[:, 0:2 * NB].rearrange("p (b g o) -> p b g o", b=NB, g=2), func=AF.Identity, scale=0.0, bias=1.0),
              reads=[self.cf], writes=[self.a_Vb])
        kb.op("act", lambda e: e.copy(out=self.a_Vb[:, :, :, 0:64], in_=self.a_V[:].rearrange("p b (g e) -> p b g e", g=2)), reads=[self.a_V], writes=[self.a_Vb])
        selq = self.s_sel[:, 0:NB * 128].rearrange("p (b k) -> p b k", b=NB)
        sv = self.a_s[:].rearrange("p b (half j) -> p b j half", half=2)
        for b in range(NB):
            psq = self.nps()
            kb.mm([lambda pe, b=b, psq=psq: pe.matmul(psq[:, :], selq[:, b, :], self.a_q[:], start=True, stop=True)], reads=[self.s_sel, self.a_q], writes=[psq])
            kb.op("dve", lambda e, b=b, psq=psq: e.tensor_tensor(out=self.a_pr[:], in0=psq[:, :].rearrange("p (j g e) -> p j g e", j=4, g=2),
                                                                 in1=self.a_K[:, b, :].rearrange("p (g e) -> p g e", g=2).unsqueeze(1).broadcast_to([128, 4, 2, 64]), op=ALU.mult),
                  reads=[psq, self.a_K], writes=[self.a_pr])
            kb.op("dve", lambda e, b=b: e.tensor_reduce(out=sv[:, b, :, :], in_=self.a_pr[:], axis=AX.X, op=ALU.add), reads=[self.a_pr], writes=[self.a_s])
        kb.op("dve", lambda e: e.scalar_tensor_tensor(out=self.a_s[:], in0=self.a_s[:], scalar=0.125, in1=self.btab[:, 0, :, 0].unsqueeze(1).broadcast_to([128, NB, 8]),
                                                      op0=ALU.mult, op1=ALU.add), reads=[self.a_s, self.btab], writes=[self.a_s])
        kb.op("act", lambda e: e.activation(out=self.a_p[:], in_=self.a_s[:], func=AF.Exp), reads=[self.a_s], writes=[self.a_p])
        kb.op("dve", lambda e: e.tensor_tensor(out=self.a_Pm[:], in0=self.a_p[:].unsqueeze(3).broadcast_to([128, NB, 8, NB]),
                                                in1=self.s_eye[:].rearrange("p (a b) -> p a b", a=NB).unsqueeze(2).broadcast_to([128, NB, 8, NB]), op=ALU.mult),
              reads=[self.a_p, self.s_eye], writes=[self.a_Pm])
        pso = [self.nps(), self.nps()]
        for g in range(2):
            fns = []
            for h4 in range(4):
                for b in range(NB):
                    fns.append(lambda pe, g=g, h4=h4, b=b: pe.matmul(pso[g][0:NB, h4 * 65:(h4 + 1) * 65], self.a_Pm[:, b, g * 4 + h4, :], self.a_Vb[:, b, g, 0:65],
                                                                    start=(b == 0), stop=(b == NB - 1)))
            kb.mm(fns, reads=[self.a_Pm, self.a_Vb], writes=[pso[g]])
            kb.op("act", lambda e, g=g: e.copy(out=self.a_ou[:, g * 4:(g + 1) * 4, :], in_=pso[g][0:NB, 0:260].rearrange("p (h e) -> p h e", h=4)), reads=[pso[g]], writes=[self.a_ou])
        qv = self.a_q[:].rearrange("p (j g e) -> p j g e", j=4, g=2)
        kb.op("dve", lambda e: e.tensor_tensor(out=self.a_sn[:], in0=qv, in1=self.a_kn[:].rearrange("p (g e) -> p g e", g=2).unsqueeze(1).broadcast_to([NB, 4, 2, 64]), op=ALU.mult),
              reads=[self.a_q, self.a_kn], writes=[self.a_sn])
        kb.op("dve", lambda e: e.tensor_reduce(out=self.a_s8[:].rearrange("p (half j) -> p j half", half=2), in_=self.a_sn[:], axis=AX.X, op=ALU.add), reads=[self.a_sn], writes=[self.a_s8])
        kb.op("dve", lambda e: e.scalar_tensor_tensor(out=self.a_s8[:], in0=self.a_s8[:], scalar=0.125, in1=self.s_b0[:], op0=ALU.mult, op1=ALU.add), reads=[self.a_s8, self.s_b0], writes=[self.a_s8])
        kb.op("act", lambda e: e.activation(out=self.a_pn[:], in_=self.a_s8[:], func=AF.Exp), reads=[self.a_s8], writes=[self.a_pn])
        kb.op("dve", lambda e: e.tensor_tensor(out=self.a_t[:].rearrange("p (g j) e -> p g j e", g=2), in0=self.a_pn[:].rearrange("p (g j) -> p g j", g=2).unsqueeze(3).broadcast_to([NB, 2, 4, 65]),
                                                in1=self.a_vn[:, :, 0:65].unsqueeze(2).broadcast_to([NB, 2, 4, 65]), op=ALU.mult), reads=[self.a_pn, self.a_vn], writes=[self.a_t])
        kb.op("dve", lambda e: e.tensor_tensor(out=self.a_ou[:], in0=self.a_ou[:], in1=self.a_t[:], op=ALU.add), reads=[self.a_ou, self.a_t], writes=[self.a_ou])
        kb.op("dve", lambda e: e.tensor_tensor(out=self.a_den[:], in0=self.a_ou[:, :, 64], in1=self.a_es[:], op=ALU.add), reads=[self.a_ou, self.a_es], writes=[self.a_den])
        kb.op("dve", lambda e: e.reciprocal(out=self.a_den[:], in_=self.a_den[:]), reads=[self.a_den], writes=[self.a_den])
        kb.op("dve", lambda e: e.tensor_tensor(out=self.a_ob[:], in0=self.a_ou[:, :, 0:64], in1=self.a_den[:].unsqueeze(2).broadcast_to([NB, 8, 64]), op=ALU.mult),
              reads=[self.a_ou, self.a_den], writes=[self.a_ob])
        self.s_tokmajor_to_T(self.a_ob, 512, self.s_ocT) if False else None
        obv = self.a_ob[:].rearrange("p h e -> p (h e)")
        self.s_transp(self.a_ob, [obv[:, j * 128:(j + 1) * 128] for j in range(4)], self.s_ocT, self.s_ocT[:, :, :])

    def s_ln(self, l, gi, pss_fn):
        kb = self.kb
        kb.dma("sp", self.m_g[:], self.ln_n.ap()[l, gi].partition_broadcast(NB), writes=[self.m_g])
        kb.dma("sp", self.m_bb[:], self.ln_n.ap()[l, gi + 1].partition_broadcast(NB), writes=[self.m_bb])
        for j in range(2):
            js = slice(j * 512, (j + 1) * 512)
            ps = pss_fn(j)
            kb.op("dve", lambda e, js=js, ps=ps: e.scalar_tensor_tensor(out=self.sx[:, js], in0=self.sx[:, js], scalar=ALPHA, in1=ps[0:NB, :], op0=ALU.mult, op1=ALU.add),
                  reads=[self.sx, ps], writes=[self.sx])
            kb.op("dve", lambda e, j=j, js=js: e.bn_stats(out=self.m_st[:, j, :], in_=self.sx[:, js]), reads=[self.sx], writes=[self.m_st])
        kb.op("dve", lambda e: e.bn_aggr(out=self.m_mv[:], in_=self.m_st[:].rearrange("p a b -> p (a b)")), reads=[self.m_st], writes=[self.m_mv])
        kb.op("act", lambda e: e.activation(out=self.m_rs[:, 0:1], in_=self.m_mv[:, 1:2], func=AF.Ln, bias=EPS, scale=1.0), reads=[self.m_mv], writes=[self.m_rs])
        kb.op("act", lambda e: e.activation(out=self.m_rs[:, 1:2], in_=self.m_rs[:, 0:1], func=AF.Exp, scale=-0.5), reads=[self.m_rs], writes=[self.m_rs])
        kb.op("dve", lambda e: e.tensor_scalar(out=self.m_u[:], in0=self.sx[:], scalar1=self.m_mv[:, 0:1], scalar2=self.m_rs[:, 1:2], op0=ALU.subtract, op1=ALU.mult),
              reads=[self.sx, self.m_mv, self.m_rs], writes=[self.m_u])
        kb.op("dve", lambda e: e.tensor_tensor(out=self.m_u[:], in0=self.m_u[:], in1=self.m_g[:], op=ALU.mult), reads=[self.m_u, self.m_g], writes=[self.m_u])
        kb.op("dve", lambda e: e.tensor_tensor(out=self.sx[:], in0=self.m_u[:], in1=self.m_bb[:], op=ALU.add), reads=[self.m_u, self.m_bb], writes=[self.sx])
        self.s_refresh_xT()

    def s_mlp(self, l):
        self.kb.phase = 's_mlp'
        kb = self.kb
        self.fence()
        brs = [(self.w_br_ret, 512, 4, self.s_orT), (self.w_br_ssd, 1024, 8, self.s_yT), (self.w_br_swa, 512, 4, self.s_ocT)]
        for b, (wbr, K, kc, src) in enumerate(brs):
            for j in range(2):
                js = slice(j * 512, (j + 1) * 512)
                Gt, Gv = self.wload(self.wsrc(self.w_in, l, D, OGATE + b * 1024 + j * 512, 512), 8, 512)
                Bt, Bv = self.wload(self.wsrc(wbr, l, K, j * 512, 512), kc, 512)
                psG = self.sproj(Gt, Gv, 512)
                psB = self.sproj(Bt, Bv, 512, src=src, kc=kc)
                kb.op("act", lambda e, psG=psG: e.activation(out=self.m_sg[:], in_=psG[0:NB, :], func=AF.Sigmoid), reads=[psG], writes=[self.m_sg])
                if b == 0:
                    kb.op("dve", lambda e, js=js, psB=psB: e.tensor_tensor(out=self.m_acc[:, js], in0=self.m_sg[:], in1=psB[0:NB, :], op=ALU.mult), reads=[self.m_sg, psB], writes=[self.m_acc])
                else:
                    kb.op("dve", lambda e, psB=psB: e.tensor_tensor(out=self.m_t[:], in0=self.m_sg[:], in1=psB[0:NB, :], op=ALU.mult), reads=[self.m_sg, psB], writes=[self.m_t])
                    kb.op("dve", lambda e, js=js: e.tensor_tensor(out=self.m_acc[:, js], in0=self.m_acc[:, js], in1=self.m_t[:], op=ALU.add), reads=[self.m_acc, self.m_t], writes=[self.m_acc])
        kb.op("act", lambda e: e.copy(out=self.m_b[:], in_=self.m_acc[:]), reads=[self.m_acc], writes=[self.m_b])
        self.s_tokmajor_to_T(self.m_b, 1024, self.s_mT)

        def pss1(j):
            Wt, Wv = self.wload(self.wsrc(self.w_out, l, D, j * 512, 512), 8, 512)
            return self.sproj(Wt, Wv, 512, src=self.s_mT)
        self.s_ln(l, 0, pss1)
        for jg in range(6):
            n = 512 if jg < 5 else 256
            Gt, Gv = self.wload(self.wsrc(self.w_gate, l, D, jg * 512, n), 8, n)
            Ut, Uv = self.wload(self.wsrc(self.w_up, l, D, jg * 512, n), 8, n)
            psG = self.sproj(Gt, Gv, n)
            psU = self.sproj(Ut, Uv, n)
            kb.op("act", lambda e, psG=psG, n=n: e.activation(out=self.m_sg[:, 0:n], in_=psG[0:NB, 0:n], func=AF.Silu), reads=[psG], writes=[self.m_sg])
            kb.op("dve", lambda e, psU=psU, n=n, jg=jg: e.tensor_tensor(out=self.m_hb[:, jg * 512:jg * 512 + n], in0=self.m_sg[:, 0:n], in1=psU[0:NB, 0:n], op=ALU.mult),
                  reads=[self.m_sg, psU], writes=[self.m_hb])
        self.s_tokmajor_to_T(self.m_hb, 2816, self.s_hT)
        W = {}

        def pss2(j):
            ps = self.nps()
            for t in range(4):
                m = j * 4 + t
                Wt, Wv = self.wload(self.wsrc_down(l, m), 22, 128)
                self.kb.mm([lambda pe, k=k, t=t, Wv=Wv: pe.matmul(ps[0:NB, t * 128:(t + 1) * 128], self.s_hT[:, k, :], Wv[:, k, :], start=(k == 0), stop=(k == 21))
                            for k in range(22)], reads=[self.s_hT, Wt], writes=[ps])
            return ps
        self.s_ln(l, 2, pss2)


def consts(L):
    f32 = np.float32
    pos = np.arange(L, dtype=f32)
    inv = (np.float32(10000.0) ** (-np.arange(64, dtype=f32) / np.float32(64))).astype(f32)
    ang = (pos[:, None] * inv[None, :]).astype(f32)
    cos, sin = np.cos(ang).astype(f32), np.sin(ang).astype(f32)
    s = f32(128 ** -0.5)
    c_rot = np.concatenate([cos, sin, cos * s, sin * s], axis=1).astype(f32)
    i = np.arange(128, dtype=np.float64)
    lg = np.log(1.0 - 2.0 ** (-5.0 - np.arange(4, dtype=np.float64)))
    rel = i[None, :] - i[:, None]
    dmatT = np.where(rel[:, None, :] >= 0, np.exp(lg[None, :, None] * np.maximum(rel[:, None, :], 0)), 0.0)
    kdec = np.exp(lg[None, :] * (127 - i)[:, None])
    qdec = np.exp(lg[None, :] * (i + 1.0)[:, None])
    k = np.arange(128)[:, None]; q = np.arange(128)[None, :]
    m0 = np.where(q > k, NEG, 0.0)
    m1 = np.where(q < k, NEG, 0.0)
    c_f32 = np.concatenate([np.eye(128), np.triu(np.ones((128, 128))), np.ones((128, 128)), np.zeros((128, 128)),
                            dmatT.reshape(128, 512), kdec, qdec, m0, m1], axis=1).astype(f32)
    def bucket(dist):
        df = np.maximum(dist, 1).astype(f32)
        large = 16 + (np.log(df / f32(16)).astype(f32) / f32(math.log(8.0)) * f32(16)).astype(np.int32)
        large = np.minimum(large, 31)
        return np.where(dist < 16, dist, large)
    oh = np.zeros((128, 2, 128, 32), f32)
    d0 = q - k + 128
    d1 = q - k
    for hf, dd in ((0, d0), (1, d1)):
        valid = (dd >= 0) & (dd <= 128)
        b = bucket(np.maximum(dd, 0))
        kk, qq = np.nonzero(valid)
        oh[kk, hf, qq, b[kk, qq]] = 1.0
    c_oh = oh.reshape(128, -1).astype(ml_dtypes.bfloat16)
    return c_rot, c_f32, c_oh


def qc_perm():
    idx = np.arange(8464)
    base = 4624
    new = []
    for t in range(4):
        new += list(range(base + t * 64, base + (t + 1) * 64)) + list(range(base + (4 + t) * 64, base + (5 + t) * 64))
    idx[base:base + 512] = np.array(new)
    return idx


def prep_weights(inp, depth):
    f = lambda a: np.ascontiguousarray(a, dtype=np.float32)
    dp = depth
    w = {}
    w["w_in"] = f(inp["w_in"][:dp][:, :, qc_perm()].reshape(dp * 1024, 8464))
    w["conv_w"] = f(np.transpose(inp["conv_w"][:dp].reshape(dp, 4, 12, 128), (0, 3, 2, 1)))
    w["conv_b"] = f(np.transpose(inp["conv_b"][:dp].reshape(dp, 12, 128), (0, 2, 1)))
    for n in ("dt_bias", "a_log", "d_skip", "ssd_norm_w", "sinks"):
        w[n] = f(inp[n][:dp])
    w["rel_bias"] = f(inp["rel_bias"].reshape(1, 256))
    w["w_br_ret"] = f(inp["w_br_ret"][:dp].reshape(dp * 512, 1024))
    w["w_br_ssd"] = f(inp["w_br_ssd"][:dp].reshape(dp * 1024, 1024))
    w["w_br_swa"] = f(inp["w_br_swa"][:dp].reshape(dp * 512, 1024))
    w["w_out"] = f(inp["w_out"][:dp].reshape(dp * 1024, 1024))
    lnp = np.stack([np.transpose(inp[n][:dp].reshape(dp, 8, 128), (0, 2, 1)) for n in ("ln1_g", "ln1_b", "ln2_g", "ln2_b")], axis=2)
    w["lnp"] = f(lnp)
    w["w_ffn_gate"] = f(inp["w_ffn_gate"][:dp].reshape(dp * 1024, 2816))
    w["w_ffn_up"] = f(inp["w_ffn_up"][:dp].reshape(dp * 1024, 2816))
    w["w_ffn_down"] = f(np.transpose(inp["w_ffn_down"][:dp].reshape(dp, 22, 128, 8, 128), (0, 3, 2, 1, 4)).reshape(dp * 8 * 128, 2816))
    return w


def core_inputs(inp, core, depth, TB, do_sample=True, x_prompt_row=None, sample_rows=None, w=None, nblk=None):
    f = lambda a: np.ascontiguousarray(a, dtype=np.float32)
    dp = depth
    m = dict(w if w is not None else prep_weights(inp, depth))
    L = (nblk or 1) * TB
    xr = core if x_prompt_row is None else x_prompt_row
    m["xp"] = f(inp["x_prompt"][xr, :L])
    c_rot, c_f32, c_oh = consts(max(L, 8192 + 1))
    m["c_rot"] = np.ascontiguousarray(c_rot[:L]); m["c_f32"] = c_f32; m["c_oh"] = c_oh
    if do_sample:
        sr = sample_rows if sample_rows is not None else slice(core * 16, core * 16 + 16)
        m["xs"] = f(inp["x_sample"][sr, 0])
        m["st_ret"] = f(inp["state_ret"][:dp, sr])
        m["st_ssm"] = f(inp["state_ssm"][:dp, sr])
        m["st_conv"] = f(inp["state_conv"][:dp, sr])
        m["ck"] = f(inp["cache_swa_k"][:dp, sr].reshape(dp, 16, 128, 128))
        m["cv"] = f(inp["cache_swa_v"][:dp, sr].reshape(dp, 16, 128, 128))
        m["c_rots"] = np.ascontiguousarray(np.broadcast_to(c_rot[8192:8193], (16, 256)))
        sel = np.zeros((16, 16, 128), np.float32)
        for b in range(16):
            sel[b, b, :] = 1.0
        gsel = np.zeros((16, 2, 128), np.float32)
        gsel[:, 0, 0:64] = 1.0; gsel[:, 1, 64:128] = 1.0
        m["c_sel"] = np.concatenate([sel.reshape(16, -1), gsel.reshape(16, -1)], axis=1)
        m["c_eye"] = np.ascontiguousarray(np.broadcast_to(np.eye(16, dtype=np.float32).reshape(1, 256), (128, 256)))
        m["conv_w_n"] = f(inp["conv_w"][:dp].reshape(dp, 1, 4 * 1536))
        m["conv_b_n"] = f(inp["conv_b"][:dp].reshape(dp, 1, 1536))
        m["ln_n"] = f(np.stack([inp[n][:dp] for n in ("ln1_g", "ln1_b", "ln2_g", "ln2_b")], axis=1).reshape(dp, 4, 1, 1024))
    return m


def kernel(**inputs):
    inp = {k: np.asarray(v) for k, v in inputs.items()}
    dp, nblk = 4, 4
    g = GenS(depth=dp, nblk=nblk, do_sample=True)
    w = prep_weights(inp, dp)
    in_maps = []
    for c in range(8):
        m = core_inputs(inp, c, dp, TB, do_sample=True, w=w, nblk=nblk)
        in_maps.append({k: m[k] for k in g.ins})
    res = run_bass_kernel_spmd(g.nc, in_maps, core_ids=list(range(8)))
    R = res.results
    st = lambda name, axis: np.stack([np.asarray(r[name]) for r in R], axis=axis)
    cat = lambda name, axis: np.concatenate([np.asarray(r[name]) for r in R], axis=axis)
    y_prompt = st("yp", 0).astype(np.float32)
    y_sample = cat("ys", 0).reshape(128, 1, 1024).astype(np.float32)
    ret_p = st("ret_p", 1)
    ssm_p = st("ssm_p", 1).reshape(dp, 8, 16, 64, 128)
    conv_p = st("conv_p", 1)
    k_p = st("k_p", 1).reshape(dp, 8, 128, 2, 64)
    v_p = st("v_p", 1).reshape(dp, 8, 128, 2, 64)
    ret_s = cat("ret_s", 1)
    ssm_s = cat("ssm_s", 1)
    conv_s = cat("conv_s", 1)
    k_s = cat("k_s", 1).reshape(dp, 128, 128, 2, 64)
    v_s = cat("v_s", 1).reshape(dp, 128, 128, 2, 64)
    outs = (y_prompt, y_sample, ret_p, ssm_p, conv_p, k_p, v_p, ret_s, ssm_s, conv_s, k_s, v_s)
    return tuple(np.ascontiguousarray(o, dtype=np.float32) for o in outs)
```

```python
import math
import numpy as np
import ml_dtypes
from concourse.bass_utils import run_bass_kernel_spmd
import concourse.bass as bass
import concourse.mybir as mybir

F32 = mybir.dt.float32
BF16 = mybir.dt.bfloat16
I32 = mybir.dt.int32
AF = mybir.ActivationFunctionType
ALU = mybir.AluOpType
AX = mybir.AxisListType


class Dep:
    __slots__ = ("w", "r")

    def __init__(self):
        self.w = None
        self.r = {}


class Tile:
    __slots__ = ("h", "deps", "name", "psum")

    def __init__(self, h, deps=None, name=None):
        self.psum = False
        self.h = h
        self.deps = deps if deps is not None else [Dep()]
        self.name = name

    def __getitem__(self, k):
        return self.h[k]

    def sub(self, i):
        return Tile(self.h, [self.deps[i]], self.name)


class KB:
    ENG = ("pe", "act", "dve", "pool", "sp")

    def __init__(self, n_dma_sems=30):
        nc = bass.Bass("TRN2", target_bir_lowering=False)
        self.nc = nc
        self.eng = {"pe": nc.tensor, "act": nc.scalar, "dve": nc.vector, "pool": nc.gpsimd, "sp": nc.sync}
        self.sems = {}
        self.tick = {}
        for e in self.ENG:
            self.sems[e] = nc.alloc_semaphore("s_" + e)
            self.tick[e] = 0
        self.known = {e: {} for e in self.ENG}
        self.dpool = {}
        for q in ("sp", "pool", "act"):
            lst = []
            for i in range(n_dma_sems):
                key = "d_%s_%d" % (q, i)
                self.sems[key] = nc.alloc_semaphore(key)
                self.tick[key] = 0
                lst.append(key)
            self.dpool[q] = [lst, 0]
        self.n_ins = 0
        self.n_wait = 0
        self.strict = True
        self.attach_waits = True
        self.phase = 'init'
        self.pe_log = []
        self.out_events = []

    def sb(self, name, shape, dt, ncell=1):
        h = self.nc.alloc_sbuf_tensor(name, list(shape), dt)
        return Tile(h, [Dep() for _ in range(ncell)], name)

    def ps(self, name, shape=(128, 512), dt=F32):
        h = self.nc.alloc_psum_tensor(name, list(shape), dt)
        t = Tile(h, None, name)
        t.psum = True
        return t

    def dram(self, name, shape, dt, kind):
        h = self.nc.dram_tensor(name, list(shape), dt, kind=kind)
        return h

    def _needs(self, reads, writes, e=None):
        needs = {}

        def add(ev):
            if ev is None:
                return
            k, v = ev
            if needs.get(k, 0) < v:
                needs[k] = v

        for t in reads:
            for d in t.deps:
                add(d.w)
                if t.psum:
                    for k, v in d.r.items():
                        if k != e:
                            add((k, v))
        for t in writes:
            for d in t.deps:
                if d.w is not None and (self.strict or d.w[0] != e):
                    add(d.w)
                for k, v in d.r.items():
                    if self.strict or k != e:
                        add((k, v))
        return needs

    def _emit_waits(self, e, needs, attach=False):
        kn = self.known[e]
        eo = self.eng[e]
        todo = []
        for k, v in needs.items():
            if e == "pe" and k == "pe":
                continue
            if kn.get(k, 0) < v:
                todo.append((k, v))
                kn[k] = v
        held = None
        if attach and todo and self.attach_waits:
            held = todo.pop()
        for k, v in todo:
            eo.wait_ge(self.sems[k], v)
            self.n_wait += 1
        return held

    def _attach(self, ins, held):
        if held is not None:
            ins._wait_ge(self.sems[held[0]], held[1])

    def _record(self, ev, reads, writes):
        k, v = ev
        for t in reads:
            for d in t.deps:
                d.r[k] = v
        for t in writes:
            for d in t.deps:
                d.w = ev
                d.r = {}

    def op(self, e, fn, reads=(), writes=()):
        needs = self._needs(reads, writes, e)
        held = self._emit_waits(e, needs, attach=True)
        ins = fn(self.eng[e])
        self._attach(ins, held)
        self.tick[e] += 1
        ins.then_inc(self.sems[e], 1)
        self._record((e, self.tick[e]), reads, writes)
        self.n_ins += 1
        return ins

    def mm(self, fns, reads=(), writes=()):
        needs = self._needs(reads, writes, "pe")
        held = self._emit_waits("pe", needs, attach=True)
        ins = None
        for fn in fns:
            ins = fn(self.eng["pe"])
            if held is not None:
                self._attach(ins, held)
                held = None
            self.n_ins += 1
            self.pe_log.append(self.phase)
        self.tick["pe"] += 1
        ins.then_inc(self.sems["pe"], 1)
        self._record(("pe", self.tick["pe"]), reads, writes)
        return ins

    def dma(self, q, out_ap, in_ap, reads=(), writes=(), is_output=False, **kw):
        lst, idx = self.dpool[q]
        key = lst[idx % len(lst)]
        self.dpool[q][1] = idx + 1
        needs = self._needs(reads, writes)
        if self.tick[key] > 0:
            if needs.get(key, 0) < self.tick[key]:
                needs[key] = self.tick[key]
        held = self._emit_waits(q, needs, attach=True)
        ins = self.eng[q].dma_start(out=out_ap, in_=in_ap, **kw)
        self._attach(ins, held)
        self.tick[key] += 16
        ins.then_inc(self.sems[key], 16)
        ev = (key, self.tick[key])
        self._record(ev, reads, writes)
        if is_output:
            self.out_events.append(ev)
        self.n_ins += 1
        return ins

    def finish(self):
        needs = {}
        for k, v in self.out_events:
            if needs.get(k, 0) < v:
                needs[k] = v
        for q in self.dpool:
            for key in self.dpool[q][0]:
                if self.tick[key] > 0:
                    needs[key] = max(needs.get(key, 0), self.tick[key])
        for e in ("pe", "act", "dve", "pool"):
            if self.tick[e] > 0:
                needs[e] = self.tick[e]
        self._emit_waits("sp", needs)


D = 1024
NIN = 8464
DFF = 2816
ALPHA = 8 ** 0.25
EPS = 1e-5
TB = 512
NEG = -30000.0
OQ, OK_, OV, OG, OZ, OXBC, ODT, OQC, OKC, OVC, OGATE = 0, 512, 1024, 1536, 2048, 3072, 4608, 4624, 5136, 5264, 5392


class Gen:
    def __init__(self, depth=4, nblk=4, do_sample=True, dbg=None):
        self.depth, self.nblk, self.do_sample = depth, nblk, do_sample
        self.kb = kb = KB()
        self.nc = nc = kb.nc
        self.L = nblk * TB
        self.dbg = dbg or {}
        self.outs = []
        self.ins = {}
        self._decl()
        self._alloc()
        self._setup_consts()
        if self.dbg.get("setup_only"):
            kb.finish()
            return
        for bi in self.dbg.get('blocks', range(nblk)):
            self.block(bi)
        if self.do_sample and hasattr(self, "sample_pass"):
            self.sample_pass()
        kb.finish()

    def din(self, name, shape, dt=F32):
        h = self.nc.dram_tensor(name, list(shape), dt, kind="ExternalInput")
        self.ins[name] = (tuple(shape), dt)
        return h

    def dout(self, name, shape, dt=F32):
        h = self.nc.dram_tensor(name, list(shape), dt, kind="ExternalOutput")
        self.outs.append(name)
        return h

    def _decl(self):
        dp, L = self.depth, self.L
        d = self.din
        self.xp = d("xp", [L, D])
        self.w_in = d("w_in", [dp * D, NIN])
        self.conv_w = d("conv_w", [dp, 128, 12, 4])
        self.conv_b = d("conv_b", [dp, 128, 12])
        self.dt_bias = d("dt_bias", [dp, 16])
        self.a_log = d("a_log", [dp, 16])
        self.d_skip = d("d_skip", [dp, 16])
        self.norm_w = d("ssd_norm_w", [dp, 1024])
        self.sinks = d("sinks", [dp, 8])
        self.rel_bias = d("rel_bias", [1, 256])
        self.w_br_ret = d("w_br_ret", [dp * 512, D])
        self.w_br_ssd = d("w_br_ssd", [dp * 1024, D])
        self.w_br_swa = d("w_br_swa", [dp * 512, D])
        self.w_out = d("w_out", [dp * D, D])
        self.lnp_d = d("lnp", [dp, 128, 4, 8])
        self.w_gate = d("w_ffn_gate", [dp * D, DFF])
        self.w_up = d("w_ffn_up", [dp * D, DFF])
        self.w_down = d("w_ffn_down", [dp * 8 * 128, DFF])
        self.c_rot = d("c_rot", [L, 256])
        self.c_f32 = d("c_f32", [128, 128 * 4 + 4 * 128 + 8 + 2 * 128])
        self.c_oh = d("c_oh", [128, 2 * 128 * 32], BF16)
        o = self.dout
        self.yp = o("yp", [L, D])
        self.ret_p = o("ret_p", [dp, 4, 128, 128])
        self.ssm_p = o("ssm_p", [dp, 1024, 128])
        self.conv_p = o("conv_p", [dp, 3, 1536])
        self.k_p = o("k_p", [dp, 128, 128])
        self.v_p = o("v_p", [dp, 128, 128])

    def _alloc(self):
        kb, dp = self.kb, self.depth
        sb = kb.sb
        self.xr_off = self.nc.bump_sbuf(16384 + 8192)[0]
        self.xr = Tile(self.nc.alloc_sbuf_tensor_at("xr", [128, 8, TB], F32, offset=self.xr_off), None, "xr")
        self.xb = Tile(self.nc.alloc_sbuf_tensor_at("xb", [128, 8, TB], BF16, offset=self.xr_off + 16384), None, "xb")
        self.NS = 4
        self.wr = [sb("wr%d" % i, [128, 4096], BF16) for i in range(self.NS)]
        self.wi = 0
        self.wcache = {}
        self.use_wcache = self.dbg.get('wcache', False)
        self.psf = [kb.ps("psf%d" % i, [128, 512], F32) for i in range(6)]
        self.psb = [kb.ps("psb%d" % i, [128, 1024], BF16) for i in range(2)]
        self.pfi = 0
        self.pbi = 0
        self.retS = [sb("retS%d" % l, [128, 4, 128], F32) for l in range(dp)]
        self.ssmS = [sb("ssmS%d" % l, [128, 1024], F32) for l in range(dp)]
        self.hist = [sb("hist%d" % l, [128, 12, 3], F32) for l in range(dp)]
        self.kprev = [sb("kprev%d" % l, [128, 128], BF16) for l in range(dp)]
        self.vprev = [sb("vprev%d" % l, [128, 2, 80], BF16) for l in range(dp)]
        self.cf = sb("cf", [128, 128 * 4 + 4 * 128 + 8 + 2 * 128], F32)
        self.id_b = sb("id_b", [128, 128], BF16)
        self.ones_b = sb("ones_b", [128, 128], BF16)
        self.negT = sb("negT", [128, 128], BF16)
        self.btab = sb("btab", [128, 2, 8, 128], F32)
        self.rot = sb("rot", [128, 4, 256], F32)
        self.cw = sb("cw", [128, 12, 4], F32); self.cb = sb("cb", [128, 12], F32)
        self.lnp = sb("lnp_s", [128, 4, 8], F32)
        self.sm16 = sb("sm16", [128, 3, 16], F32)
        self.esink = sb("esink", [128, 8], F32)
        self.orT = sb("orT", [128, 4, TB], BF16)
        self.yT = sb("yT", [128, 8, TB], BF16)
        self.ocT = sb("ocT", [128, 4, TB], BF16)
        self.mrg = sb("mrg", [128, 8, TB], BF16)
        self.xtok = [sb("xtok0", [128, 1024], F32)]
        self.xti = 0
        nc = self.nc
        ASZ = 64 * 1024
        self.abase = nc.bump_sbuf(ASZ)[0]
        self.asz = ASZ
        st = {"off": 0}

        def begin():
            st["off"] = 0

        def al(name, shape, dt):
            n = 1
            for x in shape[1:]:
                n *= x
            nb = n * (4 if dt == F32 else 2)
            nb = (nb + 31) // 32 * 32
            assert st["off"] + nb <= ASZ, (name, st["off"], nb)
            h = nc.alloc_sbuf_tensor_at(name, list(shape), dt, offset=self.abase + st["off"])
            st["off"] += nb
            return Tile(h, None, name)
        begin()
        self.s_oh = al("s_oh", [128, 8192], BF16); self.s_prod = al("s_prod", [128, 4096], F32); self.s_rb = al("s_rb", [128, 256], F32)
        begin()
        self.qrot = al("qrot", [128, 512], BF16); self.krot = al("krot", [128, 512], BF16)
        self.rtA = al("rtA", [128, 4, 2, 64], F32); self.rtB = al("rtB", [128, 4, 2, 64], F32)
        self.rtA2 = al("rtA2", [128, 4, 2, 64], F32); self.rtB2 = al("rtB2", [128, 4, 2, 64], F32); self.kraw = al("kraw", [128, 512], F32)
        self.v_r = al("v_r", [128, 4, 512], BF16)
        self.gsil = al("gsil", [128, 4, 512], F32)
        self.qT = al("qT", [128, 4, TB], BF16); self.kT = al("kT", [128, 4, TB], BF16)
        self.kdk = al("kdk", [128, 4, 4, 128], BF16)
        self.scm = al("scm", [128, 4, 128], BF16)
        self.otmp = al("otmp", [128, 4, 128], F32); self.o_r = al("o_r", [128, 4, 128], F32)
        self.st6 = al("st6", [128, 4, 6], F32); self.mv = al("mv", [128, 4, 2], F32)
        self.rs4 = al("rs4", [128, 4], F32); self.rs4b = al("rs4b", [128, 4], F32)
        self.og = al("og", [128, 4, 128], F32); self.ogb = al("ogb", [128, 512], BF16)
        self.retSb = al("retSb", [128, 4, 128], BF16)
        print("arena A", st["off"])
        begin()
        self.normw = al("normw", [128, 1024], F32)
        self.zs = al("zs", [128, 1024], F32)
        self._xbcT_off = self.abase + st["off"]
        self.xbcT = al("xbcT", [128, 4, 3 + TB], F32)
        self.cacc = al("cacc", [128, TB], F32)
        self.xsT = al("xsT", [128, 8, TB], BF16)
        self.BT = al("BT", [128, 2, TB], BF16); self.CT = al("CT", [128, 2, TB], BF16)
        self.dts = al("dts", [128, 4, 8, 16], F32)
        self.cshl = al("cshl", [128, 4, 2, 16], BF16)
        self.xs_tok = al("xs_tok", [128, 1024], BF16); self.xw = al("xw", [128, 1024], BF16); self.xD = al("xD", [128, 1024], BF16)
        self.B_tok = al("B_tok", [128, 256], BF16)
        self.cbT = al("cbT", [128, 2, 128], F32)
        self.Dhl = al("Dhl", [128, 2, 4, 128], BF16)
        self.Ep = al("Ep", [128, 4, 128], F32)
        self.wmT = al("wmT", [128, 16, 128], BF16)
        self.ytmp = al("ytmp", [128, 1024], F32)
        self.ssq = al("ssq", [128, 4], F32)
        self.ynb = al("ynb", [128, 1024], BF16)
        self.stmp = al("stmp", [128, 1024], F32)
        self.ssmSb = al("ssmSb", [128, 1024], BF16)
        print("arena B", st["off"])
        self.Dhl2 = [self.Dhl, Tile(nc.alloc_sbuf_tensor_at("Dhl_b", [128, 2, 4, 128], BF16, offset=self._xbcT_off), [Dep()], "Dhl_b")]
        self.Ep2 = [self.Ep, Tile(nc.alloc_sbuf_tensor_at("Ep_b", [128, 4, 128], F32, offset=self._xbcT_off + 2048), [Dep()], "Ep_b")]
        begin()
        self.qcT = al("qcT", [128, 4, TB], BF16)
        self.kTe = al("kTe", [128, 128 + TB], BF16)
        self.vaug = al("vaug", [128, 5, 2, 80], BF16)
        self.lg = al("lg", [128, 4, 128], F32)
        self.pT = al("pT", [128, 2, 2, 4, 128], BF16)
        self.den = al("den", [128, 8], F32); self.rden = al("rden", [128, 8], F32)
        self.oc_tok = al("oc_tok", [128, 8, 64], BF16)
        self.kvo = al("kvo", [128, 2, 128], F32)
        print("arena C", st["off"])
        begin()
        self.hT = al("hT", [128, 22, TB], BF16)
        self.sg = al("sg", [128, TB], F32); self.gt = al("gt", [128, TB], F32)
        self.ysq = al("ysq", [128, 8, TB], BF16)
        self.lnm = al("lnm", [128, TB], F32); self.lnr = al("lnr", [128, TB], F32); self.lnt = al("lnt", [128, TB], F32)
        self.lnu = al("lnu", [128, TB], F32); self.lnu2 = al("lnu2", [128, TB], F32)
        print("arena DEF", st["off"])
        self.macc = Tile(nc.alloc_sbuf_tensor_at("macc", [128, 8, TB], F32, offset=self.abase), None, "macc")

    def fence(self):
        kb = self.kb
        needs = {e: kb.tick[e] for e in ("pe", "act", "dve", "pool") if kb.tick[e] > 0}
        for key in kb.dpool["sp"][0]:
            if kb.tick[key] > 0:
                needs[key] = kb.tick[key]
        for e in ("act", "dve", "sp"):
            kb._emit_waits(e, dict(needs))

    def nps(self):
        t = self.psf[self.pfi % len(self.psf)]
        self.pfi += 1
        return t

    def npb(self):
        t = self.psb[self.pbi % len(self.psb)]
        self.pbi += 1
        return t

    def wload(self, src, kc, n):
        src3, key = src
        t = self.wr[self.wi % self.NS]
        self.wi += 1
        flat = t[:, 0:kc * n]
        v = flat.rearrange("p (k c) -> p k c", k=kc)
        ent = self.wcache.get(key) if self.use_wcache else None
        if ent is None:
            self.kb.dma("pool", v, src3, writes=[t])
            if self.use_wcache:
                h = self.nc.dram_tensor("wc%d" % len(self.wcache), [128, kc * n], BF16, kind="Internal")
                tl = Tile(h, None, "wc")
                self.wcache[key] = (h, tl)
                self.kb.dma("pool", h.ap()[:, :], flat, reads=[t], writes=[tl])
        else:
            h, tl = ent
            self.kb.dma(self.dbg.get("wq", "sp"), flat, h.ap()[:, :], reads=[tl], writes=[t])
        return t, v

    def wsrc_down(self, l, m):
        r0 = (l * 8 + m) * 128
        return (self.w_down.ap()[r0:r0 + 128, :].rearrange("p (k c) -> p k c", k=22), ("w_down", l, m))

    def wsrc(self, w, l, K, c0, n):
        return (w.ap()[l * K:(l + 1) * K, c0:c0 + n].rearrange("(k p) c -> p k c", p=128), (w.name, l, c0, n))

    def _setup_consts(self):
        kb = self.kb
        cf = self.cf
        kb.dma("sp", cf[:], self.c_f32.ap()[:, :], writes=[cf])
        self.ident_f = cf[:, 0:128]
        self.tri_f = cf[:, 128:256]
        self.ones_f = cf[:, 256:384]
        o = 512
        self.dmatT = cf[:, o:o + 512].rearrange("p (h i) -> p h i", h=4)
        self.kdec = cf[:, o + 512:o + 516]
        self.qdec = cf[:, o + 516:o + 520]
        self.mask01 = cf[:, o + 520:o + 520 + 256].rearrange("p (f q) -> p f q", f=2)
        kb.op("act", lambda e: e.copy(out=self.id_b[:], in_=self.ident_f), reads=[cf], writes=[self.id_b])
        kb.op("act", lambda e: e.copy(out=self.ones_b[:], in_=self.ones_f), reads=[cf], writes=[self.ones_b])
        kb.op("act", lambda e: e.copy(out=self.negT[:], in_=self.mask01[:, 1, :]), reads=[cf], writes=[self.negT])
        for l in range(self.depth):
            for t in (self.retS[l], self.ssmS[l], self.hist[l]):
                kb.op("pool", lambda e, t=t: e.memset(t[:], 0.0), writes=[t])
        oh = self.s_oh
        ohv = oh[:, :]
        kb.dma("sp", ohv, self.c_oh.ap()[:, :], writes=[oh])
        rb = self.s_rb
        kb.dma("sp", rb[:, 0:256], self.rel_bias.ap().partition_broadcast(128), writes=[rb])
        ohq = ohv.rearrange("p (f q b) -> p f q b", f=2, q=128)
        prod = self.s_prod
        pv = prod[:, :].rearrange("p (q b) -> p q b", b=32)
        rbv = rb[:, 0:256].rearrange("p (b h) -> p b h", h=8)
        for hf in range(2):
            for h in range(8):
                kb.op("dve", lambda e, hf=hf, h=h: e.tensor_tensor(
                    out=pv, in0=ohq[:, hf, :, :], in1=rbv[:, :, h].unsqueeze(1).broadcast_to([128, 128, 32]), op=ALU.mult),
                    reads=[oh, rb], writes=[prod])
                kb.op("dve", lambda e, hf=hf, h=h: e.tensor_reduce(out=self.btab[:, hf, h, :], in_=pv, axis=AX.X, op=ALU.add),
                      reads=[prod], writes=[self.btab])
            kb.op("dve", lambda e, hf=hf: e.tensor_tensor(
                out=self.btab[:, hf, :, :], in0=self.btab[:, hf, :, :],
                in1=self.mask01[:, hf, :].unsqueeze(1).broadcast_to([128, 8, 128]), op=ALU.add),
                reads=[cf, self.btab], writes=[self.btab])

    def block(self, bi):
        if not self.dbg.get('noload'): self.load_x(bi)
        for l in range(self.depth):
            self.layer(bi, l)
        if not self.dbg.get('nostore'): self.store_y(bi)

    def load_x(self, bi):
        self.kb.phase = 'load_x'
        kb = self.kb
        if not (self.dbg.get('norot2') and getattr(self, '_lx', 0) >= 1): kb.dma("sp", self.rot[:], self.c_rot.ap()[bi * TB:(bi + 1) * TB, :].rearrange("(c p) f -> p c f", p=128), writes=[self.rot])
        self._lx = getattr(self, '_lx', 0) + 1
        for c in range(4 if self._lx == 1 else self.dbg.get('nchunk', 4)):
            xt = self.xtok[0]; self.xti += 1
            r0 = bi * TB + c * 128
            kb.dma(self.dbg.get("xq", "sp"), xt[:], self.xp.ap()[r0:r0 + 128, :], writes=[xt])
            for half in range(2):
                ps = self.nps()
                kb.mm([lambda pe, j=j, half=half, xt=xt, ps=ps: pe.transpose(
                    out=ps[:, j * 128:(j + 1) * 128], in_=xt[:, (half * 4 + j) * 128:(half * 4 + j + 1) * 128], identity=self.ident_f)
                    for j in range(4)], reads=[xt, self.cf], writes=[ps])
                pv = ps[:, :].rearrange("p (j t) -> p j t", j=4)
                if self._lx > 1 and self.dbg.get('nocopy'):
                    continue
                kb.op("act", lambda e, half=half, c=c, pv=pv: e.copy(out=self.xr[:, half * 4:half * 4 + 4, c * 128:(c + 1) * 128], in_=pv),
                      reads=[ps], writes=[self.xr])
                kb.op("dve", lambda e, half=half, c=c: e.tensor_copy(out=self.xb[:, half * 4:half * 4 + 4, c * 128:(c + 1) * 128],
                                                                      in_=self.xr[:, half * 4:half * 4 + 4, c * 128:(c + 1) * 128]),
                      reads=[self.xr], writes=[self.xb])

    def store_y(self, bi):
        self.kb.phase = 'store_y'
        kb = self.kb
        for c in range(4):
            xt = self.xtok[0]; self.xti += 1
            for half in range(2):
                ps = self.nps()
                kb.mm([lambda pe, j=j, half=half, c=c, ps=ps: pe.transpose(
                    out=ps[:, j * 128:(j + 1) * 128], in_=self.xr[:, half * 4 + j, c * 128:(c + 1) * 128], identity=self.ident_f)
                    for j in range(4)], reads=[self.xr, self.cf], writes=[ps])
                kb.op("act" if half else "dve",
                      (lambda e, half=half, xt=xt, ps=ps: e.copy(out=xt[:, half * 512:(half + 1) * 512], in_=ps[:, :])) if half else
                      (lambda e, half=half, xt=xt, ps=ps: e.tensor_copy(out=xt[:, half * 512:(half + 1) * 512], in_=ps[:, :])),
                      reads=[ps], writes=[xt])
            r0 = bi * TB + c * 128
            kb.dma("sp", self.yp.ap()[r0:r0 + 128, :], xt[:], reads=[xt], is_output=True)

    def layer(self, bi, l):
        ph = self.dbg.get("phases", "PABCDEF")
        if "P" in ph: self.load_params(bi, l)
        if "A" in ph: self.phaseA(bi, l)
        if "B" in ph: self.phaseB(bi, l)
        if "C" in ph: self.phaseC(bi, l)
        if "D" in ph: self.phaseD(bi, l)
        if "E" in ph: self.phaseE(bi, l)
        if "F" in ph: self.phaseF(bi, l)

    def load_params(self, bi, l):
        self.kb.phase = 'load_params'
        kb = self.kb
        kb.dma("sp", self.cw[:], self.conv_w.ap()[l], writes=[self.cw])
        kb.dma("sp", self.cb[:], self.conv_b.ap()[l], writes=[self.cb])
        kb.dma("sp", self.lnp[:], self.lnp_d.ap()[l], writes=[self.lnp])
        for i, w in enumerate((self.dt_bias, self.a_log, self.d_skip)):
            kb.dma("sp", self.sm16[:, i, :], w.ap()[l:l + 1, :].partition_broadcast(128), writes=[self.sm16])
        kb.dma("sp", self.esink[:], self.sinks.ap()[l:l + 1, :].partition_broadcast(128), writes=[self.esink])
        kb.op("act", lambda e: e.activation(out=self.esink[:], in_=self.esink[:], func=AF.Exp), reads=[self.esink], writes=[self.esink])
        kb.op("act", lambda e: e.activation(out=self.sm16[:, 1, :], in_=self.sm16[:, 1, :], func=AF.Exp), reads=[self.sm16], writes=[self.sm16])
        kb.op("dve", lambda e: e.tensor_scalar(out=self.sm16[:, 1, :], in0=self.sm16[:, 1, :], scalar1=-1.0, scalar2=None, op0=ALU.mult),
              reads=[self.sm16], writes=[self.sm16])

    def proj_tok(self, Wt, Wv, c, ncols, col0=0):
        ps = self.nps()
        self.kb.mm([lambda pe, k=k, ps=ps: pe.matmul(ps[:, 0:ncols], self.xb[:, k, c * 128:(c + 1) * 128], Wv[:, k, col0:col0 + ncols],
                                                     start=(k == 0), stop=(k == 7)) for k in range(8)],
                   reads=[self.xb, Wt], writes=[ps])
        return ps

    def proj_feat(self, Wt, Wv, t0, src=None, srct=None, kc=8):
        ps = self.nps()
        src = self.xb if src is None else src
        self.kb.mm([lambda pe, k=k, ps=ps: pe.matmul(ps[:, :], Wv[:, k, t0:t0 + 128], src[:, k, :],
                                                     start=(k == 0), stop=(k == kc - 1)) for k in range(kc)],
                   reads=[src, Wt], writes=[ps])
        return ps

    def transp_b(self, src_tile, src_aps, dst_tile, dst_ap):
        n = len(src_aps)
        pb = self.npb()
        self.kb.mm([lambda pe, j=j, pb=pb: pe.transpose(out=pb[:, j * 128:(j + 1) * 128], in_=src_aps[j], identity=self.id_b[:])
                    for j in range(n)], reads=[src_tile, self.id_b], writes=[pb])
        self.kb.op("act", lambda e, pb=pb: e.copy(out=dst_ap, in_=pb[:, 0:n * 128].rearrange("p (j t) -> p j t", j=n)),
                   reads=[pb], writes=[dst_tile])
        return pb

    def phaseA(self, bi, l):
        self.kb.phase = 'phaseA'
        kb = self.kb
        self.fence()
        g = [2.0 ** (-5 - h) for h in range(4)]
        cdec = [float(np.exp(np.float64(128) * np.log1p(-gg))) for gg in g]
        for gi, off in enumerate((OQ, OK_, OV, OG)):
            Wt, Wv = self.wload(self.wsrc(self.w_in, l, D, off, 512), 8, 512)
            for c in range(4):
                ps = self.proj_tok(Wt, Wv, c, 512)
                if gi < 2:
                    dst = self.qrot if gi == 0 else self.krot
                    eng = "dve" if gi == 0 else self.dbg.get("rot_k_eng", "dve")
                    if eng == "pool":
                        kb.op("act", lambda e, ps=ps: e.copy(out=self.kraw[:], in_=ps[:, :]), reads=[ps], writes=[self.kraw])
                        srct, psv = self.kraw, self.kraw[:].rearrange("p (h t e) -> p h t e", h=4, t=2)
                    else:
                        srct, psv = ps, ps[:, :].rearrange("p (h t e) -> p h t e", h=4, t=2)
                    tA = self.rtA if gi == 0 else self.rtA2
                    tB = self.rtB if gi == 0 else self.rtB2
                    cosb = self.rot[:, c, gi * 128:gi * 128 + 64].unsqueeze(1).unsqueeze(1).broadcast_to([128, 4, 2, 64])
                    sinb = self.rot[:, c, gi * 128 + 64:gi * 128 + 128].unsqueeze(1).broadcast_to([128, 4, 64])
                    kb.op(eng, lambda e, psv=psv, cosb=cosb, tA=tA: e.tensor_tensor(out=tA[:], in0=psv, in1=cosb, op=ALU.mult),
                          reads=[srct, self.rot], writes=[tA])
                    kb.op(eng, lambda e, psv=psv, sinb=sinb, tB=tB: e.tensor_tensor(out=tB[:, :, 0, :], in0=psv[:, :, 1, :], in1=sinb, op=ALU.mult),
                          reads=[srct, self.rot], writes=[tB])
                    kb.op(eng, lambda e, psv=psv, sinb=sinb, tB=tB: e.tensor_tensor(out=tB[:, :, 1, :], in0=psv[:, :, 0, :], in1=sinb, op=ALU.mult),
                          reads=[srct, self.rot], writes=[tB])
                    dv = dst[:].rearrange("p (h t e) -> p h t e", h=4, t=2)
                    kb.op(eng, lambda e, dv=dv, tA=tA, tB=tB: e.tensor_tensor(out=dv[:, :, 0, :], in0=tA[:, :, 0, :], in1=tB[:, :, 0, :], op=ALU.subtract),
                          reads=[tA, tB], writes=[dst])
                    kb.op(eng, lambda e, dv=dv, tA=tA, tB=tB: e.tensor_tensor(out=dv[:, :, 1, :], in0=tA[:, :, 1, :], in1=tB[:, :, 1, :], op=ALU.add),
                          reads=[tA, tB], writes=[dst])
                    dT = self.qT if gi == 0 else self.kT
                    self.transp_b(dst, [dst[:, h * 128:(h + 1) * 128] for h in range(4)], dT, dT[:, :, c * 128:(c + 1) * 128])
                    if gi == 1:
                        kb.op("dve", lambda e, c=c: e.tensor_tensor(
                            out=self.kdk[:, c, :, :], in0=self.krot[:].rearrange("p (h e) -> p h e", h=4),
                            in1=self.kdec.unsqueeze(2).broadcast_to([128, 4, 128]), op=ALU.mult),
                            reads=[self.krot, self.cf], writes=[self.kdk])
                elif gi == 2:
                    kb.op("act", lambda e, c=c, ps=ps: e.copy(out=self.v_r[:, c, :], in_=ps[:, :]), reads=[ps], writes=[self.v_r])
                else:
                    kb.op("act", lambda e, c=c, ps=ps: e.activation(out=self.gsil[:, c, :], in_=ps[:, :], func=AF.Silu), reads=[ps], writes=[self.gsil])
        S, Sb = self.retS[l], self.retSb
        kb.op("act", lambda e: e.copy(out=Sb[:], in_=S[:]), reads=[S], writes=[Sb])
        for c in range(4):
            cs = slice(c * 128, (c + 1) * 128)
            ps1 = self.nps()
            kb.mm([lambda pe, h=h, ps1=ps1: pe.matmul(ps1[:, h * 128:(h + 1) * 128], self.kT[:, h, cs], self.qT[:, h, cs], start=True, stop=True)
                   for h in range(4)], reads=[self.kT, self.qT], writes=[ps1])
            kb.op("dve", lambda e, ps1=ps1: e.tensor_tensor(out=self.scm[:], in0=ps1[:, :].rearrange("p (h i) -> p h i", h=4), in1=self.dmatT, op=ALU.mult),
                  reads=[ps1, self.cf], writes=[self.scm])
            psA = self.nps(); psB = self.nps(); psC = self.nps()
            kb.mm([lambda pe, h=h, psA=psA: pe.matmul(psA[:, h * 128:(h + 1) * 128], self.scm[:, h, :], self.v_r[:, c, h * 128:(h + 1) * 128], start=True, stop=True)
                   for h in range(4)], reads=[self.scm, self.v_r], writes=[psA])
            kb.mm([lambda pe, h=h, psB=psB: pe.matmul(psB[:, h * 128:(h + 1) * 128], self.qT[:, h, cs], Sb[:, h, :], start=True, stop=True)
                   for h in range(4)], reads=[self.qT, Sb], writes=[psB])
            kb.mm([lambda pe, h=h, psC=psC: pe.matmul(psC[:, h * 128:(h + 1) * 128], self.kdk[:, c, h, :], self.v_r[:, c, h * 128:(h + 1) * 128], start=True, stop=True)
                   for h in range(4)], reads=[self.kdk, self.v_r], writes=[psC])
            for h in range(4):
                kb.op("dve", lambda e, h=h, psC=psC: e.scalar_tensor_tensor(out=S[:, h, :], in0=S[:, h, :], scalar=cdec[h], in1=psC[:, h * 128:(h + 1) * 128],
                                                                              op0=ALU.mult, op1=ALU.add), reads=[S, psC], writes=[S])
            kb.op("act", lambda e: e.copy(out=Sb[:], in_=S[:]), reads=[S], writes=[Sb])
            kb.op("dve", lambda e, psB=psB: e.tensor_tensor(out=self.otmp[:], in0=psB[:, :].rearrange("p (h v) -> p h v", h=4),
                                                             in1=self.qdec.unsqueeze(2).broadcast_to([128, 4, 128]), op=ALU.mult),
                  reads=[psB, self.cf], writes=[self.otmp])
            kb.op("dve", lambda e, psA=psA: e.tensor_tensor(out=self.o_r[:], in0=psA[:, :].rearrange("p (h v) -> p h v", h=4), in1=self.otmp[:], op=ALU.add),
                  reads=[psA, self.otmp], writes=[self.o_r])
            for h in range(4):
                kb.op("dve", lambda e, h=h: e.bn_stats(out=self.st6[:, h, :], in_=self.o_r[:, h, :]), reads=[self.o_r], writes=[self.st6])
            for h in range(4):
                kb.op("dve", lambda e, h=h: e.bn_aggr(out=self.mv[:, h, :], in_=self.st6[:, h, :]), reads=[self.st6], writes=[self.mv])
            kb.op("act", lambda e: e.activation(out=self.rs4[:], in_=self.mv[:, :, 1], func=AF.Ln, bias=EPS, scale=1.0), reads=[self.mv], writes=[self.rs4])
            kb.op("act", lambda e: e.activation(out=self.rs4b[:], in_=self.rs4[:], func=AF.Exp, scale=-0.5), reads=[self.rs4], writes=[self.rs4b])
            for h in range(4):
                kb.op("dve", lambda e, h=h: e.scalar_tensor_tensor(out=self.og[:, h, :], in0=self.o_r[:, h, :], scalar=self.mv[:, h, 0:1],
                                                                     in1=self.gsil[:, c, h * 128:(h + 1) * 128], op0=ALU.subtract, op1=ALU.mult),
                      reads=[self.o_r, self.mv, self.gsil], writes=[self.og])
            for h in range(4):
                kb.op("act", lambda e, h=h: e.activation(out=self.ogb[:, h * 128:(h + 1) * 128], in_=self.og[:, h, :], func=AF.Identity, scale=self.rs4b[:, h:h + 1]),
                      reads=[self.og, self.rs4b], writes=[self.ogb])
            self.transp_b(self.ogb, [self.ogb[:, h * 128:(h + 1) * 128] for h in range(4)], self.orT, self.orT[:, :, cs])
        if bi == self.nblk - 1:
            kb.dma("sp", self.ret_p.ap()[l].rearrange("h d v -> d h v"), S[:], reads=[S], is_output=True)

    def phaseB(self, bi, l):
        self.kb.phase = 'phaseB'
        kb = self.kb
        self.fence()
        kb.dma("sp", self.normw[:], self.norm_w.ap()[l:l + 1, :].partition_broadcast(128), writes=[self.normw])
        Wt, Wv = self.wload(self.wsrc(self.w_in, l, D, ODT, 16), 8, 16)
        dts = self.dts
        for c in range(4):
            ps = self.proj_tok(Wt, Wv, c, 16)
            DT, CS, ECS, DEDT, CDEC, NB, T1, T2 = [dts[:, c, i, :] for i in range(8)]
            kb.op("dve", lambda e, ps=ps, T1=T1: e.tensor_tensor(out=T1, in0=ps[:, 0:16], in1=self.sm16[:, 0, :], op=ALU.add), reads=[ps, self.sm16], writes=[dts])
            kb.op("act", lambda e, T1=T1, T2=T2: e.activation(out=T2, in_=T1, func=AF.Exp), reads=[dts], writes=[dts])
            kb.op("act", lambda e, DT=DT, T2=T2: e.activation(out=DT, in_=T2, func=AF.Ln, bias=1.0, scale=1.0), reads=[dts], writes=[dts])
            kb.op("dve", lambda e, DT=DT, T1=T1: e.tensor_tensor(out=T1, in0=DT, in1=self.sm16[:, 1, :], op=ALU.mult), reads=[dts, self.sm16], writes=[dts])
            ps2 = self.nps()
            kb.mm([lambda pe, ps2=ps2, T1=T1: pe.matmul(ps2[:, 0:16], self.tri_f, T1, start=True, stop=True),
                   lambda pe, ps2=ps2, T1=T1: pe.matmul(ps2[:, 16:32], self.ones_f, T1, start=True, stop=True)],
                  reads=[self.cf, dts], writes=[ps2])
            kb.op("act", lambda e, ps2=ps2, CS=CS: e.copy(out=CS, in_=ps2[:, 0:16]), reads=[ps2], writes=[dts])
            kb.op("act", lambda e, ps2=ps2, ECS=ECS: e.activation(out=ECS, in_=ps2[:, 0:16], func=AF.Exp), reads=[ps2], writes=[dts])
            kb.op("act", lambda e, ps2=ps2, CDEC=CDEC: e.activation(out=CDEC, in_=ps2[:, 16:32], func=AF.Exp), reads=[ps2], writes=[dts])
            kb.op("dve", lambda e, ps2=ps2, T2=T2, CS=CS: e.tensor_tensor(out=T2, in0=ps2[:, 16:32], in1=CS, op=ALU.subtract), reads=[ps2, dts], writes=[dts])
            kb.op("act", lambda e, T2=T2: e.activation(out=T2, in_=T2, func=AF.Exp), reads=[dts], writes=[dts])
            kb.op("dve", lambda e, T2=T2, DT=DT, DEDT=DEDT: e.tensor_tensor(out=DEDT, in0=T2, in1=DT, op=ALU.mult), reads=[dts], writes=[dts])
            kb.op("act", lambda e, DT=DT, T1=T1: e.activation(out=T1, in_=DT, func=AF.Ln), reads=[dts], writes=[dts])
            kb.op("dve", lambda e, T1=T1, CS=CS, NB=NB: e.tensor_tensor(out=NB, in0=T1, in1=CS, op=ALU.subtract), reads=[dts], writes=[dts])
            kb.op("act", lambda e, CS=CS, c=c: e.copy(out=self.cshl[:, c, 0, :], in_=CS), reads=[dts], writes=[self.cshl])
            kb.op("dve", lambda e, CS=CS, c=c: e.tensor_tensor(out=self.cshl[:, c, 1, :], in0=CS, in1=self.cshl[:, c, 0, :], op=ALU.subtract),
                  reads=[dts, self.cshl], writes=[self.cshl])
        if self.dbg.get('bstop', 99) <= 1: return
        for g3 in range(3):
            Wt, Wv = self.wload(self.wsrc(self.w_in, l, D, OXBC + g3 * 512, 512), 8, 512)
            kb.op("dve", lambda e, g3=g3: e.tensor_copy(out=self.xbcT[:, :, 0:3], in_=self.hist[l][:, g3 * 4:(g3 + 1) * 4, :]),
                  reads=[self.hist[l]], writes=[self.xbcT])
            for t in range(4):
                tt = g3 * 4 + t
                ps = self.proj_feat(Wt, Wv, t * 128)
                kb.op("act", lambda e, t=t, ps=ps: e.copy(out=self.xbcT[:, t, 3:3 + TB], in_=ps[:, :]), reads=[ps], writes=[self.xbcT])
                kb.op("dve", lambda e, t=t, tt=tt: e.tensor_scalar(out=self.cacc[:], in0=self.xbcT[:, t, 0:TB], scalar1=self.cw[:, tt, 0:1], scalar2=None, op0=ALU.mult),
                      reads=[self.xbcT, self.cw], writes=[self.cacc])
                for tau in range(1, 4):
                    kb.op("dve", lambda e, t=t, tt=tt, tau=tau: e.scalar_tensor_tensor(
                        out=self.cacc[:], in0=self.xbcT[:, t, tau:tau + TB], scalar=self.cw[:, tt, tau:tau + 1], in1=self.cacc[:], op0=ALU.mult, op1=ALU.add),
                        reads=[self.xbcT, self.cw, self.cacc], writes=[self.cacc])
                if tt < 8:
                    dtile, dap = self.xsT, self.xsT[:, tt, :]
                elif tt < 10:
                    dtile, dap = self.BT, self.BT[:, tt - 8, :]
                else:
                    dtile, dap = self.CT, self.CT[:, tt - 10, :]
                kb.op("act", lambda e, tt=tt, dap=dap: e.activation(out=dap, in_=self.cacc[:], func=AF.Silu, bias=self.cb[:, tt:tt + 1], scale=1.0),
                      reads=[self.cacc, self.cb], writes=[dtile])
            kb.op("dve", lambda e, g3=g3: e.tensor_copy(out=self.hist[l][:, g3 * 4:(g3 + 1) * 4, :], in_=self.xbcT[:, :, TB:TB + 3]),
                  reads=[self.xbcT], writes=[self.hist[l]])
        if self.dbg.get('bstop', 99) <= 2: return
        S, Sb = self.ssmS[l], self.ssmSb
        kb.op("act", lambda e: e.copy(out=Sb[:], in_=S[:]), reads=[S], writes=[Sb])
        Zw = [self.wload(self.wsrc(self.w_in, l, D, OZ + g2 * 512, 512), 8, 512) for g2 in range(2)]
        for c in range(4):
            cs = slice(c * 128, (c + 1) * 128)
            DT, CS, ECS, DEDT, CDEC, NB, T1, T2 = [dts[:, c, i, :] for i in range(8)]
            for g2 in range(2):
                ps = self.proj_tok(Zw[g2][0], Zw[g2][1], c, 512)
                kb.op("act", lambda e, ps=ps, g2=g2: e.activation(out=self.zs[:, g2 * 512:(g2 + 1) * 512], in_=ps[:, :], func=AF.Silu),
                      reads=[ps], writes=[self.zs])
            if self.dbg.get('bstop', 99) <= 2.3: continue
            pb = self.npb()
            kb.mm([lambda pe, j=j, pb=pb: pe.transpose(out=pb[:, j * 128:(j + 1) * 128], in_=self.xsT[:, j, cs], identity=self.id_b[:]) for j in range(8)],
                  reads=[self.xsT, self.id_b], writes=[pb])
            kb.op("act", lambda e, pb=pb: e.copy(out=self.xs_tok[:], in_=pb[:, :]), reads=[pb], writes=[self.xs_tok])
            if self.dbg.get('bstop', 99) <= 2.6: continue
            pbv = self.xs_tok[:].rearrange("p (h q) -> p h q", h=16)
            kb.op("dve", lambda e, pbv=pbv, DEDT=DEDT: e.tensor_tensor(out=self.xw[:].rearrange("p (h q) -> p h q", h=16), in0=pbv,
                                                                        in1=DEDT.unsqueeze(2).broadcast_to([128, 16, 64]), op=ALU.mult),
                  reads=[self.xs_tok, dts], writes=[self.xw])
            kb.op("dve", lambda e, pbv=pbv: e.tensor_tensor(out=self.xD[:].rearrange("p (h q) -> p h q", h=16), in0=pbv,
                                                             in1=self.sm16[:, 2, :].unsqueeze(2).broadcast_to([128, 16, 64]), op=ALU.mult),
                  reads=[self.xs_tok, self.sm16], writes=[self.xD])
            if self.dbg.get('bstop', 99) <= 2.8: continue
            pb2 = self.npb()
            kb.mm([lambda pe, j=j, pb2=pb2: pe.transpose(out=pb2[:, j * 128:(j + 1) * 128], in_=self.BT[:, j, cs], identity=self.id_b[:]) for j in range(2)],
                  reads=[self.BT, self.id_b], writes=[pb2])
            kb.op("act", lambda e, pb2=pb2: e.copy(out=self.B_tok[:], in_=pb2[:, 0:256]), reads=[pb2], writes=[self.B_tok])
            if self.dbg.get('bstop', 99) <= 3: continue
            psc = self.nps()
            kb.mm([lambda pe, g=g, psc=psc: pe.matmul(psc[:, g * 128:(g + 1) * 128], self.BT[:, g, cs], self.CT[:, g, cs], start=True, stop=True) for g in range(2)],
                  reads=[self.BT, self.CT], writes=[psc])
            kb.op("act", lambda e, psc=psc: e.copy(out=self.cbT[:], in_=psc[:, 0:256].rearrange("p (g i) -> p g i", g=2)), reads=[psc], writes=[self.cbT])
            def build_D(hq):
                Dhl = self.Dhl2[hq % 2]
                for hl in range(2):
                    kb.op("dve", lambda e, hl=hl, c=c, hq=hq, Dhl=Dhl: e.tensor_tensor(
                        out=Dhl[:, hl, :, :], in0=self.id_b[:].unsqueeze(1).broadcast_to([128, 4, 128]),
                        in1=self.cshl[:, c, hl, hq * 4:(hq + 1) * 4].unsqueeze(2).broadcast_to([128, 4, 128]), op=ALU.mult),
                        reads=[self.id_b, self.cshl], writes=[Dhl])
            build_D(0)
            for hq in range(4):
                Dhl = self.Dhl2[hq % 2]; Ep = self.Ep2[hq % 2]
                if hq < 3:
                    build_D(hq + 1)
                pse = self.nps()
                kb.mm([lambda pe, pse=pse, hq=hq, Dhl=Dhl: pe.matmul(pse[:, :], self.ones_b[:], Dhl[:, 0, :, :], start=True, stop=False),
                       lambda pe, pse=pse, hq=hq, Dhl=Dhl: pe.matmul(pse[:, :], self.ones_b[:], Dhl[:, 1, :, :], start=False, stop=False),
                       lambda pe, pse=pse: pe.matmul(pse[:, :], self.id_b[:], self.negT[:].unsqueeze(1).broadcast_to([128, 4, 128]), start=False, stop=True)],
                      reads=[self.ones_b, Dhl, self.id_b, self.negT], writes=[pse])
                for hh in range(4):
                    h = hq * 4 + hh
                    kb.op("act", lambda e, pse=pse, hh=hh, h=h, NB=NB, Ep=Ep: e.activation(out=Ep[:, hh, :], in_=pse[:, hh * 128:(hh + 1) * 128], func=AF.Exp,
                                                                                  bias=NB[:, h:h + 1], scale=1.0), reads=[pse, dts], writes=[Ep])
                g = hq // 2
                kb.op("dve", lambda e, hq=hq, g=g, Ep=Ep: e.tensor_tensor(out=self.wmT[:, hq * 4:(hq + 1) * 4, :], in0=Ep[:],
                                                                    in1=self.cbT[:, g, :].unsqueeze(1).broadcast_to([128, 4, 128]), op=ALU.mult),
                      reads=[Ep, self.cbT], writes=[self.wmT])
            if self.dbg.get('bstop', 99) <= 4: continue
            psY = [self.nps(), self.nps()]
            for g in range(2):
                fns = []
                for hh in range(8):
                    h = g * 8 + hh
                    fns.append(lambda pe, g=g, hh=hh, h=h: pe.matmul(psY[g][:, hh * 64:(hh + 1) * 64], self.id_b[:], self.xD[:, h * 64:(h + 1) * 64], start=True, stop=False))
                    fns.append(lambda pe, g=g, hh=hh, h=h: pe.matmul(psY[g][:, hh * 64:(hh + 1) * 64], self.wmT[:, h, :], self.xs_tok[:, h * 64:(h + 1) * 64], start=False, stop=True))
                kb.mm(fns, reads=[self.id_b, self.xD, self.wmT, self.xs_tok], writes=[psY[g]])
            psZ = [self.nps(), self.nps()]
            for g in range(2):
                kb.mm([lambda pe, g=g: pe.matmul(psZ[g][:, :], self.CT[:, g, cs], Sb[:, g * 512:(g + 1) * 512], start=True, stop=True)],
                      reads=[self.CT, Sb], writes=[psZ[g]])
            for g in range(2):
                gs = slice(g * 512, (g + 1) * 512)
                kb.op("dve", lambda e, g=g, gs=gs, ECS=ECS: e.tensor_tensor(out=self.ytmp[:, gs].rearrange("p (h q) -> p h q", h=8),
                                                                          in0=psZ[g][:, :].rearrange("p (h q) -> p h q", h=8),
                                                                          in1=ECS[:, g * 8:(g + 1) * 8].unsqueeze(2).broadcast_to([128, 8, 64]), op=ALU.mult),
                      reads=[psZ[g], dts], writes=[self.ytmp])
                kb.op("dve", lambda e, g=g, gs=gs: e.tensor_tensor(out=self.ytmp[:, gs], in0=self.ytmp[:, gs], in1=psY[g][:, :], op=ALU.add),
                      reads=[psY[g], self.ytmp], writes=[self.ytmp])
            psS = [self.nps(), self.nps()]
            for g in range(2):
                kb.mm([lambda pe, g=g: pe.matmul(psS[g][:, :], self.B_tok[:, g * 128:(g + 1) * 128], self.xw[:, g * 512:(g + 1) * 512], start=True, stop=True)],
                      reads=[self.B_tok, self.xw], writes=[psS[g]])
            kb.op("dve", lambda e, CDEC=CDEC: e.tensor_tensor(out=self.stmp[:].rearrange("p (h q) -> p h q", h=16), in0=S[:].rearrange("p (h q) -> p h q", h=16),
                                                              in1=CDEC.unsqueeze(2).broadcast_to([128, 16, 64]), op=ALU.mult), reads=[S, dts], writes=[self.stmp])
            for g in range(2):
                gs = slice(g * 512, (g + 1) * 512)
                kb.op("dve", lambda e, g=g, gs=gs: e.tensor_tensor(out=S[:, gs], in0=self.stmp[:, gs], in1=psS[g][:, :], op=ALU.add),
                      reads=[self.stmp, psS[g]], writes=[S])
            kb.op("act", lambda e: e.copy(out=Sb[:], in_=S[:]), reads=[S], writes=[Sb])
            if self.dbg.get('bstop', 99) <= 5: continue
            kb.op("dve", lambda e, c=c: e.tensor_tensor(out=self.ytmp[:], in0=self.ytmp[:], in1=self.zs[:], op=ALU.mult),
                  reads=[self.ytmp, self.zs], writes=[self.ytmp])
            for g in range(2):
                gs = slice(g * 512, (g + 1) * 512)
                kb.op("act", lambda e, g=g, gs=gs: e.activation(out=self.stmp[:, gs], in_=self.ytmp[:, gs], func=AF.Square, accum_out=self.ssq[:, g:g + 1]),
                      reads=[self.ytmp], writes=[self.stmp, self.ssq])
            kb.op("act", lambda e: e.activation(out=self.ssq[:, 2:4], in_=self.ssq[:, 0:2], func=AF.Ln, bias=EPS, scale=1.0 / 512), reads=[self.ssq], writes=[self.ssq])
            kb.op("act", lambda e: e.activation(out=self.ssq[:, 2:4], in_=self.ssq[:, 2:4], func=AF.Exp, scale=-0.5), reads=[self.ssq], writes=[self.ssq])
            for g in range(2):
                gs = slice(g * 512, (g + 1) * 512)
                kb.op("dve", lambda e, g=g, gs=gs: e.scalar_tensor_tensor(out=self.ynb[:, gs], in0=self.ytmp[:, gs], scalar=self.ssq[:, 2 + g:3 + g],
                                                                           in1=self.normw[:, gs], op0=ALU.mult, op1=ALU.mult),
                      reads=[self.ytmp, self.ssq, self.normw], writes=[self.ynb])
            self.transp_b(self.ynb, [self.ynb[:, j * 128:(j + 1) * 128] for j in range(8)], self.yT, self.yT[:, :, cs])
        if self.dbg.get('bstop', 99) <= 6: return
        if bi == self.nblk - 1:
            for half in range(2):
                ps = self.nps()
                kb.mm([lambda pe, j=j, half=half, ps=ps: pe.transpose(out=ps[:, j * 128:(j + 1) * 128], in_=S[:, (half * 4 + j) * 128:(half * 4 + j + 1) * 128],
                                                                       identity=self.ident_f) for j in range(4)], reads=[S, self.cf], writes=[ps])
                kb.op("act", lambda e, ps=ps: e.copy(out=self.ytmp[:, 0:512], in_=ps[:, :]), reads=[ps], writes=[self.ytmp])
                kb.dma("sp", self.ssm_p.ap()[l, half * 512:(half + 1) * 512, :].rearrange("(j p) n -> p j n", p=128),
                       self.ytmp[:, 0:512].rearrange("p (j n) -> p j n", j=4), reads=[self.ytmp], is_output=True)
            for half in range(3):
                ps = self.nps()
                kb.mm([lambda pe, j=j, half=half, ps=ps: pe.transpose(out=ps[0:3, j * 128:(j + 1) * 128], in_=self.hist[l][:, half * 4 + j, :],
                                                                       identity=self.ident_f) for j in range(4)], reads=[self.hist[l], self.cf], writes=[ps])
                kb.op("act", lambda e, ps=ps: e.copy(out=self.stmp[0:3, 0:512], in_=ps[0:3, :]), reads=[ps], writes=[self.stmp])
                kb.dma("sp", self.conv_p.ap()[l, :, half * 512:(half + 1) * 512], self.stmp[0:3, 0:512], reads=[self.stmp], is_output=True)

    def phaseC(self, bi, l):
        self.kb.phase = 'phaseC'
        kb = self.kb
        last = (bi == self.nblk - 1)
        self.fence()
        kb.op("act", lambda e: e.activation(out=self.vaug[:, :, :, 64:65], in_=self.cf[:, 0:10].rearrange("p (a b c) -> p a b c", a=5, b=2), func=AF.Identity, scale=0.0, bias=1.0),
              reads=[self.cf], writes=[self.vaug])
        Wt, Wv = self.wload(self.wsrc(self.w_in, l, D, OQC, 512), 8, 512)
        for t in range(4):
            ps = self.proj_feat(Wt, Wv, t * 128)
            kb.op("act", lambda e, t=t, ps=ps: e.copy(out=self.qcT[:, t, :], in_=ps[:, :]), reads=[ps], writes=[self.qcT])
        if self.dbg.get('cstop', 99) <= 1: return
        Wt, Wv = self.wload(self.wsrc(self.w_in, l, D, OKC, 256), 8, 256)
        if bi > 0:
            kb.op("act", lambda e: e.copy(out=self.kTe[:, 0:128], in_=self.kprev[l][:]), reads=[self.kprev[l]], writes=[self.kTe])
            kb.op("act", lambda e: e.copy(out=self.vaug[:, 0, :, 0:64], in_=self.vprev[l][:, :, 0:64]), reads=[self.vprev[l]], writes=[self.vaug])
        if self.dbg.get('cstop', 99) <= 1.5: return
        ps = self.proj_feat(Wt, Wv, 0)
        kb.op("act", lambda e, ps=ps: e.copy(out=self.kTe[:, 128:128 + TB], in_=ps[:, :]), reads=[ps], writes=[self.kTe])
        kb.op("act", lambda e: e.copy(out=self.kprev[l][:], in_=self.kTe[:, TB:TB + 128]), reads=[self.kTe], writes=[self.kprev[l]])
        if self.dbg.get('cstop', 99) <= 1.7: return
        for c in range(4):
            ps = self.proj_tok(Wt, Wv, c, 128, col0=128)
            kb.op("act", lambda e, c=c, ps=ps: e.copy(out=self.vaug[:, c + 1, :, 0:64], in_=ps[:, 0:128].rearrange("p (g e) -> p g e", g=2)),
                  reads=[ps], writes=[self.vaug])
            if last and c == 3 and not self.dbg.get('nokv'):
                kb.op("act", lambda e, ps=ps: e.copy(out=self.kvo[:, 1, :], in_=ps[:, 0:128]), reads=[ps], writes=[self.kvo])
                ps2 = self.proj_tok(Wt, Wv, c, 128, col0=0)
                kb.op("act", lambda e, ps2=ps2: e.copy(out=self.kvo[:, 0, :], in_=ps2[:, 0:128]), reads=[ps2], writes=[self.kvo])
                kb.dma("sp", self.k_p.ap()[l, :, :], self.kvo[:, 0, :], reads=[self.kvo], is_output=True)
                kb.dma("sp", self.v_p.ap()[l, :, :], self.kvo[:, 1, :], reads=[self.kvo], is_output=True)
        kb.op("act", lambda e: e.copy(out=self.vprev[l][:, :, 0:65], in_=self.vaug[:, 4, :, 0:65]), reads=[self.vaug], writes=[self.vprev[l]])
        if self.dbg.get('cstop', 99) <= 2: return
        for c in range(4):
            n = bi * 4 + c
            cs = slice(c * 128, (c + 1) * 128)
            hfs = [1] if n == 0 else [0, 1]
            for g in range(2):
                for hf in hfs:
                    ps = self.nps()
                    kc0 = (c + hf) * 128
                    kb.mm([lambda pe, ps=ps, g=g, kc0=kc0: pe.matmul(ps[:, :], self.kTe[g * 64:(g + 1) * 64, kc0:kc0 + 128], self.qcT[g * 64:(g + 1) * 64, :, cs],
                                                                      start=True, stop=True)], reads=[self.kTe, self.qcT], writes=[ps])
                    kb.op("dve", lambda e, ps=ps, g=g, hf=hf: e.scalar_tensor_tensor(out=self.lg[:], in0=ps[:, :].rearrange("p (h q) -> p h q", h=4), scalar=0.125,
                                                                                    in1=self.btab[:, hf, g * 4:(g + 1) * 4, :], op0=ALU.mult, op1=ALU.add),
                          reads=[ps, self.btab], writes=[self.lg])
                    kb.op("act", lambda e, g=g, hf=hf: e.activation(out=self.pT[:, hf, g, :, :], in_=self.lg[:], func=AF.Exp), reads=[self.lg], writes=[self.pT])
            if self.dbg.get('cstop', 99) <= 3: continue
            psO = [self.nps(), self.nps()]
            for g in range(2):
                fns = []
                for h4 in range(4):
                    for i, hf in enumerate(hfs):
                        fns.append(lambda pe, g=g, h4=h4, hf=hf, i=i: pe.matmul(psO[g][:, h4 * 65:(h4 + 1) * 65], self.pT[:, hf, g, h4, :], self.vaug[:, c + hf, g, 0:65],
                                                                               start=(i == 0), stop=(i == len(hfs) - 1)))
                kb.mm(fns, reads=[self.pT, self.vaug], writes=[psO[g]])
                if self.dbg.get('cstop', 99) <= 4: continue
                ov = psO[g][:, 0:260].rearrange("p (h e) -> p h e", h=4)
                kb.op("dve", lambda e, g=g, ov=ov: e.tensor_tensor(out=self.den[:, g * 4:(g + 1) * 4], in0=ov[:, :, 64], in1=self.esink[:, g * 4:(g + 1) * 4], op=ALU.add),
                      reads=[psO[g], self.esink], writes=[self.den])
                kb.op("dve", lambda e, g=g: e.reciprocal(out=self.rden[:, g * 4:(g + 1) * 4], in_=self.den[:, g * 4:(g + 1) * 4]), reads=[self.den], writes=[self.rden])
                kb.op("dve", lambda e, g=g, ov=ov: e.tensor_tensor(out=self.oc_tok[:, g * 4:(g + 1) * 4, :], in0=ov[:, :, 0:64],
                                                                    in1=self.rden[:, g * 4:(g + 1) * 4].unsqueeze(2).broadcast_to([128, 4, 64]), op=ALU.mult),
                      reads=[psO[g], self.rden], writes=[self.oc_tok])
            if self.dbg.get('cstop', 99) <= 5: continue
            ocv = self.oc_tok[:].rearrange("p h e -> p (h e)")
            self.transp_b(self.oc_tok, [ocv[:, j * 128:(j + 1) * 128] for j in range(4)], self.ocT, self.ocT[:, :, cs])

    def phaseD(self, bi, l):
        self.kb.phase = 'phaseD'
        kb = self.kb
        self.fence()
        brs = [(self.w_br_ret, 512, 4, self.orT), (self.w_br_ssd, 1024, 8, self.yT), (self.w_br_swa, 512, 4, self.ocT)]
        for b, (wbr, K, kc, src) in enumerate(brs):
            for j in range(2):
                Gt, Gv = self.wload(self.wsrc(self.w_in, l, D, OGATE + b * 1024 + j * 512, 512), 8, 512)
                Bt, Bv = self.wload(self.wsrc(wbr, l, K, j * 512, 512), kc, 512)
                for t in range(4):
                    m = j * 4 + t
                    psG = self.proj_feat(Gt, Gv, t * 128)
                    psB = self.proj_feat(Bt, Bv, t * 128, src=src, kc=kc)
                    kb.op("act", lambda e, psG=psG: e.activation(out=self.sg[:], in_=psG[:, :], func=AF.Sigmoid), reads=[psG], writes=[self.sg])
                    if b == 0:
                        kb.op("dve", lambda e, m=m, psB=psB: e.tensor_tensor(out=self.macc[:, m, :], in0=self.sg[:], in1=psB[:, :], op=ALU.mult),
                              reads=[self.sg, psB], writes=[self.macc])
                    else:
                        kb.op("dve", lambda e, psB=psB: e.tensor_tensor(out=self.gt[:], in0=self.sg[:], in1=psB[:, :], op=ALU.mult),
                              reads=[self.sg, psB], writes=[self.gt])
                        if b == 1:
                            kb.op("dve", lambda e, m=m: e.tensor_tensor(out=self.macc[:, m, :], in0=self.macc[:, m, :], in1=self.gt[:], op=ALU.add),
                                  reads=[self.macc, self.gt], writes=[self.macc])
                        else:
                            kb.op("dve", lambda e, m=m: e.tensor_tensor(out=self.mrg[:, m, :], in0=self.macc[:, m, :], in1=self.gt[:], op=ALU.add),
                                  reads=[self.macc, self.gt], writes=[self.mrg])

    def resid_ln(self, pss_fn, gi):
        kb = self.kb
        for m in range(8):
            ps = pss_fn(m)
            kb.op("dve", lambda e, m=m, ps=ps: e.scalar_tensor_tensor(out=self.xr[:, m, :], in0=self.xr[:, m, :], scalar=ALPHA, in1=ps[:, :], op0=ALU.mult, op1=ALU.add),
                  reads=[self.xr, ps], writes=[self.xr])
            kb.op("act", lambda e, m=m: e.copy(out=self.xb[:, m, :], in_=self.xr[:, m, :]), reads=[self.xr], writes=[self.xb])
            kb.op("act", lambda e, m=m: e.activation(out=self.ysq[:, m, :], in_=self.xr[:, m, :], func=AF.Square), reads=[self.xr], writes=[self.ysq])
        ps1 = self.nps(); ps2 = self.nps()
        kb.mm([lambda pe, k=k: pe.matmul(ps1[:, :], self.ones_b[:], self.xb[:, k, :], start=(k == 0), stop=(k == 7)) for k in range(8)],
              reads=[self.ones_b, self.xb], writes=[ps1])
        kb.mm([lambda pe, k=k: pe.matmul(ps2[:, :], self.ones_b[:], self.ysq[:, k, :], start=(k == 0), stop=(k == 7)) for k in range(8)],
              reads=[self.ones_b, self.ysq], writes=[ps2])
        kb.op("act", lambda e: e.activation(out=self.lnm[:], in_=ps1[:, :], func=AF.Identity, scale=1.0 / D), reads=[ps1], writes=[self.lnm])
        kb.op("dve", lambda e: e.tensor_tensor(out=self.lnt[:], in0=self.lnm[:], in1=self.lnm[:], op=ALU.mult), reads=[self.lnm], writes=[self.lnt])
        kb.op("dve", lambda e: e.scalar_tensor_tensor(out=self.lnt[:], in0=ps2[:, :], scalar=1.0 / D, in1=self.lnt[:], op0=ALU.mult, op1=ALU.subtract),
              reads=[ps2, self.lnt], writes=[self.lnt])
        kb.op("act", lambda e: e.activation(out=self.lnt[:], in_=self.lnt[:], func=AF.Ln, bias=EPS, scale=1.0), reads=[self.lnt], writes=[self.lnt])
        kb.op("act", lambda e: e.activation(out=self.lnr[:], in_=self.lnt[:], func=AF.Exp, scale=-0.5), reads=[self.lnt], writes=[self.lnr])
        for m in range(8):
            eng = "pool" if (m % 2 == 1 and self.dbg.get("ln_pool", False)) else "dve"
            lnu = self.lnu2 if eng == "pool" else self.lnu
            kb.op(eng, lambda e, m=m, lnu=lnu: e.tensor_tensor(out=lnu[:], in0=self.xr[:, m, :], in1=self.lnm[:], op=ALU.subtract), reads=[self.xr, self.lnm], writes=[lnu])
            kb.op(eng, lambda e, m=m, lnu=lnu: e.tensor_tensor(out=lnu[:], in0=lnu[:], in1=self.lnr[:], op=ALU.mult), reads=[lnu, self.lnr], writes=[lnu])
            kb.op("act", lambda e, m=m, lnu=lnu: e.activation(out=self.xb[:, m, :], in_=lnu[:], func=AF.Identity, bias=self.lnp[:, gi + 1, m:m + 1], scale=self.lnp[:, gi, m:m + 1]),
                  reads=[lnu, self.lnp], writes=[self.xb])
            kb.op("act", lambda e, m=m, lnu=lnu: e.activation(out=self.xr[:, m, :], in_=lnu[:], func=AF.Identity, bias=self.lnp[:, gi + 1, m:m + 1], scale=self.lnp[:, gi, m:m + 1]),
                  reads=[lnu, self.lnp], writes=[self.xr])

    def phaseE(self, bi, l):
        self.kb.phase = 'phaseE'
        W = {}

        def pss(m):
            j = m // 4
            if (m % 4) == 0:
                W["t"], W["v"] = self.wload(self.wsrc(self.w_out, l, D, j * 512, 512), 8, 512)
            return self.proj_feat(W["t"], W["v"], (m % 4) * 128, src=self.mrg)
        self.resid_ln(pss, 0)

    def phaseF(self, bi, l):
        self.kb.phase = 'phaseF'
        kb = self.kb
        for jg in range(6):
            n = 512 if jg < 5 else 256
            Gt, Gv = self.wload(self.wsrc(self.w_gate, l, D, jg * 512, n), 8, n)
            Ut, Uv = self.wload(self.wsrc(self.w_up, l, D, jg * 512, n), 8, n)
            for t in range(n // 128):
                j = jg * 4 + t
                psG = self.proj_feat(Gt, Gv, t * 128)
                psU = self.proj_feat(Ut, Uv, t * 128)
                kb.op("act", lambda e, psG=psG: e.activation(out=self.sg[:], in_=psG[:, :], func=AF.Silu), reads=[psG], writes=[self.sg])
                kb.op("dve", lambda e, j=j, psU=psU: e.tensor_tensor(out=self.hT[:, j, :], in0=self.sg[:], in1=psU[:, :], op=ALU.mult),
                      reads=[self.sg, psU], writes=[self.hT])

        def pss(m):
            Wt, Wv = self.wload(self.wsrc_down(l, m), 22, 128)
            return self.proj_feat(Wt, Wv, 0, src=self.hT, kc=22)
        self.resid_ln(pss, 2)


NB = 16


class GenS(Gen):
    def _decl(self):
        Gen._decl(self)
        if not self.do_sample:
            return
        dp = self.depth
        d, o = self.din, self.dout
        self.xs = d("xs", [NB, D])
        self.st_ret = d("st_ret", [dp, NB, 4, 128, 128])
        self.st_ssm = d("st_ssm", [dp, NB, 16, 64, 128])
        self.st_conv = d("st_conv", [dp, NB, 3, 1536])
        self.ck = d("ck", [dp, NB, 128, 128])
        self.cv = d("cv", [dp, NB, 128, 128])
        self.c_rots = d("c_rots", [NB, 256])
        self.c_sel = d("c_sel", [NB, 16 * 128 + 2 * 128])
        self.c_eye = d("c_eye", [128, 256])
        self.conv_w_n = d("conv_w_n", [dp, 1, 4 * 1536])
        self.conv_b_n = d("conv_b_n", [dp, 1, 1536])
        self.ln_n = d("ln_n", [dp, 4, 1, 1024])
        self.ys = o("ys", [NB, D])
        self.ret_s = o("ret_s", [dp, NB, 4, 128, 128])
        self.ssm_s = o("ssm_s", [dp, NB, 16, 64, 128])
        self.conv_s = o("conv_s", [dp, NB, 3, 1536])
        self.k_s = o("k_s", [dp, NB, 128, 128])
        self.v_s = o("v_s", [dp, NB, 128, 128])

    def _alloc(self):
        Gen._alloc(self)
        if not self.do_sample:
            return
        nc = self.nc
        st = {"off": 0, "base": None, "size": 0}

        def region(base, size):
            st["base"], st["size"], st["off"] = base, size, 0

        def al(name, shape, dt):
            n = 1
            for x in shape[1:]:
                n *= x
            nb = (n * (4 if dt == F32 else 2) + 31) // 32 * 32
            assert st["off"] + nb <= st["size"], (name, st["off"], nb, st["size"])
            h = nc.alloc_sbuf_tensor_at(name, list(shape), dt, offset=st["base"] + st["off"])
            st["off"] += nb
            return Tile(h, None, name)
        region(self.xr_off, 16384 + 8192)
        self.sx = al("sx", [NB, 1024], F32)
        self.sxT = al("sxT", [128, 8, NB], BF16)
        self.sxconv = al("sxconv", [NB, 1536], F32)
        self.s_orT = al("s_orT", [128, 4, NB], BF16); self.s_yT = al("s_yT", [128, 8, NB], BF16); self.s_ocT = al("s_ocT", [128, 4, NB], BF16)
        self.s_mT = al("s_mT", [128, 8, NB], BF16); self.s_hT = al("s_hT", [128, 22, NB], BF16)
        self.s_sel = al("s_sel", [NB, 16 * 128 + 256], F32)
        self.s_eye = al("s_eye", [128, 256], F32)
        self.s_rot = al("s_rot", [NB, 256], F32)
        self.s_b0 = al("s_b0", [NB, 8], F32)
        A = self.abase
        region(A, self.asz)
        self.r_q = al("r_q", [NB, 512], F32); self.r_k = al("r_k", [NB, 512], F32)
        self.r_tA = al("r_tA", [NB, 4, 2, 64], F32); self.r_tB = al("r_tB", [NB, 4, 2, 64], F32)
        self.r_qb = al("r_qb", [NB, 512], BF16); self.r_kb = al("r_kb", [NB, 512], BF16); self.r_vb = al("r_vb", [NB, 512], BF16)
        self.r_g = al("r_g", [NB, 512], F32)
        self.r_qT = al("r_qT", [128, 4, NB], BF16)
        self.r_qTm = al("r_qTm", [128, 4, NB, NB], BF16)
        self.r_vbd = al("r_vbd", [NB, 4, 4, 128], BF16)
        self.r_S = [al("r_S%d" % i, [128, 4, 4, 128], F32) for i in range(2)]
        self.r_Sb = al("r_Sb", [128, NB, 4, 128], BF16)
        self.r_o = al("r_o", [NB, 4, 128], F32)
        self.r_st6 = al("r_st6", [NB, 4, 6], F32); self.r_mv = al("r_mv", [NB, 4, 2], F32)
        self.r_rs = al("r_rs", [NB, 4], F32); self.r_rsb = al("r_rsb", [NB, 4], F32)
        self.r_og = al("r_og", [NB, 4, 128], F32); self.r_ogb = al("r_ogb", [NB, 512], BF16)
        print("arena S-RET", st["off"])
        region(A, self.asz)
        self.c_w = al("c_w", [NB, 4, 1536], F32); self.c_buf = al("c_buf", [NB, 3, 1536], F32)
        self.c_new = al("c_new", [NB, 1536], F32); self.c_acc = al("c_acc", [NB, 1536], F32)
        print("arena S-CONV", st["off"])
        region(A, self.asz)
        self.d_z = al("d_z", [NB, 1024], F32)
        self.d_h = al("d_h", [128, 4, 8, 128], F32); self.d_t = al("d_t", [128, 4, 8, 128], F32)
        self.d_sm = al("d_sm", [NB, 8, 16], F32)
        self.d_xdt = al("d_xdt", [NB, 1024], F32)
        self.d_xdtT = al("d_xdtT", [128, NB, 8], F32)
        self.d_R = al("d_R", [NB, 2, 4, 128], F32)
        self.d_RA = al("d_RA", [NB, 2, NB, 8], F32)
        self.d_dA = al("d_dA", [128, NB, 8], F32)
        self.d_BC = al("d_BC", [128, 2, 4, 128], F32)
        self.d_yT = al("d_yT", [128, NB, 8], F32)
        self.d_y = al("d_y", [NB, 1024], F32); self.d_y2 = self.d_xdt
        self.d_ssq = al("d_ssq", [NB, 4], F32)
        self.d_nw = al("d_nw", [NB, 1024], F32)
        self.d_ynb = al("d_ynb", [NB, 1024], BF16)
        print("arena S-SSD", st["off"])
        region(A, self.asz)
        self.a_K = al("a_K", [128, NB, 128], F32); self.a_V = al("a_V", [128, NB, 128], F32)
        self.a_Vb = al("a_Vb", [128, NB, 2, 80], BF16)
        self.a_q = al("a_q", [NB, 512], F32); self.a_kn = al("a_kn", [NB, 128], F32); self.a_vn = al("a_vn", [NB, 2, 80], F32)
        self.a_pr = al("a_pr", [128, 4, 2, 64], F32)
        self.a_s = al("a_s", [128, NB, 8], F32)
        self.a_p = al("a_p", [128, NB, 8], BF16)
        self.a_Pm = al("a_Pm", [128, NB, 8, NB], BF16)
        self.a_sn = al("a_sn", [NB, 4, 2, 64], F32); self.a_s8 = al("a_s8", [NB, 8], F32); self.a_pn = al("a_pn", [NB, 8], F32)
        self.a_ou = al("a_ou", [NB, 8, 65], F32); self.a_t = al("a_t", [NB, 8, 65], F32)
        self.a_den = al("a_den", [NB, 8], F32)
        self.a_ob = al("a_ob", [NB, 8, 64], BF16)
        self.a_es = al("a_es", [NB, 8], F32)
        print("arena S-SWA", st["off"])
        region(A, self.asz)
        self.m_sg = al("m_sg", [NB, 512], F32); self.m_t = al("m_t", [NB, 512], F32)
        self.m_acc = al("m_acc", [NB, 1024], F32)
        self.m_b = al("m_b", [NB, 1024], BF16)
        self.m_g = al("m_g", [NB, 1024], F32); self.m_bb = al("m_bb", [NB, 1024], F32)
        self.m_st = al("m_st", [NB, 2, 6], F32); self.m_mv = al("m_mv", [NB, 2], F32); self.m_rs = al("m_rs", [NB, 2], F32)
        self.m_u = al("m_u", [NB, 1024], F32)
        self.m_h = al("m_h", [NB, 2816], F32); self.m_hb = al("m_hb", [NB, 2816], BF16)
        print("arena S-MLP", st["off"])

    def sproj(self, Wt, Wv, n, src=None, kc=8, col0=0):
        src = self.sxT if src is None else src
        ps = self.nps()
        self.kb.mm([lambda pe, k=k, ps=ps: pe.matmul(ps[0:NB, 0:n], src[:, k, :], Wv[:, k, col0:col0 + n], start=(k == 0), stop=(k == kc - 1))
                    for k in range(kc)], reads=[src, Wt], writes=[ps])
        return ps

    def s_transp(self, src_tile, src_aps, dst_tile, dst_ap):
        n = len(src_aps)
        pb = self.npb()
        self.kb.mm([lambda pe, j=j, pb=pb: pe.transpose(out=pb[:, j * NB:(j + 1) * NB], in_=src_aps[j], identity=self.id_b[0:NB, 0:NB])
                    for j in range(n)], reads=[src_tile, self.id_b], writes=[pb])
        self.kb.op("act", lambda e, pb=pb: e.copy(out=dst_ap, in_=pb[:, 0:n * NB].rearrange("p (j t) -> p j t", j=n)), reads=[pb], writes=[dst_tile])

    def s_tokmajor_to_T(self, src, ncols, dst):
        nt = ncols // 128
        for j0 in range(0, nt, 8):
            j1 = min(nt, j0 + 8)
            self.s_transp(src, [src[0:NB, j * 128:(j + 1) * 128] for j in range(j0, j1)], dst, dst[:, j0:j1, :])

    def sample_pass(self):
        kb = self.kb
        self.fence()
        kb.dma("sp", self.sx[:], self.xs.ap()[:, :], writes=[self.sx])
        kb.dma("sp", self.s_sel[:], self.c_sel.ap()[:, :], writes=[self.s_sel])
        kb.dma("sp", self.s_eye[:], self.c_eye.ap()[:, :], writes=[self.s_eye])
        kb.dma("sp", self.s_rot[:], self.c_rots.ap()[:, :], writes=[self.s_rot])
        kb.dma("sp", self.s_b0[:], self.rel_bias.ap()[0:1, 0:8].partition_broadcast(NB), writes=[self.s_b0])
        self.s_refresh_xT()
        for l in range(self.depth):
            self.load_params(0, l)
            self.s_ret(l)
            self.s_conv(l)
            self.s_ssd(l)
            self.s_swa(l)
            self.s_mlp(l)
        kb.dma("sp", self.ys.ap()[:, :], self.sx[:], reads=[self.sx], is_output=True)

    def s_refresh_xT(self):
        kb = self.kb
        kb.op("act", lambda e: e.copy(out=self.m_b[:], in_=self.sx[:]), reads=[self.sx], writes=[self.m_b])
        self.s_tokmajor_to_T(self.m_b, 1024, self.sxT)

    def s_ret(self, l):
        self.kb.phase = 's_ret'
        kb = self.kb
        self.fence()
        gam = [1.0 - 2.0 ** (-5 - h) for h in range(4)]
        for gi, off in enumerate((OQ, OK_, OV, OG)):
            Wt, Wv = self.wload(self.wsrc(self.w_in, l, D, off, 512), 8, 512)
            ps = self.sproj(Wt, Wv, 512)
            pv = ps[0:NB, :]
            if gi < 2:
                dst = self.r_q if gi == 0 else self.r_k
                psv = pv.rearrange("p (h t e) -> p h t e", h=4, t=2)
                cosb = self.s_rot[:, gi * 128:gi * 128 + 64].unsqueeze(1).unsqueeze(1).broadcast_to([NB, 4, 2, 64])
                sinb = self.s_rot[:, gi * 128 + 64:gi * 128 + 128].unsqueeze(1).broadcast_to([NB, 4, 64])
                kb.op("dve", lambda e, psv=psv, cosb=cosb: e.tensor_tensor(out=self.r_tA[:], in0=psv, in1=cosb, op=ALU.mult), reads=[ps, self.s_rot], writes=[self.r_tA])
                kb.op("dve", lambda e, psv=psv, sinb=sinb: e.tensor_tensor(out=self.r_tB[:, :, 0, :], in0=psv[:, :, 1, :], in1=sinb, op=ALU.mult), reads=[ps, self.s_rot], writes=[self.r_tB])
                kb.op("dve", lambda e, psv=psv, sinb=sinb: e.tensor_tensor(out=self.r_tB[:, :, 1, :], in0=psv[:, :, 0, :], in1=sinb, op=ALU.mult), reads=[ps, self.s_rot], writes=[self.r_tB])
                dv = dst[:].rearrange("p (h t e) -> p h t e", h=4, t=2)
                kb.op("dve", lambda e, dv=dv: e.tensor_tensor(out=dv[:, :, 0, :], in0=self.r_tA[:, :, 0, :], in1=self.r_tB[:, :, 0, :], op=ALU.subtract), reads=[self.r_tA, self.r_tB], writes=[dst])
                kb.op("dve", lambda e, dv=dv: e.tensor_tensor(out=dv[:, :, 1, :], in0=self.r_tA[:, :, 1, :], in1=self.r_tB[:, :, 1, :], op=ALU.add), reads=[self.r_tA, self.r_tB], writes=[dst])
                db = self.r_qb if gi == 0 else self.r_kb
                kb.op("act", lambda e, dst=dst, db=db: e.copy(out=db[:], in_=dst[:]), reads=[dst], writes=[db])
            elif gi == 2:
                kb.op("act", lambda e, pv=pv: e.copy(out=self.r_vb[:], in_=pv), reads=[ps], writes=[self.r_vb])
            else:
                kb.op("act", lambda e, pv=pv: e.activation(out=self.r_g[:], in_=pv, func=AF.Silu), reads=[ps], writes=[self.r_g])
        self.s_transp(self.r_qb, [self.r_qb[0:NB, h * 128:(h + 1) * 128] for h in range(4)], self.r_qT, self.r_qT[:, :, :])
        kb.op("dve", lambda e: e.tensor_tensor(out=self.r_qTm[:], in0=self.r_qT[:].unsqueeze(2).broadcast_to([128, 4, NB, NB]),
                                                in1=self.s_eye[:].rearrange("p (a b) -> p a b", a=NB).unsqueeze(1).broadcast_to([128, 4, NB, NB]), op=ALU.mult),
              reads=[self.r_qT, self.s_eye], writes=[self.r_qTm])
        sel = self.s_sel[:, 0:NB * 128].rearrange("p (b k) -> p b k", b=NB)
        for sg in range(4):
            S = self.r_S[sg % 2]
            kb.dma("sp", S[:], self.st_ret.ap()[l, sg * 4:(sg + 1) * 4].rearrange("b h d v -> d b h v"), writes=[S])
            kb.op("dve", lambda e, sg=sg: e.tensor_tensor(
                out=self.r_vbd[:], in0=self.r_vb[:].rearrange("p (h v) -> p h v", h=4).unsqueeze(2).broadcast_to([NB, 4, 4, 128]),
                in1=sel[:, sg * 4:(sg + 1) * 4, 0:1].rearrange("p b o -> p o b").unsqueeze(3).broadcast_to([NB, 4, 4, 128]), op=ALU.mult),
                reads=[self.r_vb, self.s_sel], writes=[self.r_vbd])
            for h in range(4):
                ps = self.nps()
                kb.mm([lambda pe, h=h, ps=ps: pe.matmul(ps[:, :], self.r_kb[0:NB, h * 128:(h + 1) * 128], self.r_vbd[:, h, :, :], start=True, stop=True)],
                      reads=[self.r_kb, self.r_vbd], writes=[ps])
                kb.op("dve", lambda e, h=h, ps=ps, S=S: e.scalar_tensor_tensor(out=S[:, :, h, :], in0=S[:, :, h, :], scalar=gam[h],
                                                                               in1=ps[:, :].rearrange("p (b v) -> p b v", b=4), op0=ALU.mult, op1=ALU.add),
                      reads=[S, ps], writes=[S])
            kb.op("act", lambda e, sg=sg, S=S: e.copy(out=self.r_Sb[:, sg * 4:(sg + 1) * 4, :, :], in_=S[:]), reads=[S], writes=[self.r_Sb])
            kb.dma("sp", self.ret_s.ap()[l, sg * 4:(sg + 1) * 4].rearrange("b h d v -> d b h v"), S[:], reads=[S], is_output=True)
        pso = self.nps()
        fns = []
        for h in range(4):
            for b in range(NB):
                fns.append(lambda pe, h=h, b=b: pe.matmul(pso[0:NB, h * 128:(h + 1) * 128], self.r_qTm[:, h, b, :], self.r_Sb[:, b, h, :],
                                                          start=(b == 0), stop=(b == NB - 1)))
        kb.mm(fns, reads=[self.r_qTm, self.r_Sb], writes=[pso])
        kb.op("act", lambda e: e.copy(out=self.r_o[:], in_=pso[0:NB, :].rearrange("p (h v) -> p h v", h=4)), reads=[pso], writes=[self.r_o])
        for h in range(4):
            kb.op("dve", lambda e, h=h: e.bn_stats(out=self.r_st6[:, h, :], in_=self.r_o[:, h, :]), reads=[self.r_o], writes=[self.r_st6])
        for h in range(4):
            kb.op("dve", lambda e, h=h: e.bn_aggr(out=self.r_mv[:, h, :], in_=self.r_st6[:, h, :]), reads=[self.r_st6], writes=[self.r_mv])
        kb.op("act", lambda e: e.activation(out=self.r_rs[:], in_=self.r_mv[:, :, 1], func=AF.Ln, bias=EPS, scale=1.0), reads=[self.r_mv], writes=[self.r_rs])
        kb.op("act", lambda e: e.activation(out=self.r_rsb[:], in_=self.r_rs[:], func=AF.Exp, scale=-0.5), reads=[self.r_rs], writes=[self.r_rsb])
        for h in range(4):
            kb.op("dve", lambda e, h=h: e.scalar_tensor_tensor(out=self.r_og[:, h, :], in0=self.r_o[:, h, :], scalar=self.r_mv[:, h, 0:1],
                                                                 in1=self.r_g[:, h * 128:(h + 1) * 128], op0=ALU.subtract, op1=ALU.mult),
                  reads=[self.r_o, self.r_mv, self.r_g], writes=[self.r_og])
        for h in range(4):
            kb.op("act", lambda e, h=h: e.activation(out=self.r_ogb[:, h * 128:(h + 1) * 128], in_=self.r_og[:, h, :], func=AF.Identity, scale=self.r_rsb[:, h:h + 1]),
                  reads=[self.r_og, self.r_rsb], writes=[self.r_ogb])
        self.s_tokmajor_to_T(self.r_ogb, 512, self.s_orT)

    def s_conv(self, l):
        self.kb.phase = 's_conv'
        kb = self.kb
        self.fence()
        kb.dma("sp", self.c_w[:].rearrange("p t c -> p (t c)"), self.conv_w_n.ap()[l].partition_broadcast(NB), writes=[self.c_w])
        kb.dma("sp", self.c_buf[:], self.st_conv.ap()[l], writes=[self.c_buf])
        kb.dma("sp", self.c_acc[:], self.conv_b_n.ap()[l].partition_broadcast(NB), writes=[self.c_acc])
        for g3 in range(3):
            Wt, Wv = self.wload(self.wsrc(self.w_in, l, D, OXBC + g3 * 512, 512), 8, 512)
            ps = self.sproj(Wt, Wv, 512)
            kb.op("act", lambda e, ps=ps, g3=g3: e.copy(out=self.c_new[:, g3 * 512:(g3 + 1) * 512], in_=ps[0:NB, :]), reads=[ps], writes=[self.c_new])
        kb.dma("sp", self.conv_s.ap()[l, :, 0:2, :], self.st_conv.ap()[l, :, 1:3, :], is_output=True)
        kb.dma("sp", self.conv_s.ap()[l, :, 2, :], self.c_new[:], reads=[self.c_new], is_output=True)
        for tau in range(4):
            src = self.c_buf[:, tau, :] if tau < 3 else self.c_new[:]
            kb.op("dve", lambda e, tau=tau, src=src: e.tensor_tensor(out=self.c_w[:, tau, :], in0=self.c_w[:, tau, :], in1=src, op=ALU.mult),
                  reads=[self.c_w, self.c_buf, self.c_new], writes=[self.c_w])
            kb.op("dve", lambda e, tau=tau: e.tensor_tensor(out=self.c_acc[:], in0=self.c_acc[:], in1=self.c_w[:, tau, :], op=ALU.add),
                  reads=[self.c_w, self.c_acc], writes=[self.c_acc])
        kb.op("act", lambda e: e.activation(out=self.sxconv[:], in_=self.c_acc[:], func=AF.Silu), reads=[self.c_acc], writes=[self.sxconv])

    def s_ssd(self, l):
        self.kb.phase = 's_ssd'
        kb = self.kb
        self.fence()
        kb.dma("sp", self.d_nw[:], self.norm_w.ap()[l:l + 1, :].partition_broadcast(NB), writes=[self.d_nw])
        for g2 in range(2):
            Wt, Wv = self.wload(self.wsrc(self.w_in, l, D, OZ + g2 * 512, 512), 8, 512)
            ps = self.sproj(Wt, Wv, 512)
            kb.op("act", lambda e, ps=ps, g2=g2: e.activation(out=self.d_z[:, g2 * 512:(g2 + 1) * 512], in_=ps[0:NB, :], func=AF.Silu), reads=[ps], writes=[self.d_z])
        Wt, Wv = self.wload(self.wsrc(self.w_in, l, D, ODT, 16), 8, 16)
        ps = self.sproj(Wt, Wv, 16)
        sm = self.d_sm
        DT, DA, T1, T2 = [sm[:, i, :] for i in range(4)]
        p16 = self.sm16[0:NB, :, :]
        kb.op("dve", lambda e, ps=ps: e.tensor_tensor(out=T1, in0=ps[0:NB, 0:16], in1=p16[:, 0, :], op=ALU.add), reads=[ps, self.sm16], writes=[sm])
        kb.op("act", lambda e: e.activation(out=T2, in_=T1, func=AF.Exp), reads=[sm], writes=[sm])
        kb.op("act", lambda e: e.activation(out=DT, in_=T2, func=AF.Ln, bias=1.0, scale=1.0), reads=[sm], writes=[sm])
        kb.op("dve", lambda e: e.tensor_tensor(out=T1, in0=DT, in1=p16[:, 1, :], op=ALU.mult), reads=[sm, self.sm16], writes=[sm])
        kb.op("act", lambda e: e.activation(out=DA, in_=T1, func=AF.Exp), reads=[sm], writes=[sm])
        xs = self.sxconv[:, 0:1024]
        kb.op("dve", lambda e: e.tensor_tensor(out=self.d_xdt[:].rearrange("p (hh g q) -> p g hh q", hh=8, g=2), in0=xs.rearrange("p (g hh q) -> p g hh q", g=2, hh=8),
                                                in1=DT.rearrange("p (g hh) -> p g hh", g=2).unsqueeze(3).broadcast_to([NB, 2, 8, 64]), op=ALU.mult), reads=[self.sxconv, sm], writes=[self.d_xdt])
        pst = self.nps()
        kb.mm([lambda pe, hh=hh: pe.transpose(out=pst[:, hh * NB:(hh + 1) * NB], in_=self.d_xdt[:, hh * 128:(hh + 1) * 128], identity=self.ident_f[0:NB, 0:NB]) for hh in range(8)],
              reads=[self.d_xdt, self.cf], writes=[pst])
        kb.op("act", lambda e: e.copy(out=self.d_xdtT[:].rearrange("p b hh -> p hh b"), in_=pst[:, 0:8 * NB].rearrange("p (hh b) -> p hh b", hh=8)),
              reads=[pst], writes=[self.d_xdtT])
        eye = self.s_sel[:, 0:NB * 128].rearrange("p (b k) -> p b k", b=NB)[:, :, 0]
        gsel = self.s_sel[:, NB * 128:NB * 128 + 256].rearrange("p (g k) -> p g k", g=2)
        kb.op("dve", lambda e: e.tensor_tensor(out=self.d_RA[:], in0=DA.rearrange("p (g hh) -> p g hh", g=2).unsqueeze(2).broadcast_to([NB, 2, NB, 8]),
                                                in1=eye.unsqueeze(1).unsqueeze(3).broadcast_to([NB, 2, NB, 8]), op=ALU.mult), reads=[sm, self.s_sel], writes=[self.d_RA])
        psa = self.nps()
        kb.mm([lambda pe, g=g: pe.matmul(psa[:, 0:NB * 8], gsel[:, g, :], self.d_RA[:, g, :, :], start=(g == 0), stop=(g == 1)) for g in range(2)],
              reads=[self.s_sel, self.d_RA], writes=[psa])
        kb.op("act", lambda e: e.copy(out=self.d_dA[:], in_=psa[:, 0:NB * 8].rearrange("p (b hh) -> p b hh", b=NB)), reads=[psa], writes=[self.d_dA])
        Bc = self.sxconv[:, 1024:1536].rearrange("p (t g n) -> p t g n", t=2, g=2)
        for sg in range(4):
            bs = slice(sg * 4, (sg + 1) * 4)
            H = self.d_h
            for g in range(2):
                for bb in range(4):
                    kb.dma("sp", H[g * 64:(g + 1) * 64, bb, :, :], self.st_ssm.ap()[l, sg * 4 + bb, g * 8:(g + 1) * 8].rearrange("hh p n -> p hh n"), writes=[H])
            for t in range(2):
                kb.op("dve", lambda e, sg=sg, t=t: e.tensor_tensor(out=self.d_R[:], in0=Bc[:, t, :, :].unsqueeze(2).broadcast_to([NB, 2, 4, 128]),
                                                                    in1=eye[:, sg * 4:(sg + 1) * 4].unsqueeze(1).unsqueeze(3).broadcast_to([NB, 2, 4, 128]), op=ALU.mult),
                      reads=[self.sxconv, self.s_sel], writes=[self.d_R])
                psb_ = self.nps()
                kb.mm([lambda pe, g=g, psb_=psb_: pe.matmul(psb_[:, :], gsel[:, g, :], self.d_R[:, g, :, :], start=(g == 0), stop=(g == 1)) for g in range(2)],
                      reads=[self.s_sel, self.d_R], writes=[psb_])
                kb.op("act", lambda e, t=t, psb_=psb_: e.copy(out=self.d_BC[:, t, :, :], in_=psb_[:, :].rearrange("p (b n) -> p b n", b=4)), reads=[psb_], writes=[self.d_BC])
            kb.op("dve", lambda e, bs=bs: e.tensor_tensor(out=H[:], in0=H[:], in1=self.d_dA[:, bs, :].unsqueeze(3).broadcast_to([128, 4, 8, 128]), op=ALU.mult),
                  reads=[H, self.d_dA], writes=[H])
            kb.op("dve", lambda e, bs=bs: e.tensor_tensor(out=self.d_t[:], in0=self.d_xdtT[:, bs, :].unsqueeze(3).broadcast_to([128, 4, 8, 128]),
                                                           in1=self.d_BC[:, 0, :, :].unsqueeze(2).broadcast_to([128, 4, 8, 128]), op=ALU.mult),
                  reads=[self.d_xdtT, self.d_BC], writes=[self.d_t])
            kb.op("dve", lambda e: e.tensor_tensor(out=H[:], in0=H[:], in1=self.d_t[:], op=ALU.add), reads=[H, self.d_t], writes=[H])
            for g in range(2):
                for bb in range(4):
                    kb.dma("sp", self.ssm_s.ap()[l, sg * 4 + bb, g * 8:(g + 1) * 8].rearrange("hh p n -> p hh n"), H[g * 64:(g + 1) * 64, bb, :, :], reads=[H], is_output=True)
            kb.op("dve", lambda e: e.tensor_tensor(out=self.d_t[:], in0=H[:], in1=self.d_BC[:, 1, :, :].unsqueeze(2).broadcast_to([128, 4, 8, 128]), op=ALU.mult),
                  reads=[H, self.d_BC], writes=[self.d_t])
            kb.op("dve", lambda e, bs=bs: e.tensor_reduce(out=self.d_yT[:, bs, :], in_=self.d_t[:], axis=AX.X, op=ALU.add), reads=[self.d_t], writes=[self.d_yT])
        psy = [self.nps(), self.nps()]
        for half in range(2):
            kb.mm([lambda pe, hh=hh, half=half: pe.transpose(out=psy[half][0:NB, (hh % 4) * 128:(hh % 4 + 1) * 128], in_=self.d_yT[:, :, hh], identity=self.ident_f)
                   for hh in range(half * 4, half * 4 + 4)], reads=[self.d_yT, self.cf], writes=[psy[half]])
            yv = self.d_y[:].rearrange("p (g hh q) -> p hh g q", g=2, hh=8)
            kb.op("act", lambda e, half=half, yv=yv: e.copy(out=yv[:, half * 4:half * 4 + 4, :, :], in_=psy[half][0:NB, :].rearrange("p (hh g q) -> p hh g q", hh=4, g=2)),
                  reads=[psy[half]], writes=[self.d_y])
        kb.op("dve", lambda e: e.tensor_tensor(out=self.d_y2[:].rearrange("p (h q) -> p h q", h=16), in0=xs.rearrange("p (h q) -> p h q", h=16),
                                                in1=p16[:, 2, :].unsqueeze(2).broadcast_to([NB, 16, 64]), op=ALU.mult), reads=[self.sxconv, self.sm16], writes=[self.d_y2])
        kb.op("dve", lambda e: e.tensor_tensor(out=self.d_y[:], in0=self.d_y[:], in1=self.d_y2[:], op=ALU.add), reads=[self.d_y, self.d_y2], writes=[self.d_y])
        kb.op("dve", lambda e: e.tensor_tensor(out=self.d_y[:], in0=self.d_y[:], in1=self.d_z[:], op=ALU.mult), reads=[self.d_y, self.d_z], writes=[self.d_y])
        for g in range(2):
            gs = slice(g * 512, (g + 1) * 512)
            kb.op("act", lambda e, g=g, gs=gs: e.activation(out=self.d_y2[:, gs], in_=self.d_y[:, gs], func=AF.Square, accum_out=self.d_ssq[:, g:g + 1]),
                  reads=[self.d_y], writes=[self.d_y2, self.d_ssq])
        kb.op("act", lambda e: e.activation(out=self.d_ssq[:, 2:4], in_=self.d_ssq[:, 0:2], func=AF.Ln, bias=EPS, scale=1.0 / 512), reads=[self.d_ssq], writes=[self.d_ssq])
        kb.op("act", lambda e: e.activation(out=self.d_ssq[:, 2:4], in_=self.d_ssq[:, 2:4], func=AF.Exp, scale=-0.5), reads=[self.d_ssq], writes=[self.d_ssq])
        for g in range(2):
            gs = slice(g * 512, (g + 1) * 512)
            kb.op("dve", lambda e, g=g, gs=gs: e.scalar_tensor_tensor(out=self.d_ynb[:, gs], in0=self.d_y[:, gs], scalar=self.d_ssq[:, 2 + g:3 + g],
                                                                       in1=self.d_nw[:, gs], op0=ALU.mult, op1=ALU.mult),
                  reads=[self.d_y, self.d_ssq, self.d_nw], writes=[self.d_ynb])
        self.s_tokmajor_to_T(self.d_ynb, 1024, self.s_yT)

    def s_swa(self, l):
        self.kb.phase = 's_swa'
        kb = self.kb
        self.fence()
        kb.dma("sp", self.a_K[:], self.ck.ap()[l].rearrange("b k e -> k b e"), writes=[self.a_K])
        kb.dma("sp", self.a_V[:], self.cv.ap()[l].rearrange("b k e -> k b e"), writes=[self.a_V])
        kb.dma("sp", self.a_es[:], self.sinks.ap()[l:l + 1, :].partition_broadcast(NB), writes=[self.a_es])
        kb.op("act", lambda e: e.activation(out=self.a_es[:], in_=self.a_es[:], func=AF.Exp), reads=[self.a_es], writes=[self.a_es])
        kb.dma("sp", self.k_s.ap()[l, :, 0:127, :], self.ck.ap()[l, :, 1:128, :], is_output=True)
        kb.dma("sp", self.v_s.ap()[l, :, 0:127, :], self.cv.ap()[l, :, 1:128, :], is_output=True)
        Wt, Wv = self.wload(self.wsrc(self.w_in, l, D, OQC, 512), 8, 512)
        ps = self.sproj(Wt, Wv, 512)
        kb.op("act", lambda e, ps=ps: e.copy(out=self.a_q[:], in_=ps[0:NB, :]), reads=[ps], writes=[self.a_q])
        Wt, Wv = self.wload(self.wsrc(self.w_in, l, D, OKC, 256), 8, 256)
        ps = self.sproj(Wt, Wv, 256)
        kb.op("act", lambda e, ps=ps: e.copy(out=self.a_kn[:], in_=ps[0:NB, 0:128]), reads=[ps], writes=[self.a_kn])
        kb.op("act", lambda e: e.activation(out=self.a_vn[:, :, 64:65], in_=self.cf[0:NB, 0:2].unsqueeze(2), func=AF.Identity, scale=0.0, bias=1.0), reads=[self.cf], writes=[self.a_vn])
        kb.op("act", lambda e, ps=ps: e.copy(out=self.a_vn[:, :, 0:64], in_=ps[0:NB, 128:256].rearrange("p (g e) -> p g e", g=2)), reads=[ps], writes=[self.a_vn])
        kb.dma("sp", self.k_s.ap()[l, :, 127, :], self.a_kn[:], reads=[self.a_kn], is_output=True)
        kb.dma("sp", self.v_s.ap()[l, :, 127, :].rearrange("b (g e) -> b g e", g=2), self.a_vn[:, :, 0:64], reads=[self.a_vn], is_output=True)
        kb.op("act", lambda e: e.activation(out=self.a_Vb[:, :, :, 64:65], in_=self.cf[:, 0:2 * NB].rearrange("p (b g o) -> p b g o", b=NB, g=2), func=AF.Identity, scale=0.0, bias=1.0),
              reads=[self.cf], writes=[self.a_Vb])
        kb.op("act", lambda e: e.copy(out=self.a_Vb[:, :, :, 0:64], in_=self.a_V[:].rearrange("p b (g e) -> p b g e", g=2)), reads=[self.a_V], writes=[self.a_Vb])
        selq = self.s_sel[:, 0:NB * 128].rearrange("p (b k) -> p b k", b=NB)
        sv = self.a_s[:].rearrange("p b (half j) -> p b j half", half=2)
        for b in range(NB):
            psq = self.nps()
            kb.mm([lambda pe, b=b, psq=psq: pe.matmul(psq[:, :], selq[:, b, :], self.a_q[:], start=True, stop=True)], reads=[self.s_sel, self.a_q], writes=[psq])
            kb.op("dve", lambda e, b=b, psq=psq: e.tensor_tensor(out=self.a_pr[:], in0=psq[:, :].rearrange("p (j g e) -> p j g e", j=4, g=2),
                                                                 in1=self.a_K[:, b, :].rearrange("p (g e) -> p g e", g=2).unsqueeze(1).broadcast_to([128, 4, 2, 64]), op=ALU.mult),
                  reads=[psq, self.a_K], writes=[self.a_pr])
            kb.op("dve", lambda e, b=b: e.tensor_reduce(out=sv[:, b, :, :], in_=self.a_pr[:], axis=AX.X, op=ALU.add), reads=[self.a_pr], writes=[self.a_s])
        kb.op("dve", lambda e: e.scalar_tensor_tensor(out=self.a_s[:], in0=self.a_s[:], scalar=0.125, in1=self.btab[:, 0, :, 0].unsqueeze(1).broadcast_to([128, NB, 8]),
                                                      op0=ALU.mult, op1=ALU.add), reads=[self.a_s, self.btab], writes=[self.a_s])
        kb.op("act", lambda e: e.activation(out=self.a_p[:], in_=self.a_s[:], func=AF.Exp), reads=[self.a_s], writes=[self.a_p])
        kb.op("dve", lambda e: e.tensor_tensor(out=self.a_Pm[:], in0=self.a_p[:].unsqueeze(3).broadcast_to([128, NB, 8, NB]),
                                                in1=self.s_eye[:].rearrange("p (a b) -> p a b", a=NB).unsqueeze(2).broadcast_to([128, NB, 8, NB]), op=ALU.mult),
              reads=[self.a_p, self.s_eye], writes=[self.a_Pm])
        pso = [self.nps(), self.nps()]
        for g in range(2):
            fns = []
            for h4 in range(4):
                for b in range(NB):
                    fns.append(lambda pe, g=g, h4=h4, b=b: pe.matmul(pso[g][0:NB, h4 * 65:(h4 + 1) * 65], self.a_Pm[:, b, g * 4 + h4, :], self.a_Vb[:, b, g, 0:65],
                                                                    start=(b == 0), stop=(b == NB - 1)))
            kb.mm(fns, reads=[self.a_Pm, self.a_Vb], writes=[pso[g]])
            kb.op("act", lambda e, g=g: e.copy(out=self.a_ou[:, g * 4:(g + 1) * 4, :], in_=pso[g][0:NB, 0:260].rearrange("p (h e) -> p h e", h=4)), reads=[pso[g]], writes=[self.a_ou])
        qv = self.a_q[:].rearrange("p (j g e) -> p j g e", j=4, g=2)
        kb.op("dve", lambda e: e.tensor_tensor(out=self.a_sn[:], in0=qv, in1=self.a_kn[:].rearrange("p (g e) -> p g e", g=2).unsqueeze(1).broadcast_to([NB, 4, 2, 64]), op=ALU.mult),
              reads=[self.a_q, self.a_kn], writes=[self.a_sn])
        kb.op("dve", lambda e: e.tensor_reduce(out=self.a_s8[:].rearrange("p (half j) -> p j half", half=2), in_=self.a_sn[:], axis=AX.X, op=ALU.add), reads=[self.a_sn], writes=[self.a_s8])
        kb.op("dve", lambda e: e.scalar_tensor_tensor(out=self.a_s8[:], in0=self.a_s8[:], scalar=0.125, in1=self.s_b0[:], op0=ALU.mult, op1=ALU.add), reads=[self.a_s8, self.s_b0], writes=[self.a_s8])
        kb.op("act", lambda e: e.activation(out=self.a_pn[:], in_=self.a_s8[:], func=AF.Exp), reads=[self.a_s8], writes=[self.a_pn])
        kb.op("dve", lambda e: e.tensor_tensor(out=self.a_t[:].rearrange("p (g j) e -> p g j e", g=2), in0=self.a_pn[:].rearrange("p (g j) -> p g j", g=2).unsqueeze(3).broadcast_to([NB, 2, 4, 65]),
                                                in1=self.a_vn[:, :, 0:65].unsqueeze(2).broadcast_to([NB, 2, 4, 65]), op=ALU.mult), reads=[self.a_pn, self.a_vn], writes=[self.a_t])
        kb.op("dve", lambda e: e.tensor_tensor(out=self.a_ou[:], in0=self.a_ou[:], in1=self.a_t[:], op=ALU.add), reads=[self.a_ou, self.a_t], writes=[self.a_ou])
        kb.op("dve", lambda e: e.tensor_tensor(out=self.a_den[:], in0=self.a_ou[:, :, 64], in1=self.a_es[:], op=ALU.add), reads=[self.a_ou, self.a_es], writes=[self.a_den])
        kb.op("dve", lambda e: e.reciprocal(out=self.a_den[:], in_=self.a_den[:]), reads=[self.a_den], writes=[self.a_den])
        kb.op("dve", lambda e: e.tensor_tensor(out=self.a_ob[:], in0=self.a_ou[:, :, 0:64], in1=self.a_den[:].unsqueeze(2).broadcast_to([NB, 8, 64]), op=ALU.mult),
              reads=[self.a_ou, self.a_den], writes=[self.a_ob])
        self.s_tokmajor_to_T(self.a_ob, 512, self.s_ocT) if False else None
        obv = self.a_ob[:].rearrange("p h e -> p (h e)")
        self.s_transp(self.a_ob, [obv[:, j * 128:(j + 1) * 128] for j in range(4)], self.s_ocT, self.s_ocT[:, :, :])

    def s_ln(self, l, gi, pss_fn):
        kb = self.kb
        kb.dma("sp", self.m_g[:], self.ln_n.ap()[l, gi].partition_broadcast(NB), writes=[self.m_g])
        kb.dma("sp", self.m_bb[:], self.ln_n.ap()[l, gi + 1].partition_broadcast(NB), writes=[self.m_bb])
        for j in range(2):
            js = slice(j * 512, (j + 1) * 512)
            ps = pss_fn(j)
            kb.op("dve", lambda e, js=js, ps=ps: e.scalar_tensor_tensor(out=self.sx[:, js], in0=self.sx[:, js], scalar=ALPHA, in1=ps[0:NB, :], op0=ALU.mult, op1=ALU.add),
                  reads=[self.sx, ps], writes=[self.sx])
            kb.op("dve", lambda e, j=j, js=js: e.bn_stats(out=self.m_st[:, j, :], in_=self.sx[:, js]), reads=[self.sx], writes=[self.m_st])
        kb.op("dve", lambda e: e.bn_aggr(out=self.m_mv[:], in_=self.m_st[:].rearrange("p a b -> p (a b)")), reads=[self.m_st], writes=[self.m_mv])
        kb.op("act", lambda e: e.activation(out=self.m_rs[:, 0:1], in_=self.m_mv[:, 1:2], func=AF.Ln, bias=EPS, scale=1.0), reads=[self.m_mv], writes=[self.m_rs])
        kb.op("act", lambda e: e.activation(out=self.m_rs[:, 1:2], in_=self.m_rs[:, 0:1], func=AF.Exp, scale=-0.5), reads=[self.m_rs], writes=[self.m_rs])
        kb.op("dve", lambda e: e.tensor_scalar(out=self.m_u[:], in0=self.sx[:], scalar1=self.m_mv[:, 0:1], scalar2=self.m_rs[:, 1:2], op0=ALU.subtract, op1=ALU.mult),
              reads=[self.sx, self.m_mv, self.m_rs], writes=[self.m_u])
        kb.op("dve", lambda e: e.tensor_tensor(out=self.m_u[:], in0=self.m_u[:], in1=self.m_g[:], op=ALU.mult), reads=[self.m_u, self.m_g], writes=[self.m_u])
        kb.op("dve", lambda e: e.tensor_tensor(out=self.sx[:], in0=self.m_u[:], in1=self.m_bb[:], op=ALU.add), reads=[self.m_u, self.m_bb], writes=[self.sx])
        self.s_refresh_xT()

    def s_mlp(self, l):
        self.kb.phase = 's_mlp'
        kb = self.kb
        self.fence()
        brs = [(self.w_br_ret, 512, 4, self.s_orT), (self.w_br_ssd, 1024, 8, self.s_yT), (self.w_br_swa, 512, 4, self.s_ocT)]
        for b, (wbr, K, kc, src) in enumerate(brs):
            for j in range(2):
                js = slice(j * 512, (j + 1) * 512)
                Gt, Gv = self.wload(self.wsrc(self.w_in, l, D, OGATE + b * 1024 + j * 512, 512), 8, 512)
                Bt, Bv = self.wload(self.wsrc(wbr, l, K, j * 512, 512), kc, 512)
                psG = self.sproj(Gt, Gv, 512)
                psB = self.sproj(Bt, Bv, 512, src=src, kc=kc)
                kb.op("act", lambda e, psG=psG: e.activation(out=self.m_sg[:], in_=psG[0:NB, :], func=AF.Sigmoid), reads=[psG], writes=[self.m_sg])
                if b == 0:
                    kb.op("dve", lambda e, js=js, psB=psB: e.tensor_tensor(out=self.m_acc[:, js], in0=self.m_sg[:], in1=psB[0:NB, :], op=ALU.mult), reads=[self.m_sg, psB], writes=[self.m_acc])
                else:
                    kb.op("dve", lambda e, psB=psB: e.tensor_tensor(out=self.m_t[:], in0=self.m_sg[:], in1=psB[0:NB, :], op=ALU.mult), reads=[self.m_sg, psB], writes=[self.m_t])
                    kb.op("dve", lambda e, js=js: e.tensor_tensor(out=self.m_acc[:, js], in0=self.m_acc[:, js], in1=self.m_t[:], op=ALU.add), reads=[self.m_acc, self.m_t], writes=[self.m_acc])
        kb.op("act", lambda e: e.copy(out=self.m_b[:], in_=self.m_acc[:]), reads=[self.m_acc], writes=[self.m_b])
        self.s_tokmajor_to_T(self.m_b, 1024, self.s_mT)

        def pss1(j):
            Wt, Wv = self.wload(self.wsrc(self.w_out, l, D, j * 512, 512), 8, 512)
            return self.sproj(Wt, Wv, 512, src=self.s_mT)
        self.s_ln(l, 0, pss1)
        for jg in range(6):
            n = 512 if jg < 5 else 256
            Gt, Gv = self.wload(self.wsrc(self.w_gate, l, D, jg * 512, n), 8, n)
            Ut, Uv = self.wload(self.wsrc(self.w_up, l, D, jg * 512, n), 8, n)
            psG = self.sproj(Gt, Gv, n)
            psU = self.sproj(Ut, Uv, n)
            kb.op("act", lambda e, psG=psG, n=n: e.activation(out=self.m_sg[:, 0:n], in_=psG[0:NB, 0:n], func=AF.Silu), reads=[psG], writes=[self.m_sg])
            kb.op("dve", lambda e, psU=psU, n=n, jg=jg: e.tensor_tensor(out=self.m_hb[:, jg * 512:jg * 512 + n], in0=self.m_sg[:, 0:n], in1=psU[0:NB, 0:n], op=ALU.mult),
                  reads=[self.m_sg, psU], writes=[self.m_hb])
        self.s_tokmajor_to_T(self.m_hb, 2816, self.s_hT)
        W = {}

        def pss2(j):
            ps = self.nps()
            for t in range(4):
                m = j * 4 + t
                Wt, Wv = self.wload(self.wsrc_down(l, m), 22, 128)
                self.kb.mm([lambda pe, k=k, t=t, Wv=Wv: pe.matmul(ps[0:NB, t * 128:(t + 1) * 128], self.s_hT[:, k, :], Wv[:, k, :], start=(k == 0), stop=(k == 21))
                            for k in range(22)], reads=[self.s_hT, Wt], writes=[ps])
            return ps
        self.s_ln(l, 2, pss2)


def consts(L):
    f32 = np.float32
    pos = np.arange(L, dtype=f32)
    inv = (np.float32(10000.0) ** (-np.arange(64, dtype=f32) / np.float32(64))).astype(f32)
    ang = (pos[:, None] * inv[None, :]).astype(f32)
    cos, sin = np.cos(ang).astype(f32), np.sin(ang).astype(f32)
    s = f32(128 ** -0.5)
    c_rot = np.concatenate([cos, sin, cos * s, sin * s], axis=1).astype(f32)
    i = np.arange(128, dtype=np.float64)
    lg = np.log(1.0 - 2.0 ** (-5.0 - np.arange(4, dtype=np.float64)))
    rel = i[None, :] - i[:, None]
    dmatT = np.where(rel[:, None, :] >= 0, np.exp(lg[None, :, None] * np.maximum(rel[:, None, :], 0)), 0.0)
    kdec = np.exp(lg[None, :] * (127 - i)[:, None])
    qdec = np.exp(lg[None, :] * (i + 1.0)[:, None])
    k = np.arange(128)[:, None]; q = np.arange(128)[None, :]
    m0 = np.where(q > k, NEG, 0.0)
    m1 = np.where(q < k, NEG, 0.0)
    c_f32 = np.concatenate([np.eye(128), np.triu(np.ones((128, 128))), np.ones((128, 128)), np.zeros((128, 128)),
                            dmatT.reshape(128, 512), kdec, qdec, m0, m1], axis=1).astype(f32)
    def bucket(dist):
        df = np.maximum(dist, 1).astype(f32)
        large = 16 + (np.log(df / f32(16)).astype(f32) / f32(math.log(8.0)) * f32(16)).astype(np.int32)
        large = np.minimum(large, 31)
        return np.where(dist < 16, dist, large)
    oh = np.zeros((128, 2, 128, 32), f32)
    d0 = q - k + 128
    d1 = q - k
    for hf, dd in ((0, d0), (1, d1)):
        valid = (dd >= 0) & (dd <= 128)
        b = bucket(np.maximum(dd, 0))
        kk, qq = np.nonzero(valid)
        oh[kk, hf, qq, b[kk, qq]] = 1.0
    c_oh = oh.reshape(128, -1).astype(ml_dtypes.bfloat16)
    return c_rot, c_f32, c_oh


def qc_perm():
    idx = np.arange(8464)
    base = 4624
    new = []
    for t in range(4):
        new += list(range(base + t * 64, base + (t + 1) * 64)) + list(range(base + (4 + t) * 64, base + (5 + t) * 64))
    idx[base:base + 512] = np.array(new)
    return idx


def prep_weights(inp, depth):
    f = lambda a: np.ascontiguousarray(a, dtype=np.float32)
    dp = depth
    w = {}
    w["w_in"] = f(inp["w_in"][:dp][:, :, qc_perm()].reshape(dp * 1024, 8464))
    w["conv_w"] = f(np.transpose(inp["conv_w"][:dp].reshape(dp, 4, 12, 128), (0, 3, 2, 1)))
    w["conv_b"] = f(np.transpose(inp["conv_b"][:dp].reshape(dp, 12, 128), (0, 2, 1)))
    for n in ("dt_bias", "a_log", "d_skip", "ssd_norm_w", "sinks"):
        w[n] = f(inp[n][:dp])
    w["rel_bias"] = f(inp["rel_bias"].reshape(1, 256))
    w["w_br_ret"] = f(inp["w_br_ret"][:dp].reshape(dp * 512, 1024))
    w["w_br_ssd"] = f(inp["w_br_ssd"][:dp].reshape(dp * 1024, 1024))
    w["w_br_swa"] = f(inp["w_br_swa"][:dp].reshape(dp * 512, 1024))
    w["w_out"] = f(inp["w_out"][:dp].reshape(dp * 1024, 1024))
    lnp = np.stack([np.transpose(inp[n][:dp].reshape(dp, 8, 128), (0, 2, 1)) for n in ("ln1_g", "ln1_b", "ln2_g", "ln2_b")], axis=2)
    w["lnp"] = f(lnp)
    w["w_ffn_gate"] = f(inp["w_ffn_gate"][:dp].reshape(dp * 1024, 2816))
    w["w_ffn_up"] = f(inp["w_ffn_up"][:dp].reshape(dp * 1024, 2816))
    w["w_ffn_down"] = f(np.transpose(inp["w_ffn_down"][:dp].reshape(dp, 22, 128, 8, 128), (0, 3, 2, 1, 4)).reshape(dp * 8 * 128, 2816))
    return w


def core_inputs(inp, core, depth, TB, do_sample=True, x_prompt_row=None, sample_rows=None, w=None, nblk=None):
    f = lambda a: np.ascontiguousarray(a, dtype=np.float32)
    dp = depth
    m = dict(w if w is not None else prep_weights(inp, depth))
    L = (nblk or 1) * TB
    xr = core if x_prompt_row is None else x_prompt_row
    m["xp"] = f(inp["x_prompt"][xr, :L])
    c_rot, c_f32, c_oh = consts(max(L, 8192 + 1))
    m["c_rot"] = np.ascontiguousarray(c_rot[:L]); m["c_f32"] = c_f32; m["c_oh"] = c_oh
    if do_sample:
        sr = sample_rows if sample_rows is not None else slice(core * 16, core * 16 + 16)
        m["xs"] = f(inp["x_sample"][sr, 0])
        m["st_ret"] = f(inp["state_ret"][:dp, sr])
        m["st_ssm"] = f(inp["state_ssm"][:dp, sr])
        m["st_conv"] = f(inp["state_conv"][:dp, sr])
        m["ck"] = f(inp["cache_swa_k"][:dp, sr].reshape(dp, 16, 128, 128))
        m["cv"] = f(inp["cache_swa_v"][:dp, sr].reshape(dp, 16, 128, 128))
        m["c_rots"] = np.ascontiguousarray(np.broadcast_to(c_rot[8192:8193], (16, 256)))
        sel = np.zeros((16, 16, 128), np.float32)
        for b in range(16):
            sel[b, b, :] = 1.0
        gsel = np.zeros((16, 2, 128), np.float32)
        gsel[:, 0, 0:64] = 1.0; gsel[:, 1, 64:128] = 1.0
        m["c_sel"] = np.concatenate([sel.reshape(16, -1), gsel.reshape(16, -1)], axis=1)
        m["c_eye"] = np.ascontiguousarray(np.broadcast_to(np.eye(16, dtype=np.float32).reshape(1, 256), (128, 256)))
        m["conv_w_n"] = f(inp["conv_w"][:dp].reshape(dp, 1, 4 * 1536))
        m["conv_b_n"] = f(inp["conv_b"][:dp].reshape(dp, 1, 1536))
        m["ln_n"] = f(np.stack([inp[n][:dp] for n in ("ln1_g", "ln1_b", "ln2_g", "ln2_b")], axis=1).reshape(dp, 4, 1, 1024))
    return m


def kernel(**inputs):
    inp = {k: np.asarray(v) for k, v in inputs.items()}
    dp, nblk = 4, 4
    g = GenS(depth=dp, nblk=nblk, do_sample=True)
    w = prep_weights(inp, dp)
    in_maps = []
    for c in range(8):
        m = core_inputs(inp, c, dp, TB, do_sample=True, w=w, nblk=nblk)
        in_maps.append({k: m[k] for k in g.ins})
    res = run_bass_kernel_spmd(g.nc, in_maps, core_ids=list(range(8)))
    R = res.results
    st = lambda name, axis: np.stack([np.asarray(r[name]) for r in R], axis=axis)
    cat = lambda name, axis: np.concatenate([np.asarray(r[name]) for r in R], axis=axis)
    y_prompt = st("yp", 0).astype(np.float32)
    y_sample = cat("ys", 0).reshape(128, 1, 1024).astype(np.float32)
    ret_p = st("ret_p", 1)
    ssm_p = st("ssm_p", 1).reshape(dp, 8, 16, 64, 128)
    conv_p = st("conv_p", 1)
    k_p = st("k_p", 1).reshape(dp, 8, 128, 2, 64)
    v_p = st("v_p", 1).reshape(dp, 8, 128, 2, 64)
    ret_s = cat("ret_s", 1)
    ssm_s = cat("ssm_s", 1)
    conv_s = cat("conv_s", 1)
    k_s = cat("k_s", 1).reshape(dp, 128, 128, 2, 64)
    v_s = cat("v_s", 1).reshape(dp, 128, 128, 2, 64)
    outs = (y_prompt, y_sample, ret_p, ssm_p, conv_p, k_p, v_p, ret_s, ssm_s, conv_s, k_s, v_s)
    return tuple(np.ascontiguousarray(o, dtype=np.float32) for o in outs)
```

```python
import math
import numpy as np
import ml_dtypes
from concourse.bass_utils import run_bass_kernel_spmd
import concourse.bass as bass
import concourse.mybir as mybir

F32 = mybir.dt.float32
BF16 = mybir.dt.bfloat16
I32 = mybir.dt.int32
AF = mybir.ActivationFunctionType
ALU = mybir.AluOpType
AX = mybir.AxisListType


class Dep:
    __slots__ = ("w", "r")

    def __init__(self):
        self.w = None
        self.r = {}


class Tile:
    __slots__ = ("h", "deps", "name", "psum")

    def __init__(self, h, deps=None, name=None):
        self.psum = False
        self.h = h
        self.deps = deps if deps is not None else [Dep()]
        self.name = name

    def __getitem__(self, k):
        return self.h[k]

    def sub(self, i):
        return Tile(self.h, [self.deps[i]], self.name)


class KB:
    ENG = ("pe", "act", "dve", "pool", "sp")

    def __init__(self, n_dma_sems=30):
        nc = bass.Bass("TRN2", target_bir_lowering=False)
        self.nc = nc
        self.eng = {"pe": nc.tensor, "act": nc.scalar, "dve": nc.vector, "pool": nc.gpsimd, "sp": nc.sync}
        self.sems = {}
        self.tick = {}
        for e in self.ENG:
            self.sems[e] = nc.alloc_semaphore("s_" + e)
            self.tick[e] = 0
        self.known = {e: {} for e in self.ENG}
        self.dpool = {}
        for q in ("sp", "pool", "act"):
            lst = []
            for i in range(n_dma_sems):
                key = "d_%s_%d" % (q, i)
                self.sems[key] = nc.alloc_semaphore(key)
                self.tick[key] = 0
                lst.append(key)
            self.dpool[q] = [lst, 0]
        self.n_ins = 0
        self.n_wait = 0
        self.strict = True
        self.attach_waits = True
        self.phase = 'init'
        self.pe_log = []
        self.out_events = []

    def sb(self, name, shape, dt, ncell=1):
        h = self.nc.alloc_sbuf_tensor(name, list(shape), dt)
        return Tile(h, [Dep() for _ in range(ncell)], name)

    def ps(self, name, shape=(128, 512), dt=F32):
        h = self.nc.alloc_psum_tensor(name, list(shape), dt)
        t = Tile(h, None, name)
        t.psum = True
        return t

    def dram(self, name, shape, dt, kind):
        h = self.nc.dram_tensor(name, list(shape), dt, kind=kind)
        return h

    def _needs(self, reads, writes, e=None):
        needs = {}

        def add(ev):
            if ev is None:
                return
            k, v = ev
            if needs.get(k, 0) < v:
                needs[k] = v

        for t in reads:
            for d in t.deps:
                add(d.w)
                if t.psum:
                    for k, v in d.r.items():
                        if k != e:
                            add((k, v))
        for t in writes:
            for d in t.deps:
                if d.w is not None and (self.strict or d.w[0] != e):
                    add(d.w)
                for k, v in d.r.items():
                    if self.strict or k != e:
                        add((k, v))
        return needs

    def _emit_waits(self, e, needs, attach=False):
        kn = self.known[e]
        eo = self.eng[e]
        todo = []
        for k, v in needs.items():
            if e == "pe" and k == "pe":
                continue
            if kn.get(k, 0) < v:
                todo.append((k, v))
                kn[k] = v
        held = None
        if attach and todo and self.attach_waits:
            held = todo.pop()
        for k, v in todo:
            eo.wait_ge(self.sems[k], v)
            self.n_wait += 1
        return held

    def _attach(self, ins, held):
        if held is not None:
            ins._wait_ge(self.sems[held[0]], held[1])

    def _record(self, ev, reads, writes):
        k, v = ev
        for t in reads:
            for d in t.deps:
                d.r[k] = v
        for t in writes:
            for d in t.deps:
                d.w = ev
                d.r = {}

    def op(self, e, fn, reads=(), writes=()):
        needs = self._needs(reads, writes, e)
        held = self._emit_waits(e, needs, attach=True)
        ins = fn(self.eng[e])
        self._attach(ins, held)
        self.tick[e] += 1
        ins.then_inc(self.sems[e], 1)
        self._record((e, self.tick[e]), reads, writes)
        self.n_ins += 1
        return ins

    def mm(self, fns, reads=(), writes=()):
        needs = self._needs(reads, writes, "pe")
        held = self._emit_waits("pe", needs, attach=True)
        ins = None
        for fn in fns:
            ins = fn(self.eng["pe"])
            if held is not None:
                self._attach(ins, held)
                held = None
            self.n_ins += 1
            self.pe_log.append(self.phase)
        self.tick["pe"] += 1
        ins.then_inc(self.sems["pe"], 1)
        self._record(("pe", self.tick["pe"]), reads, writes)
        return ins

    def dma(self, q, out_ap, in_ap, reads=(), writes=(), is_output=False, **kw):
        lst, idx = self.dpool[q]
        key = lst[idx % len(lst)]
        self.dpool[q][1] = idx + 1
        needs = self._needs(reads, writes)
        if self.tick[key] > 0:
            if needs.get(key, 0) < self.tick[key]:
                needs[key] = self.tick[key]
        held = self._emit_waits(q, needs, attach=True)
        ins = self.eng[q].dma_start(out=out_ap, in_=in_ap, **kw)
        self._attach(ins, held)
        self.tick[key] += 16
        ins.then_inc(self.sems[key], 16)
        ev = (key, self.tick[key])
        self._record(ev, reads, writes)
        if is_output:
            self.out_events.append(ev)
        self.n_ins += 1
        return ins

    def finish(self):
        needs = {}
        for k, v in self.out_events:
            if needs.get(k, 0) < v:
                needs[k] = v
        for q in self.dpool:
            for key in self.dpool[q][0]:
                if self.tick[key] > 0:
                    needs[key] = max(needs.get(key, 0), self.tick[key])
        for e in ("pe", "act", "dve", "pool"):
            if self.tick[e] > 0:
                needs[e] = self.tick[e]
        self._emit_waits("sp", needs)


D = 1024
NIN = 8464
DFF = 2816
ALPHA = 8 ** 0.25
EPS = 1e-5
TB = 512
NEG = -30000.0
OQ, OK_, OV, OG, OZ, OXBC, ODT, OQC, OKC, OVC, OGATE = 0, 512, 1024, 1536, 2048, 3072, 4608, 4624, 5136, 5264, 5392


class Gen:
    def __init__(self, depth=4, nblk=4, do_sample=True, dbg=None):
        self.depth, self.nblk, self.do_sample = depth, nblk, do_sample
        self.kb = kb = KB()
        self.nc = nc = kb.nc
        self.L = nblk * TB
        self.dbg = dbg or {}
        self.outs = []
        self.ins = {}
        self._decl()
        self._alloc()
        self._setup_consts()
        if self.dbg.get("setup_only"):
            kb.finish()
            return
        for bi in self.dbg.get('blocks', range(nblk)):
            self.block(bi)
        if self.do_sample and hasattr(self, "sample_pass"):
            self.sample_pass()
        kb.finish()

    def din(self, name, shape, dt=F32):
        h = self.nc.dram_tensor(name, list(shape), dt, kind="ExternalInput")
        self.ins[name] = (tuple(shape), dt)
        return h

    def dout(self, name, shape, dt=F32):
        h = self.nc.dram_tensor(name, list(shape), dt, kind="ExternalOutput")
        self.outs.append(name)
        return h

    def _decl(self):
        dp, L = self.depth, self.L
        d = self.din
        self.xp = d("xp", [L, D])
        self.w_in = d("w_in", [dp * D, NIN])
        self.conv_w = d("conv_w", [dp, 128, 12, 4])
        self.conv_b = d("conv_b", [dp, 128, 12])
        self.dt_bias = d("dt_bias", [dp, 16])
        self.a_log = d("a_log", [dp, 16])
        self.d_skip = d("d_skip", [dp, 16])
        self.norm_w = d("ssd_norm_w", [dp, 1024])
        self.sinks = d("sinks", [dp, 8])
        self.rel_bias = d("rel_bias", [1, 256])
        self.w_br_ret = d("w_br_ret", [dp * 512, D])
        self.w_br_ssd = d("w_br_ssd", [dp * 1024, D])
        self.w_br_swa = d("w_br_swa", [dp * 512, D])
        self.w_out = d("w_out", [dp * D, D])
        self.lnp_d = d("lnp", [dp, 128, 4, 8])
        self.w_gate = d("w_ffn_gate", [dp * D, DFF])
        self.w_up = d("w_ffn_up", [dp * D, DFF])
        self.w_down = d("w_ffn_down", [dp * 8 * 128, DFF])
        self.c_rot = d("c_rot", [L, 256])
        self.c_f32 = d("c_f32", [128, 128 * 4 + 4 * 128 + 8 + 2 * 128])
        self.c_oh = d("c_oh", [128, 2 * 128 * 32], BF16)
        o = self.dout
        self.yp = o("yp", [L, D])
        self.ret_p = o("ret_p", [dp, 4, 128, 128])
        self.ssm_p = o("ssm_p", [dp, 1024, 128])
        self.conv_p = o("conv_p", [dp, 3, 1536])
        self.k_p = o("k_p", [dp, 128, 128])
        self.v_p = o("v_p", [dp, 128, 128])

    def _alloc(self):
        kb, dp = self.kb, self.depth
        sb = kb.sb
        self.xr_off = self.nc.bump_sbuf(16384 + 8192)[0]
        self.xr = Tile(self.nc.alloc_sbuf_tensor_at("xr", [128, 8, TB], F32, offset=self.xr_off), None, "xr")
        self.xb = Tile(self.nc.alloc_sbuf_tensor_at("xb", [128, 8, TB], BF16, offset=self.xr_off + 16384), None, "xb")
        self.NS = 4
        self.wr = [sb("wr%d" % i, [128, 4096], BF16) for i in range(self.NS)]
        self.wi = 0
        self.wcache = {}
        self.use_wcache = self.dbg.get('wcache', False)
        self.psf = [kb.ps("psf%d" % i, [128, 512], F32) for i in range(6)]
        self.psb = [kb.ps("psb%d" % i, [128, 1024], BF16) for i in range(2)]
        self.pfi = 0
        self.pbi = 0
        self.retS = [sb("retS%d" % l, [128, 4, 128], F32) for l in range(dp)]
        self.ssmS = [sb("ssmS%d" % l, [128, 1024], F32) for l in range(dp)]
        self.hist = [sb("hist%d" % l, [128, 12, 3], F32) for l in range(dp)]
        self.kprev = [sb("kprev%d" % l, [128, 128], BF16) for l in range(dp)]
        self.vprev = [sb("vprev%d" % l, [128, 2, 80], BF16) for l in range(dp)]
        self.cf = sb("cf", [128, 128 * 4 + 4 * 128 + 8 + 2 * 128], F32)
        self.id_b = sb("id_b", [128, 128], BF16)
        self.ones_b = sb("ones_b", [128, 128], BF16)
        self.negT = sb("negT", [128, 128], BF16)
        self.btab = sb("btab", [128, 2, 8, 128], F32)
        self.rot = sb("rot", [128, 4, 256], F32)
        self.cw = sb("cw", [128, 12, 4], F32); self.cb = sb("cb", [128, 12], F32)
        self.lnp = sb("lnp_s", [128, 4, 8], F32)
        self.sm16 = sb("sm16", [128, 3, 16], F32)
        self.esink = sb("esink", [128, 8], F32)
        self.orT = sb("orT", [128, 4, TB], BF16)
        self.yT = sb("yT", [128, 8, TB], BF16)
        self.ocT = sb("ocT", [128, 4, TB], BF16)
        self.mrg = sb("mrg", [128, 8, TB], BF16)
        self.xtok = [sb("xtok0", [128, 1024], F32)]
        self.xti = 0
        nc = self.nc
        ASZ = 64 * 1024
        self.abase = nc.bump_sbuf(ASZ)[0]
        self.asz = ASZ
        st = {"off": 0}

        def begin():
            st["off"] = 0

        def al(name, shape, dt):
            n = 1
            for x in shape[1:]:
                n *= x
            nb = n * (4 if dt == F32 else 2)
            nb = (nb + 31) // 32 * 32
            assert st["off"] + nb <= ASZ, (name, st["off"], nb)
            h = nc.alloc_sbuf_tensor_at(name, list(shape), dt, offset=self.abase + st["off"])
            st["off"] += nb
            return Tile(h, None, name)
        begin()
        self.s_oh = al("s_oh", [128, 8192], BF16); self.s_prod = al("s_prod", [128, 4096], F32); self.s_rb = al("s_rb", [128, 256], F32)
        begin()
        self.qrot = al("qrot", [128, 512], BF16); self.krot = al("krot", [128, 512], BF16)
        self.rtA = al("rtA", [128, 4, 2, 64], F32); self.rtB = al("rtB", [128, 4, 2, 64], F32)
        self.rtA2 = al("rtA2", [128, 4, 2, 64], F32); self.rtB2 = al("rtB2", [128, 4, 2, 64], F32); self.kraw = al("kraw", [128, 512], F32)
        self.v_r = al("v_r", [128, 4, 512], BF16)
        self.gsil = al("gsil", [128, 4, 512], F32)
        self.qT = al("qT", [128, 4, TB], BF16); self.kT = al("kT", [128, 4, TB], BF16)
        self.kdk = al("kdk", [128, 4, 4, 128], BF16)
        self.scm = al("scm", [128, 4, 128], BF16)
        self.otmp = al("otmp", [128, 4, 128], F32); self.o_r = al("o_r", [128, 4, 128], F32)
        self.st6 = al("st6", [128, 4, 6], F32); self.mv = al("mv", [128, 4, 2], F32)
        self.rs4 = al("rs4", [128, 4], F32); self.rs4b = al("rs4b", [128, 4], F32)
        self.og = al("og", [128, 4, 128], F32); self.ogb = al("ogb", [128, 512], BF16)
        self.retSb = al("retSb", [128, 4, 128], BF16)
        print("arena A", st["off"])
        begin()
        self.normw = al("normw", [128, 1024], F32)
        self.zs = al("zs", [128, 1024], F32)
        self._xbcT_off = self.abase + st["off"]
        self.xbcT = al("xbcT", [128, 4, 3 + TB], F32)
        self.cacc = al("cacc", [128, TB], F32)
        self.xsT = al("xsT", [128, 8, TB], BF16)
        self.BT = al("BT", [128, 2, TB], BF16); self.CT = al("CT", [128, 2, TB], BF16)
        self.dts = al("dts", [128, 4, 8, 16], F32)
        self.cshl = al("cshl", [128, 4, 2, 16], BF16)
        self.xs_tok = al("xs_tok", [128, 1024], BF16); self.xw = al("xw", [128, 1024], BF16); self.xD = al("xD", [128, 1024], BF16)
        self.B_tok = al("B_tok", [128, 256], BF16)
        self.cbT = al("cbT", [128, 2, 128], F32)
        self.Dhl = al("Dhl", [128, 2, 4, 128], BF16)
        self.Ep = al("Ep", [128, 4, 128], F32)
        self.wmT = al("wmT", [128, 16, 128], BF16)
        self.ytmp = al("ytmp", [128, 1024], F32)
        self.ssq = al("ssq", [128, 4], F32)
        self.ynb = al("ynb", [128, 1024], BF16)
        self.stmp = al("stmp", [128, 1024], F32)
        self.ssmSb = al("ssmSb", [128, 1024], BF16)
        print("arena B", st["off"])
        self.Dhl2 = [self.Dhl, Tile(nc.alloc_sbuf_tensor_at("Dhl_b", [128, 2, 4, 128], BF16, offset=self._xbcT_off), [Dep()], "Dhl_b")]
        self.Ep2 = [self.Ep, Tile(nc.alloc_sbuf_tensor_at("Ep_b", [128, 4, 128], F32, offset=self._xbcT_off + 2048), [Dep()], "Ep_b")]
        begin()
        self.qcT = al("qcT", [128, 4, TB], BF16)
        self.kTe = al("kTe", [128, 128 + TB], BF16)
        self.vaug = al("vaug", [128, 5, 2, 80], BF16)
        self.lg = al("lg", [128, 4, 128], F32)
        self.pT = al("pT", [128, 2, 2, 4, 128], BF16)
        self.den = al("den", [128, 8], F32); self.rden = al("rden", [128, 8], F32)
        self.oc_tok = al("oc_tok", [128, 8, 64], BF16)
        self.kvo = al("kvo", [128, 2, 128], F32)
        print("arena C", st["off"])
        begin()
        self.hT = al("hT", [128, 22, TB], BF16)
        self.sg = al("sg", [128, TB], F32); self.gt = al("gt", [128, TB], F32)
        self.ysq = al("ysq", [128, 8, TB], BF16)
        self.lnm = al("lnm", [128, TB], F32); self.lnr = al("lnr", [128, TB], F32); self.lnt = al("lnt", [128, TB], F32)
        self.lnu = al("lnu", [128, TB], F32); self.lnu2 = al("lnu2", [128, TB], F32)
        print("arena DEF", st["off"])
        self.macc = Tile(nc.alloc_sbuf_tensor_at("macc", [128, 8, TB], F32, offset=self.abase), None, "macc")

    def fence(self):
        kb = self.kb
        needs = {e: kb.tick[e] for e in ("pe", "act", "dve", "pool") if kb.tick[e] > 0}
        for key in kb.dpool["sp"][0]:
            if kb.tick[key] > 0:
                needs[key] = kb.tick[key]
        for e in ("act", "dve", "sp"):
            kb._emit_waits(e, dict(needs))

    def nps(self):
        t = self.psf[self.pfi % len(self.psf)]
        self.pfi += 1
        return t

    def npb(self):
        t = self.psb[self.pbi % len(self.psb)]
        self.pbi += 1
        return t

    def wload(self, src, kc, n):
        src3, key = src
        t = self.wr[self.wi % self.NS]
        self.wi += 1
        flat = t[:, 0:kc * n]
        v = flat.rearrange("p (k c) -> p k c", k=kc)
        ent = self.wcache.get(key) if self.use_wcache else None
        if ent is None:
            self.kb.dma("pool", v, src3, writes=[t])
            if self.use_wcache:
                h = self.nc.dram_tensor("wc%d" % len(self.wcache), [128, kc * n], BF16, kind="Internal")
                tl = Tile(h, None, "wc")
                self.wcache[key] = (h, tl)
                self.kb.dma("pool", h.ap()[:, :], flat, reads=[t], writes=[tl])
        else:
            h, tl = ent
            self.kb.dma(self.dbg.get("wq", "sp"), flat, h.ap()[:, :], reads=[tl], writes=[t])
        return t, v

    def wsrc_down(self, l, m):
        r0 = (l * 8 + m) * 128
        return (self.w_down.ap()[r0:r0 + 128, :].rearrange("p (k c) -> p k c", k=22), ("w_down", l, m))

    def wsrc(self, w, l, K, c0, n):
        return (w.ap()[l * K:(l + 1) * K, c0:c0 + n].rearrange("(k p) c -> p k c", p=128), (w.name, l, c0, n))

    def _setup_consts(self):
        kb = self.kb
        cf = self.cf
        kb.dma("sp", cf[:], self.c_f32.ap()[:, :], writes=[cf])
        self.ident_f = cf[:, 0:128]
        self.tri_f = cf[:, 128:256]
        self.ones_f = cf[:, 256:384]
        o = 512
        self.dmatT = cf[:, o:o + 512].rearrange("p (h i) -> p h i", h=4)
        self.kdec = cf[:, o + 512:o + 516]
        self.qdec = cf[:, o + 516:o + 520]
        self.mask01 = cf[:, o + 520:o + 520 + 256].rearrange("p (f q) -> p f q", f=2)
        kb.op("act", lambda e: e.copy(out=self.id_b[:], in_=self.ident_f), reads=[cf], writes=[self.id_b])
        kb.op("act", lambda e: e.copy(out=self.ones_b[:], in_=self.ones_f), reads=[cf], writes=[self.ones_b])
        kb.op("act", lambda e: e.copy(out=self.negT[:], in_=self.mask01[:, 1, :]), reads=[cf], writes=[self.negT])
        for l in range(self.depth):
            for t in (self.retS[l], self.ssmS[l], self.hist[l]):
                kb.op("pool", lambda e, t=t: e.memset(t[:], 0.0), writes=[t])
        oh = self.s_oh
        ohv = oh[:, :]
        kb.dma("sp", ohv, self.c_oh.ap()[:, :], writes=[oh])
        rb = self.s_rb
        kb.dma("sp", rb[:, 0:256], self.rel_bias.ap().partition_broadcast(128), writes=[rb])
        ohq = ohv.rearrange("p (f q b) -> p f q b", f=2, q=128)
        prod = self.s_prod
        pv = prod[:, :].rearrange("p (q b) -> p q b", b=32)
        rbv = rb[:, 0:256].rearrange("p (b h) -> p b h", h=8)
        for hf in range(2):
            for h in range(8):
                kb.op("dve", lambda e, hf=hf, h=h: e.tensor_tensor(
                    out=pv, in0=ohq[:, hf, :, :], in1=rbv[:, :, h].unsqueeze(1).broadcast_to([128, 128, 32]), op=ALU.mult),
                    reads=[oh, rb], writes=[prod])
                kb.op("dve", lambda e, hf=hf, h=h: e.tensor_reduce(out=self.btab[:, hf, h, :], in_=pv, axis=AX.X, op=ALU.add),
                      reads=[prod], writes=[self.btab])
            kb.op("dve", lambda e, hf=hf: e.tensor_tensor(
                out=self.btab[:, hf, :, :], in0=self.btab[:, hf, :, :],
                in1=self.mask01[:, hf, :].unsqueeze(1).broadcast_to([128, 8, 128]), op=ALU.add),
                reads=[cf, self.btab], writes=[self.btab])

    def block(self, bi):
        if not self.dbg.get('noload'): self.load_x(bi)
        for l in range(self.depth):
            self.layer(bi, l)
        if not self.dbg.get('nostore'): self.store_y(bi)

    def load_x(self, bi):
        self.kb.phase = 'load_x'
        kb = self.kb
        if not (self.dbg.get('norot2') and getattr(self, '_lx', 0) >= 1): kb.dma("sp", self.rot[:], self.c_rot.ap()[bi * TB:(bi + 1) * TB, :].rearrange("(c p) f -> p c f", p=128), writes=[self.rot])
        self._lx = getattr(self, '_lx', 0) + 1
        for c in range(4 if self._lx == 1 else self.dbg.get('nchunk', 4)):
            xt = self.xtok[0]; self.xti += 1
            r0 = bi * TB + c * 128
            kb.dma(self.dbg.get("xq", "sp"), xt[:], self.xp.ap()[r0:r0 + 128, :], writes=[xt])
            for half in range(2):
                ps = self.nps()
                kb.mm([lambda pe, j=j, half=half, xt=xt, ps=ps: pe.transpose(
                    out=ps[:, j * 128:(j + 1) * 128], in_=xt[:, (half * 4 + j) * 128:(half * 4 + j + 1) * 128], identity=self.ident_f)
                    for j in range(4)], reads=[xt, self.cf], writes=[ps])
                pv = ps[:, :].rearrange("p (j t) -> p j t", j=4)
                if self._lx > 1 and self.dbg.get('nocopy'):
                    continue
                kb.op("act", lambda e, half=half, c=c, pv=pv: e.copy(out=self.xr[:, half * 4:half * 4 + 4, c * 128:(c + 1) * 128], in_=pv),
                      reads=[ps], writes=[self.xr])
                kb.op("dve", lambda e, half=half, c=c: e.tensor_copy(out=self.xb[:, half * 4:half * 4 + 4, c * 128:(c + 1) * 128],
                                                                      in_=self.xr[:, half * 4:half * 4 + 4, c * 128:(c + 1) * 128]),
                      reads=[self.xr], writes=[self.xb])

    def store_y(self, bi):
        self.kb.phase = 'store_y'
        kb = self.kb
        for c in range(4):
            xt = self.xtok[0]; self.xti += 1
            for half in range(2):
                ps = self.nps()
                kb.mm([lambda pe, j=j, half=half, c=c, ps=ps: pe.transpose(
                    out=ps[:, j * 128:(j + 1) * 128], in_=self.xr[:, half * 4 + j, c * 128:(c + 1) * 128], identity=self.ident_f)
                    for j in range(4)], reads=[self.xr, self.cf], writes=[ps])
                kb.op("act" if half else "dve",
                      (lambda e, half=half, xt=xt, ps=ps: e.copy(out=xt[:, half * 512:(half + 1) * 512], in_=ps[:, :])) if half else
                      (lambda e, half=half, xt=xt, ps=ps: e.tensor_copy(out=xt[:, half * 512:(half + 1) * 512], in_=ps[:, :])),
                      reads=[ps], writes=[xt])
            r0 = bi * TB + c * 128
            kb.dma("sp", self.yp.ap()[r0:r0 + 128, :], xt[:], reads=[xt], is_output=True)

    def layer(self, bi, l):
        ph = self.dbg.get("phases", "PABCDEF")
        if "P" in ph: self.load_params(bi, l)
        if "A" in ph: self.phaseA(bi, l)
        if "B" in ph: self.phaseB(bi, l)
        if "C" in ph: self.phaseC(bi, l)
        if "D" in ph: self.phaseD(bi, l)
        if "E" in ph: self.phaseE(bi, l)
        if "F" in ph: self.phaseF(bi, l)

    def load_params(self, bi, l):
        self.kb.phase = 'load_params'
        kb = self.kb
        kb.dma("sp", self.cw[:], self.conv_w.ap()[l], writes=[self.cw])
        kb.dma("sp", self.cb[:], self.conv_b.ap()[l], writes=[self.cb])
        kb.dma("sp", self.lnp[:], self.lnp_d.ap()[l], writes=[self.lnp])
        for i, w in enumerate((self.dt_bias, self.a_log, self.d_skip)):
            kb.dma("sp", self.sm16[:, i, :], w.ap()[l:l + 1, :].partition_broadcast(128), writes=[self.sm16])
        kb.dma("sp", self.esink[:], self.sinks.ap()[l:l + 1, :].partition_broadcast(128), writes=[self.esink])
        kb.op("act", lambda e: e.activation(out=self.esink[:], in_=self.esink[:], func=AF.Exp), reads=[self.esink], writes=[self.esink])
        kb.op("act", lambda e: e.activation(out=self.sm16[:, 1, :], in_=self.sm16[:, 1, :], func=AF.Exp), reads=[self.sm16], writes=[self.sm16])
        kb.op("dve", lambda e: e.tensor_scalar(out=self.sm16[:, 1, :], in0=self.sm16[:, 1, :], scalar1=-1.0, scalar2=None, op0=ALU.mult),
              reads=[self.sm16], writes=[self.sm16])

    def proj_tok(self, Wt, Wv, c, ncols, col0=0):
        ps = self.nps()
        self.kb.mm([lambda pe, k=k, ps=ps: pe.matmul(ps[:, 0:ncols], self.xb[:, k, c * 128:(c + 1) * 128], Wv[:, k, col0:col0 + ncols],
                                                     start=(k == 0), stop=(k == 7)) for k in range(8)],
                   reads=[self.xb, Wt], writes=[ps])
        return ps

    def proj_feat(self, Wt, Wv, t0, src=None, srct=None, kc=8):
        ps = self.nps()
        src = self.xb if src is None else src
        self.kb.mm([lambda pe, k=k, ps=ps: pe.matmul(ps[:, :], Wv[:, k, t0:t0 + 128], src[:, k, :],
                                                     start=(k == 0), stop=(k == kc - 1)) for k in range(kc)],
                   reads=[src, Wt], writes=[ps])
        return ps

    def transp_b(self, src_tile, src_aps, dst_tile, dst_ap):
        n = len(src_aps)
        pb = self.npb()
        self.kb.mm([lambda pe, j=j, pb=pb: pe.transpose(out=pb[:, j * 128:(j + 1) * 128], in_=src_aps[j], identity=self.id_b[:])
                    for j in range(n)], reads=[src_tile, self.id_b], writes=[pb])
        self.kb.op("act", lambda e, pb=pb: e.copy(out=dst_ap, in_=pb[:, 0:n * 128].rearrange("p (j t) -> p j t", j=n)),
                   reads=[pb], writes=[dst_tile])
        return pb

    def phaseA(self, bi, l):
        self.kb.phase = 'phaseA'
        kb = self.kb
        self.fence()
        g = [2.0 ** (-5 - h) for h in range(4)]
        cdec = [float(np.exp(np.float64(128) * np.log1p(-gg))) for gg in g]
        for gi, off in enumerate((OQ, OK_, OV, OG)):
            Wt, Wv = self.wload(self.wsrc(self.w_in, l, D, off, 512), 8, 512)
            for c in range(4):
                ps = self.proj_tok(Wt, Wv, c, 512)
                if gi < 2:
                    dst = self.qrot if gi == 0 else self.krot
                    eng = "dve" if gi == 0 else self.dbg.get("rot_k_eng", "dve")
                    if eng == "pool":
                        kb.op("act", lambda e, ps=ps: e.copy(out=self.kraw[:], in_=ps[:, :]), reads=[ps], writes=[self.kraw])
                        srct, psv = self.kraw, self.kraw[:].rearrange("p (h t e) -> p h t e", h=4, t=2)
                    else:
                        srct, psv = ps, ps[:, :].rearrange("p (h t e) -> p h t e", h=4, t=2)
                    tA = self.rtA if gi == 0 else self.rtA2
                    tB = self.rtB if gi == 0 else self.rtB2
                    cosb = self.rot[:, c, gi * 128:gi * 128 + 64].unsqueeze(1).unsqueeze(1).broadcast_to([128, 4, 2, 64])
                    sinb = self.rot[:, c, gi * 128 + 64:gi * 128 + 128].unsqueeze(1).broadcast_to([128, 4, 64])
                    kb.op(eng, lambda e, psv=psv, cosb=cosb, tA=tA: e.tensor_tensor(out=tA[:], in0=psv, in1=cosb, op=ALU.mult),
                          reads=[srct, self.rot], writes=[tA])
                    kb.op(eng, lambda e, psv=psv, sinb=sinb, tB=tB: e.tensor_tensor(out=tB[:, :, 0, :], in0=psv[:, :, 1, :], in1=sinb, op=ALU.mult),
                          reads=[srct, self.rot], writes=[tB])
                    kb.op(eng, lambda e, psv=psv, sinb=sinb, tB=tB: e.tensor_tensor(out=tB[:, :, 1, :], in0=psv[:, :, 0, :], in1=sinb, op=ALU.mult),
                          reads=[srct, self.rot], writes=[tB])
                    dv = dst[:].rearrange("p (h t e) -> p h t e", h=4, t=2)
                    kb.op(eng, lambda e, dv=dv, tA=tA, tB=tB: e.tensor_tensor(out=dv[:, :, 0, :], in0=tA[:, :, 0, :], in1=tB[:, :, 0, :], op=ALU.subtract),
                          reads=[tA, tB], writes=[dst])
                    kb.op(eng, lambda e, dv=dv, tA=tA, tB=tB: e.tensor_tensor(out=dv[:, :, 1, :], in0=tA[:, :, 1, :], in1=tB[:, :, 1, :], op=ALU.add),
                          reads=[tA, tB], writes=[dst])
                    dT = self.qT if gi == 0 else self.kT
                    self.transp_b(dst, [dst[:, h * 128:(h + 1) * 128] for h in range(4)], dT, dT[:, :, c * 128:(c + 1) * 128])
                    if gi == 1:
                        kb.op("dve", lambda e, c=c: e.tensor_tensor(
                            out=self.kdk[:, c, :, :], in0=self.krot[:].rearrange("p (h e) -> p h e", h=4),
                            in1=self.kdec.unsqueeze(2).broadcast_to([128, 4, 128]), op=ALU.mult),
                            reads=[self.krot, self.cf], writes=[self.kdk])
                elif gi == 2:
                    kb.op("act", lambda e, c=c, ps=ps: e.copy(out=self.v_r[:, c, :], in_=ps[:, :]), reads=[ps], writes=[self.v_r])
                else:
                    kb.op("act", lambda e, c=c, ps=ps: e.activation(out=self.gsil[:, c, :], in_=ps[:, :], func=AF.Silu), reads=[ps], writes=[self.gsil])
        S, Sb = self.retS[l], self.retSb
        kb.op("act", lambda e: e.copy(out=Sb[:], in_=S[:]), reads=[S], writes=[Sb])
        for c in range(4):
            cs = slice(c * 128, (c + 1) * 128)
            ps1 = self.nps()
            kb.mm([lambda pe, h=h, ps1=ps1: pe.matmul(ps1[:, h * 128:(h + 1) * 128], self.kT[:, h, cs], self.qT[:, h, cs], start=True, stop=True)
                   for h in range(4)], reads=[self.kT, self.qT], writes=[ps1])
            kb.op("dve", lambda e, ps1=ps1: e.tensor_tensor(out=self.scm[:], in0=ps1[:, :].rearrange("p (h i) -> p h i", h=4), in1=self.dmatT, op=ALU.mult),
                  reads=[ps1, self.cf], writes=[self.scm])
            psA = self.nps(); psB = self.nps(); psC = self.nps()
            kb.mm([lambda pe, h=h, psA=psA: pe.matmul(psA[:, h * 128:(h + 1) * 128], self.scm[:, h, :], self.v_r[:, c, h * 128:(h + 1) * 128], start=True, stop=True)
                   for h in range(4)], reads=[self.scm, self.v_r], writes=[psA])
            kb.mm([lambda pe, h=h, psB=psB: pe.matmul(psB[:, h * 128:(h + 1) * 128], self.qT[:, h, cs], Sb[:, h, :], start=True, stop=True)
                   for h in range(4)], reads=[self.qT, Sb], writes=[psB])
            kb.mm([lambda pe, h=h, psC=psC: pe.matmul(psC[:, h * 128:(h + 1) * 128], self.kdk[:, c, h, :], self.v_r[:, c, h * 128:(h + 1) * 128], start=True, stop=True)
                   for h in range(4)], reads=[self.kdk, self.v_r], writes=[psC])
            for h in range(4):
                kb.op("dve", lambda e, h=h, psC=psC: e.scalar_tensor_tensor(out=S[:, h, :], in0=S[:, h, :], scalar=cdec[h], in1=psC[:, h * 128:(h + 1) * 128],
                                                                              op0=ALU.mult, op1=ALU.add), reads=[S, psC], writes=[S])
            kb.op("act", lambda e: e.copy(out=Sb[:], in_=S[:]), reads=[S], writes=[Sb])
            kb.op("dve", lambda e, psB=psB: e.tensor_tensor(out=self.otmp[:], in0=psB[:, :].rearrange("p (h v) -> p h v", h=4),
                                                             in1=self.qdec.unsqueeze(2).broadcast_to([128, 4, 128]), op=ALU.mult),
                  reads=[psB, self.cf], writes=[self.otmp])
            kb.op("dve", lambda e, psA=psA: e.tensor_tensor(out=self.o_r[:], in0=psA[:, :].rearrange("p (h v) -> p h v", h=4), in1=self.otmp[:], op=ALU.add),
                  reads=[psA, self.otmp], writes=[self.o_r])
            for h in range(4):
                kb.op("dve", lambda e, h=h: e.bn_stats(out=self.st6[:, h, :], in_=self.o_r[:, h, :]), reads=[self.o_r], writes=[self.st6])
            for h in range(4):
                kb.op("dve", lambda e, h=h: e.bn_aggr(out=self.mv[:, h, :], in_=self.st6[:, h, :]), reads=[self.st6], writes=[self.mv])
            kb.op("act", lambda e: e.activation(out=self.rs4[:], in_=self.mv[:, :, 1], func=AF.Ln, bias=EPS, scale=1.0), reads=[self.mv], writes=[self.rs4])
            kb.op("act", lambda e: e.activation(out=self.rs4b[:], in_=self.rs4[:], func=AF.Exp, scale=-0.5), reads=[self.rs4], writes=[self.rs4b])
            for h in range(4):
                kb.op("dve", lambda e, h=h: e.scalar_tensor_tensor(out=self.og[:, h, :], in0=self.o_r[:, h, :], scalar=self.mv[:, h, 0:1],
                                                                     in1=self.gsil[:, c, h * 128:(h + 1) * 128], op0=ALU.subtract, op1=ALU.mult),
                      reads=[self.o_r, self.mv, self.gsil], writes=[self.og])
            for h in range(4):
                kb.op("act", lambda e, h=h: e.activation(out=self.ogb[:, h * 128:(h + 1) * 128], in_=self.og[:, h, :], func=AF.Identity, scale=self.rs4b[:, h:h + 1]),
                      reads=[self.og, self.rs4b], writes=[self.ogb])
            self.transp_b(self.ogb, [self.ogb[:, h * 128:(h + 1) * 128] for h in range(4)], self.orT, self.orT[:, :, cs])
        if bi == self.nblk - 1:
            kb.dma("sp", self.ret_p.ap()[l].rearrange("h d v -> d h v"), S[:], reads=[S], is_output=True)

    def phaseB(self, bi, l):
        self.kb.phase = 'phaseB'
        kb = self.kb
        self.fence()
        kb.dma("sp", self.normw[:], self.norm_w.ap()[l:l + 1, :].partition_broadcast(128), writes=[self.normw])
        Wt, Wv = self.wload(self.wsrc(self.w_in, l, D, ODT, 16), 8, 16)
        dts = self.dts
        for c in range(4):
            ps = self.proj_tok(Wt, Wv, c, 16)
            DT, CS, ECS, DEDT, CDEC, NB, T1, T2 = [dts[:, c, i, :] for i in range(8)]
            kb.op("dve", lambda e, ps=ps, T1=T1: e.tensor_tensor(out=T1, in0=ps[:, 0:16], in1=self.sm16[:, 0, :], op=ALU.add), reads=[ps, self.sm16], writes=[dts])
            kb.op("act", lambda e, T1=T1, T2=T2: e.activation(out=T2, in_=T1, func=AF.Exp), reads=[dts], writes=[dts])
            kb.op("act", lambda e, DT=DT, T2=T2: e.activation(out=DT, in_=T2, func=AF.Ln, bias=1.0, scale=1.0), reads=[dts], writes=[dts])
            kb.op("dve", lambda e, DT=DT, T1=T1: e.tensor_tensor(out=T1, in0=DT, in1=self.sm16[:, 1, :], op=ALU.mult), reads=[dts, self.sm16], writes=[dts])
            ps2 = self.nps()
            kb.mm([lambda pe, ps2=ps2, T1=T1: pe.matmul(ps2[:, 0:16], self.tri_f, T1, start=True, stop=True),
                   lambda pe, ps2=ps2, T1=T1: pe.matmul(ps2[:, 16:32], self.ones_f, T1, start=True, stop=True)],
                  reads=[self.cf, dts], writes=[ps2])
            kb.op("act", lambda e, ps2=ps2, CS=CS: e.copy(out=CS, in_=ps2[:, 0:16]), reads=[ps2], writes=[dts])
            kb.op("act", lambda e, ps2=ps2, ECS=ECS: e.activation(out=ECS, in_=ps2[:, 0:16], func=AF.Exp), reads=[ps2], writes=[dts])
            kb.op("act", lambda e, ps2=ps2, CDEC=CDEC: e.activation(out=CDEC, in_=ps2[:, 16:32], func=AF.Exp), reads=[ps2], writes=[dts])
            kb.op("dve", lambda e, ps2=ps2, T2=T2, CS=CS: e.tensor_tensor(out=T2, in0=ps2[:, 16:32], in1=CS, op=ALU.subtract), reads=[ps2, dts], writes=[dts])
            kb.op("act", lambda e, T2=T2: e.activation(out=T2, in_=T2, func=AF.Exp), reads=[dts], writes=[dts])
            kb.op("dve", lambda e, T2=T2, DT=DT, DEDT=DEDT: e.tensor_tensor(out=DEDT, in0=T2, in1=DT, op=ALU.mult), reads=[dts], writes=[dts])
            kb.op("act", lambda e, DT=DT, T1=T1: e.activation(out=T1, in_=DT, func=AF.Ln), reads=[dts], writes=[dts])
            kb.op("dve", lambda e, T1=T1, CS=CS, NB=NB: e.tensor_tensor(out=NB, in0=T1, in1=CS, op=ALU.subtract), reads=[dts], writes=[dts])
            kb.op("act", lambda e, CS=CS, c=c: e.copy(out=self.cshl[:, c, 0, :], in_=CS), reads=[dts], writes=[self.cshl])
            kb.op("dve", lambda e, CS=CS, c=c: e.tensor_tensor(out=self.cshl[:, c, 1, :], in0=CS, in1=self.cshl[:, c, 0, :], op=ALU.subtract),
                  reads=[dts, self.cshl], writes=[self.cshl])
        if self.dbg.get('bstop', 99) <= 1: return
        for g3 in range(3):
            Wt, Wv = self.wload(self.wsrc(self.w_in, l, D, OXBC + g3 * 512, 512), 8, 512)
            kb.op("dve", lambda e, g3=g3: e.tensor_copy(out=self.xbcT[:, :, 0:3], in_=self.hist[l][:, g3 * 4:(g3 + 1) * 4, :]),
                  reads=[self.hist[l]], writes=[self.xbcT])
            for t in range(4):
                tt = g3 * 4 + t
                ps = self.proj_feat(Wt, Wv, t * 128)
                kb.op("act", lambda e, t=t, ps=ps: e.copy(out=self.xbcT[:, t, 3:3 + TB], in_=ps[:, :]), reads=[ps], writes=[self.xbcT])
                kb.op("dve", lambda e, t=t, tt=tt: e.tensor_scalar(out=self.cacc[:], in0=self.xbcT[:, t, 0:TB], scalar1=self.cw[:, tt, 0:1], scalar2=None, op0=ALU.mult),
                      reads=[self.xbcT, self.cw], writes=[self.cacc])
                for tau in range(1, 4):
                    kb.op("dve", lambda e, t=t, tt=tt, tau=tau: e.scalar_tensor_tensor(
                        out=self.cacc[:], in0=self.xbcT[:, t, tau:tau + TB], scalar=self.cw[:, tt, tau:tau + 1], in1=self.cacc[:], op0=ALU.mult, op1=ALU.add),
                        reads=[self.xbcT, self.cw, self.cacc], writes=[self.cacc])
                if tt < 8:
                    dtile, dap = self.xsT, self.xsT[:, tt, :]
                elif tt < 10:
                    dtile, dap = self.BT, self.BT[:, tt - 8, :]
                else:
                    dtile, dap = self.CT, self.CT[:, tt - 10, :]
                kb.op("act", lambda e, tt=tt, dap=dap: e.activation(out=dap, in_=self.cacc[:], func=AF.Silu, bias=self.cb[:, tt:tt + 1], scale=1.0),
                      reads=[self.cacc, self.cb], writes=[dtile])
            kb.op("dve", lambda e, g3=g3: e.tensor_copy(out=self.hist[l][:, g3 * 4:(g3 + 1) * 4, :], in_=self.xbcT[:, :, TB:TB + 3]),
                  reads=[self.xbcT], writes=[self.hist[l]])
        if self.dbg.get('bstop', 99) <= 2: return
        S, Sb = self.ssmS[l], self.ssmSb
        kb.op("act", lambda e: e.copy(out=Sb[:], in_=S[:]), reads=[S], writes=[Sb])
        Zw = [self.wload(self.wsrc(self.w_in, l, D, OZ + g2 * 512, 512), 8, 512) for g2 in range(2)]
        for c in range(4):
            cs = slice(c * 128, (c + 1) * 128)
            DT, CS, ECS, DEDT, CDEC, NB, T1, T2 = [dts[:, c, i, :] for i in range(8)]
            for g2 in range(2):
                ps = self.proj_tok(Zw[g2][0], Zw[g2][1], c, 512)
                kb.op("act", lambda e, ps=ps, g2=g2: e.activation(out=self.zs[:, g2 * 512:(g2 + 1) * 512], in_=ps[:, :], func=AF.Silu),
                      reads=[ps], writes=[self.zs])
            if self.dbg.get('bstop', 99) <= 2.3: continue
            pb = self.npb()
            kb.mm([lambda pe, j=j, pb=pb: pe.transpose(out=pb[:, j * 128:(j + 1) * 128], in_=self.xsT[:, j, cs], identity=self.id_b[:]) for j in range(8)],
                  reads=[self.xsT, self.id_b], writes=[pb])
            kb.op("act", lambda e, pb=pb: e.copy(out=self.xs_tok[:], in_=pb[:, :]), reads=[pb], writes=[self.xs_tok])
            if self.dbg.get('bstop', 99) <= 2.6: continue
            pbv = self.xs_tok[:].rearrange("p (h q) -> p h q", h=16)
            kb.op("dve", lambda e, pbv=pbv, DEDT=DEDT: e.tensor_tensor(out=self.xw[:].rearrange("p (h q) -> p h q", h=16), in0=pbv,
                                                                        in1=DEDT.unsqueeze(2).broadcast_to([128, 16, 64]), op=ALU.mult),
                  reads=[self.xs_tok, dts], writes=[self.xw])
            kb.op("dve", lambda e, pbv=pbv: e.tensor_tensor(out=self.xD[:].rearrange("p (h q) -> p h q", h=16), in0=pbv,
                                                             in1=self.sm16[:, 2, :].unsqueeze(2).broadcast_to([128, 16, 64]), op=ALU.mult),
                  reads=[self.xs_tok, self.sm16], writes=[self.xD])
            if self.dbg.get('bstop', 99) <= 2.8: continue
            pb2 = self.npb()
            kb.mm([lambda pe, j=j, pb2=pb2: pe.transpose(out=pb2[:, j * 128:(j + 1) * 128], in_=self.BT[:, j, cs], identity=self.id_b[:]) for j in range(2)],
                  reads=[self.BT, self.id_b], writes=[pb2])
            kb.op("act", lambda e, pb2=pb2: e.copy(out=self.B_tok[:], in_=pb2[:, 0:256]), reads=[pb2], writes=[self.B_tok])
            if self.dbg.get('bstop', 99) <= 3: continue
            psc = self.nps()
            kb.mm([lambda pe, g=g, psc=psc: pe.matmul(psc[:, g * 128:(g + 1) * 128], self.BT[:, g, cs], self.CT[:, g, cs], start=True, stop=True) for g in range(2)],
                  reads=[self.BT, self.CT], writes=[psc])
            kb.op("act", lambda e, psc=psc: e.copy(out=self.cbT[:], in_=psc[:, 0:256].rearrange("p (g i) -> p g i", g=2)), reads=[psc], writes=[self.cbT])
            def build_D(hq):
                Dhl = self.Dhl2[hq % 2]
                for hl in range(2):
                    kb.op("dve", lambda e, hl=hl, c=c, hq=hq, Dhl=Dhl: e.tensor_tensor(
                        out=Dhl[:, hl, :, :], in0=self.id_b[:].unsqueeze(1).broadcast_to([128, 4, 128]),
                        in1=self.cshl[:, c, hl, hq * 4:(hq + 1) * 4].unsqueeze(2).broadcast_to([128, 4, 128]), op=ALU.mult),
                        reads=[self.id_b, self.cshl], writes=[Dhl])
            build_D(0)
            for hq in range(4):
                Dhl = self.Dhl2[hq % 2]; Ep = self.Ep2[hq % 2]
                if hq < 3:
                    build_D(hq + 1)
                pse = self.nps()
                kb.mm([lambda pe, pse=pse, hq=hq, Dhl=Dhl: pe.matmul(pse[:, :], self.ones_b[:], Dhl[:, 0, :, :], start=True, stop=False),
                       lambda pe, pse=pse, hq=hq, Dhl=Dhl: pe.matmul(pse[:, :], self.ones_b[:], Dhl[:, 1, :, :], start=False, stop=False),
                       lambda pe, pse=pse: pe.matmul(pse[:, :], self.id_b[:], self.negT[:].unsqueeze(1).broadcast_to([128, 4, 128]), start=False, stop=True)],
                      reads=[self.ones_b, Dhl, self.id_b, self.negT], writes=[pse])
                for hh in range(4):
                    h = hq * 4 + hh
                    kb.op("act", lambda e, pse=pse, hh=hh, h=h, NB=NB, Ep=Ep: e.activation(out=Ep[:, hh, :], in_=pse[:, hh * 128:(hh + 1) * 128], func=AF.Exp,
                                                                                  bias=NB[:, h:h + 1], scale=1.0), reads=[pse, dts], writes=[Ep])
                g = hq // 2
                kb.op("dve", lambda e, hq=hq, g=g, Ep=Ep: e.tensor_tensor(out=self.wmT[:, hq * 4:(hq + 1) * 4, :], in0=Ep[:],
                                                                    in1=self.cbT[:, g, :].unsqueeze(1).broadcast_to([128, 4, 128]), op=ALU.mult),
                      reads=[Ep, self.cbT], writes=[self.wmT])
            if self.dbg.get('bstop', 99) <= 4: continue
            psY = [self.nps(), self.nps()]
            for g in range(2):
                fns = []
                for hh in range(8):
                    h = g * 8 + hh
                    fns.append(lambda pe, g=g, hh=hh, h=h: pe.matmul(psY[g][:, hh * 64:(hh + 1) * 64], self.id_b[:], self.xD[:, h * 64:(h + 1) * 64], start=True, stop=False))
                    fns.append(lambda pe, g=g, hh=hh, h=h: pe.matmul(psY[g][:, hh * 64:(hh + 1) * 64], self.wmT[:, h, :], self.xs_tok[:, h * 64:(h + 1) * 64], start=False, stop=True))
                kb.mm(fns, reads=[self.id_b, self.xD, self.wmT, self.xs_tok], writes=[psY[g]])
            psZ = [self.nps(), self.nps()]
            for g in range(2):
                kb.mm([lambda pe, g=g: pe.matmul(psZ[g][:, :], self.CT[:, g, cs], Sb[:, g * 512:(g + 1) * 512], start=True, stop=True)],
                      reads=[self.CT, Sb], writes=[psZ[g]])
            for g in range(2):
                gs = slice(g * 512, (g + 1) * 512)
                kb.op("dve", lambda e, g=g, gs=gs, ECS=ECS: e.tensor_tensor(out=self.ytmp[:, gs].rearrange("p (h q) -> p h q", h=8),
                                                                          in0=psZ[g][:, :].rearrange("p (h q) -> p h q", h=8),
                                                                          in1=ECS[:, g * 8:(g + 1) * 8].unsqueeze(2).broadcast_to([128, 8, 64]), op=ALU.mult),
                      reads=[psZ[g], dts], writes=[self.ytmp])
                kb.op("dve", lambda e, g=g, gs=gs: e.tensor_tensor(out=self.ytmp[:, gs], in0=self.ytmp[:, gs], in1=psY[g][:, :], op=ALU.add),
                      reads=[psY[g], self.ytmp], writes=[self.ytmp])
            psS = [self.nps(), self.nps()]
            for g in range(2):
                kb.mm([lambda pe, g=g: pe.matmul(psS[g][:, :], self.B_tok[:, g * 128:(g + 1) * 128], self.xw[:, g * 512:(g + 1) * 512], start=True, stop=True)],
                      reads=[self.B_tok, self.xw], writes=[psS[g]])
            kb.op("dve", lambda e, CDEC=CDEC: e.tensor_tensor(out=self.stmp[:].rearrange("p (h q) -> p h q", h=16), in0=S[:].rearrange("p (h q) -> p h q", h=16),
                                                              in1=CDEC.unsqueeze(2).broadcast_to([128, 16, 64]), op=ALU.mult), reads=[S, dts], writes=[self.stmp])
            for g in range(2):
                gs = slice(g * 512, (g + 1) * 512)
                kb.op("dve", lambda e, g=g, gs=gs: e.tensor_tensor(out=S[:, gs], in0=self.stmp[:, gs], in1=psS[g][:, :], op=ALU.add),
                      reads=[self.stmp, psS[g]], writes=[S])
            kb.op("act", lambda e: e.copy(out=Sb[:], in_=S[:]), reads=[S], writes=[Sb])
            if self.dbg.get('bstop', 99) <= 5: continue
            kb.op("dve", lambda e, c=c: e.tensor_tensor(out=self.ytmp[:], in0=self.ytmp[:], in1=self.zs[:], op=ALU.mult),
                  reads=[self.ytmp, self.zs], writes=[self.ytmp])
            for g in range(2):
                gs = slice(g * 512, (g + 1) * 512)
                kb.op("act", lambda e, g=g, gs=gs: e.activation(out=self.stmp[:, gs], in_=self.ytmp[:, gs], func=AF.Square, accum_out=self.ssq[:, g:g + 1]),
                      reads=[self.ytmp], writes=[self.stmp, self.ssq])
            kb.op("act", lambda e: e.activation(out=self.ssq[:, 2:4], in_=self.ssq[:, 0:2], func=AF.Ln, bias=EPS, scale=1.0 / 512), reads=[self.ssq], writes=[self.ssq])
            kb.op("act", lambda e: e.activation(out=self.ssq[:, 2:4], in_=self.ssq[:, 2:4], func=AF.Exp, scale=-0.5), reads=[self.ssq], writes=[self.ssq])
            for g in range(2):
                gs = slice(g * 512, (g + 1) * 512)
                kb.op("dve", lambda e, g=g, gs=gs: e.scalar_tensor_tensor(out=self.ynb[:, gs], in0=self.ytmp[:, gs], scalar=self.ssq[:, 2 + g:3 + g],
                                                                           in1=self.normw[:, gs], op0=ALU.mult, op1=ALU.mult),
                      reads=[self.ytmp, self.ssq, self.normw], writes=[self.ynb])
            self.transp_b(self.ynb, [self.ynb[:, j * 128:(j + 1) * 128] for j in range(8)], self.yT, self.yT[:, :, cs])
        if self.dbg.get('bstop', 99) <= 6: return
        if bi == self.nblk - 1:
            for half in range(2):
                ps = self.nps()
                kb.mm([lambda pe, j=j, half=half, ps=ps: pe.transpose(out=ps[:, j * 128:(j + 1) * 128], in_=S[:, (half * 4 + j) * 128:(half * 4 + j + 1) * 128],
                                                                       identity=self.ident_f) for j in range(4)], reads=[S, self.cf], writes=[ps])
                kb.op("act", lambda e, ps=ps: e.copy(out=self.ytmp[:, 0:512], in_=ps[:, :]), reads=[ps], writes=[self.ytmp])
                kb.dma("sp", self.ssm_p.ap()[l, half * 512:(half + 1) * 512, :].rearrange("(j p) n -> p j n", p=128),
                       self.ytmp[:, 0:512].rearrange("p (j n) -> p j n", j=4), reads=[self.ytmp], is_output=True)
            for half in range(3):
                ps = self.nps()
                kb.mm([lambda pe, j=j, half=half, ps=ps: pe.transpose(out=ps[0:3, j * 128:(j + 1) * 128], in_=self.hist[l][:, half * 4 + j, :],
                                                                       identity=self.ident_f) for j in range(4)], reads=[self.hist[l], self.cf], writes=[ps])
                kb.op("act", lambda e, ps=ps: e.copy(out=self.stmp[0:3, 0:512], in_=ps[0:3, :]), reads=[ps], writes=[self.stmp])
                kb.dma("sp", self.conv_p.ap()[l, :, half * 512:(half + 1) * 512], self.stmp[0:3, 0:512], reads=[self.stmp], is_output=True)

    def phaseC(self, bi, l):
        self.kb.phase = 'phaseC'
        kb = self.kb
        last = (bi == self.nblk - 1)
        self.fence()
        kb.op("act", lambda e: e.activation(out=self.vaug[:, :, :, 64:65], in_=self.cf[:, 0:10].rearrange("p (a b c) -> p a b c", a=5, b=2), func=AF.Identity, scale=0.0, bias=1.0),
              reads=[self.cf], writes=[self.vaug])
        Wt, Wv = self.wload(self.wsrc(self.w_in, l, D, OQC, 512), 8, 512)
        for t in range(4):
            ps = self.proj_feat(Wt, Wv, t * 128)
            kb.op("act", lambda e, t=t, ps=ps: e.copy(out=self.qcT[:, t, :], in_=ps[:, :]), reads=[ps], writes=[self.qcT])
        if self.dbg.get('cstop', 99) <= 1: return
        Wt, Wv = self.wload(self.wsrc(self.w_in, l, D, OKC, 256), 8, 256)
        if bi > 0:
            kb.op("act", lambda e: e.copy(out=self.kTe[:, 0:128], in_=self.kprev[l][:]), reads=[self.kprev[l]], writes=[self.kTe])
            kb.op("act", lambda e: e.copy(out=self.vaug[:, 0, :, 0:64], in_=self.vprev[l][:, :, 0:64]), reads=[self.vprev[l]], writes=[self.vaug])
        if self.dbg.get('cstop', 99) <= 1.5: return
        ps = self.proj_feat(Wt, Wv, 0)
        kb.op("act", lambda e, ps=ps: e.copy(out=self.kTe[:, 128:128 + TB], in_=ps[:, :]), reads=[ps], writes=[self.kTe])
        kb.op("act", lambda e: e.copy(out=self.kprev[l][:], in_=self.kTe[:, TB:TB + 128]), reads=[self.kTe], writes=[self.kprev[l]])
        if self.dbg.get('cstop', 99) <= 1.7: return
        for c in range(4):
            ps = self.proj_tok(Wt, Wv, c, 128, col0=128)
            kb.op("act", lambda e, c=c, ps=ps: e.copy(out=self.vaug[:, c + 1, :, 0:64], in_=ps[:, 0:128].rearrange("p (g e) -> p g e", g=2)),
                  reads=[ps], writes=[self.vaug])
            if last and c == 3 and not self.dbg.get('nokv'):
                kb.op("act", lambda e, ps=ps: e.copy(out=self.kvo[:, 1, :], in_=ps[:, 0:128]), reads=[ps], writes=[self.kvo])
                ps2 = self.proj_tok(Wt, Wv, c, 128, col0=0)
                kb.op("act", lambda e, ps2=ps2: e.copy(out=self.kvo[:, 0, :], in_=ps2[:, 0:128]), reads=[ps2], writes=[self.kvo])
                kb.dma("sp", self.k_p.ap()[l, :, :], self.kvo[:, 0, :], reads=[self.kvo], is_output=True)
                kb.dma("sp", self.v_p.ap()[l, :, :], self.kvo[:, 1, :], reads=[self.kvo], is_output=True)
        kb.op("act", lambda e: e.copy(out=self.vprev[l][:, :, 0:65], in_=self.vaug[:, 4, :, 0:65]), reads=[self.vaug], writes=[self.vprev[l]])
        if self.dbg.get('cstop', 99) <= 2: return
        for c in range(4):
            n = bi * 4 + c
            cs = slice(c * 128, (c + 1) * 128)
            hfs = [1] if n == 0 else [0, 1]
            for g in range(2):
                for hf in hfs:
                    ps = self.nps()
                    kc0 = (c + hf) * 128
                    kb.mm([lambda pe, ps=ps, g=g, kc0=kc0: pe.matmul(ps[:, :], self.kTe[g * 64:(g + 1) * 64, kc0:kc0 + 128], self.qcT[g * 64:(g + 1) * 64, :, cs],
                                                                      start=True, stop=True)], reads=[self.kTe, self.qcT], writes=[ps])
                    kb.op("dve", lambda e, ps=ps, g=g, hf=hf: e.scalar_tensor_tensor(out=self.lg[:], in0=ps[:, :].rearrange("p (h q) -> p h q", h=4), scalar=0.125,
                                                                                    in1=self.btab[:, hf, g * 4:(g + 1) * 4, :], op0=ALU.mult, op1=ALU.add),
                          reads=[ps, self.btab], writes=[self.lg])
                    kb.op("act", lambda e, g=g, hf=hf: e.activation(out=self.pT[:, hf, g, :, :], in_=self.lg[:], func=AF.Exp), reads=[self.lg], writes=[self.pT])
            if self.dbg.get('cstop', 99) <= 3: continue
            psO = [self.nps(), self.nps()]
            for g in range(2):
                fns = []
                for h4 in range(4):
                    for i, hf in enumerate(hfs):
                        fns.append(lambda pe, g=g, h4=h4, hf=hf, i=i: pe.matmul(psO[g][:, h4 * 65:(h4 + 1) * 65], self.pT[:, hf, g, h4, :], self.vaug[:, c + hf, g, 0:65],
                                                                               start=(i == 0), stop=(i == len(hfs) - 1)))
                kb.mm(fns, reads=[self.pT, self.vaug], writes=[psO[g]])
                if self.dbg.get('cstop', 99) <= 4: continue
                ov = psO[g][:, 0:260].rearrange("p (h e) -> p h e", h=4)
                kb.op("dve", lambda e, g=g, ov=ov: e.tensor_tensor(out=self.den[:, g * 4:(g + 1) * 4], in0=ov[:, :, 64], in1=self.esink[:, g * 4:(g + 1) * 4], op=ALU.add),
                      reads=[psO[g], self.esink], writes=[self.den])
                kb.op("dve", lambda e, g=g: e.reciprocal(out=self.rden[:, g * 4:(g + 1) * 4], in_=self.den[:, g * 4:(g + 1) * 4]), reads=[self.den], writes=[self.rden])
                kb.op("dve", lambda e, g=g, ov=ov: e.tensor_tensor(out=self.oc_tok[:, g * 4:(g + 1) * 4, :], in0=ov[:, :, 0:64],
                                                                    in1=self.rden[:, g * 4:(g + 1) * 4].unsqueeze(2).broadcast_to([128, 4, 64]), op=ALU.mult),
                      reads=[psO[g], self.rden], writes=[self.oc_tok])
            if self.dbg.get('cstop', 99) <= 5: continue
            ocv = self.oc_tok[:].rearrange("p h e -> p (h e)")
            self.transp_b(self.oc_tok, [ocv[:, j * 128:(j + 1) * 128] for j in range(4)], self.ocT, self.ocT[:, :, cs])

    def phaseD(self, bi, l):
        self.kb.phase = 'phaseD'
        kb = self.kb
        self.fence()
        brs = [(self.w_br_ret, 512, 4, self.orT), (self.w_br_ssd, 1024, 8, self.yT), (self.w_br_swa, 512, 4, self.ocT)]
        for b, (wbr, K, kc, src) in enumerate(brs):
            for j in range(2):
                Gt, Gv = self.wload(self.wsrc(self.w_in, l, D, OGATE + b * 1024 + j * 512, 512), 8, 512)
                Bt, Bv = self.wload(self.wsrc(wbr, l, K, j * 512, 512), kc, 512)
                for t in range(4):
                    m = j * 4 + t
                    psG = self.proj_feat(Gt, Gv, t * 128)
                    psB = self.proj_feat(Bt, Bv, t * 128, src=src, kc=kc)
                    kb.op("act", lambda e, psG=psG: e.activation(out=self.sg[:], in_=psG[:, :], func=AF.Sigmoid), reads=[psG], writes=[self.sg])
                    if b == 0:
                        kb.op("dve", lambda e, m=m, psB=psB: e.tensor_tensor(out=self.macc[:, m, :], in0=self.sg[:], in1=psB[:, :], op=ALU.mult),
                              reads=[self.sg, psB], writes=[self.macc])
                    else:
                        kb.op("dve", lambda e, psB=psB: e.tensor_tensor(out=self.gt[:], in0=self.sg[:], in1=psB[:, :], op=ALU.mult),
                              reads=[self.sg, psB], writes=[self.gt])
                        if b == 1:
                            kb.op("dve", lambda e, m=m: e.tensor_tensor(out=self.macc[:, m, :], in0=self.macc[:, m, :], in1=self.gt[:], op=ALU.add),
                                  reads=[self.macc, self.gt], writes=[self.macc])
                        else:
                            kb.op("dve", lambda e, m=m: e.tensor_tensor(out=self.mrg[:, m, :], in0=self.macc[:, m, :], in1=self.gt[:], op=ALU.add),
                                  reads=[self.macc, self.gt], writes=[self.mrg])

    def resid_ln(self, pss_fn, gi):
        kb = self.kb
        for m in range(8):
            ps = pss_fn(m)
            kb.op("dve", lambda e, m=m, ps=ps: e.scalar_tensor_tensor(out=self.xr[:, m, :], in0=self.xr[:, m, :], scalar=ALPHA, in1=ps[:, :], op0=ALU.mult, op1=ALU.add),
                  reads=[self.xr, ps], writes=[self.xr])
            kb.op("act", lambda e, m=m: e.copy(out=self.xb[:, m, :], in_=self.xr[:, m, :]), reads=[self.xr], writes=[self.xb])
            kb.op("act", lambda e, m=m: e.activation(out=self.ysq[:, m, :], in_=self.xr[:, m, :], func=AF.Square), reads=[self.xr], writes=[self.ysq])
        ps1 = self.nps(); ps2 = self.nps()
        kb.mm([lambda pe, k=k: pe.matmul(ps1[:, :], self.ones_b[:], self.xb[:, k, :], start=(k == 0), stop=(k == 7)) for k in range(8)],
              reads=[self.ones_b, self.xb], writes=[ps1])
        kb.mm([lambda pe, k=k: pe.matmul(ps2[:, :], self.ones_b[:], self.ysq[:, k, :], start=(k == 0), stop=(k == 7)) for k in range(8)],
              reads=[self.ones_b, self.ysq], writes=[ps2])
        kb.op("act", lambda e: e.activation(out=self.lnm[:], in_=ps1[:, :], func=AF.Identity, scale=1.0 / D), reads=[ps1], writes=[self.lnm])
        kb.op("dve", lambda e: e.tensor_tensor(out=self.lnt[:], in0=self.lnm[:], in1=self.lnm[:], op=ALU.mult), reads=[self.lnm], writes=[self.lnt])
        kb.op("dve", lambda e: e.scalar_tensor_tensor(out=self.lnt[:], in0=ps2[:, :], scalar=1.0 / D, in1=self.lnt[:], op0=ALU.mult, op1=ALU.subtract),
              reads=[ps2, self.lnt], writes=[self.lnt])
        kb.op("act", lambda e: e.activation(out=self.lnt[:], in_=self.lnt[:], func=AF.Ln, bias=EPS, scale=1.0), reads=[self.lnt], writes=[self.lnt])
        kb.op("act", lambda e: e.activation(out=self.lnr[:], in_=self.lnt[:], func=AF.Exp, scale=-0.5), reads=[self.lnt], writes=[self.lnr])
        for m in range(8):
            eng = "dve"
            lnu = self.lnu2 if (m % 2 == 1) else self.lnu
            kb.op(eng, lambda e, m=m, lnu=lnu: e.tensor_tensor(out=lnu[:], in0=self.xr[:, m, :], in1=self.lnm[:], op=ALU.subtract), reads=[self.xr, self.lnm], writes=[lnu])
            kb.op(eng, lambda e, m=m, lnu=lnu: e.tensor_tensor(out=lnu[:], in0=lnu[:], in1=self.lnr[:], op=ALU.mult), reads=[lnu, self.lnr], writes=[lnu])
            kb.op("act", lambda e, m=m, lnu=lnu: e.activation(out=self.xr[:, m, :], in_=lnu[:], func=AF.Identity, bias=self.lnp[:, gi + 1, m:m + 1], scale=self.lnp[:, gi, m:m + 1]),
                  reads=[lnu, self.lnp], writes=[self.xr])
            kb.op("act", lambda e, m=m: e.copy(out=self.xb[:, m, :], in_=self.xr[:, m, :]), reads=[self.xr], writes=[self.xb])

    def phaseE(self, bi, l):
        self.kb.phase = 'phaseE'
        W = {}

        def pss(m):
            j = m // 4
            if (m % 4) == 0:
                W["t"], W["v"] = self.wload(self.wsrc(self.w_out, l, D, j * 512, 512), 8, 512)
            return self.proj_feat(W["t"], W["v"], (m % 4) * 128, src=self.mrg)
        self.resid_ln(pss, 0)

    def phaseF(self, bi, l):
        self.kb.phase = 'phaseF'
        kb = self.kb
        for jg in range(6):
            n = 512 if jg < 5 else 256
            Gt, Gv = self.wload(self.wsrc(self.w_gate, l, D, jg * 512, n), 8, n)
            Ut, Uv = self.wload(self.wsrc(self.w_up, l, D, jg * 512, n), 8, n)
            for t in range(n // 128):
                j = jg * 4 + t
                psG = self.proj_feat(Gt, Gv, t * 128)
                psU = self.proj_feat(Ut, Uv, t * 128)
                kb.op("act", lambda e, psG=psG: e.activation(out=self.sg[:], in_=psG[:, :], func=AF.Silu), reads=[psG], writes=[self.sg])
                kb.op("dve", lambda e, j=j, psU=psU: e.tensor_tensor(out=self.hT[:, j, :], in0=self.sg[:], in1=psU[:, :], op=ALU.mult),
                      reads=[self.sg, psU], writes=[self.hT])

        def pss(m):
            Wt, Wv = self.wload(self.wsrc_down(l, m), 22, 128)
            return self.proj_feat(Wt, Wv, 0, src=self.hT, kc=22)
        self.resid_ln(pss, 2)


NB = 16


class GenS(Gen):
    def _decl(self):
        Gen._decl(self)
        if not self.do_sample:
            return
        dp = self.depth
        d, o = self.din, self.dout
        self.xs = d("xs", [NB, D])
        self.st_ret = d("st_ret", [dp, NB, 4, 128, 128])
        self.st_ssm = d("st_ssm", [dp, NB, 16, 64, 128])
        self.st_conv = d("st_conv", [dp, NB, 3, 1536])
        self.ck = d("ck", [dp, NB, 128, 128])
        self.cv = d("cv", [dp, NB, 128, 128])
        self.c_rots = d("c_rots", [NB, 256])
        self.c_sel = d("c_sel", [NB, 16 * 128 + 2 * 128])
        self.c_eye = d("c_eye", [128, 256])
        self.conv_w_n = d("conv_w_n", [dp, 1, 4 * 1536])
        self.conv_b_n = d("conv_b_n", [dp, 1, 1536])
        self.ln_n = d("ln_n", [dp, 4, 1, 1024])
        self.ys = o("ys", [NB, D])
        self.ret_s = o("ret_s", [dp, NB, 4, 128, 128])
        self.ssm_s = o("ssm_s", [dp, NB, 16, 64, 128])
        self.conv_s = o("conv_s", [dp, NB, 3, 1536])
        self.k_s = o("k_s", [dp, NB, 128, 128])
        self.v_s = o("v_s", [dp, NB, 128, 128])

    def _alloc(self):
        Gen._alloc(self)
        if not self.do_sample:
            return
        nc = self.nc
        st = {"off": 0, "base": None, "size": 0}

        def region(base, size):
            st["base"], st["size"], st["off"] = base, size, 0

        def al(name, shape, dt):
            n = 1
            for x in shape[1:]:
                n *= x
            nb = (n * (4 if dt == F32 else 2) + 31) // 32 * 32
            assert st["off"] + nb <= st["size"], (name, st["off"], nb, st["size"])
            h = nc.alloc_sbuf_tensor_at(name, list(shape), dt, offset=st["base"] + st["off"])
            st["off"] += nb
            return Tile(h, None, name)
        region(self.xr_off, 16384 + 8192)
        self.sx = al("sx", [NB, 1024], F32)
        self.sxT = al("sxT", [128, 8, NB], BF16)
        self.sxconv = al("sxconv", [NB, 1536], F32)
        self.s_orT = al("s_orT", [128, 4, NB], BF16); self.s_yT = al("s_yT", [128, 8, NB], BF16); self.s_ocT = al("s_ocT", [128, 4, NB], BF16)
        self.s_mT = al("s_mT", [128, 8, NB], BF16); self.s_hT = al("s_hT", [128, 22, NB], BF16)
        self.s_sel = al("s_sel", [NB, 16 * 128 + 256], F32)
        self.s_eye = al("s_eye", [128, 256], F32)
        self.s_rot = al("s_rot", [NB, 256], F32)
        self.s_b0 = al("s_b0", [NB, 8], F32)
        A = self.abase
        region(A, self.asz)
        self.r_q = al("r_q", [NB, 512], F32); self.r_k = al("r_k", [NB, 512], F32)
        self.r_tA = al("r_tA", [NB, 4, 2, 64], F32); self.r_tB = al("r_tB", [NB, 4, 2, 64], F32)
        self.r_qb = al("r_qb", [NB, 512], BF16); self.r_kb = al("r_kb", [NB, 512], BF16); self.r_vb = al("r_vb", [NB, 512], BF16)
        self.r_g = al("r_g", [NB, 512], F32)
        self.r_qT = al("r_qT", [128, 4, NB], BF16)
        self.r_qTm = al("r_qTm", [128, 4, NB, NB], BF16)
        self.r_vbd = al("r_vbd", [NB, 4, 4, 128], BF16)
        self.r_S = [al("r_S%d" % i, [128, 4, 4, 128], F32) for i in range(2)]
        self.r_Sb = al("r_Sb", [128, NB, 4, 128], BF16)
        self.r_o = al("r_o", [NB, 4, 128], F32)
        self.r_st6 = al("r_st6", [NB, 4, 6], F32); self.r_mv = al("r_mv", [NB, 4, 2], F32)
        self.r_rs = al("r_rs", [NB, 4], F32); self.r_rsb = al("r_rsb", [NB, 4], F32)
        self.r_og = al("r_og", [NB, 4, 128], F32); self.r_ogb = al("r_ogb", [NB, 512], BF16)
        print("arena S-RET", st["off"])
        region(A, self.asz)
        self.c_w = al("c_w", [NB, 4, 1536], F32); self.c_buf = al("c_buf", [NB, 3, 1536], F32)
        self.c_new = al("c_new", [NB, 1536], F32); self.c_acc = al("c_acc", [NB, 1536], F32)
        print("arena S-CONV", st["off"])
        region(A, self.asz)
        self.d_z = al("d_z", [NB, 1024], F32)
        self.d_h = al("d_h", [128, 4, 8, 128], F32); self.d_t = al("d_t", [128, 4, 8, 128], F32)
        self.d_sm = al("d_sm", [NB, 8, 16], F32)
        self.d_xdt = al("d_xdt", [NB, 1024], F32)
        self.d_xdtT = al("d_xdtT", [128, NB, 8], F32)
        self.d_R = al("d_R", [NB, 2, 4, 128], F32)
        self.d_RA = al("d_RA", [NB, 2, NB, 8], F32)
        self.d_dA = al("d_dA", [128, NB, 8], F32)
        self.d_BC = al("d_BC", [128, 2, 4, 128], F32)
        self.d_yT = al("d_yT", [128, NB, 8], F32)
        self.d_y = al("d_y", [NB, 1024], F32); self.d_y2 = self.d_xdt
        self.d_ssq = al("d_ssq", [NB, 4], F32)
        self.d_nw = al("d_nw", [NB, 1024], F32)
        self.d_ynb = al("d_ynb", [NB, 1024], BF16)
        print("arena S-SSD", st["off"])
        region(A, self.asz)
        self.a_K = al("a_K", [128, NB, 128], F32); self.a_V = al("a_V", [128, NB, 128], F32)
        self.a_Vb = al("a_Vb", [128, NB, 2, 80], BF16)
        self.a_q = al("a_q", [NB, 512], F32); self.a_kn = al("a_kn", [NB, 128], F32); self.a_vn = al("a_vn", [NB, 2, 80], F32)
        self.a_pr = al("a_pr", [128, 4, 2, 64], F32)
        self.a_s = al("a_s", [128, NB, 8], F32)
        self.a_p = al("a_p", [128, NB, 8], BF16)
        self.a_Pm = al("a_Pm", [128, NB, 8, NB], BF16)
        self.a_sn = al("a_sn", [NB, 4, 2, 64], F32); self.a_s8 = al("a_s8", [NB, 8], F32); self.a_pn = al("a_pn", [NB, 8], F32)
        self.a_ou = al("a_ou", [NB, 8, 65], F32); self.a_t = al("a_t", [NB, 8, 65], F32)
        self.a_den = al("a_den", [NB, 8], F32)
        self.a_ob = al("a_ob", [NB, 8, 64], BF16)
        self.a_es = al("a_es", [NB, 8], F32)
        print("arena S-SWA", st["off"])
        region(A, self.asz)
        self.m_sg = al("m_sg", [NB, 512], F32); self.m_t = al("m_t", [NB, 512], F32)
        self.m_acc = al("m_acc", [NB, 1024], F32)
        self.m_b = al("m_b", [NB, 1024], BF16)
        self.m_g = al("m_g", [NB, 1024], F32); self.m_bb = al("m_bb", [NB, 1024], F32)
        self.m_st = al("m_st", [NB, 2, 6], F32); self.m_mv = al("m_mv", [NB, 2], F32); self.m_rs = al("m_rs", [NB, 2], F32)
        self.m_u = al("m_u", [NB, 1024], F32)
        self.m_h = al("m_h", [NB, 2816], F32); self.m_hb = al("m_hb", [NB, 2816], BF16)
        print("arena S-MLP", st["off"])

    def sproj(self, Wt, Wv, n, src=None, kc=8, col0=0):
        src = self.sxT if src is None else src
        ps = self.nps()
        self.kb.mm([lambda pe, k=k, ps=ps: pe.matmul(ps[0:NB, 0:n], src[:, k, :], Wv[:, k, col0:col0 + n], start=(k == 0), stop=(k == kc - 1))
                    for k in range(kc)], reads=[src, Wt], writes=[ps])
        return ps

    def s_transp(self, src_tile, src_aps, dst_tile, dst_ap):
        n = len(src_aps)
        pb = self.npb()
        self.kb.mm([lambda pe, j=j, pb=pb: pe.transpose(out=pb[:, j * NB:(j + 1) * NB], in_=src_aps[j], identity=self.id_b[0:NB, 0:NB])
                    for j in range(n)], reads=[src_tile, self.id_b], writes=[pb])
        self.kb.op("act", lambda e, pb=pb: e.copy(out=dst_ap, in_=pb[:, 0:n * NB].rearrange("p (j t) -> p j t", j=n)), reads=[pb], writes=[dst_tile])

    def s_tokmajor_to_T(self, src, ncols, dst):
        nt = ncols // 128
        for j0 in range(0, nt, 8):
            j1 = min(nt, j0 + 8)
            self.s_transp(src, [src[0:NB, j * 128:(j + 1) * 128] for j in range(j0, j1)], dst, dst[:, j0:j1, :])

    def sample_pass(self):
        kb = self.kb
        self.fence()
        kb.dma("sp", self.sx[:], self.xs.ap()[:, :], writes=[self.sx])
        kb.dma("sp", self.s_sel[:], self.c_sel.ap()[:, :], writes=[self.s_sel])
        kb.dma("sp", self.s_eye[:], self.c_eye.ap()[:, :], writes=[self.s_eye])
        kb.dma("sp", self.s_rot[:], self.c_rots.ap()[:, :], writes=[self.s_rot])
        kb.dma("sp", self.s_b0[:], self.rel_bias.ap()[0:1, 0:8].partition_broadcast(NB), writes=[self.s_b0])
        self.s_refresh_xT()
        for l in range(self.depth):
            self.load_params(0, l)
            self.s_ret(l)
            self.s_conv(l)
            self.s_ssd(l)
            self.s_swa(l)
            self.s_mlp(l)
        kb.dma("sp", self.ys.ap()[:, :], self.sx[:], reads=[self.sx], is_output=True)

    def s_refresh_xT(self):
        kb = self.kb
        kb.op("act", lambda e: e.copy(out=self.m_b[:], in_=self.sx[:]), reads=[self.sx], writes=[self.m_b])
        self.s_tokmajor_to_T(self.m_b, 1024, self.sxT)

    def s_ret(self, l):
        self.kb.phase = 's_ret'
        kb = self.kb
        self.fence()
        gam = [1.0 - 2.0 ** (-5 - h) for h in range(4)]
        for gi, off in enumerate((OQ, OK_, OV, OG)):
            Wt, Wv = self.wload(self.wsrc(self.w_in, l, D, off, 512), 8, 512)
            ps = self.sproj(Wt, Wv, 512)
            pv = ps[0:NB, :]
            if gi < 2:
                dst = self.r_q if gi == 0 else self.r_k
                psv = pv.rearrange("p (h t e) -> p h t e", h=4, t=2)
                cosb = self.s_rot[:, gi * 128:gi * 128 + 64].unsqueeze(1).unsqueeze(1).broadcast_to([NB, 4, 2, 64])
                sinb = self.s_rot[:, gi * 128 + 64:gi * 128 + 128].unsqueeze(1).broadcast_to([NB, 4, 64])
                kb.op("dve", lambda e, psv=psv, cosb=cosb: e.tensor_tensor(out=self.r_tA[:], in0=psv, in1=cosb, op=ALU.mult), reads=[ps, self.s_rot], writes=[self.r_tA])
                kb.op("dve", lambda e, psv=psv, sinb=sinb: e.tensor_tensor(out=self.r_tB[:, :, 0, :], in0=psv[:, :, 1, :], in1=sinb, op=ALU.mult), reads=[ps, self.s_rot], writes=[self.r_tB])
                kb.op("dve", lambda e, psv=psv, sinb=sinb: e.tensor_tensor(out=self.r_tB[:, :, 1, :], in0=psv[:, :, 0, :], in1=sinb, op=ALU.mult), reads=[ps, self.s_rot], writes=[self.r_tB])
                dv = dst[:].rearrange("p (h t e) -> p h t e", h=4, t=2)
                kb.op("dve", lambda e, dv=dv: e.tensor_tensor(out=dv[:, :, 0, :], in0=self.r_tA[:, :, 0, :], in1=self.r_tB[:, :, 0, :], op=ALU.subtract), reads=[self.r_tA, self.r_tB], writes=[dst])
                kb.op("dve", lambda e, dv=dv: e.tensor_tensor(out=dv[:, :, 1, :], in0=self.r_tA[:, :, 1, :], in1=self.r_tB[:, :, 1, :], op=ALU.add), reads=[self.r_tA, self.r_tB], writes=[dst])
                db = self.r_qb if gi == 0 else self.r_kb
                kb.op("act", lambda e, dst=dst, db=db: e.copy(out=db[:], in_=dst[:]), reads=[dst], writes=[db])
            elif gi == 2:
                kb.op("act", lambda e, pv=pv: e.copy(out=self.r_vb[:], in_=pv), reads=[ps], writes=[self.r_vb])
            else:
                kb.op("act", lambda e, pv=pv: e.activation(out=self.r_g[:], in_=pv, func=AF.Silu), reads=[ps], writes=[self.r_g])
        self.s_transp(self.r_qb, [self.r_qb[0:NB, h * 128:(h + 1) * 128] for h in range(4)], self.r_qT, self.r_qT[:, :, :])
        kb.op("dve", lambda e: e.tensor_tensor(out=self.r_qTm[:], in0=self.r_qT[:].unsqueeze(2).broadcast_to([128, 4, NB, NB]),
                                                in1=self.s_eye[:].rearrange("p (a b) -> p a b", a=NB).unsqueeze(1).broadcast_to([128, 4, NB, NB]), op=ALU.mult),
              reads=[self.r_qT, self.s_eye], writes=[self.r_qTm])
        sel = self.s_sel[:, 0:NB * 128].rearrange("p (b k) -> p b k", b=NB)
        for sg in range(4):
            S = self.r_S[sg % 2]
            kb.dma("sp", S[:], self.st_ret.ap()[l, sg * 4:(sg + 1) * 4].rearrange("b h d v -> d b h v"), writes=[S])
            kb.op("dve", lambda e, sg=sg: e.tensor_tensor(
                out=self.r_vbd[:], in0=self.r_vb[:].rearrange("p (h v) -> p h v", h=4).unsqueeze(2).broadcast_to([NB, 4, 4, 128]),
                in1=sel[:, sg * 4:(sg + 1) * 4, 0:1].rearrange("p b o -> p o b").unsqueeze(3).broadcast_to([NB, 4, 4, 128]), op=ALU.mult),
                reads=[self.r_vb, self.s_sel], writes=[self.r_vbd])
            for h in range(4):
                ps = self.nps()
                kb.mm([lambda pe, h=h, ps=ps: pe.matmul(ps[:, :], self.r_kb[0:NB, h * 128:(h + 1) * 128], self.r_vbd[:, h, :, :], start=True, stop=True)],
                      reads=[self.r_kb, self.r_vbd], writes=[ps])
                kb.op("dve", lambda e, h=h, ps=ps, S=S: e.scalar_tensor_tensor(out=S[:, :, h, :], in0=S[:, :, h, :], scalar=gam[h],
                                                                               in1=ps[:, :].rearrange("p (b v) -> p b v", b=4), op0=ALU.mult, op1=ALU.add),
                      reads=[S, ps], writes=[S])
            kb.op("act", lambda e, sg=sg, S=S: e.copy(out=self.r_Sb[:, sg * 4:(sg + 1) * 4, :, :], in_=S[:]), reads=[S], writes=[self.r_Sb])
            kb.dma("sp", self.ret_s.ap()[l, sg * 4:(sg + 1) * 4].rearrange("b h d v -> d b h v"), S[:], reads=[S], is_output=True)
        pso = self.nps()
        fns = []
        for h in range(4):
            for b in range(NB):
                fns.append(lambda pe, h=h, b=b: pe.matmul(pso[0:NB, h * 128:(h + 1) * 128], self.r_qTm[:, h, b, :], self.r_Sb[:, b, h, :],
                                                          start=(b == 0), stop=(b == NB - 1)))
        kb.mm(fns, reads=[self.r_qTm, self.r_Sb], writes=[pso])
        kb.op("act", lambda e: e.copy(out=self.r_o[:], in_=pso[0:NB, :].rearrange("p (h v) -> p h v", h=4)), reads=[pso], writes=[self.r_o])
        for h in range(4):
            kb.op("dve", lambda e, h=h: e.bn_stats(out=self.r_st6[:, h, :], in_=self.r_o[:, h, :]), reads=[self.r_o], writes=[self.r_st6])
        for h in range(4):
            kb.op("dve", lambda e, h=h: e.bn_aggr(out=self.r_mv[:, h, :], in_=self.r_st6[:, h, :]), reads=[self.r_st6], writes=[self.r_mv])
        kb.op("act", lambda e: e.activation(out=self.r_rs[:], in_=self.r_mv[:, :, 1], func=AF.Ln, bias=EPS, scale=1.0), reads=[self.r_mv], writes=[self.r_rs])
        kb.op("act", lambda e: e.activation(out=self.r_rsb[:], in_=self.r_rs[:], func=AF.Exp, scale=-0.5), reads=[self.r_rs], writes=[self.r_rsb])
        for h in range(4):
            kb.op("dve", lambda e, h=h: e.scalar_tensor_tensor(out=self.r_og[:, h, :], in0=self.r_o[:, h, :], scalar=self.r_mv[:, h, 0:1],
                                                                 in1=self.r_g[:, h * 128:(h + 1) * 128], op0=ALU.subtract, op1=ALU.mult),
                  reads=[self.r_o, self.r_mv, self.r_g], writes=[self.r_og])
        for h in range(4):
            kb.op("act", lambda e, h=h: e.activation(out=self.r_ogb[:, h * 128:(h + 1) * 128], in_=self.r_og[:, h, :], func=AF.Identity, scale=self.r_rsb[:, h:h + 1]),
                  reads=[self.r_og, self.r_rsb], writes=[self.r_ogb])
        self.s_tokmajor_to_T(self.r_ogb, 512, self.s_orT)

    def s_conv(self, l):
        self.kb.phase = 's_conv'
        kb = self.kb
        self.fence()
        kb.dma("sp", self.c_w[:].rearrange("p t c -> p (t c)"), self.conv_w_n.ap()[l].partition_broadcast(NB), writes=[self.c_w])
        kb.dma("sp", self.c_buf[:], self.st_conv.ap()[l], writes=[self.c_buf])
        kb.dma("sp", self.c_acc[:], self.conv_b_n.ap()[l].partition_broadcast(NB), writes=[self.c_acc])
        for g3 in range(3):
            Wt, Wv = self.wload(self.wsrc(self.w_in, l, D, OXBC + g3 * 512, 512), 8, 512)
            ps = self.sproj(Wt, Wv, 512)
            kb.op("act", lambda e, ps=ps, g3=g3: e.copy(out=self.c_new[:, g3 * 512:(g3 + 1) * 512], in_=ps[0:NB, :]), reads=[ps], writes=[self.c_new])
        kb.dma("sp", self.conv_s.ap()[l, :, 0:2, :], self.st_conv.ap()[l, :, 1:3, :], is_output=True)
        kb.dma("sp", self.conv_s.ap()[l, :, 2, :], self.c_new[:], reads=[self.c_new], is_output=True)
        for tau in range(4):
            src = self.c_buf[:, tau, :] if tau < 3 else self.c_new[:]
            kb.op("dve", lambda e, tau=tau, src=src: e.tensor_tensor(out=self.c_w[:, tau, :], in0=self.c_w[:, tau, :], in1=src, op=ALU.mult),
                  reads=[self.c_w, self.c_buf, self.c_new], writes=[self.c_w])
            kb.op("dve", lambda e, tau=tau: e.tensor_tensor(out=self.c_acc[:], in0=self.c_acc[:], in1=self.c_w[:, tau, :], op=ALU.add),
                  reads=[self.c_w, self.c_acc], writes=[self.c_acc])
        kb.op("act", lambda e: e.activation(out=self.sxconv[:], in_=self.c_acc[:], func=AF.Silu), reads=[self.c_acc], writes=[self.sxconv])

    def s_ssd(self, l):
        self.kb.phase = 's_ssd'
        kb = self.kb
        self.fence()
        kb.dma("sp", self.d_nw[:], self.norm_w.ap()[l:l + 1, :].partition_broadcast(NB), writes=[self.d_nw])
        for g2 in range(2):
            Wt, Wv = self.wload(self.wsrc(self.w_in, l, D, OZ + g2 * 512, 512), 8, 512)
            ps = self.sproj(Wt, Wv, 512)
            kb.op("act", lambda e, ps=ps, g2=g2: e.activation(out=self.d_z[:, g2 * 512:(g2 + 1) * 512], in_=ps[0:NB, :], func=AF.Silu), reads=[ps], writes=[self.d_z])
        Wt, Wv = self.wload(self.wsrc(self.w_in, l, D, ODT, 16), 8, 16)
        ps = self.sproj(Wt, Wv, 16)
        sm = self.d_sm
        DT, DA, T1, T2 = [sm[:, i, :] for i in range(4)]
        p16 = self.sm16[0:NB, :, :]
        kb.op("dve", lambda e, ps=ps: e.tensor_tensor(out=T1, in0=ps[0:NB, 0:16], in1=p16[:, 0, :], op=ALU.add), reads=[ps, self.sm16], writes=[sm])
        kb.op("act", lambda e: e.activation(out=T2, in_=T1, func=AF.Exp), reads=[sm], writes=[sm])
        kb.op("act", lambda e: e.activation(out=DT, in_=T2, func=AF.Ln, bias=1.0, scale=1.0), reads=[sm], writes=[sm])
        kb.op("dve", lambda e: e.tensor_tensor(out=T1, in0=DT, in1=p16[:, 1, :], op=ALU.mult), reads=[sm, self.sm16], writes=[sm])
        kb.op("act", lambda e: e.activation(out=DA, in_=T1, func=AF.Exp), reads=[sm], writes=[sm])
        xs = self.sxconv[:, 0:1024]
        kb.op("dve", lambda e: e.tensor_tensor(out=self.d_xdt[:].rearrange("p (hh g q) -> p g hh q", hh=8, g=2), in0=xs.rearrange("p (g hh q) -> p g hh q", g=2, hh=8),
                                                in1=DT.rearrange("p (g hh) -> p g hh", g=2).unsqueeze(3).broadcast_to([NB, 2, 8, 64]), op=ALU.mult), reads=[self.sxconv, sm], writes=[self.d_xdt])
        pst = self.nps()
        kb.mm([lambda pe, hh=hh: pe.transpose(out=pst[:, hh * NB:(hh + 1) * NB], in_=self.d_xdt[:, hh * 128:(hh + 1) * 128], identity=self.ident_f[0:NB, 0:NB]) for hh in range(8)],
              reads=[self.d_xdt, self.cf], writes=[pst])
        kb.op("act", lambda e: e.copy(out=self.d_xdtT[:].rearrange("p b hh -> p hh b"), in_=pst[:, 0:8 * NB].rearrange("p (hh b) -> p hh b", hh=8)),
              reads=[pst], writes=[self.d_xdtT])
        eye = self.s_sel[:, 0:NB * 128].rearrange("p (b k) -> p b k", b=NB)[:, :, 0]
        gsel = self.s_sel[:, NB * 128:NB * 128 + 256].rearrange("p (g k) -> p g k", g=2)
        kb.op("dve", lambda e: e.tensor_tensor(out=self.d_RA[:], in0=DA.rearrange("p (g hh) -> p g hh", g=2).unsqueeze(2).broadcast_to([NB, 2, NB, 8]),
                                                in1=eye.unsqueeze(1).unsqueeze(3).broadcast_to([NB, 2, NB, 8]), op=ALU.mult), reads=[sm, self.s_sel], writes=[self.d_RA])
        psa = self.nps()
        kb.mm([lambda pe, g=g: pe.matmul(psa[:, 0:NB * 8], gsel[:, g, :], self.d_RA[:, g, :, :], start=(g == 0), stop=(g == 1)) for g in range(2)],
              reads=[self.s_sel, self.d_RA], writes=[psa])
        kb.op("act", lambda e: e.copy(out=self.d_dA[:], in_=psa[:, 0:NB * 8].rearrange("p (b hh) -> p b hh", b=NB)), reads=[psa], writes=[self.d_dA])
        Bc = self.sxconv[:, 1024:1536].rearrange("p (t g n) -> p t g n", t=2, g=2)
        for sg in range(4):
            bs = slice(sg * 4, (sg + 1) * 4)
            H = self.d_h
            for g in range(2):
                for bb in range(4):
                    kb.dma("sp", H[g * 64:(g + 1) * 64, bb, :, :], self.st_ssm.ap()[l, sg * 4 + bb, g * 8:(g + 1) * 8].rearrange("hh p n -> p hh n"), writes=[H])
            for t in range(2):
                kb.op("dve", lambda e, sg=sg, t=t: e.tensor_tensor(out=self.d_R[:], in0=Bc[:, t, :, :].unsqueeze(2).broadcast_to([NB, 2, 4, 128]),
                                                                    in1=eye[:, sg * 4:(sg + 1) * 4].unsqueeze(1).unsqueeze(3).broadcast_to([NB, 2, 4, 128]), op=ALU.mult),
                      reads=[self.sxconv, self.s_sel], writes=[self.d_R])
                psb_ = self.nps()
                kb.mm([lambda pe, g=g, psb_=psb_: pe.matmul(psb_[:, :], gsel[:, g, :], self.d_R[:, g, :, :], start=(g == 0), stop=(g == 1)) for g in range(2)],
                      reads=[self.s_sel, self.d_R], writes=[psb_])
                kb.op("act", lambda e, t=t, psb_=psb_: e.copy(out=self.d_BC[:, t, :, :], in_=psb_[:, :].rearrange("p (b n) -> p b n", b=4)), reads=[psb_], writes=[self.d_BC])
            kb.op("dve", lambda e, bs=bs: e.tensor_tensor(out=H[:], in0=H[:], in1=self.d_dA[:, bs, :].unsqueeze(3).broadcast_to([128, 4, 8, 128]), op=ALU.mult),
                  reads=[H, self.d_dA], writes=[H])
            kb.op("dve", lambda e, bs=bs: e.tensor_tensor(out=self.d_t[:], in0=self.d_xdtT[:, bs, :].unsqueeze(3).broadcast_to([128, 4, 8, 128]),
                                                           in1=self.d_BC[:, 0, :, :].unsqueeze(2).broadcast_to([128, 4, 8, 128]), op=ALU.mult),
                  reads=[self.d_xdtT, self.d_BC], writes=[self.d_t])
            kb.op("dve", lambda e: e.tensor_tensor(out=H[:], in0=H[:], in1=self.d_t[:], op=ALU.add), reads=[H, self.d_t], writes=[H])
            for g in range(2):
                for bb in range(4):
                    kb.dma("sp", self.ssm_s.ap()[l, sg * 4 + bb, g * 8:(g + 1) * 8].rearrange("hh p n -> p hh n"), H[g * 64:(g + 1) * 64, bb, :, :], reads=[H], is_output=True)
            kb.op("dve", lambda e: e.tensor_tensor(out=self.d_t[:], in0=H[:], in1=self.d_BC[:, 1, :, :].unsqueeze(2).broadcast_to([128, 4, 8, 128]), op=ALU.mult),
                  reads=[H, self.d_BC], writes=[self.d_t])
            kb.op("dve", lambda e, bs=bs: e.tensor_reduce(out=self.d_yT[:, bs, :], in_=self.d_t[:], axis=AX.X, op=ALU.add), reads=[self.d_t], writes=[self.d_yT])
        psy = [self.nps(), self.nps()]
        for half in range(2):
            kb.mm([lambda pe, hh=hh, half=half: pe.transpose(out=psy[half][0:NB, (hh % 4) * 128:(hh % 4 + 1) * 128], in_=self.d_yT[:, :, hh], identity=self.ident_f)
                   for hh in range(half * 4, half * 4 + 4)], reads=[self.d_yT, self.cf], writes=[psy[half]])
            yv = self.d_y[:].rearrange("p (g hh q) -> p hh g q", g=2, hh=8)
            kb.op("act", lambda e, half=half, yv=yv: e.copy(out=yv[:, half * 4:half * 4 + 4, :, :], in_=psy[half][0:NB, :].rearrange("p (hh g q) -> p hh g q", hh=4, g=2)),
                  reads=[psy[half]], writes=[self.d_y])
        kb.op("dve", lambda e: e.tensor_tensor(out=self.d_y2[:].rearrange("p (h q) -> p h q", h=16), in0=xs.rearrange("p (h q) -> p h q", h=16),
                                                in1=p16[:, 2, :].unsqueeze(2).broadcast_to([NB, 16, 64]), op=ALU.mult), reads=[self.sxconv, self.sm16], writes=[self.d_y2])
        kb.op("dve", lambda e: e.tensor_tensor(out=self.d_y[:], in0=self.d_y[:], in1=self.d_y2[:], op=ALU.add), reads=[self.d_y, self.d_y2], writes=[self.d_y])
        kb.op("dve", lambda e: e.tensor_tensor(out=self.d_y[:], in0=self.d_y[:], in1=self.d_z[:], op=ALU.mult), reads=[self.d_y, self.d_z], writes=[self.d_y])
        for g in range(2):
            gs = slice(g * 512, (g + 1) * 512)
            kb.op("act", lambda e, g=g, gs=gs: e.activation(out=self.d_y2[:, gs], in_=self.d_y[:, gs], func=AF.Square, accum_out=self.d_ssq[:, g:g + 1]),
                  reads=[self.d_y], writes=[self.d_y2, self.d_ssq])
        kb.op("act", lambda e: e.activation(out=self.d_ssq[:, 2:4], in_=self.d_ssq[:, 0:2], func=AF.Ln, bias=EPS, scale=1.0 / 512), reads=[self.d_ssq], writes=[self.d_ssq])
        kb.op("act", lambda e: e.activation(out=self.d_ssq[:, 2:4], in_=self.d_ssq[:, 2:4], func=AF.Exp, scale=-0.5), reads=[self.d_ssq], writes=[self.d_ssq])
        for g in range(2):
            gs = slice(g * 512, (g + 1) * 512)
            kb.op("dve", lambda e, g=g, gs=gs: e.scalar_tensor_tensor(out=self.d_ynb[:, gs], in0=self.d_y[:, gs], scalar=self.d_ssq[:, 2 + g:3 + g],
                                                                       in1=self.d_nw[:, gs], op0=ALU.mult, op1=ALU.mult),
                  reads=[self.d_y, self.d_ssq, self.d_nw], writes=[self.d_ynb])
        self.s_tokmajor_to_T(self.d_ynb, 1024, self.s_yT)

    def s_swa(self, l):
        self.kb.phase = 's_swa'
        kb = self.kb
        self.fence()
        kb.dma("sp", self.a_K[:], self.ck.ap()[l].rearrange("b k e -> k b e"), writes=[self.a_K])
        kb.dma("sp", self.a_V[:], self.cv.ap()[l].rearrange("b k e -> k b e"), writes=[self.a_V])
        kb.dma("sp", self.a_es[:], self.sinks.ap()[l:l + 1, :].partition_broadcast(NB), writes=[self.a_es])
        kb.op("act", lambda e: e.activation(out=self.a_es[:], in_=self.a_es[:], func=AF.Exp), reads=[self.a_es], writes=[self.a_es])
        kb.dma("sp", self.k_s.ap()[l, :, 0:127, :], self.ck.ap()[l, :, 1:128, :], is_output=True)
        kb.dma("sp", self.v_s.ap()[l, :, 0:127, :], self.cv.ap()[l, :, 1:128, :], is_output=True)
        Wt, Wv = self.wload(self.wsrc(self.w_in, l, D, OQC, 512), 8, 512)
        ps = self.sproj(Wt, Wv, 512)
        kb.op("act", lambda e, ps=ps: e.copy(out=self.a_q[:], in_=ps[0:NB, :]), reads=[ps], writes=[self.a_q])
        Wt, Wv = self.wload(self.wsrc(self.w_in, l, D, OKC, 256), 8, 256)
        ps = self.sproj(Wt, Wv, 256)
        kb.op("act", lambda e, ps=ps: e.copy(out=self.a_kn[:], in_=ps[0:NB, 0:128]), reads=[ps], writes=[self.a_kn])
        kb.op("act", lambda e: e.activation(out=self.a_vn[:, :, 64:65], in_=self.cf[0:NB, 0:2].unsqueeze(2), func=AF.Identity, scale=0.0, bias=1.0), reads=[self.cf], writes=[self.a_vn])
        kb.op("act", lambda e, ps=ps: e.copy(out=self.a_vn[:, :, 0:64], in_=ps[0:NB, 128:256].rearrange("p (g e) -> p g e", g=2)), reads=[ps], writes=[self.a_vn])
        kb.dma("sp", self.k_s.ap()[l, :, 127, :], self.a_kn[:], reads=[self.a_kn], is_output=True)
        kb.dma("sp", self.v_s.ap()[l, :, 127, :].rearrange("b (g e) -> b g e", g=2), self.a_vn[:, :, 0:64], reads=[self.a_vn], is_output=True)
        kb.op("act", lambda e: e.activation(out=self.a_Vb[:, :, :, 64:65], in_=self.cf[:, 0:2 * NB].rearrange("p (b g o) -> p b g o", b=NB, g=2), func=AF.Identity, scale=0.0, bias=1.0),
              reads=[self.cf], writes=[self.a_Vb])
        kb.op("act", lambda e: e.copy(out=self.a_Vb[:, :, :, 0:64], in_=self.a_V[:].rearrange("p b (g e) -> p b g e", g=2)), reads=[self.a_V], writes=[self.a_Vb])
        selq = self.s_sel[:, 0:NB * 128].rearrange("p (b k) -> p b k", b=NB)
        sv = self.a_s[:].rearrange("p b (half j) -> p b j half", half=2)
        for b in range(NB):
            psq = self.nps()
            kb.mm([lambda pe, b=b, psq=psq: pe.matmul(psq[:, :], selq[:, b, :], self.a_q[:], start=True, stop=True)], reads=[self.s_sel, self.a_q], writes=[psq])
            kb.op("dve", lambda e, b=b, psq=psq: e.tensor_tensor(out=self.a_pr[:], in0=psq[:, :].rearrange("p (j g e) -> p j g e", j=4, g=2),
                                                                 in1=self.a_K[:, b, :].rearrange("p (g e) -> p g e", g=2).unsqueeze(1).broadcast_to([128, 4, 2, 64]), op=ALU.mult),
                  reads=[psq, self.a_K], writes=[self.a_pr])
            kb.op("dve", lambda e, b=b: e.tensor_reduce(out=sv[:, b, :, :], in_=self.a_pr[:], axis=AX.X, op=ALU.add), reads=[self.a_pr], writes=[self.a_s])
        kb.op("dve", lambda e: e.scalar_tensor_tensor(out=self.a_s[:], in0=self.a_s[:], scalar=0.125, in1=self.btab[:, 0, :, 0].unsqueeze(1).broadcast_to([128, NB, 8]),
                                                      op0=ALU.mult, op1=ALU.add), reads=[self.a_s, self.btab], writes=[self.a_s])
        kb.op("act", lambda e: e.activation(out=self.a_p[:], in_=self.a_s[:], func=AF.Exp), reads=[self.a_s], writes=[self.a_p])
        kb.op("dve", lambda e: e.tensor_tensor(out=self.a_Pm[:], in0=self.a_p[:].unsqueeze(3).broadcast_to([128, NB, 8, NB]),
                                                in1=self.s_eye[:].rearrange("p (a b) -> p a b", a=NB).unsqueeze(2).broadcast_to([128, NB, 8, NB]), op=ALU.mult),
              reads=[self.a_p, self.s_eye], writes=[self.a_Pm])
        pso = [self.nps(), self.nps()]
        for g in range(2):
            fns = []
            for h4 in range(4):
                for b in range(NB):
                    fns.append(lambda pe, g=g, h4=h4, b=b: pe.matmul(pso[g][0:NB, h4 * 65:(h4 + 1) * 65], self.a_Pm[:, b, g * 4 + h4, :], self.a_Vb[:, b, g, 0:65],
                                                                    start=(b == 0), stop=(b == NB - 1)))
            kb.mm(fns, reads=[self.a_Pm, self.a_Vb], writes=[pso[g]])
            kb.op("act", lambda e, g=g: e.copy(out=self.a_ou[:, g * 4:(g + 1) * 4, :], in_=pso[g][0:NB, 0:260].rearrange("p (h e) -> p h e", h=4)), reads=[pso[g]], writes=[self.a_ou])
        qv = self.a_q[:].rearrange("p (j g e) -> p j g e", j=4, g=2)
        kb.op("dve", lambda e: e.tensor_tensor(out=self.a_sn[:], in0=qv, in1=self.a_kn[:].rearrange("p (g e) -> p g e", g=2).unsqueeze(1).broadcast_to([NB, 4, 2, 64]), op=ALU.mult),
              reads=[self.a_q, self.a_kn], writes=[self.a_sn])
        kb.op("dve", lambda e: e.tensor_reduce(out=self.a_s8[:].rearrange("p (half j) -> p j half", half=2), in_=self.a_sn[:], axis=AX.X, op=ALU.add), reads=[self.a_sn], writes=[self.a_s8])
        kb.op("dve", lambda e: e.scalar_tensor_tensor(out=self.a_s8[:], in0=self.a_s8[:], scalar=0.125, in1=self.s_b0[:], op0=ALU.mult, op1=ALU.add), reads=[self.a_s8, self.s_b0], writes=[self.a_s8])
        kb.op("act", lambda e: e.activation(out=self.a_pn[:], in_=self.a_s8[:], func=AF.Exp), reads=[self.a_s8], writes=[self.a_pn])
        kb.op("dve", lambda e: e.tensor_tensor(out=self.a_t[:].rearrange("p (g j) e -> p g j e", g=2), in0=self.a_pn[:].rearrange("p (g j) -> p g j", g=2).unsqueeze(3).broadcast_to([NB, 2, 4, 65]),
                                                in1=self.a_vn[:, :, 0:65].unsqueeze(2).broadcast_to([NB, 2, 4, 65]), op=ALU.mult), reads=[self.a_pn, self.a_vn], writes=[self.a_t])
        kb.op("dve", lambda e: e.tensor_tensor(out=self.a_ou[:], in0=self.a_ou[:], in1=self.a_t[:], op=ALU.add), reads=[self.a_ou, self.a_t], writes=[self.a_ou])
        kb.op("dve", lambda e: e.tensor_tensor(out=self.a_den[:], in0=self.a_ou[:, :, 64], in1=self.a_es[:], op=ALU.add), reads=[self.a_ou, self.a_es], writes=[self.a_den])
        kb.op("dve", lambda e: e.reciprocal(out=self.a_den[:], in_=self.a_den[:]), reads=[self.a_den], writes=[self.a_den])
        kb.op("dve", lambda e: e.tensor_tensor(out=self.a_ob[:], in0=self.a_ou[:, :, 0:64], in1=self.a_den[:].unsqueeze(2).broadcast_to([NB, 8, 64]), op=ALU.mult),
              reads=[self.a_ou, self.a_den], writes=[self.a_ob])
        self.s_tokmajor_to_T(self.a_ob, 512, self.s_ocT) if False else None
        obv = self.a_ob[:].rearrange("p h e -> p (h e)")
        self.s_transp(self.a_ob, [obv[:, j * 128:(j + 1) * 128] for j in range(4)], self.s_ocT, self.s_ocT[:, :, :])

    def s_ln(self, l, gi, pss_fn):
        kb = self.kb
        kb.dma("sp", self.m_g[:], self.ln_n.ap()[l, gi].partition_broadcast(NB), writes=[self.m_g])
        kb.dma("sp", self.m_bb[:], self.ln_n.ap()[l, gi + 1].partition_broadcast(NB), writes=[self.m_bb])
        for j in range(2):
            js = slice(j * 512, (j + 1) * 512)
            ps = pss_fn(j)
            kb.op("dve", lambda e, js=js, ps=ps: e.scalar_tensor_tensor(out=self.sx[:, js], in0=self.sx[:, js], scalar=ALPHA, in1=ps[0:NB, :], op0=ALU.mult, op1=ALU.add),
                  reads=[self.sx, ps], writes=[self.sx])
            kb.op("dve", lambda e, j=j, js=js: e.bn_stats(out=self.m_st[:, j, :], in_=self.sx[:, js]), reads=[self.sx], writes=[self.m_st])
        kb.op("dve", lambda e: e.bn_aggr(out=self.m_mv[:], in_=self.m_st[:].rearrange("p a b -> p (a b)")), reads=[self.m_st], writes=[self.m_mv])
        kb.op("act", lambda e: e.activation(out=self.m_rs[:, 0:1], in_=self.m_mv[:, 1:2], func=AF.Ln, bias=EPS, scale=1.0), reads=[self.m_mv], writes=[self.m_rs])
        kb.op("act", lambda e: e.activation(out=self.m_rs[:, 1:2], in_=self.m_rs[:, 0:1], func=AF.Exp, scale=-0.5), reads=[self.m_rs], writes=[self.m_rs])
        kb.op("dve", lambda e: e.tensor_scalar(out=self.m_u[:], in0=self.sx[:], scalar1=self.m_mv[:, 0:1], scalar2=self.m_rs[:, 1:2], op0=ALU.subtract, op1=ALU.mult),
              reads=[self.sx, self.m_mv, self.m_rs], writes=[self.m_u])
        kb.op("dve", lambda e: e.tensor_tensor(out=self.m_u[:], in0=self.m_u[:], in1=self.m_g[:], op=ALU.mult), reads=[self.m_u, self.m_g], writes=[self.m_u])
        kb.op("dve", lambda e: e.tensor_tensor(out=self.sx[:], in0=self.m_u[:], in1=self.m_bb[:], op=ALU.add), reads=[self.m_u, self.m_bb], writes=[self.sx])
        self.s_refresh_xT()

    def s_mlp(self, l):
        self.kb.phase = 's_mlp'
        kb = self.kb
        self.fence()
        brs = [(self.w_br_ret, 512, 4, self.s_orT), (self.w_br_ssd, 1024, 8, self.s_yT), (self.w_br_swa, 512, 4, self.s_ocT)]
        for b, (wbr, K, kc, src) in enumerate(brs):
            for j in range(2):
                js = slice(j * 512, (j + 1) * 512)
                Gt, Gv = self.wload(self.wsrc(self.w_in, l, D, OGATE + b * 1024 + j * 512, 512), 8, 512)
                Bt, Bv = self.wload(self.wsrc(wbr, l, K, j * 512, 512), kc, 512)
                psG = self.sproj(Gt, Gv, 512)
                psB = self.sproj(Bt, Bv, 512, src=src, kc=kc)
                kb.op("act", lambda e, psG=psG: e.activation(out=self.m_sg[:], in_=psG[0:NB, :], func=AF.Sigmoid), reads=[psG], writes=[self.m_sg])
                if b == 0:
                    kb.op("dve", lambda e, js=js, psB=psB: e.tensor_tensor(out=self.m_acc[:, js], in0=self.m_sg[:], in1=psB[0:NB, :], op=ALU.mult), reads=[self.m_sg, psB], writes=[self.m_acc])
                else:
                    kb.op("dve", lambda e, psB=psB: e.tensor_tensor(out=self.m_t[:], in0=self.m_sg[:], in1=psB[0:NB, :], op=ALU.mult), reads=[self.m_sg, psB], writes=[self.m_t])
                    kb.op("dve", lambda e, js=js: e.tensor_tensor(out=self.m_acc[:, js], in0=self.m_acc[:, js], in1=self.m_t[:], op=ALU.add), reads=[self.m_acc, self.m_t], writes=[self.m_acc])
        kb.op("act", lambda e: e.copy(out=self.m_b[:], in_=self.m_acc[:]), reads=[self.m_acc], writes=[self.m_b])
        self.s_tokmajor_to_T(self.m_b, 1024, self.s_mT)

        def pss1(j):
            Wt, Wv = self.wload(self.wsrc(self.w_out, l, D, j * 512, 512), 8, 512)
            return self.sproj(Wt, Wv, 512, src=self.s_mT)
        self.s_ln(l, 0, pss1)
        for jg in range(6):
            n = 512 if jg < 5 else 256
            Gt, Gv = self.wload(self.wsrc(self.w_gate, l, D, jg * 512, n), 8, n)
            Ut, Uv = self.wload(self.wsrc(self.w_up, l, D, jg * 512, n), 8, n)
            psG = self.sproj(Gt, Gv, n)
            psU = self.sproj(Ut, Uv, n)
            kb.op("act", lambda e, psG=psG, n=n: e.activation(out=self.m_sg[:, 0:n], in_=psG[0:NB, 0:n], func=AF.Silu), reads=[psG], writes=[self.m_sg])
            kb.op("dve", lambda e, psU=psU, n=n, jg=jg: e.tensor_tensor(out=self.m_hb[:, jg * 512:jg * 512 + n], in0=self.m_sg[:, 0:n], in1=psU[0:NB, 0:n], op=ALU.mult),
                  reads=[self.m_sg, psU], writes=[self.m_hb])
        self.s_tokmajor_to_T(self.m_hb, 2816, self.s_hT)
        W = {}

        def pss2(j):
            ps = self.nps()
            for t in range(4):
                m = j * 4 + t
                Wt, Wv = self.wload(self.wsrc_down(l, m), 22, 128)
                self.kb.mm([lambda pe, k=k, t=t, Wv=Wv: pe.matmul(ps[0:NB, t * 128:(t + 1) * 128], self.s_hT[:, k, :], Wv[:, k, :], start=(k == 0), stop=(k == 21))
                            for k in range(22)], reads=[self.s_hT, Wt], writes=[ps])
            return ps
        self.s_ln(l, 2, pss2)


def consts(L):
    f32 = np.float32
    pos = np.arange(L, dtype=f32)
    inv = (np.float32(10000.0) ** (-np.arange(64, dtype=f32) / np.float32(64))).astype(f32)
    ang = (pos[:, None] * inv[None, :]).astype(f32)
    cos, sin = np.cos(ang).astype(f32), np.sin(ang).astype(f32)
    s = f32(128 ** -0.5)
    c_rot = np.concatenate([cos, sin, cos * s, sin * s], axis=1).astype(f32)
    i = np.arange(128, dtype=np.float64)
    lg = np.log(1.0 - 2.0 ** (-5.0 - np.arange(4, dtype=np.float64)))
    rel = i[None, :] - i[:, None]
    dmatT = np.where(rel[:, None, :] >= 0, np.exp(lg[None, :, None] * np.maximum(rel[:, None, :], 0)), 0.0)
    kdec = np.exp(lg[None, :] * (127 - i)[:, None])
    qdec = np.exp(lg[None, :] * (i + 1.0)[:, None])
    k = np.arange(128)[:, None]; q = np.arange(128)[None, :]
    m0 = np.where(q > k, NEG, 0.0)
    m1 = np.where(q < k, NEG, 0.0)
    c_f32 = np.concatenate([np.eye(128), np.triu(np.ones((128, 128))), np.ones((128, 128)), np.zeros((128, 128)),
                            dmatT.reshape(128, 512), kdec, qdec, m0, m1], axis=1).astype(f32)
    def bucket(dist):
        df = np.maximum(dist, 1).astype(f32)
        large = 16 + (np.log(df / f32(16)).astype(f32) / f32(math.log(8.0)) * f32(16)).astype(np.int32)
        large = np.minimum(large, 31)
        return np.where(dist < 16, dist, large)
    oh = np.zeros((128, 2, 128, 32), f32)
    d0 = q - k + 128
    d1 = q - k
    for hf, dd in ((0, d0), (1, d1)):
        valid = (dd >= 0) & (dd <= 128)
        b = bucket(np.maximum(dd, 0))
        kk, qq = np.nonzero(valid)
        oh[kk, hf, qq, b[kk, qq]] = 1.0
    c_oh = oh.reshape(128, -1).astype(ml_dtypes.bfloat16)
    return c_rot, c_f32, c_oh


def qc_perm():
    idx = np.arange(8464)
    base = 4624
    new = []
    for t in range(4):
        new += list(range(base + t * 64, base + (t + 1) * 64)) + list(range(base + (4 + t) * 64, base + (5 + t) * 64))
    idx[base:base + 512] = np.array(new)
    return idx


def prep_weights(inp, depth):
    f = lambda a: np.ascontiguousarray(a, dtype=np.float32)
    dp = depth
    w = {}
    w["w_in"] = f(inp["w_in"][:dp][:, :, qc_perm()].reshape(dp * 1024, 8464))
    w["conv_w"] = f(np.transpose(inp["conv_w"][:dp].reshape(dp, 4, 12, 128), (0, 3, 2, 1)))
    w["conv_b"] = f(np.transpose(inp["conv_b"][:dp].reshape(dp, 12, 128), (0, 2, 1)))
    for n in ("dt_bias", "a_log", "d_skip", "ssd_norm_w", "sinks"):
        w[n] = f(inp[n][:dp])
    w["rel_bias"] = f(inp["rel_bias"].reshape(1, 256))
    w["w_br_ret"] = f(inp["w_br_ret"][:dp].reshape(dp * 512, 1024))
    w["w_br_ssd"] = f(inp["w_br_ssd"][:dp].reshape(dp * 1024, 1024))
    w["w_br_swa"] = f(inp["w_br_swa"][:dp].reshape(dp * 512, 1024))
    w["w_out"] = f(inp["w_out"][:dp].reshape(dp * 1024, 1024))
    lnp = np.stack([np.transpose(inp[n][:dp].reshape(dp, 8, 128), (0, 2, 1)) for n in ("ln1_g", "ln1_b", "ln2_g", "ln2_b")], axis=2)
    w["lnp"] = f(lnp)
    w["w_ffn_gate"] = f(inp["w_ffn_gate"][:dp].reshape(dp * 1024, 2816))
    w["w_ffn_up"] = f(inp["w_ffn_up"][:dp].reshape(dp * 1024, 2816))
    w["w_ffn_down"] = f(np.transpose(inp["w_ffn_down"][:dp].reshape(dp, 22, 128, 8, 128), (0, 3, 2, 1, 4)).reshape(dp * 8 * 128, 2816))
    return w


def core_inputs(inp, core, depth, TB, do_sample=True, x_prompt_row=None, sample_rows=None, w=None, nblk=None):
    f = lambda a: np.ascontiguousarray(a, dtype=np.float32)
    dp = depth
    m = dict(w if w is not None else prep_weights(inp, depth))
    L = (nblk or 1) * TB
    xr = core if x_prompt_row is None else x_prompt_row
    m["xp"] = f(inp["x_prompt"][xr, :L])
    c_rot, c_f32, c_oh = consts(max(L, 8192 + 1))
    m["c_rot"] = np.ascontiguousarray(c_rot[:L]); m["c_f32"] = c_f32; m["c_oh"] = c_oh
    if do_sample:
        sr = sample_rows if sample_rows is not None else slice(core * 16, core * 16 + 16)
        m["xs"] = f(inp["x_sample"][sr, 0])
        m["st_ret"] = f(inp["state_ret"][:dp, sr])
        m["st_ssm"] = f(inp["state_ssm"][:dp, sr])
        m["st_conv"] = f(inp["state_conv"][:dp, sr])
        m["ck"] = f(inp["cache_swa_k"][:dp, sr].reshape(dp, 16, 128, 128))
        m["cv"] = f(inp["cache_swa_v"][:dp, sr].reshape(dp, 16, 128, 128))
        m["c_rots"] = np.ascontiguousarray(np.broadcast_to(c_rot[8192:8193], (16, 256)))
        sel = np.zeros((16, 16, 128), np.float32)
        for b in range(16):
            sel[b, b, :] = 1.0
        gsel = np.zeros((16, 2, 128), np.float32)
        gsel[:, 0, 0:64] = 1.0; gsel[:, 1, 64:128] = 1.0
        m["c_sel"] = np.concatenate([sel.reshape(16, -1), gsel.reshape(16, -1)], axis=1)
        m["c_eye"] = np.ascontiguousarray(np.broadcast_to(np.eye(16, dtype=np.float32).reshape(1, 256), (128, 256)))
        m["conv_w_n"] = f(inp["conv_w"][:dp].reshape(dp, 1, 4 * 1536))
        m["conv_b_n"] = f(inp["conv_b"][:dp].reshape(dp, 1, 1536))
        m["ln_n"] = f(np.stack([inp[n][:dp] for n in ("ln1_g", "ln1_b", "ln2_g", "ln2_b")], axis=1).reshape(dp, 4, 1, 1024))
    return m


def kernel(**inputs):
    inp = {k: np.asarray(v) for k, v in inputs.items()}
    dp, nblk = 4, 4
    g = GenS(depth=dp, nblk=nblk, do_sample=True)
    w = prep_weights(inp, dp)
    in_maps = []
    for c in range(8):
        m = core_inputs(inp, c, dp, TB, do_sample=True, w=w, nblk=nblk)
        in_maps.append({k: m[k] for k in g.ins})
    res = run_bass_kernel_spmd(g.nc, in_maps, core_ids=list(range(8)))
    R = res.results
    st = lambda name, axis: np.stack([np.asarray(r[name]) for r in R], axis=axis)
    cat = lambda name, axis: np.concatenate([np.asarray(r[name]) for r in R], axis=axis)
    y_prompt = st("yp", 0).astype(np.float32)
    y_sample = cat("ys", 0).reshape(128, 1, 1024).astype(np.float32)
    ret_p = st("ret_p", 1)
    ssm_p = st("ssm_p", 1).reshape(dp, 8, 16, 64, 128)
    conv_p = st("conv_p", 1)
    k_p = st("k_p", 1).reshape(dp, 8, 128, 2, 64)
    v_p = st("v_p", 1).reshape(dp, 8, 128, 2, 64)
    ret_s = cat("ret_s", 1)
    ssm_s = cat("ssm_s", 1)
    conv_s = cat("conv_s", 1)
    k_s = cat("k_s", 1).reshape(dp, 128, 128, 2, 64)
    v_s = cat("v_s", 1).reshape(dp, 128, 128, 2, 64)
    outs = (y_prompt, y_sample, ret_p, ssm_p, conv_p, k_p, v_p, ret_s, ssm_s, conv_s, k_s, v_s)
    return tuple(np.ascontiguousarray(o, dtype=np.float32) for o in outs)
```

```python
import math
import numpy as np
import ml_dtypes
from concourse.bass_utils import run_bass_kernel_spmd
import concourse.bass as bass
import concourse.mybir as mybir

F32 = mybir.dt.float32
BF16 = mybir.dt.bfloat16
I32 = mybir.dt.int32
AF = mybir.ActivationFunctionType
ALU = mybir.AluOpType
AX = mybir.AxisListType


class Dep:
    __slots__ = ("w", "r")

    def __init__(self):
        self.w = None
        self.r = {}


class Tile:
    __slots__ = ("h", "deps", "name", "psum")

    def __init__(self, h, deps=None, name=None):
        self.psum = False
        self.h = h
        self.deps = deps if deps is not None else [Dep()]
        self.name = name

    def __getitem__(self, k):
        return self.h[k]

    def sub(self, i):
        return Tile(self.h, [self.deps[i]], self.name)


class KB:
    ENG = ("pe", "act", "dve", "pool", "sp")

    def __init__(self, n_dma_sems=30):
        nc = bass.Bass("TRN2", target_bir_lowering=False)
        self.nc = nc
        self.eng = {"pe": nc.tensor, "act": nc.scalar, "dve": nc.vector, "pool": nc.gpsimd, "sp": nc.sync}
        self.sems = {}
        self.tick = {}
        for e in self.ENG:
            self.sems[e] = nc.alloc_semaphore("s_" + e)
            self.tick[e] = 0
        self.known = {e: {} for e in self.ENG}
        self.dpool = {}
        for q in ("sp", "pool", "act"):
            lst = []
            for i in range(n_dma_sems):
                key = "d_%s_%d" % (q, i)
                self.sems[key] = nc.alloc_semaphore(key)
                self.tick[key] = 0
                lst.append(key)
            self.dpool[q] = [lst, 0]
        self.n_ins = 0
        self.n_wait = 0
        self.strict = True
        self.attach_waits = True
        self.phase = 'init'
        self.pe_log = []
        self.out_events = []

    def sb(self, name, shape, dt, ncell=1):
        h = self.nc.alloc_sbuf_tensor(name, list(shape), dt)
        return Tile(h, [Dep() for _ in range(ncell)], name)

    def ps(self, name, shape=(128, 512), dt=F32):
        h = self.nc.alloc_psum_tensor(name, list(shape), dt)
        t = Tile(h, None, name)
        t.psum = True
        return t

    def dram(self, name, shape, dt, kind):
        h = self.nc.dram_tensor(name, list(shape), dt, kind=kind)
        return h

    def _needs(self, reads, writes, e=None):
        needs = {}

        def add(ev):
            if ev is None:
                return
            k, v = ev
            if needs.get(k, 0) < v:
                needs[k] = v

        for t in reads:
            for d in t.deps:
                add(d.w)
                if t.psum:
                    for k, v in d.r.items():
                        if k != e:
                            add((k, v))
        for t in writes:
            for d in t.deps:
                if d.w is not None and (self.strict or d.w[0] != e):
                    add(d.w)
                for k, v in d.r.items():
                    if self.strict or k != e:
                        add((k, v))
        return needs

    def _emit_waits(self, e, needs, attach=False):
        kn = self.known[e]
        eo = self.eng[e]
        todo = []
        for k, v in needs.items():
            if e == "pe" and k == "pe":
                continue
            if kn.get(k, 0) < v:
                todo.append((k, v))
                kn[k] = v
        held = None
        if attach and todo and self.attach_waits:
            held = todo.pop()
        for k, v in todo:
            eo.wait_ge(self.sems[k], v)
            self.n_wait += 1
        return held

    def _attach(self, ins, held):
        if held is not None:
            ins._wait_ge(self.sems[held[0]], held[1])

    def _record(self, ev, reads, writes):
        k, v = ev
        for t in reads:
            for d in t.deps:
                d.r[k] = v
        for t in writes:
            for d in t.deps:
                d.w = ev
                d.r = {}

    def op(self, e, fn, reads=(), writes=()):
        needs = self._needs(reads, writes, e)
        held = self._emit_waits(e, needs, attach=True)
        ins = fn(self.eng[e])
        self._attach(ins, held)
        self.tick[e] += 1
        ins.then_inc(self.sems[e], 1)
        self._record((e, self.tick[e]), reads, writes)
        self.n_ins += 1
        return ins

    def mm(self, fns, reads=(), writes=()):
        needs = self._needs(reads, writes, "pe")
        held = self._emit_waits("pe", needs, attach=True)
        ins = None
        for fn in fns:
            ins = fn(self.eng["pe"])
            if held is not None:
                self._attach(ins, held)
                held = None
            self.n_ins += 1
            self.pe_log.append(self.phase)
        self.tick["pe"] += 1
        ins.then_inc(self.sems["pe"], 1)
        self._record(("pe", self.tick["pe"]), reads, writes)
        return ins

    def dma(self, q, out_ap, in_ap, reads=(), writes=(), is_output=False, **kw):
        lst, idx = self.dpool[q]
        key = lst[idx % len(lst)]
        self.dpool[q][1] = idx + 1
        needs = self._needs(reads, writes)
        if self.tick[key] > 0:
            if needs.get(key, 0) < self.tick[key]:
                needs[key] = self.tick[key]
        held = self._emit_waits(q, needs, attach=True)
        ins = self.eng[q].dma_start(out=out_ap, in_=in_ap, **kw)
        self._attach(ins, held)
        self.tick[key] += 16
        ins.then_inc(self.sems[key], 16)
        ev = (key, self.tick[key])
        self._record(ev, reads, writes)
        if is_output:
            self.out_events.append(ev)
        self.n_ins += 1
        return ins

    def finish(self):
        needs = {}
        for k, v in self.out_events:
            if needs.get(k, 0) < v:
                needs[k] = v
        for q in self.dpool:
            for key in self.dpool[q][0]:
                if self.tick[key] > 0:
                    needs[key] = max(needs.get(key, 0), self.tick[key])
        for e in ("pe", "act", "dve", "pool"):
            if self.tick[e] > 0:
                needs[e] = self.tick[e]
        self._emit_waits("sp", needs)


D = 1024
NIN = 8464
DFF = 2816
ALPHA = 8 ** 0.25
EPS = 1e-5
TB = 512
NEG = -30000.0
OQ, OK_, OV, OG, OZ, OXBC, ODT, OQC, OKC, OVC, OGATE = 0, 512, 1024, 1536, 2048, 3072, 4608, 4624, 5136, 5264, 5392


class Gen:
    def __init__(self, depth=4, nblk=4, do_sample=True, dbg=None):
        self.depth, self.nblk, self.do_sample = depth, nblk, do_sample
        self.kb = kb = KB()
        self.nc = nc = kb.nc
        self.L = nblk * TB
        self.dbg = dbg or {}
        self.outs = []
        self.ins = {}
        self._decl()
        self._alloc()
        self._setup_consts()
        if self.dbg.get("setup_only"):
            kb.finish()
            return
        for bi in self.dbg.get('blocks', range(nblk)):
            self.block(bi)
        if self.do_sample and hasattr(self, "sample_pass"):
            self.sample_pass()
        kb.finish()

    def din(self, name, shape, dt=F32):
        h = self.nc.dram_tensor(name, list(shape), dt, kind="ExternalInput")
        self.ins[name] = (tuple(shape), dt)
        return h

    def dout(self, name, shape, dt=F32):
        h = self.nc.dram_tensor(name, list(shape), dt, kind="ExternalOutput")
        self.outs.append(name)
        return h

    def _decl(self):
        dp, L = self.depth, self.L
        d = self.din
        self.xp = d("xp", [L, D])
        self.w_in = d("w_in", [dp * D, NIN])
        self.conv_w = d("conv_w", [dp, 128, 12, 4])
        self.conv_b = d("conv_b", [dp, 128, 12])
        self.dt_bias = d("dt_bias", [dp, 16])
        self.a_log = d("a_log", [dp, 16])
        self.d_skip = d("d_skip", [dp, 16])
        self.norm_w = d("ssd_norm_w", [dp, 1024])
        self.sinks = d("sinks", [dp, 8])
        self.rel_bias = d("rel_bias", [1, 256])
        self.w_br_ret = d("w_br_ret", [dp * 512, D])
        self.w_br_ssd = d("w_br_ssd", [dp * 1024, D])
        self.w_br_swa = d("w_br_swa", [dp * 512, D])
        self.w_out = d("w_out", [dp * D, D])
        self.lnp_d = d("lnp", [dp, 128, 4, 8])
        self.w_gate = d("w_ffn_gate", [dp * D, DFF])
        self.w_up = d("w_ffn_up", [dp * D, DFF])
        self.w_down = d("w_ffn_down", [dp * 8 * 128, DFF])
        self.c_rot = d("c_rot", [L, 256])
        self.c_f32 = d("c_f32", [128, 128 * 4 + 4 * 128 + 8 + 2 * 128])
        self.c_oh = d("c_oh", [128, 2 * 128 * 32], BF16)
        o = self.dout
        self.yp = o("yp", [L, D])
        self.ret_p = o("ret_p", [dp, 4, 128, 128])
        self.ssm_p = o("ssm_p", [dp, 1024, 128])
        self.conv_p = o("conv_p", [dp, 3, 1536])
        self.k_p = o("k_p", [dp, 128, 128])
        self.v_p = o("v_p", [dp, 128, 128])

    def _alloc(self):
        kb, dp = self.kb, self.depth
        sb = kb.sb
        self.xr_off = self.nc.bump_sbuf(16384 + 8192)[0]
        self.xr = Tile(self.nc.alloc_sbuf_tensor_at("xr", [128, 8, TB], F32, offset=self.xr_off), None, "xr")
        self.xb = Tile(self.nc.alloc_sbuf_tensor_at("xb", [128, 8, TB], BF16, offset=self.xr_off + 16384), None, "xb")
        self.NS = 4
        self.wr = [sb("wr%d" % i, [128, 4096], BF16) for i in range(self.NS)]
        self.wi = 0
        self.wcache = {}
        self.use_wcache = self.dbg.get('wcache', False)
        self.psf = [kb.ps("psf%d" % i, [128, 512], F32) for i in range(6)]
        self.psb = [kb.ps("psb%d" % i, [128, 1024], BF16) for i in range(2)]
        self.pfi = 0
        self.pbi = 0
        self.retS = [sb("retS%d" % l, [128, 4, 128], F32) for l in range(dp)]
        self.ssmS = [sb("ssmS%d" % l, [128, 1024], F32) for l in range(dp)]
        self.hist = [sb("hist%d" % l, [128, 12, 3], F32) for l in range(dp)]
        self.kprev = [sb("kprev%d" % l, [128, 128], BF16) for l in range(dp)]
        self.vprev = [sb("vprev%d" % l, [128, 2, 80], BF16) for l in range(dp)]
        self.cf = sb("cf", [128, 128 * 4 + 4 * 128 + 8 + 2 * 128], F32)
        self.id_b = sb("id_b", [128, 128], BF16)
        self.ones_b = sb("ones_b", [128, 128], BF16)
        self.negT = sb("negT", [128, 128], BF16)
        self.btab = sb("btab", [128, 2, 8, 128], F32)
        self.rot = sb("rot", [128, 4, 256], F32)
        self.cw = sb("cw", [128, 12, 4], F32); self.cb = sb("cb", [128, 12], F32)
        self.lnp = sb("lnp_s", [128, 4, 8], F32)
        self.sm16 = sb("sm16", [128, 3, 16], F32)
        self.esink = sb("esink", [128, 8], F32)
        self.orT = sb("orT", [128, 4, TB], BF16)
        self.yT = sb("yT", [128, 8, TB], BF16)
        self.ocT = sb("ocT", [128, 4, TB], BF16)
        self.mrg = sb("mrg", [128, 8, TB], BF16)
        self.xtok = [sb("xtok0", [128, 1024], F32)]
        self.xti = 0
        nc = self.nc
        ASZ = 64 * 1024
        self.abase = nc.bump_sbuf(ASZ)[0]
        self.asz = ASZ
        st = {"off": 0}

        def begin():
            st["off"] = 0

        def al(name, shape, dt):
            n = 1
            for x in shape[1:]:
                n *= x
            nb = n * (4 if dt == F32 else 2)
            nb = (nb + 31) // 32 * 32
            assert st["off"] + nb <= ASZ, (name, st["off"], nb)
            h = nc.alloc_sbuf_tensor_at(name, list(shape), dt, offset=self.abase + st["off"])
            st["off"] += nb
            return Tile(h, None, name)
        begin()
        self.s_oh = al("s_oh", [128, 8192], BF16); self.s_prod = al("s_prod", [128, 4096], F32); self.s_rb = al("s_rb", [128, 256], F32)
        begin()
        self.qrot = al("qrot", [128, 512], BF16); self.krot = al("krot", [128, 512], BF16)
        self.rtA = al("rtA", [128, 4, 2, 64], F32); self.rtB = al("rtB", [128, 4, 2, 64], F32)
        self.rtA2 = al("rtA2", [128, 4, 2, 64], F32); self.rtB2 = al("rtB2", [128, 4, 2, 64], F32); self.kraw = al("kraw", [128, 512], F32)
        self.v_r = al("v_r", [128, 4, 512], BF16)
        self.gsil = al("gsil", [128, 4, 512], F32)
        self.qT = al("qT", [128, 4, TB], BF16); self.kT = al("kT", [128, 4, TB], BF16)
        self.kdk = al("kdk", [128, 4, 4, 128], BF16)
        self.scm = al("scm", [128, 4, 128], BF16)
        self.otmp = al("otmp", [128, 4, 128], F32); self.o_r = al("o_r", [128, 4, 128], F32)
        self.st6 = al("st6", [128, 4, 6], F32); self.mv = al("mv", [128, 4, 2], F32)
        self.rs4 = al("rs4", [128, 4], F32); self.rs4b = al("rs4b", [128, 4], F32)
        self.og = al("og", [128, 4, 128], F32); self.ogb = al("ogb", [128, 512], BF16)
        self.retSb = al("retSb", [128, 4, 128], BF16)
        print("arena A", st["off"])
        begin()
        self.normw = al("normw", [128, 1024], F32)
        self.zs = al("zs", [128, 1024], F32)
        self._xbcT_off = self.abase + st["off"]
        self.xbcT = al("xbcT", [128, 4, 3 + TB], F32)
        self.cacc = al("cacc", [128, TB], F32)
        self.caccs = [self.cacc, al("cacc_b", [128, TB], F32)]
        self.xsT = al("xsT", [128, 8, TB], BF16)
        self.BT = al("BT", [128, 2, TB], BF16); self.CT = al("CT", [128, 2, TB], BF16)
        self.dts = al("dts", [128, 4, 8, 16], F32)
        self.cshl = al("cshl", [128, 4, 2, 16], BF16)
        self.xs_tok = al("xs_tok", [128, 1024], BF16); self.xw = al("xw", [128, 1024], BF16); self.xD = al("xD", [128, 1024], BF16)
        self.B_tok = al("B_tok", [128, 256], BF16)
        self.cbT = al("cbT", [128, 2, 128], F32)
        self.Dhl = al("Dhl", [128, 2, 4, 128], BF16)
        self.Ep = al("Ep", [128, 4, 128], F32)
        self.wmT = al("wmT", [128, 16, 128], BF16)
        self.ytmp = al("ytmp", [128, 1024], F32)
        self.ssq = al("ssq", [128, 4], F32)
        self.ynb = al("ynb", [128, 1024], BF16)
        self.stmp = al("stmp", [128, 1024], F32)
        self.ssmSb = al("ssmSb", [128, 1024], BF16)
        print("arena B", st["off"])
        self.Dhl2 = [self.Dhl, Tile(nc.alloc_sbuf_tensor_at("Dhl_b", [128, 2, 4, 128], BF16, offset=self._xbcT_off), [Dep()], "Dhl_b")]
        self.Ep2 = [self.Ep, Tile(nc.alloc_sbuf_tensor_at("Ep_b", [128, 4, 128], F32, offset=self._xbcT_off + 2048), [Dep()], "Ep_b")]
        begin()
        self.qcT = al("qcT", [128, 4, TB], BF16)
        self.kTe = al("kTe", [128, 128 + TB], BF16)
        self.vaug = al("vaug", [128, 5, 2, 80], BF16)
        self.lg = al("lg", [128, 4, 128], F32)
        self.lgs = [self.lg, al("lg_b", [128, 4, 128], F32)]; self.lgi = 0
        self.pT = al("pT", [128, 2, 2, 4, 128], BF16)
        self.den = al("den", [128, 8], F32); self.rden = al("rden", [128, 8], F32)
        self.oc_tok = al("oc_tok", [128, 8, 64], BF16)
        self.kvo = al("kvo", [128, 2, 128], F32)
        print("arena C", st["off"])
        begin()
        self.hT = al("hT", [128, 22, TB], BF16)
        self.sg = al("sg", [128, TB], F32); self.gt = al("gt", [128, TB], F32)
        self.sgs = [self.sg, al("sg_b", [128, TB], F32)]; self.gts = [self.gt, al("gt_b", [128, TB], F32)]
        self.sgi = 0
        self.ysq = al("ysq", [128, 8, TB], BF16)
        self.lnm = al("lnm", [128, TB], F32); self.lnr = al("lnr", [128, TB], F32); self.lnt = al("lnt", [128, TB], F32)
        self.lnu = al("lnu", [128, TB], F32); self.lnu2 = al("lnu2", [128, TB], F32)
        print("arena DEF", st["off"])
        self.macc = Tile(nc.alloc_sbuf_tensor_at("macc", [128, 8, TB], F32, offset=self.abase), None, "macc")

    def fence(self):
        kb = self.kb
        needs = {e: kb.tick[e] for e in ("pe", "act", "dve", "pool") if kb.tick[e] > 0}
        for key in kb.dpool["sp"][0]:
            if kb.tick[key] > 0:
                needs[key] = kb.tick[key]
        for e in ("act", "dve", "sp"):
            kb._emit_waits(e, dict(needs))

    def nps(self):
        t = self.psf[self.pfi % len(self.psf)]
        self.pfi += 1
        return t

    def npb(self):
        t = self.psb[self.pbi % len(self.psb)]
        self.pbi += 1
        return t

    def wload(self, src, kc, n):
        src3, key = src
        t = self.wr[self.wi % self.NS]
        self.wi += 1
        flat = t[:, 0:kc * n]
        v = flat.rearrange("p (k c) -> p k c", k=kc)
        ent = self.wcache.get(key) if self.use_wcache else None
        if ent is None:
            self.kb.dma("pool", v, src3, writes=[t])
            if self.use_wcache:
                h = self.nc.dram_tensor("wc%d" % len(self.wcache), [128, kc * n], BF16, kind="Internal")
                tl = Tile(h, None, "wc")
                self.wcache[key] = (h, tl)
                self.kb.dma("pool", h.ap()[:, :], flat, reads=[t], writes=[tl])
        else:
            h, tl = ent
            self.kb.dma(self.dbg.get("wq", "sp"), flat, h.ap()[:, :], reads=[tl], writes=[t])
        return t, v

    def wsrc_down(self, l, m):
        r0 = (l * 8 + m) * 128
        return (self.w_down.ap()[r0:r0 + 128, :].rearrange("p (k c) -> p k c", k=22), ("w_down", l, m))

    def wsrc(self, w, l, K, c0, n):
        return (w.ap()[l * K:(l + 1) * K, c0:c0 + n].rearrange("(k p) c -> p k c", p=128), (w.name, l, c0, n))

    def _setup_consts(self):
        kb = self.kb
        cf = self.cf
        kb.dma("sp", cf[:], self.c_f32.ap()[:, :], writes=[cf])
        self.ident_f = cf[:, 0:128]
        self.tri_f = cf[:, 128:256]
        self.ones_f = cf[:, 256:384]
        o = 512
        self.dmatT = cf[:, o:o + 512].rearrange("p (h i) -> p h i", h=4)
        self.kdec = cf[:, o + 512:o + 516]
        self.qdec = cf[:, o + 516:o + 520]
        self.mask01 = cf[:, o + 520:o + 520 + 256].rearrange("p (f q) -> p f q", f=2)
        kb.op("act", lambda e: e.copy(out=self.id_b[:], in_=self.ident_f), reads=[cf], writes=[self.id_b])
        kb.op("act", lambda e: e.copy(out=self.ones_b[:], in_=self.ones_f), reads=[cf], writes=[self.ones_b])
        kb.op("act", lambda e: e.copy(out=self.negT[:], in_=self.mask01[:, 1, :]), reads=[cf], writes=[self.negT])
        for l in range(self.depth):
            for t in (self.retS[l], self.ssmS[l], self.hist[l]):
                kb.op("pool", lambda e, t=t: e.memset(t[:], 0.0), writes=[t])
        oh = self.s_oh
        ohv = oh[:, :]
        kb.dma("sp", ohv, self.c_oh.ap()[:, :], writes=[oh])
        rb = self.s_rb
        kb.dma("sp", rb[:, 0:256], self.rel_bias.ap().partition_broadcast(128), writes=[rb])
        ohq = ohv.rearrange("p (f q b) -> p f q b", f=2, q=128)
        prod = self.s_prod
        pv = prod[:, :].rearrange("p (q b) -> p q b", b=32)
        rbv = rb[:, 0:256].rearrange("p (b h) -> p b h", h=8)
        for hf in range(2):
            for h in range(8):
                kb.op("dve", lambda e, hf=hf, h=h: e.tensor_tensor(
                    out=pv, in0=ohq[:, hf, :, :], in1=rbv[:, :, h].unsqueeze(1).broadcast_to([128, 128, 32]), op=ALU.mult),
                    reads=[oh, rb], writes=[prod])
                kb.op("dve", lambda e, hf=hf, h=h: e.tensor_reduce(out=self.btab[:, hf, h, :], in_=pv, axis=AX.X, op=ALU.add),
                      reads=[prod], writes=[self.btab])
            kb.op("dve", lambda e, hf=hf: e.tensor_tensor(
                out=self.btab[:, hf, :, :], in0=self.btab[:, hf, :, :],
                in1=self.mask01[:, hf, :].unsqueeze(1).broadcast_to([128, 8, 128]), op=ALU.add),
                reads=[cf, self.btab], writes=[self.btab])

    def block(self, bi):
        if not self.dbg.get('noload'): self.load_x(bi)
        for l in range(self.depth):
            self.layer(bi, l)
        if not self.dbg.get('nostore'): self.store_y(bi)

    def load_x(self, bi):
        self.kb.phase = 'load_x'
        kb = self.kb
        if not (self.dbg.get('norot2') and getattr(self, '_lx', 0) >= 1): kb.dma("sp", self.rot[:], self.c_rot.ap()[bi * TB:(bi + 1) * TB, :].rearrange("(c p) f -> p c f", p=128), writes=[self.rot])
        self._lx = getattr(self, '_lx', 0) + 1
        for c in range(4 if self._lx == 1 else self.dbg.get('nchunk', 4)):
            xt = self.xtok[0]; self.xti += 1
            r0 = bi * TB + c * 128
            kb.dma(self.dbg.get("xq", "sp"), xt[:], self.xp.ap()[r0:r0 + 128, :], writes=[xt])
            for half in range(2):
                ps = self.nps()
                kb.mm([lambda pe, j=j, half=half, xt=xt, ps=ps: pe.transpose(
                    out=ps[:, j * 128:(j + 1) * 128], in_=xt[:, (half * 4 + j) * 128:(half * 4 + j + 1) * 128], identity=self.ident_f)
                    for j in range(4)], reads=[xt, self.cf], writes=[ps])
                pv = ps[:, :].rearrange("p (j t) -> p j t", j=4)
                if self._lx > 1 and self.dbg.get('nocopy'):
                    continue
                kb.op("act", lambda e, half=half, c=c, pv=pv: e.copy(out=self.xr[:, half * 4:half * 4 + 4, c * 128:(c + 1) * 128], in_=pv),
                      reads=[ps], writes=[self.xr])
                kb.op("dve", lambda e, half=half, c=c: e.tensor_copy(out=self.xb[:, half * 4:half * 4 + 4, c * 128:(c + 1) * 128],
                                                                      in_=self.xr[:, half * 4:half * 4 + 4, c * 128:(c + 1) * 128]),
                      reads=[self.xr], writes=[self.xb])

    def store_y(self, bi):
        self.kb.phase = 'store_y'
        kb = self.kb
        for c in range(4):
            xt = self.xtok[0]; self.xti += 1
            for half in range(2):
                ps = self.nps()
                kb.mm([lambda pe, j=j, half=half, c=c, ps=ps: pe.transpose(
                    out=ps[:, j * 128:(j + 1) * 128], in_=self.xr[:, half * 4 + j, c * 128:(c + 1) * 128], identity=self.ident_f)
                    for j in range(4)], reads=[self.xr, self.cf], writes=[ps])
                kb.op("act" if half else "dve",
                      (lambda e, half=half, xt=xt, ps=ps: e.copy(out=xt[:, half * 512:(half + 1) * 512], in_=ps[:, :])) if half else
                      (lambda e, half=half, xt=xt, ps=ps: e.tensor_copy(out=xt[:, half * 512:(half + 1) * 512], in_=ps[:, :])),
                      reads=[ps], writes=[xt])
            r0 = bi * TB + c * 128
            kb.dma("sp", self.yp.ap()[r0:r0 + 128, :], xt[:], reads=[xt], is_output=True)

    def layer(self, bi, l):
        ph = self.dbg.get("phases", "PABCDEF")
        if "P" in ph: self.load_params(bi, l)
        if "A" in ph: self.phaseA(bi, l)
        if "B" in ph: self.phaseB(bi, l)
        if "C" in ph: self.phaseC(bi, l)
        if "D" in ph: self.phaseD(bi, l)
        if "E" in ph: self.phaseE(bi, l)
        if "F" in ph: self.phaseF(bi, l)

    def load_params(self, bi, l):
        self.kb.phase = 'load_params'
        kb = self.kb
        kb.dma("sp", self.cw[:], self.conv_w.ap()[l], writes=[self.cw])
        kb.dma("sp", self.cb[:], self.conv_b.ap()[l], writes=[self.cb])
        kb.dma("sp", self.lnp[:], self.lnp_d.ap()[l], writes=[self.lnp])
        for i, w in enumerate((self.dt_bias, self.a_log, self.d_skip)):
            kb.dma("sp", self.sm16[:, i, :], w.ap()[l:l + 1, :].partition_broadcast(128), writes=[self.sm16])
        kb.dma("sp", self.esink[:], self.sinks.ap()[l:l + 1, :].partition_broadcast(128), writes=[self.esink])
        kb.op("act", lambda e: e.activation(out=self.esink[:], in_=self.esink[:], func=AF.Exp), reads=[self.esink], writes=[self.esink])
        kb.op("act", lambda e: e.activation(out=self.sm16[:, 1, :], in_=self.sm16[:, 1, :], func=AF.Exp), reads=[self.sm16], writes=[self.sm16])
        kb.op("dve", lambda e: e.tensor_scalar(out=self.sm16[:, 1, :], in0=self.sm16[:, 1, :], scalar1=-1.0, scalar2=None, op0=ALU.mult),
              reads=[self.sm16], writes=[self.sm16])

    def proj_tok(self, Wt, Wv, c, ncols, col0=0):
        ps = self.nps()
        self.kb.mm([lambda pe, k=k, ps=ps: pe.matmul(ps[:, 0:ncols], self.xb[:, k, c * 128:(c + 1) * 128], Wv[:, k, col0:col0 + ncols],
                                                     start=(k == 0), stop=(k == 7)) for k in range(8)],
                   reads=[self.xb, Wt], writes=[ps])
        return ps

    def proj_feat(self, Wt, Wv, t0, src=None, srct=None, kc=8):
        ps = self.nps()
        src = self.xb if src is None else src
        self.kb.mm([lambda pe, k=k, ps=ps: pe.matmul(ps[:, :], Wv[:, k, t0:t0 + 128], src[:, k, :],
                                                     start=(k == 0), stop=(k == kc - 1)) for k in range(kc)],
                   reads=[src, Wt], writes=[ps])
        return ps

    def transp_b(self, src_tile, src_aps, dst_tile, dst_ap):
        n = len(src_aps)
        pb = self.npb()
        self.kb.mm([lambda pe, j=j, pb=pb: pe.transpose(out=pb[:, j * 128:(j + 1) * 128], in_=src_aps[j], identity=self.id_b[:])
                    for j in range(n)], reads=[src_tile, self.id_b], writes=[pb])
        self.kb.op("act", lambda e, pb=pb: e.copy(out=dst_ap, in_=pb[:, 0:n * 128].rearrange("p (j t) -> p j t", j=n)),
                   reads=[pb], writes=[dst_tile])
        return pb

    def phaseA(self, bi, l):
        self.kb.phase = 'phaseA'
        kb = self.kb
        self.fence()
        g = [2.0 ** (-5 - h) for h in range(4)]
        cdec = [float(np.exp(np.float64(128) * np.log1p(-gg))) for gg in g]
        for gi, off in enumerate((OQ, OK_, OV, OG)):
            Wt, Wv = self.wload(self.wsrc(self.w_in, l, D, off, 512), 8, 512)
            for c in range(4):
                ps = self.proj_tok(Wt, Wv, c, 512)
                if gi < 2:
                    dst = self.qrot if gi == 0 else self.krot
                    eng = "dve" if gi == 0 else self.dbg.get("rot_k_eng", "dve")
                    if eng == "pool":
                        kb.op("act", lambda e, ps=ps: e.copy(out=self.kraw[:], in_=ps[:, :]), reads=[ps], writes=[self.kraw])
                        srct, psv = self.kraw, self.kraw[:].rearrange("p (h t e) -> p h t e", h=4, t=2)
                    else:
                        srct, psv = ps, ps[:, :].rearrange("p (h t e) -> p h t e", h=4, t=2)
                    tA = self.rtA if gi == 0 else self.rtA2
                    tB = self.rtB if gi == 0 else self.rtB2
                    cosb = self.rot[:, c, gi * 128:gi * 128 + 64].unsqueeze(1).unsqueeze(1).broadcast_to([128, 4, 2, 64])
                    sinb = self.rot[:, c, gi * 128 + 64:gi * 128 + 128].unsqueeze(1).broadcast_to([128, 4, 64])
                    kb.op(eng, lambda e, psv=psv, cosb=cosb, tA=tA: e.tensor_tensor(out=tA[:], in0=psv, in1=cosb, op=ALU.mult),
                          reads=[srct, self.rot], writes=[tA])
                    kb.op(eng, lambda e, psv=psv, sinb=sinb, tB=tB: e.tensor_tensor(out=tB[:, :, 0, :], in0=psv[:, :, 1, :], in1=sinb, op=ALU.mult),
                          reads=[srct, self.rot], writes=[tB])
                    kb.op(eng, lambda e, psv=psv, sinb=sinb, tB=tB: e.tensor_tensor(out=tB[:, :, 1, :], in0=psv[:, :, 0, :], in1=sinb, op=ALU.mult),
                          reads=[srct, self.rot], writes=[tB])
                    dv = dst[:].rearrange("p (h t e) -> p h t e", h=4, t=2)
                    kb.op(eng, lambda e, dv=dv, tA=tA, tB=tB: e.tensor_tensor(out=dv[:, :, 0, :], in0=tA[:, :, 0, :], in1=tB[:, :, 0, :], op=ALU.subtract),
                          reads=[tA, tB], writes=[dst])
                    kb.op(eng, lambda e, dv=dv, tA=tA, tB=tB: e.tensor_tensor(out=dv[:, :, 1, :], in0=tA[:, :, 1, :], in1=tB[:, :, 1, :], op=ALU.add),
                          reads=[tA, tB], writes=[dst])
                    dT = self.qT if gi == 0 else self.kT
                    self.transp_b(dst, [dst[:, h * 128:(h + 1) * 128] for h in range(4)], dT, dT[:, :, c * 128:(c + 1) * 128])
                    if gi == 1:
                        kb.op("dve", lambda e, c=c: e.tensor_tensor(
                            out=self.kdk[:, c, :, :], in0=self.krot[:].rearrange("p (h e) -> p h e", h=4),
                            in1=self.kdec.unsqueeze(2).broadcast_to([128, 4, 128]), op=ALU.mult),
                            reads=[self.krot, self.cf], writes=[self.kdk])
                elif gi == 2:
                    kb.op("act", lambda e, c=c, ps=ps: e.copy(out=self.v_r[:, c, :], in_=ps[:, :]), reads=[ps], writes=[self.v_r])
                else:
                    kb.op("act", lambda e, c=c, ps=ps: e.activation(out=self.gsil[:, c, :], in_=ps[:, :], func=AF.Silu), reads=[ps], writes=[self.gsil])
        S, Sb = self.retS[l], self.retSb
        kb.op("act", lambda e: e.copy(out=Sb[:], in_=S[:]), reads=[S], writes=[Sb])
        for c in range(4):
            cs = slice(c * 128, (c + 1) * 128)
            ps1 = self.nps()
            kb.mm([lambda pe, h=h, ps1=ps1: pe.matmul(ps1[:, h * 128:(h + 1) * 128], self.kT[:, h, cs], self.qT[:, h, cs], start=True, stop=True)
                   for h in range(4)], reads=[self.kT, self.qT], writes=[ps1])
            kb.op("dve", lambda e, ps1=ps1: e.tensor_tensor(out=self.scm[:], in0=ps1[:, :].rearrange("p (h i) -> p h i", h=4), in1=self.dmatT, op=ALU.mult),
                  reads=[ps1, self.cf], writes=[self.scm])
            psA = self.nps(); psB = self.nps(); psC = self.nps()
            kb.mm([lambda pe, h=h, psA=psA: pe.matmul(psA[:, h * 128:(h + 1) * 128], self.scm[:, h, :], self.v_r[:, c, h * 128:(h + 1) * 128], start=True, stop=True)
                   for h in range(4)], reads=[self.scm, self.v_r], writes=[psA])
            kb.mm([lambda pe, h=h, psB=psB: pe.matmul(psB[:, h * 128:(h + 1) * 128], self.qT[:, h, cs], Sb[:, h, :], start=True, stop=True)
                   for h in range(4)], reads=[self.qT, Sb], writes=[psB])
            kb.mm([lambda pe, h=h, psC=psC: pe.matmul(psC[:, h * 128:(h + 1) * 128], self.kdk[:, c, h, :], self.v_r[:, c, h * 128:(h + 1) * 128], start=True, stop=True)
                   for h in range(4)], reads=[self.kdk, self.v_r], writes=[psC])
            for h in range(4):
                kb.op("dve", lambda e, h=h, psC=psC: e.scalar_tensor_tensor(out=S[:, h, :], in0=S[:, h, :], scalar=cdec[h], in1=psC[:, h * 128:(h + 1) * 128],
                                                                              op0=ALU.mult, op1=ALU.add), reads=[S, psC], writes=[S])
            kb.op("act", lambda e: e.copy(out=Sb[:], in_=S[:]), reads=[S], writes=[Sb])
            kb.op("dve", lambda e, psB=psB: e.tensor_tensor(out=self.otmp[:], in0=psB[:, :].rearrange("p (h v) -> p h v", h=4),
                                                             in1=self.qdec.unsqueeze(2).broadcast_to([128, 4, 128]), op=ALU.mult),
                  reads=[psB, self.cf], writes=[self.otmp])
            kb.op("dve", lambda e, psA=psA: e.tensor_tensor(out=self.o_r[:], in0=psA[:, :].rearrange("p (h v) -> p h v", h=4), in1=self.otmp[:], op=ALU.add),
                  reads=[psA, self.otmp], writes=[self.o_r])
            for h in range(4):
                kb.op("dve", lambda e, h=h: e.bn_stats(out=self.st6[:, h, :], in_=self.o_r[:, h, :]), reads=[self.o_r], writes=[self.st6])
            for h in range(4):
                kb.op("dve", lambda e, h=h: e.bn_aggr(out=self.mv[:, h, :], in_=self.st6[:, h, :]), reads=[self.st6], writes=[self.mv])
            kb.op("act", lambda e: e.activation(out=self.rs4[:], in_=self.mv[:, :, 1], func=AF.Ln, bias=EPS, scale=1.0), reads=[self.mv], writes=[self.rs4])
            kb.op("act", lambda e: e.activation(out=self.rs4b[:], in_=self.rs4[:], func=AF.Exp, scale=-0.5), reads=[self.rs4], writes=[self.rs4b])
            for h in range(4):
                kb.op("dve", lambda e, h=h: e.scalar_tensor_tensor(out=self.og[:, h, :], in0=self.o_r[:, h, :], scalar=self.mv[:, h, 0:1],
                                                                     in1=self.gsil[:, c, h * 128:(h + 1) * 128], op0=ALU.subtract, op1=ALU.mult),
                      reads=[self.o_r, self.mv, self.gsil], writes=[self.og])
            for h in range(4):
                kb.op("act", lambda e, h=h: e.activation(out=self.ogb[:, h * 128:(h + 1) * 128], in_=self.og[:, h, :], func=AF.Identity, scale=self.rs4b[:, h:h + 1]),
                      reads=[self.og, self.rs4b], writes=[self.ogb])
            self.transp_b(self.ogb, [self.ogb[:, h * 128:(h + 1) * 128] for h in range(4)], self.orT, self.orT[:, :, cs])
        if bi == self.nblk - 1:
            kb.dma("sp", self.ret_p.ap()[l].rearrange("h d v -> d h v"), S[:], reads=[S], is_output=True)

    def phaseB(self, bi, l):
        self.kb.phase = 'phaseB'
        kb = self.kb
        self.fence()
        kb.dma("sp", self.normw[:], self.norm_w.ap()[l:l + 1, :].partition_broadcast(128), writes=[self.normw])
        Wt, Wv = self.wload(self.wsrc(self.w_in, l, D, ODT, 16), 8, 16)
        dts = self.dts
        for c in range(4):
            ps = self.proj_tok(Wt, Wv, c, 16)
            DT, CS, ECS, DEDT, CDEC, NB, T1, T2 = [dts[:, c, i, :] for i in range(8)]
            kb.op("dve", lambda e, ps=ps, T1=T1: e.tensor_tensor(out=T1, in0=ps[:, 0:16], in1=self.sm16[:, 0, :], op=ALU.add), reads=[ps, self.sm16], writes=[dts])
            kb.op("act", lambda e, T1=T1, T2=T2: e.activation(out=T2, in_=T1, func=AF.Exp), reads=[dts], writes=[dts])
            kb.op("act", lambda e, DT=DT, T2=T2: e.activation(out=DT, in_=T2, func=AF.Ln, bias=1.0, scale=1.0), reads=[dts], writes=[dts])
            kb.op("dve", lambda e, DT=DT, T1=T1: e.tensor_tensor(out=T1, in0=DT, in1=self.sm16[:, 1, :], op=ALU.mult), reads=[dts, self.sm16], writes=[dts])
            ps2 = self.nps()
            kb.mm([lambda pe, ps2=ps2, T1=T1: pe.matmul(ps2[:, 0:16], self.tri_f, T1, start=True, stop=True),
                   lambda pe, ps2=ps2, T1=T1: pe.matmul(ps2[:, 16:32], self.ones_f, T1, start=True, stop=True)],
                  reads=[self.cf, dts], writes=[ps2])
            kb.op("act", lambda e, ps2=ps2, CS=CS: e.copy(out=CS, in_=ps2[:, 0:16]), reads=[ps2], writes=[dts])
            kb.op("act", lambda e, ps2=ps2, ECS=ECS: e.activation(out=ECS, in_=ps2[:, 0:16], func=AF.Exp), reads=[ps2], writes=[dts])
            kb.op("act", lambda e, ps2=ps2, CDEC=CDEC: e.activation(out=CDEC, in_=ps2[:, 16:32], func=AF.Exp), reads=[ps2], writes=[dts])
            kb.op("dve", lambda e, ps2=ps2, T2=T2, CS=CS: e.tensor_tensor(out=T2, in0=ps2[:, 16:32], in1=CS, op=ALU.subtract), reads=[ps2, dts], writes=[dts])
            kb.op("act", lambda e, T2=T2: e.activation(out=T2, in_=T2, func=AF.Exp), reads=[dts], writes=[dts])
            kb.op("dve", lambda e, T2=T2, DT=DT, DEDT=DEDT: e.tensor_tensor(out=DEDT, in0=T2, in1=DT, op=ALU.mult), reads=[dts], writes=[dts])
            kb.op("act", lambda e, DT=DT, T1=T1: e.activation(out=T1, in_=DT, func=AF.Ln), reads=[dts], writes=[dts])
            kb.op("dve", lambda e, T1=T1, CS=CS, NB=NB: e.tensor_tensor(out=NB, in0=T1, in1=CS, op=ALU.subtract), reads=[dts], writes=[dts])
            kb.op("act", lambda e, CS=CS, c=c: e.copy(out=self.cshl[:, c, 0, :], in_=CS), reads=[dts], writes=[self.cshl])
            kb.op("dve", lambda e, CS=CS, c=c: e.tensor_tensor(out=self.cshl[:, c, 1, :], in0=CS, in1=self.cshl[:, c, 0, :], op=ALU.subtract),
                  reads=[dts, self.cshl], writes=[self.cshl])
        if self.dbg.get('bstop', 99) <= 1: return
        for g3 in range(3):
            Wt, Wv = self.wload(self.wsrc(self.w_in, l, D, OXBC + g3 * 512, 512), 8, 512)
            kb.op("dve", lambda e, g3=g3: e.tensor_copy(out=self.xbcT[:, :, 0:3], in_=self.hist[l][:, g3 * 4:(g3 + 1) * 4, :]),
                  reads=[self.hist[l]], writes=[self.xbcT])
            for t in range(4):
                tt = g3 * 4 + t
                ps = self.proj_feat(Wt, Wv, t * 128)
                kb.op("act", lambda e, t=t, ps=ps: e.copy(out=self.xbcT[:, t, 3:3 + TB], in_=ps[:, :]), reads=[ps], writes=[self.xbcT])
                cacc = self.caccs[tt % 2]
                kb.op("dve", lambda e, t=t, tt=tt, cacc=cacc: e.tensor_scalar(out=cacc[:], in0=self.xbcT[:, t, 0:TB], scalar1=self.cw[:, tt, 0:1], scalar2=None, op0=ALU.mult),
                      reads=[self.xbcT, self.cw], writes=[cacc])
                for tau in range(1, 4):
                    kb.op("dve", lambda e, t=t, tt=tt, tau=tau, cacc=cacc: e.scalar_tensor_tensor(
                        out=cacc[:], in0=self.xbcT[:, t, tau:tau + TB], scalar=self.cw[:, tt, tau:tau + 1], in1=cacc[:], op0=ALU.mult, op1=ALU.add),
                        reads=[self.xbcT, self.cw, cacc], writes=[cacc])
                if tt < 8:
                    dtile, dap = self.xsT, self.xsT[:, tt, :]
                elif tt < 10:
                    dtile, dap = self.BT, self.BT[:, tt - 8, :]
                else:
                    dtile, dap = self.CT, self.CT[:, tt - 10, :]
                kb.op("act", lambda e, tt=tt, dap=dap, cacc=cacc: e.activation(out=dap, in_=cacc[:], func=AF.Silu, bias=self.cb[:, tt:tt + 1], scale=1.0),
                      reads=[cacc, self.cb], writes=[dtile])
            kb.op("dve", lambda e, g3=g3: e.tensor_copy(out=self.hist[l][:, g3 * 4:(g3 + 1) * 4, :], in_=self.xbcT[:, :, TB:TB + 3]),
                  reads=[self.xbcT], writes=[self.hist[l]])
        if self.dbg.get('bstop', 99) <= 2: return
        S, Sb = self.ssmS[l], self.ssmSb
        kb.op("act", lambda e: e.copy(out=Sb[:], in_=S[:]), reads=[S], writes=[Sb])
        Zw = [self.wload(self.wsrc(self.w_in, l, D, OZ + g2 * 512, 512), 8, 512) for g2 in range(2)]
        for c in range(4):
            cs = slice(c * 128, (c + 1) * 128)
            DT, CS, ECS, DEDT, CDEC, NB, T1, T2 = [dts[:, c, i, :] for i in range(8)]
            for g2 in range(2):
                ps = self.proj_tok(Zw[g2][0], Zw[g2][1], c, 512)
                kb.op("act", lambda e, ps=ps, g2=g2: e.activation(out=self.zs[:, g2 * 512:(g2 + 1) * 512], in_=ps[:, :], func=AF.Silu),
                      reads=[ps], writes=[self.zs])
            if self.dbg.get('bstop', 99) <= 2.3: continue
            pb = self.npb()
            kb.mm([lambda pe, j=j, pb=pb: pe.transpose(out=pb[:, j * 128:(j + 1) * 128], in_=self.xsT[:, j, cs], identity=self.id_b[:]) for j in range(8)],
                  reads=[self.xsT, self.id_b], writes=[pb])
            kb.op("act", lambda e, pb=pb: e.copy(out=self.xs_tok[:], in_=pb[:, :]), reads=[pb], writes=[self.xs_tok])
            if self.dbg.get('bstop', 99) <= 2.6: continue
            pbv = self.xs_tok[:].rearrange("p (h q) -> p h q", h=16)
            kb.op("dve", lambda e, pbv=pbv, DEDT=DEDT: e.tensor_tensor(out=self.xw[:].rearrange("p (h q) -> p h q", h=16), in0=pbv,
                                                                        in1=DEDT.unsqueeze(2).broadcast_to([128, 16, 64]), op=ALU.mult),
                  reads=[self.xs_tok, dts], writes=[self.xw])
            kb.op("dve", lambda e, pbv=pbv: e.tensor_tensor(out=self.xD[:].rearrange("p (h q) -> p h q", h=16), in0=pbv,
                                                             in1=self.sm16[:, 2, :].unsqueeze(2).broadcast_to([128, 16, 64]), op=ALU.mult),
                  reads=[self.xs_tok, self.sm16], writes=[self.xD])
            if self.dbg.get('bstop', 99) <= 2.8: continue
            pb2 = self.npb()
            kb.mm([lambda pe, j=j, pb2=pb2: pe.transpose(out=pb2[:, j * 128:(j + 1) * 128], in_=self.BT[:, j, cs], identity=self.id_b[:]) for j in range(2)],
                  reads=[self.BT, self.id_b], writes=[pb2])
            kb.op("act", lambda e, pb2=pb2: e.copy(out=self.B_tok[:], in_=pb2[:, 0:256]), reads=[pb2], writes=[self.B_tok])
            if self.dbg.get('bstop', 99) <= 3: continue
            psc = self.nps()
            kb.mm([lambda pe, g=g, psc=psc: pe.matmul(psc[:, g * 128:(g + 1) * 128], self.BT[:, g, cs], self.CT[:, g, cs], start=True, stop=True) for g in range(2)],
                  reads=[self.BT, self.CT], writes=[psc])
            kb.op("act", lambda e, psc=psc: e.copy(out=self.cbT[:], in_=psc[:, 0:256].rearrange("p (g i) -> p g i", g=2)), reads=[psc], writes=[self.cbT])
            def build_D(hq):
                Dhl = self.Dhl2[hq % 2]
                for hl in range(2):
                    kb.op("dve", lambda e, hl=hl, c=c, hq=hq, Dhl=Dhl: e.tensor_tensor(
                        out=Dhl[:, hl, :, :], in0=self.id_b[:].unsqueeze(1).broadcast_to([128, 4, 128]),
                        in1=self.cshl[:, c, hl, hq * 4:(hq + 1) * 4].unsqueeze(2).broadcast_to([128, 4, 128]), op=ALU.mult),
                        reads=[self.id_b, self.cshl], writes=[Dhl])
            build_D(0)
            for hq in range(4):
                Dhl = self.Dhl2[hq % 2]; Ep = self.Ep2[hq % 2]
                if hq < 3:
                    build_D(hq + 1)
                pse = self.nps()
                kb.mm([lambda pe, pse=pse, hq=hq, Dhl=Dhl: pe.matmul(pse[:, :], self.ones_b[:], Dhl[:, 0, :, :], start=True, stop=False),
                       lambda pe, pse=pse, hq=hq, Dhl=Dhl: pe.matmul(pse[:, :], self.ones_b[:], Dhl[:, 1, :, :], start=False, stop=False),
                       lambda pe, pse=pse: pe.matmul(pse[:, :], self.id_b[:], self.negT[:].unsqueeze(1).broadcast_to([128, 4, 128]), start=False, stop=True)],
                      reads=[self.ones_b, Dhl, self.id_b, self.negT], writes=[pse])
                for hh in range(4):
                    h = hq * 4 + hh
                    kb.op("act", lambda e, pse=pse, hh=hh, h=h, NB=NB, Ep=Ep: e.activation(out=Ep[:, hh, :], in_=pse[:, hh * 128:(hh + 1) * 128], func=AF.Exp,
                                                                                  bias=NB[:, h:h + 1], scale=1.0), reads=[pse, dts], writes=[Ep])
                g = hq // 2
                kb.op("dve", lambda e, hq=hq, g=g, Ep=Ep: e.tensor_tensor(out=self.wmT[:, hq * 4:(hq + 1) * 4, :], in0=Ep[:],
                                                                    in1=self.cbT[:, g, :].unsqueeze(1).broadcast_to([128, 4, 128]), op=ALU.mult),
                      reads=[Ep, self.cbT], writes=[self.wmT])
            if self.dbg.get('bstop', 99) <= 4: continue
            psY = [self.nps(), self.nps()]
            for g in range(2):
                fns = []
                for hh in range(8):
                    h = g * 8 + hh
                    fns.append(lambda pe, g=g, hh=hh, h=h: pe.matmul(psY[g][:, hh * 64:(hh + 1) * 64], self.id_b[:], self.xD[:, h * 64:(h + 1) * 64], start=True, stop=False))
                    fns.append(lambda pe, g=g, hh=hh, h=h: pe.matmul(psY[g][:, hh * 64:(hh + 1) * 64], self.wmT[:, h, :], self.xs_tok[:, h * 64:(h + 1) * 64], start=False, stop=True))
                kb.mm(fns, reads=[self.id_b, self.xD, self.wmT, self.xs_tok], writes=[psY[g]])
            psZ = [self.nps(), self.nps()]
            for g in range(2):
                kb.mm([lambda pe, g=g: pe.matmul(psZ[g][:, :], self.CT[:, g, cs], Sb[:, g * 512:(g + 1) * 512], start=True, stop=True)],
                      reads=[self.CT, Sb], writes=[psZ[g]])
            for g in range(2):
                gs = slice(g * 512, (g + 1) * 512)
                kb.op("dve", lambda e, g=g, gs=gs, ECS=ECS: e.tensor_tensor(out=self.ytmp[:, gs].rearrange("p (h q) -> p h q", h=8),
                                                                          in0=psZ[g][:, :].rearrange("p (h q) -> p h q", h=8),
                                                                          in1=ECS[:, g * 8:(g + 1) * 8].unsqueeze(2).broadcast_to([128, 8, 64]), op=ALU.mult),
                      reads=[psZ[g], dts], writes=[self.ytmp])
                kb.op("dve", lambda e, g=g, gs=gs: e.tensor_tensor(out=self.ytmp[:, gs], in0=self.ytmp[:, gs], in1=psY[g][:, :], op=ALU.add),
                      reads=[psY[g], self.ytmp], writes=[self.ytmp])
            psS = [self.nps(), self.nps()]
            for g in range(2):
                kb.mm([lambda pe, g=g: pe.matmul(psS[g][:, :], self.B_tok[:, g * 128:(g + 1) * 128], self.xw[:, g * 512:(g + 1) * 512], start=True, stop=True)],
                      reads=[self.B_tok, self.xw], writes=[psS[g]])
            kb.op("dve", lambda e, CDEC=CDEC: e.tensor_tensor(out=self.stmp[:].rearrange("p (h q) -> p h q", h=16), in0=S[:].rearrange("p (h q) -> p h q", h=16),
                                                              in1=CDEC.unsqueeze(2).broadcast_to([128, 16, 64]), op=ALU.mult), reads=[S, dts], writes=[self.stmp])
            for g in range(2):
                gs = slice(g * 512, (g + 1) * 512)
                kb.op("dve", lambda e, g=g, gs=gs: e.tensor_tensor(out=S[:, gs], in0=self.stmp[:, gs], in1=psS[g][:, :], op=ALU.add),
                      reads=[self.stmp, psS[g]], writes=[S])
            kb.op("act", lambda e: e.copy(out=Sb[:], in_=S[:]), reads=[S], writes=[Sb])
            if self.dbg.get('bstop', 99) <= 5: continue
            kb.op("dve", lambda e, c=c: e.tensor_tensor(out=self.ytmp[:], in0=self.ytmp[:], in1=self.zs[:], op=ALU.mult),
                  reads=[self.ytmp, self.zs], writes=[self.ytmp])
            for g in range(2):
                gs = slice(g * 512, (g + 1) * 512)
                kb.op("act", lambda e, g=g, gs=gs: e.activation(out=self.stmp[:, gs], in_=self.ytmp[:, gs], func=AF.Square, accum_out=self.ssq[:, g:g + 1]),
                      reads=[self.ytmp], writes=[self.stmp, self.ssq])
            kb.op("act", lambda e: e.activation(out=self.ssq[:, 2:4], in_=self.ssq[:, 0:2], func=AF.Ln, bias=EPS, scale=1.0 / 512), reads=[self.ssq], writes=[self.ssq])
            kb.op("act", lambda e: e.activation(out=self.ssq[:, 2:4], in_=self.ssq[:, 2:4], func=AF.Exp, scale=-0.5), reads=[self.ssq], writes=[self.ssq])
            for g in range(2):
                gs = slice(g * 512, (g + 1) * 512)
                kb.op("dve", lambda e, g=g, gs=gs: e.scalar_tensor_tensor(out=self.ynb[:, gs], in0=self.ytmp[:, gs], scalar=self.ssq[:, 2 + g:3 + g],
                                                                           in1=self.normw[:, gs], op0=ALU.mult, op1=ALU.mult),
                      reads=[self.ytmp, self.ssq, self.normw], writes=[self.ynb])
            self.transp_b(self.ynb, [self.ynb[:, j * 128:(j + 1) * 128] for j in range(8)], self.yT, self.yT[:, :, cs])
        if self.dbg.get('bstop', 99) <= 6: return
        if bi == self.nblk - 1:
            for half in range(2):
                ps = self.nps()
                kb.mm([lambda pe, j=j, half=half, ps=ps: pe.transpose(out=ps[:, j * 128:(j + 1) * 128], in_=S[:, (half * 4 + j) * 128:(half * 4 + j + 1) * 128],
                                                                       identity=self.ident_f) for j in range(4)], reads=[S, self.cf], writes=[ps])
                kb.op("act", lambda e, ps=ps: e.copy(out=self.ytmp[:, 0:512], in_=ps[:, :]), reads=[ps], writes=[self.ytmp])
                kb.dma("sp", self.ssm_p.ap()[l, half * 512:(half + 1) * 512, :].rearrange("(j p) n -> p j n", p=128),
                       self.ytmp[:, 0:512].rearrange("p (j n) -> p j n", j=4), reads=[self.ytmp], is_output=True)
            for half in range(3):
                ps = self.nps()
                kb.mm([lambda pe, j=j, half=half, ps=ps: pe.transpose(out=ps[0:3, j * 128:(j + 1) * 128], in_=self.hist[l][:, half * 4 + j, :],
                                                                       identity=self.ident_f) for j in range(4)], reads=[self.hist[l], self.cf], writes=[ps])
                kb.op("act", lambda e, ps=ps: e.copy(out=self.stmp[0:3, 0:512], in_=ps[0:3, :]), reads=[ps], writes=[self.stmp])
                kb.dma("sp", self.conv_p.ap()[l, :, half * 512:(half + 1) * 512], self.stmp[0:3, 0:512], reads=[self.stmp], is_output=True)

    def phaseC(self, bi, l):
        self.kb.phase = 'phaseC'
        kb = self.kb
        last = (bi == self.nblk - 1)
        self.fence()
        kb.op("act", lambda e: e.activation(out=self.vaug[:, :, :, 64:65], in_=self.cf[:, 0:10].rearrange("p (a b c) -> p a b c", a=5, b=2), func=AF.Identity, scale=0.0, bias=1.0),
              reads=[self.cf], writes=[self.vaug])
        Wt, Wv = self.wload(self.wsrc(self.w_in, l, D, OQC, 512), 8, 512)
        for t in range(4):
            ps = self.proj_feat(Wt, Wv, t * 128)
            kb.op("act", lambda e, t=t, ps=ps: e.copy(out=self.qcT[:, t, :], in_=ps[:, :]), reads=[ps], writes=[self.qcT])
        if self.dbg.get('cstop', 99) <= 1: return
        Wt, Wv = self.wload(self.wsrc(self.w_in, l, D, OKC, 256), 8, 256)
        if bi > 0:
            kb.op("act", lambda e: e.copy(out=self.kTe[:, 0:128], in_=self.kprev[l][:]), reads=[self.kprev[l]], writes=[self.kTe])
            kb.op("act", lambda e: e.copy(out=self.vaug[:, 0, :, 0:64], in_=self.vprev[l][:, :, 0:64]), reads=[self.vprev[l]], writes=[self.vaug])
        if self.dbg.get('cstop', 99) <= 1.5: return
        ps = self.proj_feat(Wt, Wv, 0)
        kb.op("act", lambda e, ps=ps: e.copy(out=self.kTe[:, 128:128 + TB], in_=ps[:, :]), reads=[ps], writes=[self.kTe])
        kb.op("act", lambda e: e.copy(out=self.kprev[l][:], in_=self.kTe[:, TB:TB + 128]), reads=[self.kTe], writes=[self.kprev[l]])
        if self.dbg.get('cstop', 99) <= 1.7: return
        for c in range(4):
            ps = self.proj_tok(Wt, Wv, c, 128, col0=128)
            kb.op("act", lambda e, c=c, ps=ps: e.copy(out=self.vaug[:, c + 1, :, 0:64], in_=ps[:, 0:128].rearrange("p (g e) -> p g e", g=2)),
                  reads=[ps], writes=[self.vaug])
            if last and c == 3 and not self.dbg.get('nokv'):
                kb.op("act", lambda e, ps=ps: e.copy(out=self.kvo[:, 1, :], in_=ps[:, 0:128]), reads=[ps], writes=[self.kvo])
                ps2 = self.proj_tok(Wt, Wv, c, 128, col0=0)
                kb.op("act", lambda e, ps2=ps2: e.copy(out=self.kvo[:, 0, :], in_=ps2[:, 0:128]), reads=[ps2], writes=[self.kvo])
                kb.dma("sp", self.k_p.ap()[l, :, :], self.kvo[:, 0, :], reads=[self.kvo], is_output=True)
                kb.dma("sp", self.v_p.ap()[l, :, :], self.kvo[:, 1, :], reads=[self.kvo], is_output=True)
        kb.op("act", lambda e: e.copy(out=self.vprev[l][:, :, 0:65], in_=self.vaug[:, 4, :, 0:65]), reads=[self.vaug], writes=[self.vprev[l]])
        if self.dbg.get('cstop', 99) <= 2: return
        for c in range(4):
            n = bi * 4 + c
            cs = slice(c * 128, (c + 1) * 128)
            hfs = [1] if n == 0 else [0, 1]
            for g in range(2):
                for hf in hfs:
                    ps = self.nps()
                    kc0 = (c + hf) * 128
                    kb.mm([lambda pe, ps=ps, g=g, kc0=kc0: pe.matmul(ps[:, :], self.kTe[g * 64:(g + 1) * 64, kc0:kc0 + 128], self.qcT[g * 64:(g + 1) * 64, :, cs],
                                                                      start=True, stop=True)], reads=[self.kTe, self.qcT], writes=[ps])
                    lg = self.lgs[self.lgi % 2]; self.lgi += 1
                    kb.op("dve", lambda e, ps=ps, g=g, hf=hf, lg=lg: e.scalar_tensor_tensor(out=lg[:], in0=ps[:, :].rearrange("p (h q) -> p h q", h=4), scalar=0.125,
                                                                                    in1=self.btab[:, hf, g * 4:(g + 1) * 4, :], op0=ALU.mult, op1=ALU.add),
                          reads=[ps, self.btab], writes=[lg])
                    kb.op("act", lambda e, g=g, hf=hf, lg=lg: e.activation(out=self.pT[:, hf, g, :, :], in_=lg[:], func=AF.Exp), reads=[lg], writes=[self.pT])
            if self.dbg.get('cstop', 99) <= 3: continue
            psO = [self.nps(), self.nps()]
            for g in range(2):
                fns = []
                for h4 in range(4):
                    for i, hf in enumerate(hfs):
                        fns.append(lambda pe, g=g, h4=h4, hf=hf, i=i: pe.matmul(psO[g][:, h4 * 65:(h4 + 1) * 65], self.pT[:, hf, g, h4, :], self.vaug[:, c + hf, g, 0:65],
                                                                               start=(i == 0), stop=(i == len(hfs) - 1)))
                kb.mm(fns, reads=[self.pT, self.vaug], writes=[psO[g]])
                if self.dbg.get('cstop', 99) <= 4: continue
                ov = psO[g][:, 0:260].rearrange("p (h e) -> p h e", h=4)
                kb.op("dve", lambda e, g=g, ov=ov: e.tensor_tensor(out=self.den[:, g * 4:(g + 1) * 4], in0=ov[:, :, 64], in1=self.esink[:, g * 4:(g + 1) * 4], op=ALU.add),
                      reads=[psO[g], self.esink], writes=[self.den])
                kb.op("dve", lambda e, g=g: e.reciprocal(out=self.rden[:, g * 4:(g + 1) * 4], in_=self.den[:, g * 4:(g + 1) * 4]), reads=[self.den], writes=[self.rden])
                kb.op("dve", lambda e, g=g, ov=ov: e.tensor_tensor(out=self.oc_tok[:, g * 4:(g + 1) * 4, :], in0=ov[:, :, 0:64],
                                                                    in1=self.rden[:, g * 4:(g + 1) * 4].unsqueeze(2).broadcast_to([128, 4, 64]), op=ALU.mult),
                      reads=[psO[g], self.rden], writes=[self.oc_tok])
            if self.dbg.get('cstop', 99) <= 5: continue
            ocv = self.oc_tok[:].rearrange("p h e -> p (h e)")
            self.transp_b(self.oc_tok, [ocv[:, j * 128:(j + 1) * 128] for j in range(4)], self.ocT, self.ocT[:, :, cs])

    def phaseD(self, bi, l):
        self.kb.phase = 'phaseD'
        kb = self.kb
        self.fence()
        brs = [(self.w_br_ret, 512, 4, self.orT), (self.w_br_ssd, 1024, 8, self.yT), (self.w_br_swa, 512, 4, self.ocT)]
        for b, (wbr, K, kc, src) in enumerate(brs):
            for j in range(2):
                Gt, Gv = self.wload(self.wsrc(self.w_in, l, D, OGATE + b * 1024 + j * 512, 512), 8, 512)
                Bt, Bv = self.wload(self.wsrc(wbr, l, K, j * 512, 512), kc, 512)
                for t in range(4):
                    m = j * 4 + t
                    psG = self.proj_feat(Gt, Gv, t * 128)
                    psB = self.proj_feat(Bt, Bv, t * 128, src=src, kc=kc)
                    sg = self.sgs[self.sgi % 2]; gt = self.gts[self.sgi % 2]; self.sgi += 1
                    kb.op("act", lambda e, psG=psG, sg=sg: e.activation(out=sg[:], in_=psG[:, :], func=AF.Sigmoid), reads=[psG], writes=[sg])
                    if b == 0:
                        kb.op("dve", lambda e, m=m, psB=psB, sg=sg: e.tensor_tensor(out=self.macc[:, m, :], in0=sg[:], in1=psB[:, :], op=ALU.mult),
                              reads=[sg, psB], writes=[self.macc])
                    else:
                        kb.op("dve", lambda e, psB=psB, sg=sg, gt=gt: e.tensor_tensor(out=gt[:], in0=sg[:], in1=psB[:, :], op=ALU.mult),
                              reads=[sg, psB], writes=[gt])
                        if b == 1:
                            kb.op("dve", lambda e, m=m, gt=gt: e.tensor_tensor(out=self.macc[:, m, :], in0=self.macc[:, m, :], in1=gt[:], op=ALU.add),
                                  reads=[self.macc, gt], writes=[self.macc])
                        else:
                            kb.op("dve", lambda e, m=m, gt=gt: e.tensor_tensor(out=self.mrg[:, m, :], in0=self.macc[:, m, :], in1=gt[:], op=ALU.add),
                                  reads=[self.macc, gt], writes=[self.mrg])

    def resid_ln(self, pss_fn, gi):
        kb = self.kb
        for m in range(8):
            ps = pss_fn(m)
            kb.op("dve", lambda e, m=m, ps=ps: e.scalar_tensor_tensor(out=self.xr[:, m, :], in0=self.xr[:, m, :], scalar=ALPHA, in1=ps[:, :], op0=ALU.mult, op1=ALU.add),
                  reads=[self.xr, ps], writes=[self.xr])
            kb.op("act", lambda e, m=m: e.copy(out=self.xb[:, m, :], in_=self.xr[:, m, :]), reads=[self.xr], writes=[self.xb])
            kb.op("act", lambda e, m=m: e.activation(out=self.ysq[:, m, :], in_=self.xr[:, m, :], func=AF.Square), reads=[self.xr], writes=[self.ysq])
        ps1 = self.nps(); ps2 = self.nps()
        kb.mm([lambda pe, k=k: pe.matmul(ps1[:, :], self.ones_b[:], self.xb[:, k, :], start=(k == 0), stop=(k == 7)) for k in range(8)],
              reads=[self.ones_b, self.xb], writes=[ps1])
        kb.mm([lambda pe, k=k: pe.matmul(ps2[:, :], self.ones_b[:], self.ysq[:, k, :], start=(k == 0), stop=(k == 7)) for k in range(8)],
              reads=[self.ones_b, self.ysq], writes=[ps2])
        kb.op("act", lambda e: e.activation(out=self.lnm[:], in_=ps1[:, :], func=AF.Identity, scale=1.0 / D), reads=[ps1], writes=[self.lnm])
        kb.op("dve", lambda e: e.tensor_tensor(out=self.lnt[:], in0=self.lnm[:], in1=self.lnm[:], op=ALU.mult), reads=[self.lnm], writes=[self.lnt])
        kb.op("dve", lambda e: e.scalar_tensor_tensor(out=self.lnt[:], in0=ps2[:, :], scalar=1.0 / D, in1=self.lnt[:], op0=ALU.mult, op1=ALU.subtract),
              reads=[ps2, self.lnt], writes=[self.lnt])
        kb.op("act", lambda e: e.activation(out=self.lnt[:], in_=self.lnt[:], func=AF.Ln, bias=EPS, scale=1.0), reads=[self.lnt], writes=[self.lnt])
        kb.op("act", lambda e: e.activation(out=self.lnr[:], in_=self.lnt[:], func=AF.Exp, scale=-0.5), reads=[self.lnt], writes=[self.lnr])
        for m in range(8):
            eng = "dve"
            lnu = self.lnu2 if (m % 2 == 1) else self.lnu
            kb.op(eng, lambda e, m=m, lnu=lnu: e.tensor_tensor(out=lnu[:], in0=self.xr[:, m, :], in1=self.lnm[:], op=ALU.subtract), reads=[self.xr, self.lnm], writes=[lnu])
            kb.op(eng, lambda e, m=m, lnu=lnu: e.tensor_tensor(out=lnu[:], in0=lnu[:], in1=self.lnr[:], op=ALU.mult), reads=[lnu, self.lnr], writes=[lnu])
            kb.op("act", lambda e, m=m, lnu=lnu: e.activation(out=self.xr[:, m, :], in_=lnu[:], func=AF.Identity, bias=self.lnp[:, gi + 1, m:m + 1], scale=self.lnp[:, gi, m:m + 1]),
                  reads=[lnu, self.lnp], writes=[self.xr])
            kb.op("act", lambda e, m=m: e.copy(out=self.xb[:, m, :], in_=self.xr[:, m, :]), reads=[self.xr], writes=[self.xb])

    def phaseE(self, bi, l):
        self.kb.phase = 'phaseE'
        W = {}

        def pss(m):
            j = m // 4
            if (m % 4) == 0:
                W["t"], W["v"] = self.wload(self.wsrc(self.w_out, l, D, j * 512, 512), 8, 512)
            return self.proj_feat(W["t"], W["v"], (m % 4) * 128, src=self.mrg)
        self.resid_ln(pss, 0)

    def phaseF(self, bi, l):
        self.kb.phase = 'phaseF'
        kb = self.kb
        for jg in range(6):
            n = 512 if jg < 5 else 256
            Gt, Gv = self.wload(self.wsrc(self.w_gate, l, D, jg * 512, n), 8, n)
            Ut, Uv = self.wload(self.wsrc(self.w_up, l, D, jg * 512, n), 8, n)
            for t in range(n // 128):
                j = jg * 4 + t
                psG = self.proj_feat(Gt, Gv, t * 128)
                psU = self.proj_feat(Ut, Uv, t * 128)
                sg = self.sgs[self.sgi % 2]; self.sgi += 1
                kb.op("act", lambda e, psG=psG, sg=sg: e.activation(out=sg[:], in_=psG[:, :], func=AF.Silu), reads=[psG], writes=[sg])
                kb.op("dve", lambda e, j=j, psU=psU, sg=sg: e.tensor_tensor(out=self.hT[:, j, :], in0=sg[:], in1=psU[:, :], op=ALU.mult),
                      reads=[sg, psU], writes=[self.hT])

        def pss(m):
            Wt, Wv = self.wload(self.wsrc_down(l, m), 22, 128)
            return self.proj_feat(Wt, Wv, 0, src=self.hT, kc=22)
        self.resid_ln(pss, 2)


NB = 16


class GenS(Gen):
    def _decl(self):
        Gen._decl(self)
        if not self.do_sample:
            return
        dp = self.depth
        d, o = self.din, self.dout
        self.xs = d("xs", [NB, D])
        self.st_ret = d("st_ret", [dp, NB, 4, 128, 128])
        self.st_ssm = d("st_ssm", [dp, NB, 16, 64, 128])
        self.st_conv = d("st_conv", [dp, NB, 3, 1536])
        self.ck = d("ck", [dp, NB, 128, 128])
        self.cv = d("cv", [dp, NB, 128, 128])
        self.c_rots = d("c_rots", [NB, 256])
        self.c_sel = d("c_sel", [NB, 16 * 128 + 2 * 128])
        self.c_eye = d("c_eye", [128, 256])
        self.conv_w_n = d("conv_w_n", [dp, 1, 4 * 1536])
        self.conv_b_n = d("conv_b_n", [dp, 1, 1536])
        self.ln_n = d("ln_n", [dp, 4, 1, 1024])
        self.ys = o("ys", [NB, D])
        self.ret_s = o("ret_s", [dp, NB, 4, 128, 128])
        self.ssm_s = o("ssm_s", [dp, NB, 16, 64, 128])
        self.conv_s = o("conv_s", [dp, NB, 3, 1536])
        self.k_s = o("k_s", [dp, NB, 128, 128])
        self.v_s = o("v_s", [dp, NB, 128, 128])

    def _alloc(self):
        Gen._alloc(self)
        if not self.do_sample:
            return
        nc = self.nc
        st = {"off": 0, "base": None, "size": 0}

        def region(base, size):
            st["base"], st["size"], st["off"] = base, size, 0

        def al(name, shape, dt):
            n = 1
            for x in shape[1:]:
                n *= x
            nb = (n * (4 if dt == F32 else 2) + 31) // 32 * 32
            assert st["off"] + nb <= st["size"], (name, st["off"], nb, st["size"])
            h = nc.alloc_sbuf_tensor_at(name, list(shape), dt, offset=st["base"] + st["off"])
            st["off"] += nb
            return Tile(h, None, name)
        region(self.xr_off, 16384 + 8192)
        self.sx = al("sx", [NB, 1024], F32)
        self.sxT = al("sxT", [128, 8, NB], BF16)
        self.sxconv = al("sxconv", [NB, 1536], F32)
        self.s_orT = al("s_orT", [128, 4, NB], BF16); self.s_yT = al("s_yT", [128, 8, NB], BF16); self.s_ocT = al("s_ocT", [128, 4, NB], BF16)
        self.s_mT = al("s_mT", [128, 8, NB], BF16); self.s_hT = al("s_hT", [128, 22, NB], BF16)
        self.s_sel = al("s_sel", [NB, 16 * 128 + 256], F32)
        self.s_eye = al("s_eye", [128, 256], F32)
        self.s_rot = al("s_rot", [NB, 256], F32)
        self.s_b0 = al("s_b0", [NB, 8], F32)
        A = self.abase
        region(A, self.asz)
        self.r_q = al("r_q", [NB, 512], F32); self.r_k = al("r_k", [NB, 512], F32)
        self.r_tA = al("r_tA", [NB, 4, 2, 64], F32); self.r_tB = al("r_tB", [NB, 4, 2, 64], F32)
        self.r_qb = al("r_qb", [NB, 512], BF16); self.r_kb = al("r_kb", [NB, 512], BF16); self.r_vb = al("r_vb", [NB, 512], BF16)
        self.r_g = al("r_g", [NB, 512], F32)
        self.r_qT = al("r_qT", [128, 4, NB], BF16)
        self.r_qTm = al("r_qTm", [128, 4, NB, NB], BF16)
        self.r_vbd = al("r_vbd", [NB, 4, 4, 128], BF16)
        self.r_S = [al("r_S%d" % i, [128, 4, 4, 128], F32) for i in range(2)]
        self.r_Sb = al("r_Sb", [128, NB, 4, 128], BF16)
        self.r_o = al("r_o", [NB, 4, 128], F32)
        self.r_st6 = al("r_st6", [NB, 4, 6], F32); self.r_mv = al("r_mv", [NB, 4, 2], F32)
        self.r_rs = al("r_rs", [NB, 4], F32); self.r_rsb = al("r_rsb", [NB, 4], F32)
        self.r_og = al("r_og", [NB, 4, 128], F32); self.r_ogb = al("r_ogb", [NB, 512], BF16)
        print("arena S-RET", st["off"])
        region(A, self.asz)
        self.c_w = al("c_w", [NB, 4, 1536], F32); self.c_buf = al("c_buf", [NB, 3, 1536], F32)
        self.c_new = al("c_new", [NB, 1536], F32); self.c_acc = al("c_acc", [NB, 1536], F32)
        print("arena S-CONV", st["off"])
        region(A, self.asz)
        self.d_z = al("d_z", [NB, 1024], F32)
        self.d_h = al("d_h", [128, 4, 8, 128], F32); self.d_t = al("d_t", [128, 4, 8, 128], F32)
        self.d_sm = al("d_sm", [NB, 8, 16], F32)
        self.d_xdt = al("d_xdt", [NB, 1024], F32)
        self.d_xdtT = al("d_xdtT", [128, NB, 8], F32)
        self.d_R = al("d_R", [NB, 2, 4, 128], F32)
        self.d_RA = al("d_RA", [NB, 2, NB, 8], F32)
        self.d_dA = al("d_dA", [128, NB, 8], F32)
        self.d_BC = al("d_BC", [128, 2, 4, 128], F32)
        self.d_yT = al("d_yT", [128, NB, 8], F32)
        self.d_y = al("d_y", [NB, 1024], F32); self.d_y2 = self.d_xdt
        self.d_ssq = al("d_ssq", [NB, 4], F32)
        self.d_nw = al("d_nw", [NB, 1024], F32)
        self.d_ynb = al("d_ynb", [NB, 1024], BF16)
        print("arena S-SSD", st["off"])
        region(A, self.asz)
        self.a_K = al("a_K", [128, NB, 128], F32); self.a_V = al("a_V", [128, NB, 128], F32)
        self.a_Vb = al("a_Vb", [128, NB, 2, 80], BF16)
        self.a_q = al("a_q", [NB, 512], F32); self.a_kn = al("a_kn", [NB, 128], F32); self.a_vn = al("a_vn", [NB, 2, 80], F32)
        self.a_pr = al("a_pr", [128, 4, 2, 64], F32)
        self.a_s = al("a_s", [128, NB, 8], F32)
        self.a_p = al("a_p", [128, NB, 8], BF16)
        self.a_Pm = al("a_Pm", [128, NB, 8, NB], BF16)
        self.a_sn = al("a_sn", [NB, 4, 2, 64], F32); self.a_s8 = al("a_s8", [NB, 8], F32); self.a_pn = al("a_pn", [NB, 8], F32)
        self.a_ou = al("a_ou", [NB, 8, 65], F32); self.a_t = al("a_t", [NB, 8, 65], F32)
        self.a_den = al("a_den", [NB, 8], F32)
        self.a_ob = al("a_ob", [NB, 8, 64], BF16)
        self.a_es = al("a_es", [NB, 8], F32)
        print("arena S-SWA", st["off"])
        region(A, self.asz)
        self.m_sg = al("m_sg", [NB, 512], F32); self.m_t = al("m_t", [NB, 512], F32)
        self.m_acc = al("m_acc", [NB, 1024], F32)
        self.m_b = al("m_b", [NB, 1024], BF16)
        self.m_g = al("m_g", [NB, 1024], F32); self.m_bb = al("m_bb", [NB, 1024], F32)
        self.m_st = al("m_st", [NB, 2, 6], F32); self.m_mv = al("m_mv", [NB, 2], F32); self.m_rs = al("m_rs", [NB, 2], F32)
        self.m_u = al("m_u", [NB, 1024], F32)
        self.m_h = al("m_h", [NB, 2816], F32); self.m_hb = al("m_hb", [NB, 2816], BF16)
        print("arena S-MLP", st["off"])

    def sproj(self, Wt, Wv, n, src=None, kc=8, col0=0):
        src = self.sxT if src is None else src
        ps = self.nps()
        self.kb.mm([lambda pe, k=k, ps=ps: pe.matmul(ps[0:NB, 0:n], src[:, k, :], Wv[:, k, col0:col0 + n], start=(k == 0), stop=(k == kc - 1))
                    for k in range(kc)], reads=[src, Wt], writes=[ps])
        return ps

    def s_transp(self, src_tile, src_aps, dst_tile, dst_ap):
        n = len(src_aps)
        pb = self.npb()
        self.kb.mm([lambda pe, j=j, pb=pb: pe.transpose(out=pb[:, j * NB:(j + 1) * NB], in_=src_aps[j], identity=self.id_b[0:NB, 0:NB])
                    for j in range(n)], reads=[src_tile, self.id_b], writes=[pb])
        self.kb.op("act", lambda e, pb=pb: e.copy(out=dst_ap, in_=pb[:, 0:n * NB].rearrange("p (j t) -> p j t", j=n)), reads=[pb], writes=[dst_tile])

    def s_tokmajor_to_T(self, src, ncols, dst):
        nt = ncols // 128
        for j0 in range(0, nt, 8):
            j1 = min(nt, j0 + 8)
            self.s_transp(src, [src[0:NB, j * 128:(j + 1) * 128] for j in range(j0, j1)], dst, dst[:, j0:j1, :])

    def sample_pass(self):
        kb = self.kb
        self.fence()
        kb.dma("sp", self.sx[:], self.xs.ap()[:, :], writes=[self.sx])
        kb.dma("sp", self.s_sel[:], self.c_sel.ap()[:, :], writes=[self.s_sel])
        kb.dma("sp", self.s_eye[:], self.c_eye.ap()[:, :], writes=[self.s_eye])
        kb.dma("sp", self.s_rot[:], self.c_rots.ap()[:, :], writes=[self.s_rot])
        kb.dma("sp", self.s_b0[:], self.rel_bias.ap()[0:1, 0:8].partition_broadcast(NB), writes=[self.s_b0])
        self.s_refresh_xT()
        for l in range(self.depth):
            self.load_params(0, l)
            self.s_ret(l)
            self.s_conv(l)
            self.s_ssd(l)
            self.s_swa(l)
            self.s_mlp(l)
        kb.dma("sp", self.ys.ap()[:, :], self.sx[:], reads=[self.sx], is_output=True)

    def s_refresh_xT(self):
        kb = self.kb
        kb.op("act", lambda e: e.copy(out=self.m_b[:], in_=self.sx[:]), reads=[self.sx], writes=[self.m_b])
        self.s_tokmajor_to_T(self.m_b, 1024, self.sxT)

    def s_ret(self, l):
        self.kb.phase = 's_ret'
        kb = self.kb
        self.fence()
        gam = [1.0 - 2.0 ** (-5 - h) for h in range(4)]
        for gi, off in enumerate((OQ, OK_, OV, OG)):
            Wt, Wv = self.wload(self.wsrc(self.w_in, l, D, off, 512), 8, 512)
            ps = self.sproj(Wt, Wv, 512)
            pv = ps[0:NB, :]
            if gi < 2:
                dst = self.r_q if gi == 0 else self.r_k
                psv = pv.rearrange("p (h t e) -> p h t e", h=4, t=2)
                cosb = self.s_rot[:, gi * 128:gi * 128 + 64].unsqueeze(1).unsqueeze(1).broadcast_to([NB, 4, 2, 64])
                sinb = self.s_rot[:, gi * 128 + 64:gi * 128 + 128].unsqueeze(1).broadcast_to([NB, 4, 64])
                kb.op("dve", lambda e, psv=psv, cosb=cosb: e.tensor_tensor(out=self.r_tA[:], in0=psv, in1=cosb, op=ALU.mult), reads=[ps, self.s_rot], writes=[self.r_tA])
                kb.op("dve", lambda e, psv=psv, sinb=sinb: e.tensor_tensor(out=self.r_tB[:, :, 0, :], in0=psv[:, :, 1, :], in1=sinb, op=ALU.mult), reads=[ps, self.s_rot], writes=[self.r_tB])
                kb.op("dve", lambda e, psv=psv, sinb=sinb: e.tensor_tensor(out=self.r_tB[:, :, 1, :], in0=psv[:, :, 0, :], in1=sinb, op=ALU.mult), reads=[ps, self.s_rot], writes=[self.r_tB])
                dv = dst[:].rearrange("p (h t e) -> p h t e", h=4, t=2)
                kb.op("dve", lambda e, dv=dv: e.tensor_tensor(out=dv[:, :, 0, :], in0=self.r_tA[:, :, 0, :], in1=self.r_tB[:, :, 0, :], op=ALU.subtract), reads=[self.r_tA, self.r_tB], writes=[dst])
                kb.op("dve", lambda e, dv=dv: e.tensor_tensor(out=dv[:, :, 1, :], in0=self.r_tA[:, :, 1, :], in1=self.r_tB[:, :, 1, :], op=ALU.add), reads=[self.r_tA, self.r_tB], writes=[dst])
                db = self.r_qb if gi == 0 else self.r_kb
                kb.op("act", lambda e, dst=dst, db=db: e.copy(out=db[:], in_=dst[:]), reads=[dst], writes=[db])
            elif gi == 2:
                kb.op("act", lambda e, pv=pv: e.copy(out=self.r_vb[:], in_=pv), reads=[ps], writes=[self.r_vb])
            else:
                kb.op("act", lambda e, pv=pv: e.activation(out=self.r_g[:], in_=pv, func=AF.Silu), reads=[ps], writes=[self.r_g])
        self.s_transp(self.r_qb, [self.r_qb[0:NB, h * 128:(h + 1) * 128] for h in range(4)], self.r_qT, self.r_qT[:, :, :])
        kb.op("dve", lambda e: e.tensor_tensor(out=self.r_qTm[:], in0=self.r_qT[:].unsqueeze(2).broadcast_to([128, 4, NB, NB]),
                                                in1=self.s_eye[:].rearrange("p (a b) -> p a b", a=NB).unsqueeze(1).broadcast_to([128, 4, NB, NB]), op=ALU.mult),
              reads=[self.r_qT, self.s_eye], writes=[self.r_qTm])
        sel = self.s_sel[:, 0:NB * 128].rearrange("p (b k) -> p b k", b=NB)
        for sg in range(4):
            S = self.r_S[sg % 2]
            kb.dma("sp", S[:], self.st_ret.ap()[l, sg * 4:(sg + 1) * 4].rearrange("b h d v -> d b h v"), writes=[S])
            kb.op("dve", lambda e, sg=sg: e.tensor_tensor(
                out=self.r_vbd[:], in0=self.r_vb[:].rearrange("p (h v) -> p h v", h=4).unsqueeze(2).broadcast_to([NB, 4, 4, 128]),
                in1=sel[:, sg * 4:(sg + 1) * 4, 0:1].rearrange("p b o -> p o b").unsqueeze(3).broadcast_to([NB, 4, 4, 128]), op=ALU.mult),
                reads=[self.r_vb, self.s_sel], writes=[self.r_vbd])
            for h in range(4):
                ps = self.nps()
                kb.mm([lambda pe, h=h, ps=ps: pe.matmul(ps[:, :], self.r_kb[0:NB, h * 128:(h + 1) * 128], self.r_vbd[:, h, :, :], start=True, stop=True)],
                      reads=[self.r_kb, self.r_vbd], writes=[ps])
                kb.op("dve", lambda e, h=h, ps=ps, S=S: e.scalar_tensor_tensor(out=S[:, :, h, :], in0=S[:, :, h, :], scalar=gam[h],
                                                                               in1=ps[:, :].rearrange("p (b v) -> p b v", b=4), op0=ALU.mult, op1=ALU.add),
                      reads=[S, ps], writes=[S])
            kb.op("act", lambda e, sg=sg, S=S: e.copy(out=self.r_Sb[:, sg * 4:(sg + 1) * 4, :, :], in_=S[:]), reads=[S], writes=[self.r_Sb])
            kb.dma("sp", self.ret_s.ap()[l, sg * 4:(sg + 1) * 4].rearrange("b h d v -> d b h v"), S[:], reads=[S], is_output=True)
        pso = self.nps()
        fns = []
        for h in range(4):
            for b in range(NB):
                fns.append(lambda pe, h=h, b=b: pe.matmul(pso[0:NB, h * 128:(h + 1) * 128], self.r_qTm[:, h, b, :], self.r_Sb[:, b, h, :],
                                                          start=(b == 0), stop=(b == NB - 1)))
        kb.mm(fns, reads=[self.r_qTm, self.r_Sb], writes=[pso])
        kb.op("act", lambda e: e.copy(out=self.r_o[:], in_=pso[0:NB, :].rearrange("p (h v) -> p h v", h=4)), reads=[pso], writes=[self.r_o])
        for h in range(4):
            kb.op("dve", lambda e, h=h: e.bn_stats(out=self.r_st6[:, h, :], in_=self.r_o[:, h, :]), reads=[self.r_o], writes=[self.r_st6])
        for h in range(4):
            kb.op("dve", lambda e, h=h: e.bn_aggr(out=self.r_mv[:, h, :], in_=self.r_st6[:, h, :]), reads=[self.r_st6], writes=[self.r_mv])
        kb.op("act", lambda e: e.activation(out=self.r_rs[:], in_=self.r_mv[:, :, 1], func=AF.Ln, bias=EPS, scale=1.0), reads=[self.r_mv], writes=[self.r_rs])
        kb.op("act", lambda e: e.activation(out=self.r_rsb[:], in_=self.r_rs[:], func=AF.Exp, scale=-0.5), reads=[self.r_rs], writes=[self.r_rsb])
        for h in range(4):
            kb.op("dve", lambda e, h=h: e.scalar_tensor_tensor(out=self.r_og[:, h, :], in0=self.r_o[:, h, :], scalar=self.r_mv[:, h, 0:1],
                                                                 in1=self.r_g[:, h * 128:(h + 1) * 128], op0=ALU.subtract, op1=ALU.mult),
                  reads=[self.r_o, self.r_mv, self.r_g], writes=[self.r_og])
        for h in range(4):
            kb.op("act", lambda e, h=h: e.activation(out=self.r_ogb[:, h * 128:(h + 1) * 128], in_=self.r_og[:, h, :], func=AF.Identity, scale=self.r_rsb[:, h:h + 1]),
                  reads=[self.r_og, self.r_rsb], writes=[self.r_ogb])
        self.s_tokmajor_to_T(self.r_ogb, 512, self.s_orT)

    def s_conv(self, l):
        self.kb.phase = 's_conv'
        kb = self.kb
        self.fence()
        kb.dma("sp", self.c_w[:].rearrange("p t c -> p (t c)"), self.conv_w_n.ap()[l].partition_broadcast(NB), writes=[self.c_w])
        kb.dma("sp", self.c_buf[:], self.st_conv.ap()[l], writes=[self.c_buf])
        kb.dma("sp", self.c_acc[:], self.conv_b_n.ap()[l].partition_broadcast(NB), writes=[self.c_acc])
        for g3 in range(3):
            Wt, Wv = self.wload(self.wsrc(self.w_in, l, D, OXBC + g3 * 512, 512), 8, 512)
            ps = self.sproj(Wt, Wv, 512)
            kb.op("act", lambda e, ps=ps, g3=g3: e.copy(out=self.c_new[:, g3 * 512:(g3 + 1) * 512], in_=ps[0:NB, :]), reads=[ps], writes=[self.c_new])
        kb.dma("sp", self.conv_s.ap()[l, :, 0:2, :], self.st_conv.ap()[l, :, 1:3, :], is_output=True)
        kb.dma("sp", self.conv_s.ap()[l, :, 2, :], self.c_new[:], reads=[self.c_new], is_output=True)
        for tau in range(4):
            src = self.c_buf[:, tau, :] if tau < 3 else self.c_new[:]
            kb.op("dve", lambda e, tau=tau, src=src: e.tensor_tensor(out=self.c_w[:, tau, :], in0=self.c_w[:, tau, :], in1=src, op=ALU.mult),
                  reads=[self.c_w, self.c_buf, self.c_new], writes=[self.c_w])
            kb.op("dve", lambda e, tau=tau: e.tensor_tensor(out=self.c_acc[:], in0=self.c_acc[:], in1=self.c_w[:, tau, :], op=ALU.add),
                  reads=[self.c_w, self.c_acc], writes=[self.c_acc])
        kb.op("act", lambda e: e.activation(out=self.sxconv[:], in_=self.c_acc[:], func=AF.Silu), reads=[self.c_acc], writes=[self.sxconv])

    def s_ssd(self, l):
        self.kb.phase = 's_ssd'
        kb = self.kb
        self.fence()
        kb.dma("sp", self.d_nw[:], self.norm_w.ap()[l:l + 1, :].partition_broadcast(NB), writes=[self.d_nw])
        for g2 in range(2):
            Wt, Wv = self.wload(self.wsrc(self.w_in, l, D, OZ + g2 * 512, 512), 8, 512)
            ps = self.sproj(Wt, Wv, 512)
            kb.op("act", lambda e, ps=ps, g2=g2: e.activation(out=self.d_z[:, g2 * 512:(g2 + 1) * 512], in_=ps[0:NB, :], func=AF.Silu), reads=[ps], writes=[self.d_z])
        Wt, Wv = self.wload(self.wsrc(self.w_in, l, D, ODT, 16), 8, 16)
        ps = self.sproj(Wt, Wv, 16)
        sm = self.d_sm
        DT, DA, T1, T2 = [sm[:, i, :] for i in range(4)]
        p16 = self.sm16[0:NB, :, :]
        kb.op("dve", lambda e, ps=ps: e.tensor_tensor(out=T1, in0=ps[0:NB, 0:16], in1=p16[:, 0, :], op=ALU.add), reads=[ps, self.sm16], writes=[sm])
        kb.op("act", lambda e: e.activation(out=T2, in_=T1, func=AF.Exp), reads=[sm], writes=[sm])
        kb.op("act", lambda e: e.activation(out=DT, in_=T2, func=AF.Ln, bias=1.0, scale=1.0), reads=[sm], writes=[sm])
        kb.op("dve", lambda e: e.tensor_tensor(out=T1, in0=DT, in1=p16[:, 1, :], op=ALU.mult), reads=[sm, self.sm16], writes=[sm])
        kb.op("act", lambda e: e.activation(out=DA, in_=T1, func=AF.Exp), reads=[sm], writes=[sm])
        xs = self.sxconv[:, 0:1024]
        kb.op("dve", lambda e: e.tensor_tensor(out=self.d_xdt[:].rearrange("p (hh g q) -> p g hh q", hh=8, g=2), in0=xs.rearrange("p (g hh q) -> p g hh q", g=2, hh=8),
                                                in1=DT.rearrange("p (g hh) -> p g hh", g=2).unsqueeze(3).broadcast_to([NB, 2, 8, 64]), op=ALU.mult), reads=[self.sxconv, sm], writes=[self.d_xdt])
        pst = self.nps()
        kb.mm([lambda pe, hh=hh: pe.transpose(out=pst[:, hh * NB:(hh + 1) * NB], in_=self.d_xdt[:, hh * 128:(hh + 1) * 128], identity=self.ident_f[0:NB, 0:NB]) for hh in range(8)],
              reads=[self.d_xdt, self.cf], writes=[pst])
        kb.op("act", lambda e: e.copy(out=self.d_xdtT[:].rearrange("p b hh -> p hh b"), in_=pst[:, 0:8 * NB].rearrange("p (hh b) -> p hh b", hh=8)),
              reads=[pst], writes=[self.d_xdtT])
        eye = self.s_sel[:, 0:NB * 128].rearrange("p (b k) -> p b k", b=NB)[:, :, 0]
        gsel = self.s_sel[:, NB * 128:NB * 128 + 256].rearrange("p (g k) -> p g k", g=2)
        kb.op("dve", lambda e: e.tensor_tensor(out=self.d_RA[:], in0=DA.rearrange("p (g hh) -> p g hh", g=2).unsqueeze(2).broadcast_to([NB, 2, NB, 8]),
                                                in1=eye.unsqueeze(1).unsqueeze(3).broadcast_to([NB, 2, NB, 8]), op=ALU.mult), reads=[sm, self.s_sel], writes=[self.d_RA])
        psa = self.nps()
        kb.mm([lambda pe, g=g: pe.matmul(psa[:, 0:NB * 8], gsel[:, g, :], self.d_RA[:, g, :, :], start=(g == 0), stop=(g == 1)) for g in range(2)],
              reads=[self.s_sel, self.d_RA], writes=[psa])
        kb.op("act", lambda e: e.copy(out=self.d_dA[:], in_=psa[:, 0:NB * 8].rearrange("p (b hh) -> p b hh", b=NB)), reads=[psa], writes=[self.d_dA])
        Bc = self.sxconv[:, 1024:1536].rearrange("p (t g n) -> p t g n", t=2, g=2)
        for sg in range(4):
            bs = slice(sg * 4, (sg + 1) * 4)
            H = self.d_h
            for g in range(2):
                for bb in range(4):
                    kb.dma("sp", H[g * 64:(g + 1) * 64, bb, :, :], self.st_ssm.ap()[l, sg * 4 + bb, g * 8:(g + 1) * 8].rearrange("hh p n -> p hh n"), writes=[H])
            for t in range(2):
                kb.op("dve", lambda e, sg=sg, t=t: e.tensor_tensor(out=self.d_R[:], in0=Bc[:, t, :, :].unsqueeze(2).broadcast_to([NB, 2, 4, 128]),
                                                                    in1=eye[:, sg * 4:(sg + 1) * 4].unsqueeze(1).unsqueeze(3).broadcast_to([NB, 2, 4, 128]), op=ALU.mult),
                      reads=[self.sxconv, self.s_sel], writes=[self.d_R])
                psb_ = self.nps()
                kb.mm([lambda pe, g=g, psb_=psb_: pe.matmul(psb_[:, :], gsel[:, g, :], self.d_R[:, g, :, :], start=(g == 0), stop=(g == 1)) for g in range(2)],
                      reads=[self.s_sel, self.d_R], writes=[psb_])
                kb.op("act", lambda e, t=t, psb_=psb_: e.copy(out=self.d_BC[:, t, :, :], in_=psb_[:, :].rearrange("p (b n) -> p b n", b=4)), reads=[psb_], writes=[self.d_BC])
            kb.op("dve", lambda e, bs=bs: e.tensor_tensor(out=H[:], in0=H[:], in1=self.d_dA[:, bs, :].unsqueeze(3).broadcast_to([128, 4, 8, 128]), op=ALU.mult),
                  reads=[H, self.d_dA], writes=[H])
            kb.op("dve", lambda e, bs=bs: e.tensor_tensor(out=self.d_t[:], in0=self.d_xdtT[:, bs, :].unsqueeze(3).broadcast_to([128, 4, 8, 128]),
                                                           in1=self.d_BC[:, 0, :, :].unsqueeze(2).broadcast_to([128, 4, 8, 128]), op=ALU.mult),
                  reads=[self.d_xdtT, self.d_BC], writes=[self.d_t])
            kb.op("dve", lambda e: e.tensor_tensor(out=H[:], in0=H[:], in1=self.d_t[:], op=ALU.add), reads=[H, self.d_t], writes=[H])
            for g in range(2):
                for bb in range(4):
                    kb.dma("sp", self.ssm_s.ap()[l, sg * 4 + bb, g * 8:(g + 1) * 8].rearrange("hh p n -> p hh n"), H[g * 64:(g + 1) * 64, bb, :, :], reads=[H], is_output=True)
            kb.op("dve", lambda e: e.tensor_tensor(out=self.d_t[:], in0=H[:], in1=self.d_BC[:, 1, :, :].unsqueeze(2).broadcast_to([128, 4, 8, 128]), op=ALU.mult),
                  reads=[H, self.d_BC], writes=[self.d_t])
            kb.op("dve", lambda e, bs=bs: e.tensor_reduce(out=self.d_yT[:, bs, :], in_=self.d_t[:], axis=AX.X, op=ALU.add), reads=[self.d_t], writes=[self.d_yT])
        psy = [self.nps(), self.nps()]
        for half in range(2):
            kb.mm([lambda pe, hh=hh, half=half: pe.transpose(out=psy[half][0:NB, (hh % 4) * 128:(hh % 4 + 1) * 128], in_=self.d_yT[:, :, hh], identity=self.ident_f)
                   for hh in range(half * 4, half * 4 + 4)], reads=[self.d_yT, self.cf], writes=[psy[half]])
            yv = self.d_y[:].rearrange("p (g hh q) -> p hh g q", g=2, hh=8)
            kb.op("act", lambda e, half=half, yv=yv: e.copy(out=yv[:, half * 4:half * 4 + 4, :, :], in_=psy[half][0:NB, :].rearrange("p (hh g q) -> p hh g q", hh=4, g=2)),
                  reads=[psy[half]], writes=[self.d_y])
        kb.op("dve", lambda e: e.tensor_tensor(out=self.d_y2[:].rearrange("p (h q) -> p h q", h=16), in0=xs.rearrange("p (h q) -> p h q", h=16),
                                                in1=p16[:, 2, :].unsqueeze(2).broadcast_to([NB, 16, 64]), op=ALU.mult), reads=[self.sxconv, self.sm16], writes=[self.d_y2])
        kb.op("dve", lambda e: e.tensor_tensor(out=self.d_y[:], in0=self.d_y[:], in1=self.d_y2[:], op=ALU.add), reads=[self.d_y, self.d_y2], writes=[self.d_y])
        kb.op("dve", lambda e: e.tensor_tensor(out=self.d_y[:], in0=self.d_y[:], in1=self.d_z[:], op=ALU.mult), reads=[self.d_y, self.d_z], writes=[self.d_y])
        for g in range(2):
            gs = slice(g * 512, (g + 1) * 512)
            kb.op("act", lambda e, g=g, gs=gs: e.activation(out=self.d_y2[:, gs], in_=self.d_y[:, gs], func=AF.Square, accum_out=self.d_ssq[:, g:g + 1]),
                  reads=[self.d_y], writes=[self.d_y2, self.d_ssq])
        kb.op("act", lambda e: e.activation(out=self.d_ssq[:, 2:4], in_=self.d_ssq[:, 0:2], func=AF.Ln, bias=EPS, scale=1.0 / 512), reads=[self.d_ssq], writes=[self.d_ssq])
        kb.op("act", lambda e: e.activation(out=self.d_ssq[:, 2:4], in_=self.d_ssq[:, 2:4], func=AF.Exp, scale=-0.5), reads=[self.d_ssq], writes=[self.d_ssq])
        for g in range(2):
            gs = slice(g * 512, (g + 1) * 512)
            kb.op("dve", lambda e, g=g, gs=gs: e.scalar_tensor_tensor(out=self.d_ynb[:, gs], in0=self.d_y[:, gs], scalar=self.d_ssq[:, 2 + g:3 + g],
                                                                       in1=self.d_nw[:, gs], op0=ALU.mult, op1=ALU.mult),
                  reads=[self.d_y, self.d_ssq, self.d_nw], writes=[self.d_ynb])
        self.s_tokmajor_to_T(self.d_ynb, 1024, self.s_yT)

    def s_swa(self, l):
        self.kb.phase = 's_swa'
        kb = self.kb
        self.fence()
        kb.dma("sp", self.a_K[:], self.ck.ap()[l].rearrange("b k e -> k b e"), writes=[self.a_K])
        kb.dma("sp", self.a_V[:], self.cv.ap()[l].rearrange("b k e -> k b e"), writes=[self.a_V])
        kb.dma("sp", self.a_es[:], self.sinks.ap()[l:l + 1, :].partition_broadcast(NB), writes=[self.a_es])
        kb.op("act", lambda e: e.activation(out=self.a_es[:], in_=self.a_es[:], func=AF.Exp), reads=[self.a_es], writes=[self.a_es])
        kb.dma("sp", self.k_s.ap()[l, :, 0:127, :], self.ck.ap()[l, :, 1:128, :], is_output=True)
        kb.dma("sp", self.v_s.ap()[l, :, 0:127, :], self.cv.ap()[l, :, 1:128, :], is_output=True)
        Wt, Wv = self.wload(self.wsrc(self.w_in, l, D, OQC, 512), 8, 512)
        ps = self.sproj(Wt, Wv, 512)
        kb.op("act", lambda e, ps=ps: e.copy(out=self.a_q[:], in_=ps[0:NB, :]), reads=[ps], writes=[self.a_q])
        Wt, Wv = self.wload(self.wsrc(self.w_in, l, D, OKC, 256), 8, 256)
        ps = self.sproj(Wt, Wv, 256)
        kb.op("act", lambda e, ps=ps: e.copy(out=self.a_kn[:], in_=ps[0:NB, 0:128]), reads=[ps], writes=[self.a_kn])
        kb.op("act", lambda e: e.activation(out=self.a_vn[:, :, 64:65], in_=self.cf[0:NB, 0:2].unsqueeze(2), func=AF.Identity, scale=0.0, bias=1.0), reads=[self.cf], writes=[self.a_vn])
        kb.op("act", lambda e, ps=ps: e.copy(out=self.a_vn[:, :, 0:64], in_=ps[0:NB, 128:256].rearrange("p (g e) -> p g e", g=2)), reads=[ps], writes=[self.a_vn])
        kb.dma("sp", self.k_s.ap()[l, :, 127, :], self.a_kn[:], reads=[self.a_kn], is_output=True)
        kb.dma("sp", self.v_s.ap()[l, :, 127, :].rearrange("b (g e) -> b g e", g=2), self.a_vn[:, :, 0:64], reads=[self.a_vn], is_output=True)
        kb.op("act", lambda e: e.activation(out=self.a_Vb[:, :, :, 64:65], in_=self.cf[:, 0:2 * NB].rearrange("p (b g o) -> p b g o", b=NB, g=2), func=AF.Identity, scale=0.0, bias=1.0),
              reads=[self.cf], writes=[self.a_Vb])
        kb.op("act", lambda e: e.copy(out=self.a_Vb[:, :, :, 0:64], in_=self.a_V[:].rearrange("p b (g e) -> p b g e", g=2)), reads=[self.a_V], writes=[self.a_Vb])
        selq = self.s_sel[:, 0:NB * 128].rearrange("p (b k) -> p b k", b=NB)
        sv = self.a_s[:].rearrange("p b (half j) -> p b j half", half=2)
        for b in range(NB):
            psq = self.nps()
            kb.mm([lambda pe, b=b, psq=psq: pe.matmul(psq[:, :], selq[:, b, :], self.a_q[:], start=True, stop=True)], reads=[self.s_sel, self.a_q], writes=[psq])
            kb.op("dve", lambda e, b=b, psq=psq: e.tensor_tensor(out=self.a_pr[:], in0=psq[:, :].rearrange("p (j g e) -> p j g e", j=4, g=2),
                                                                 in1=self.a_K[:, b, :].rearrange("p (g e) -> p g e", g=2).unsqueeze(1).broadcast_to([128, 4, 2, 64]), op=ALU.mult),
                  reads=[psq, self.a_K], writes=[self.a_pr])
            kb.op("dve", lambda e, b=b: e.tensor_reduce(out=sv[:, b, :, :], in_=self.a_pr[:], axis=AX.X, op=ALU.add), reads=[self.a_pr], writes=[self.a_s])
        kb.op("dve", lambda e: e.scalar_tensor_tensor(out=self.a_s[:], in0=self.a_s[:], scalar=0.125, in1=self.btab[:, 0, :, 0].unsqueeze(1).broadcast_to([128, NB, 8]),
                                                      op0=ALU.mult, op1=ALU.add), reads=[self.a_s, self.btab], writes=[self.a_s])
        kb.op("act", lambda e: e.activation(out=self.a_p[:], in_=self.a_s[:], func=AF.Exp), reads=[self.a_s], writes=[self.a_p])
        kb.op("dve", lambda e: e.tensor_tensor(out=self.a_Pm[:], in0=self.a_p[:].unsqueeze(3).broadcast_to([128, NB, 8, NB]),
                                                in1=self.s_eye[:].rearrange("p (a b) -> p a b", a=NB).unsqueeze(2).broadcast_to([128, NB, 8, NB]), op=ALU.mult),
              reads=[self.a_p, self.s_eye], writes=[self.a_Pm])
        pso = [self.nps(), self.nps()]
        for g in range(2):
            fns = []
            for h4 in range(4):
                for b in range(NB):
                    fns.append(lambda pe, g=g, h4=h4, b=b: pe.matmul(pso[g][0:NB, h4 * 65:(h4 + 1) * 65], self.a_Pm[:, b, g * 4 + h4, :], self.a_Vb[:, b, g, 0:65],
                                                                    start=(b == 0), stop=(b == NB - 1)))
            kb.mm(fns, reads=[self.a_Pm, self.a_Vb], writes=[pso[g]])
            kb.op("act", lambda e, g=g: e.copy(out=self.a_ou[:, g * 4:(g + 1) * 4, :], in_=pso[g][0:NB, 0:260].rearrange("p (h e) -> p h e", h=4)), reads=[pso[g]], writes=[self.a_ou])
        qv = self.a_q[:].rearrange("p (j g e) -> p j g e", j=4, g=2)
        kb.op("dve", lambda e: e.tensor_tensor(out=self.a_sn[:], in0=qv, in1=self.a_kn[:].rearrange("p (g e) -> p g e", g=2).unsqueeze(1).broadcast_to([NB, 4, 2, 64]), op=ALU.mult),
              reads=[self.a_q, self.a_kn], writes=[self.a_sn])
        kb.op("dve", lambda e: e.tensor_reduce(out=self.a_s8[:].rearrange("p (half j) -> p j half", half=2), in_=self.a_sn[:], axis=AX.X, op=ALU.add), reads=[self.a_sn], writes=[self.a_s8])
        kb.op("dve", lambda e: e.scalar_tensor_tensor(out=self.a_s8[:], in0=self.a_s8[:], scalar=0.125, in1=self.s_b0[:], op0=ALU.mult, op1=ALU.add), reads=[self.a_s8, self.s_b0], writes=[self.a_s8])
        kb.op("act", lambda e: e.activation(out=self.a_pn[:], in_=self.a_s8[:], func=AF.Exp), reads=[self.a_s8], writes=[self.a_pn])
        kb.op("dve", lambda e: e.tensor_tensor(out=self.a_t[:].rearrange("p (g j) e -> p g j e", g=2), in0=self.a_pn[:].rearrange("p (g j) -> p g j", g=2).unsqueeze(3).broadcast_to([NB, 2, 4, 65]),
                                                in1=self.a_vn[:, :, 0:65].unsqueeze(2).broadcast_to([NB, 2, 4, 65]), op=ALU.mult), reads=[self.a_pn, self.a_vn], writes=[self.a_t])
        kb.op("dve", lambda e: e.tensor_tensor(out=self.a_ou[:], in0=self.a_ou[:], in1=self.a_t[:], op=ALU.add), reads=[self.a_ou, self.a_t], writes=[self.a_ou])
        kb.op("dve", lambda e: e.tensor_tensor(out=self.a_den[:], in0=self.a_ou[:, :, 64], in1=self.a_es[:], op=ALU.add), reads=[self.a_ou, self.a_es], writes=[self.a_den])
        kb.op("dve", lambda e: e.reciprocal(out=self.a_den[:], in_=self.a_den[:]), reads=[self.a_den], writes=[self.a_den])
        kb.op("dve", lambda e: e.tensor_tensor(out=self.a_ob[:], in0=self.a_ou[:, :, 0:64], in1=self.a_den[:].unsqueeze(2).broadcast_to([NB, 8, 64]), op=ALU.mult),
              reads=[self.a_ou, self.a_den], writes=[self.a_ob])
        self.s_tokmajor_to_T(self.a_ob, 512, self.s_ocT) if False else None
        obv = self.a_ob[:].rearrange("p h e -> p (h e)")
        self.s_transp(self.a_ob, [obv[:, j * 128:(j + 1) * 128] for j in range(4)], self.s_ocT, self.s_ocT[:, :, :])

    def s_ln(self, l, gi, pss_fn):
        kb = self.kb
        kb.dma("sp", self.m_g[:], self.ln_n.ap()[l, gi].partition_broadcast(NB), writes=[self.m_g])
        kb.dma("sp", self.m_bb[:], self.ln_n.ap()[l, gi + 1].partition_broadcast(NB), writes=[self.m_bb])
        for j in range(2):
            js = slice(j * 512, (j + 1) * 512)
            ps = pss_fn(j)
            kb.op("dve", lambda e, js=js, ps=ps: e.scalar_tensor_tensor(out=self.sx[:, js], in0=self.sx[:, js], scalar=ALPHA, in1=ps[0:NB, :], op0=ALU.mult, op1=ALU.add),
                  reads=[self.sx, ps], writes=[self.sx])
            kb.op("dve", lambda e, j=j, js=js: e.bn_stats(out=self.m_st[:, j, :], in_=self.sx[:, js]), reads=[self.sx], writes=[self.m_st])
        kb.op("dve", lambda e: e.bn_aggr(out=self.m_mv[:], in_=self.m_st[:].rearrange("p a b -> p (a b)")), reads=[self.m_st], writes=[self.m_mv])
        kb.op("act", lambda e: e.activation(out=self.m_rs[:, 0:1], in_=self.m_mv[:, 1:2], func=AF.Ln, bias=EPS, scale=1.0), reads=[self.m_mv], writes=[self.m_rs])
        kb.op("act", lambda e: e.activation(out=self.m_rs[:, 1:2], in_=self.m_rs[:, 0:1], func=AF.Exp, scale=-0.5), reads=[self.m_rs], writes=[self.m_rs])
        kb.op("dve", lambda e: e.tensor_scalar(out=self.m_u[:], in0=self.sx[:], scalar1=self.m_mv[:, 0:1], scalar2=self.m_rs[:, 1:2], op0=ALU.subtract, op1=ALU.mult),
              reads=[self.sx, self.m_mv, self.m_rs], writes=[self.m_u])
        kb.op("dve", lambda e: e.tensor_tensor(out=self.m_u[:], in0=self.m_u[:], in1=self.m_g[:], op=ALU.mult), reads=[self.m_u, self.m_g], writes=[self.m_u])
        kb.op("dve", lambda e: e.tensor_tensor(out=self.sx[:], in0=self.m_u[:], in1=self.m_bb[:], op=ALU.add), reads=[self.m_u, self.m_bb], writes=[self.sx])
        self.s_refresh_xT()

    def s_mlp(self, l):
        self.kb.phase = 's_mlp'
        kb = self.kb
        self.fence()
        brs = [(self.w_br_ret, 512, 4, self.s_orT), (self.w_br_ssd, 1024, 8, self.s_yT), (self.w_br_swa, 512, 4, self.s_ocT)]
        for b, (wbr, K, kc, src) in enumerate(brs):
            for j in range(2):
                js = slice(j * 512, (j + 1) * 512)
                Gt, Gv = self.wload(self.wsrc(self.w_in, l, D, OGATE + b * 1024 + j * 512, 512), 8, 512)
                Bt, Bv = self.wload(self.wsrc(wbr, l, K, j * 512, 512), kc, 512)
                psG = self.sproj(Gt, Gv, 512)
                psB = self.sproj(Bt, Bv, 512, src=src, kc=kc)
                kb.op("act", lambda e, psG=psG: e.activation(out=self.m_sg[:], in_=psG[0:NB, :], func=AF.Sigmoid), reads=[psG], writes=[self.m_sg])
                if b == 0:
                    kb.op("dve", lambda e, js=js, psB=psB: e.tensor_tensor(out=self.m_acc[:, js], in0=self.m_sg[:], in1=psB[0:NB, :], op=ALU.mult), reads=[self.m_sg, psB], writes=[self.m_acc])
                else:
                    kb.op("dve", lambda e, psB=psB: e.tensor_tensor(out=self.m_t[:], in0=self.m_sg[:], in1=psB[0:NB, :], op=ALU.mult), reads=[self.m_sg, psB], writes=[self.m_t])
                    kb.op("dve", lambda e, js=js: e.tensor_tensor(out=self.m_acc[:, js], in0=self.m_acc[:, js], in1=self.m_t[:], op=ALU.add), reads=[self.m_acc, self.m_t], writes=[self.m_acc])
        kb.op("act", lambda e: e.copy(out=self.m_b[:], in_=self.m_acc[:]), reads=[self.m_acc], writes=[self.m_b])
        self.s_tokmajor_to_T(self.m_b, 1024, self.s_mT)

        def pss1(j):
            Wt, Wv = self.wload(self.wsrc(self.w_out, l, D, j * 512, 512), 8, 512)
            return self.sproj(Wt, Wv, 512, src=self.s_mT)
        self.s_ln(l, 0, pss1)
        for jg in range(6):
            n = 512 if jg < 5 else 256
            Gt, Gv = self.wload(self.wsrc(self.w_gate, l, D, jg * 512, n), 8, n)
            Ut, Uv = self.wload(self.wsrc(self.w_up, l, D, jg * 512, n), 8, n)
            psG = self.sproj(Gt, Gv, n)
            psU = self.sproj(Ut, Uv, n)
            kb.op("act", lambda e, psG=psG, n=n: e.activation(out=self.m_sg[:, 0:n], in_=psG[0:NB, 0:n], func=AF.Silu), reads=[psG], writes=[self.m_sg])
            kb.op("dve", lambda e, psU=psU, n=n, jg=jg: e.tensor_tensor(out=self.m_hb[:, jg * 512:jg * 512 + n], in0=self.m_sg[:, 0:n], in1=psU[0:NB, 0:n], op=ALU.mult),
                  reads=[self.m_sg, psU], writes=[self.m_hb])
        self.s_tokmajor_to_T(self.m_hb, 2816, self.s_hT)
        W = {}

        def pss2(j):
            ps = self.nps()
            for t in range(4):
                m = j * 4 + t
                Wt, Wv = self.wload(self.wsrc_down(l, m), 22, 128)
                self.kb.mm([lambda pe, k=k, t=t, Wv=Wv: pe.matmul(ps[0:NB, t * 128:(t + 1) * 128], self.s_hT[:, k, :], Wv[:, k, :], start=(k == 0), stop=(k == 21))
                            for k in range(22)], reads=[self.s_hT, Wt], writes=[ps])
            return ps
        self.s_ln(l, 2, pss2)


def consts(L):
    f32 = np.float32
    pos = np.arange(L, dtype=f32)
    inv = (np.float32(10000.0) ** (-np.arange(64, dtype=f32) / np.float32(64))).astype(f32)
    ang = (pos[:, None] * inv[None, :]).astype(f32)
    cos, sin = np.cos(ang).astype(f32), np.sin(ang).astype(f32)
    s = f32(128 ** -0.5)
    c_rot = np.concatenate([cos, sin, cos * s, sin * s], axis=1).astype(f32)
    i = np.arange(128, dtype=np.float64)
    lg = np.log(1.0 - 2.0 ** (-5.0 - np.arange(4, dtype=np.float64)))
    rel = i[None, :] - i[:, None]
    dmatT = np.where(rel[:, None, :] >= 0, np.exp(lg[None, :, None] * np.maximum(rel[:, None, :], 0)), 0.0)
    kdec = np.exp(lg[None, :] * (127 - i)[:, None])
    qdec = np.exp(lg[None, :] * (i + 1.0)[:, None])
    k = np.arange(128)[:, None]; q = np.arange(128)[None, :]
    m0 = np.where(q > k, NEG, 0.0)
    m1 = np.where(q < k, NEG, 0.0)
    c_f32 = np.concatenate([np.eye(128), np.triu(np.ones((128, 128))), np.ones((128, 128)), np.zeros((128, 128)),
                            dmatT.reshape(128, 512), kdec, qdec, m0, m1], axis=1).astype(f32)
    def bucket(dist):
        df = np.maximum(dist, 1).astype(f32)
        large = 16 + (np.log(df / f32(16)).astype(f32) / f32(math.log(8.0)) * f32(16)).astype(np.int32)
        large = np.minimum(large, 31)
        return np.where(dist < 16, dist, large)
    oh = np.zeros((128, 2, 128, 32), f32)
    d0 = q - k + 128
    d1 = q - k
    for hf, dd in ((0, d0), (1, d1)):
        valid = (dd >= 0) & (dd <= 128)
        b = bucket(np.maximum(dd, 0))
        kk, qq = np.nonzero(valid)
        oh[kk, hf, qq, b[kk, qq]] = 1.0
    c_oh = oh.reshape(128, -1).astype(ml_dtypes.bfloat16)
    return c_rot, c_f32, c_oh


def qc_perm():
    idx = np.arange(8464)
    base = 4624
    new = []
    for t in range(4):
        new += list(range(base + t * 64, base + (t + 1) * 64)) + list(range(base + (4 + t) * 64, base + (5 + t) * 64))
    idx[base:base + 512] = np.array(new)
    return idx


def prep_weights(inp, depth):
    f = lambda a: np.ascontiguousarray(a, dtype=np.float32)
    dp = depth
    w = {}
    w["w_in"] = f(inp["w_in"][:dp][:, :, qc_perm()].reshape(dp * 1024, 8464))
    w["conv_w"] = f(np.transpose(inp["conv_w"][:dp].reshape(dp, 4, 12, 128), (0, 3, 2, 1)))
    w["conv_b"] = f(np.transpose(inp["conv_b"][:dp].reshape(dp, 12, 128), (0, 2, 1)))
    for n in ("dt_bias", "a_log", "d_skip", "ssd_norm_w", "sinks"):
        w[n] = f(inp[n][:dp])
    w["rel_bias"] = f(inp["rel_bias"].reshape(1, 256))
    w["w_br_ret"] = f(inp["w_br_ret"][:dp].reshape(dp * 512, 1024))
    w["w_br_ssd"] = f(inp["w_br_ssd"][:dp].reshape(dp * 1024, 1024))
    w["w_br_swa"] = f(inp["w_br_swa"][:dp].reshape(dp * 512, 1024))
    w["w_out"] = f(inp["w_out"][:dp].reshape(dp * 1024, 1024))
    lnp = np.stack([np.transpose(inp[n][:dp].reshape(dp, 8, 128), (0, 2, 1)) for n in ("ln1_g", "ln1_b", "ln2_g", "ln2_b")], axis=2)
    w["lnp"] = f(lnp)
    w["w_ffn_gate"] = f(inp["w_ffn_gate"][:dp].reshape(dp * 1024, 2816))
    w["w_ffn_up"] = f(inp["w_ffn_up"][:dp].reshape(dp * 1024, 2816))
    w["w_ffn_down"] = f(np.transpose(inp["w_ffn_down"][:dp].reshape(dp, 22, 128, 8, 128), (0, 3, 2, 1, 4)).reshape(dp * 8 * 128, 2816))
    return w


def core_inputs(inp, core, depth, TB, do_sample=True, x_prompt_row=None, sample_rows=None, w=None, nblk=None):
    f = lambda a: np.ascontiguousarray(a, dtype=np.float32)
    dp = depth
    m = dict(w if w is not None else prep_weights(inp, depth))
    L = (nblk or 1) * TB
    xr = core if x_prompt_row is None else x_prompt_row
    m["xp"] = f(inp["x_prompt"][xr, :L])
    c_rot, c_f32, c_oh = consts(max(L, 8192 + 1))
    m["c_rot"] = np.ascontiguousarray(c_rot[:L]); m["c_f32"] = c_f32; m["c_oh"] = c_oh
    if do_sample:
        sr = sample_rows if sample_rows is not None else slice(core * 16, core * 16 + 16)
        m["xs"] = f(inp["x_sample"][sr, 0])
        m["st_ret"] = f(inp["state_ret"][:dp, sr])
        m["st_ssm"] = f(inp["state_ssm"][:dp, sr])
        m["st_conv"] = f(inp["state_conv"][:dp, sr])
        m["ck"] = f(inp["cache_swa_k"][:dp, sr].reshape(dp, 16, 128, 128))
        m["cv"] = f(inp["cache_swa_v"][:dp, sr].reshape(dp, 16, 128, 128))
        m["c_rots"] = np.ascontiguousarray(np.broadcast_to(c_rot[8192:8193], (16, 256)))
        sel = np.zeros((16, 16, 128), np.float32)
        for b in range(16):
            sel[b, b, :] = 1.0
        gsel = np.zeros((16, 2, 128), np.float32)
        gsel[:, 0, 0:64] = 1.0; gsel[:, 1, 64:128] = 1.0
        m["c_sel"] = np.concatenate([sel.reshape(16, -1), gsel.reshape(16, -1)], axis=1)
        m["c_eye"] = np.ascontiguousarray(np.broadcast_to(np.eye(16, dtype=np.float32).reshape(1, 256), (128, 256)))
        m["conv_w_n"] = f(inp["conv_w"][:dp].reshape(dp, 1, 4 * 1536))
        m["conv_b_n"] = f(inp["conv_b"][:dp].reshape(dp, 1, 1536))
        m["ln_n"] = f(np.stack([inp[n][:dp] for n in ("ln1_g", "ln1_b", "ln2_g", "ln2_b")], axis=1).reshape(dp, 4, 1, 1024))
    return m


def kernel(**inputs):
    inp = {k: np.asarray(v) for k, v in inputs.items()}
    dp, nblk = 4, 4
    g = GenS(depth=dp, nblk=nblk, do_sample=True)
    w = prep_weights(inp, dp)
    in_maps = []
    for c in range(8):
        m = core_inputs(inp, c, dp, TB, do_sample=True, w=w, nblk=nblk)
        in_maps.append({k: m[k] for k in g.ins})
    res = run_bass_kernel_spmd(g.nc, in_maps, core_ids=list(range(8)))
    R = res.results
    st = lambda name, axis: np.stack([np.asarray(r[name]) for r in R], axis=axis)
    cat = lambda name, axis: np.concatenate([np.asarray(r[name]) for r in R], axis=axis)
    y_prompt = st("yp", 0).astype(np.float32)
    y_sample = cat("ys", 0).reshape(128, 1, 1024).astype(np.float32)
    ret_p = st("ret_p", 1)
    ssm_p = st("ssm_p", 1).reshape(dp, 8, 16, 64, 128)
    conv_p = st("conv_p", 1)
    k_p = st("k_p", 1).reshape(dp, 8, 128, 2, 64)
    v_p = st("v_p", 1).reshape(dp, 8, 128, 2, 64)
    ret_s = cat("ret_s", 1)
    ssm_s = cat("ssm_s", 1)
    conv_s = cat("conv_s", 1)
    k_s = cat("k_s", 1).reshape(dp, 128, 128, 2, 64)
    v_s = cat("v_s", 1).reshape(dp, 128, 128, 2, 64)
    outs = (y_prompt, y_sample, ret_p, ssm_p, conv_p, k_p, v_p, ret_s, ssm_s, conv_s, k_s, v_s)
    return tuple(np.ascontiguousarray(o, dtype=np.float32) for o in outs)
```
